# Optimizing a Trainium2 kernel written in Bass

```python
import math
import jax, jax.numpy as jnp
from jax import lax
import numpy as np

D_MODEL = 2048
BATCH = 4
SEQ = 2048
DEPTH = 1
DEC_BATCH = 128
DEC_SEQ = 1
PAST_LEN = 16384
PAGE_SIZE = 128

D_S5 = D_MODEL // 2
S5_GROUP = 16
G_S5 = D_S5 // S5_GROUP
P_S5 = 64
D_RW = D_MODEL // 2
RW_HEAD = 64
H_RW = D_RW // RW_HEAD
LORA_W = 64
LORA_A = 64

RMS_EPS = 1e-6
GN_EPS = 64e-5

OFF_S5_U = 0
OFF_S5_Z = D_S5
OFF_RW = 2 * D_S5
N_SHIFT = 3 * D_RW + LORA_W + LORA_A
OFF_RW_Z = OFF_RW + N_SHIFT
OFF_GATE_S5 = OFF_RW_Z + D_RW
OFF_GATE_RW = OFF_GATE_S5 + D_MODEL
N_IN = OFF_GATE_RW + D_MODEL

kernel_name = "hybrid_s5_rwkv7_adaln_step"


def rmsnorm(x, g):
    xf = x.astype(jnp.float32)
    y = xf * lax.rsqrt(jnp.mean(xf * xf, axis=-1, keepdims=True) + RMS_EPS)
    return (y * g.astype(jnp.float32)).astype(x.dtype)


def s5_branch(u, z, x0_re, x0_im, A_re, A_im, log_step, B_re, B_im, C_re, C_im, D_skip, w_glu, b_glu):
    bsz, T, _ = u.shape
    f32 = jnp.float32
    ug = u.astype(f32).reshape(bsz, T, G_S5, S5_GROUP)
    step = jnp.exp(log_step.astype(f32))[:, None]
    lam_re = jnp.minimum(A_re.astype(f32), -1e-4)
    lam_im = A_im.astype(f32)
    mag = jnp.exp(lam_re * step)
    ab_re = mag * jnp.cos(lam_im * step)
    ab_im = mag * jnp.sin(lam_im * step)
    den = lam_re * lam_re + lam_im * lam_im
    f_re = ((ab_re - 1.0) * lam_re + ab_im * lam_im) / den
    f_im = (ab_im * lam_re - (ab_re - 1.0) * lam_im) / den
    Br, Bi = B_re.astype(f32), B_im.astype(f32)
    bb_re = f_re[:, :, None] * Br - f_im[:, :, None] * Bi
    bb_im = f_re[:, :, None] * Bi + f_im[:, :, None] * Br
    bu_re = jnp.einsum('btgc,gpc->btgp', ug, bb_re)
    bu_im = jnp.einsum('btgc,gpc->btgp', ug, bb_im)
    a_re = jnp.broadcast_to(ab_re, bu_re.shape)
    a_im = jnp.broadcast_to(ab_im, bu_im.shape)

    def combine(e1, e2):
        a1r, a1i, b1r, b1i = e1
        a2r, a2i, b2r, b2i = e2
        return (a2r * a1r - a2i * a1i,
                a2r * a1i + a2i * a1r,
                a2r * b1r - a2i * b1i + b2r,
                a2r * b1i + a2i * b1r + b2i)

    pr, pi, sr, si = lax.associative_scan(combine, (a_re, a_im, bu_re, bu_im), axis=1)
    x0r = x0_re.astype(f32)[:, None]
    x0i = x0_im.astype(f32)[:, None]
    xr = sr + pr * x0r - pi * x0i
    xi = si + pr * x0i + pi * x0r
    y = (jnp.einsum('btgp,gcp->btgc', xr, C_re.astype(f32))
         - jnp.einsum('btgp,gcp->btgc', xi, C_im.astype(f32))
         + D_skip.astype(f32) * ug)
    y = jax.nn.gelu(y.reshape(bsz, T, D_S5)).astype(u.dtype)
    y = y * jax.nn.sigmoid(y @ w_glu + b_glu)
    out = y * jax.nn.silu(z)
    return out, xr[:, -1].astype(x0_re.dtype), xi[:, -1].astype(x0_im.dtype)


def rwkv_branch(cur, prev, z, S0, mu_rw, w0, w2, a0, a2, k_k, k_a, r_k, gn_w, gn_b):
    bsz, T, _ = cur.shape
    f32 = jnp.float32
    xs = cur + (prev - cur) * mu_rw
    r = xs[..., :D_RW]
    k = xs[..., D_RW:2 * D_RW]
    v = xs[..., 2 * D_RW:3 * D_RW]
    wd = xs[..., 3 * D_RW:3 * D_RW + LORA_W]
    ad = xs[..., 3 * D_RW + LORA_W:]
    w = -jax.nn.softplus(-(w0 + jnp.tanh(wd) @ w2)) - 0.5
    decay = jnp.exp(-jnp.exp(w.astype(f32)))
    a = jax.nn.sigmoid((a0 + ad @ a2).astype(f32))
    hs = lambda t: t.astype(f32).reshape(bsz, T, H_RW, RW_HEAD)
    r, k, v, decay, a = hs(r), hs(k), hs(v), hs(decay), hs(a)
    kk = k * k_k.astype(f32).reshape(H_RW, RW_HEAD)
    kk = kk / jnp.maximum(jnp.sqrt(jnp.sum(kk * kk, axis=-1, keepdims=True)), 1e-12)
    k = k * (1.0 + (a - 1.0) * k_a.astype(f32).reshape(H_RW, RW_HEAD))
    b = kk * a

    def step(S, inp):
        r_t, d_t, k_t, v_t, kk_t, b_t = inp
        sa = jnp.einsum('bhij,bhj->bhi', S, -kk_t)
        S = S * d_t[:, :, None, :] + sa[..., None] * b_t[:, :, None, :] + v_t[..., None] * k_t[:, :, None, :]
        o = jnp.einsum('bhij,bhj->bhi', S, r_t)
        return S, o

    tm = lambda t: jnp.moveaxis(t, 1, 0)
    S_fin, o = lax.scan(step, S0.astype(f32), (tm(r), tm(decay), tm(k), tm(v), tm(kk), tm(b)))
    o = jnp.moveaxis(o, 0, 1)
    mu = jnp.mean(o, axis=-1, keepdims=True)
    var = jnp.mean((o - mu) ** 2, axis=-1, keepdims=True)
    o = (o - mu) * lax.rsqrt(var + GN_EPS)
    o = o * gn_w.astype(f32).reshape(H_RW, RW_HEAD) + gn_b.astype(f32).reshape(H_RW, RW_HEAD)
    o = o + jnp.sum(r * k * r_k.astype(f32), axis=-1, keepdims=True) * v
    o = o.reshape(bsz, T, D_RW).astype(z.dtype)
    out = o * jax.nn.silu(z)
    return out, S_fin.astype(S0.dtype)


def hybrid_layer(x, c, shift_prev, s5_re0, s5_im0, wkv0,
                 norm_g, w_ada, b_ada, w_in, mu_rw,
                 A_re, A_im, log_step, B_re, B_im, C_re, C_im, D_skip, w_glu, b_glu,
                 w0, w2, a0, a2, k_k, k_a, r_k, gn_w, gn_b, w_out):
    mod = jax.nn.silu(c) @ w_ada + b_ada
    sh = mod[:, None, :D_MODEL]
    sc = mod[:, None, D_MODEL:2 * D_MODEL]
    gt = mod[:, None, 2 * D_MODEL:]
    h = rmsnorm(x, norm_g) * (1.0 + sc) + sh
    h_all = jnp.concatenate([shift_prev[:, None].astype(h.dtype), h], axis=1)
    p_all = h_all @ w_in
    p = p_all[:, 1:]
    p_prev = p_all[:, :-1, OFF_RW:OFF_RW_Z]
    o_s, s5_re, s5_im = s5_branch(p[..., OFF_S5_U:OFF_S5_Z], p[..., OFF_S5_Z:OFF_RW], s5_re0, s5_im0,
                                  A_re, A_im, log_step, B_re, B_im, C_re, C_im, D_skip, w_glu, b_glu)
    o_r, wkv = rwkv_branch(p[..., OFF_RW:OFF_RW_Z], p_prev, p[..., OFF_RW_Z:OFF_GATE_S5], wkv0,
                           mu_rw, w0, w2, a0, a2, k_k, k_a, r_k, gn_w, gn_b)
    mixed = (jax.nn.sigmoid(p[..., OFF_GATE_S5:OFF_GATE_RW]) * (o_s @ w_out[:D_S5])
             + jax.nn.sigmoid(p[..., OFF_GATE_RW:]) * (o_r @ w_out[D_S5:]))
    x = x + gt * mixed
    return x, h[:, -1], s5_re, s5_im, wkv


def setup_inputs(seed: int = 0) -> dict:
    key = jax.random.key(seed)
    ks = jax.random.split(key, 40)
    f32 = jnp.float32
    nrm = lambda k, s, sc: jax.random.normal(k, s, f32) * sc
    L = DEPTH
    inp = {}
    inp["x_prompt"] = nrm(ks[0], (BATCH, SEQ, D_MODEL), 1.0)
    inp["x_sample"] = nrm(ks[1], (DEC_BATCH, DEC_SEQ, D_MODEL), 1.0)
    inp["c_prompt"] = nrm(ks[2], (BATCH, D_MODEL), 1.0)
    inp["c_sample"] = nrm(ks[3], (DEC_BATCH, D_MODEL), 1.0)
    inp["state_s5_re"] = nrm(ks[4], (L, DEC_BATCH, G_S5, P_S5), 0.1)
    inp["state_s5_im"] = nrm(ks[5], (L, DEC_BATCH, G_S5, P_S5), 0.1)
    inp["state_wkv"] = nrm(ks[6], (L, DEC_BATCH, H_RW, RW_HEAD, RW_HEAD), 0.1)
    inp["state_shift"] = nrm(ks[7], (L, DEC_BATCH, D_MODEL), 1.0)
    inp["norm_g"] = 1.0 + nrm(ks[8], (L, D_MODEL), 0.01)
    inp["w_ada"] = nrm(ks[9], (L, D_MODEL, 3 * D_MODEL), 0.5 * D_MODEL ** -0.5)
    inp["b_ada"] = nrm(ks[10], (L, 3 * D_MODEL), 0.01)
    inp["w_in"] = nrm(ks[11], (L, D_MODEL, N_IN), D_MODEL ** -0.5)
    inp["mu_rw"] = jax.random.uniform(ks[12], (L, N_SHIFT), f32)
    inp["A_re"] = -0.5 + nrm(ks[13], (L, G_S5, P_S5), 0.01)
    inp["A_im"] = jnp.pi * jnp.arange(P_S5, dtype=f32) + nrm(ks[14], (L, G_S5, P_S5), 0.01)
    inp["log_step"] = jax.random.uniform(ks[15], (L, G_S5), f32, math.log(1e-3), math.log(1e-1))
    inp["B_re"] = nrm(ks[16], (L, G_S5, P_S5, S5_GROUP), (2.0 * S5_GROUP) ** -0.5)
    inp["B_im"] = nrm(ks[17], (L, G_S5, P_S5, S5_GROUP), (2.0 * S5_GROUP) ** -0.5)
    inp["C_re"] = nrm(ks[18], (L, G_S5, S5_GROUP, P_S5), (2.0 * P_S5) ** -0.5)
    inp["C_im"] = nrm(ks[19], (L, G_S5, S5_GROUP, P_S5), (2.0 * P_S5) ** -0.5)
    inp["D_skip"] = nrm(ks[20], (L, G_S5, S5_GROUP), 1.0)
    inp["w_glu"] = nrm(ks[21], (L, D_S5, D_S5), D_S5 ** -0.5)
    inp["b_glu"] = nrm(ks[22], (L, D_S5), 0.01)
    inp["w0"] = jax.random.uniform(ks[23], (L, D_RW), f32, -6.0, -1.0)
    inp["w2"] = nrm(ks[24], (L, LORA_W, D_RW), 0.5 * LORA_W ** -0.5)
    inp["a0"] = nrm(ks[25], (L, D_RW), 0.1)
    inp["a2"] = nrm(ks[26], (L, LORA_A, D_RW), 0.5 * LORA_A ** -0.5)
    inp["k_k"] = 0.85 + nrm(ks[27], (L, D_RW), 0.02)
    inp["k_a"] = 1.0 + nrm(ks[28], (L, D_RW), 0.02)
    inp["r_k"] = nrm(ks[29], (L, H_RW, RW_HEAD), 0.1)
    inp["gn_w"] = 1.0 + nrm(ks[30], (L, D_RW), 0.01)
    inp["gn_b"] = nrm(ks[31], (L, D_RW), 0.01)
    inp["w_out"] = nrm(ks[32], (L, D_S5 + D_RW, D_MODEL), (D_S5 + D_RW) ** -0.5)
    inp["final_g"] = 1.0 + nrm(ks[33], (D_MODEL,), 0.01)
    return inp


def reference(x_prompt, x_sample, c_prompt, c_sample, state_s5_re, state_s5_im, state_wkv, state_shift,
              norm_g, w_ada, b_ada, w_in, mu_rw, A_re, A_im, log_step, B_re, B_im, C_re, C_im, D_skip,
              w_glu, b_glu, w0, w2, a0, a2, k_k, k_a, r_k, gn_w, gn_b, w_out, final_g):
    xp, xs = x_prompt, x_sample
    p_re, p_im, p_wkv, p_shift = [], [], [], []
    s_re, s_im, s_wkv, s_shift = [], [], [], []
    dt = x_prompt.dtype
    for l in range(DEPTH):
        params = (norm_g[l], w_ada[l], b_ada[l], w_in[l], mu_rw[l],
                  A_re[l], A_im[l], log_step[l], B_re[l], B_im[l], C_re[l], C_im[l], D_skip[l], w_glu[l], b_glu[l],
                  w0[l], w2[l], a0[l], a2[l], k_k[l], k_a[l], r_k[l], gn_w[l], gn_b[l], w_out[l])
        xp, sh_p, re_p, im_p, wkv_p = hybrid_layer(
            xp, c_prompt,
            jnp.zeros((xp.shape[0], D_MODEL), dt),
            jnp.zeros((xp.shape[0], G_S5, P_S5), dt),
            jnp.zeros((xp.shape[0], G_S5, P_S5), dt),
            jnp.zeros((xp.shape[0], H_RW, RW_HEAD, RW_HEAD), dt),
            *params)
        xs, sh_s, re_s, im_s, wkv_s = hybrid_layer(
            xs, c_sample, state_shift[l], state_s5_re[l], state_s5_im[l], state_wkv[l], *params)
        p_re.append(re_p); p_im.append(im_p); p_wkv.append(wkv_p); p_shift.append(sh_p)
        s_re.append(re_s); s_im.append(im_s); s_wkv.append(wkv_s); s_shift.append(sh_s)
    y_prompt = rmsnorm(xp, final_g)
    y_sample = rmsnorm(xs, final_g)
    return (y_prompt, y_sample,
            jnp.stack(p_re), jnp.stack(p_im), jnp.stack(p_wkv), jnp.stack(p_shift),
            jnp.stack(s_re), jnp.stack(s_im), jnp.stack(s_wkv), jnp.stack(s_shift))
```

```python
import math
import os
from contextlib import ExitStack
import numpy as np
import concourse.bass as bass
import concourse.mybir as mybir
from concourse.bass_utils import run_bass_kernel_spmd

F32 = mybir.dt.float32
F32R = mybir.dt.float32r
AF = mybir.ActivationFunctionType
ALU = mybir.AluOpType
AX = mybir.AxisListType


class Cfg:
    def __init__(self, D=2048, TP=512, NSC=16, NPASS=4):
        self.NPASS = NPASS
        self.D = D; self.TP = TP; self.NSC = NSC
        self.KC = D // 128
        self.DS = D // 2; self.DR = D // 2
        self.G = self.DS // 16; self.NSB = self.G // 2; self.NUB = self.DS // 128
        self.H = self.DR // 64; self.NHP = self.DR // 128
        self.NCH = TP // 64
        self.OFF_U = 0; self.OFF_Z = self.DS; self.OFF_RW = 2 * self.DS
        self.NSH = 3 * self.DR + 128
        self.OFF_RWZ = self.OFF_RW + self.NSH
        self.OFF_G1 = self.OFF_RWZ + self.DR
        self.OFF_G2 = self.OFF_G1 + D
        self.NIN = self.OFF_G2 + D
        self.NRB = self.NSH // 128
        self.GB = min(2, self.NUB)
        self.TT = [(i, min(512, TP - i)) for i in range(0, TP, 512)]
        self.TW = TP + 2 * NSC


class TL:
    def __init__(self, sem, name):
        self.sem = sem; self.count = 0; self.name = name


class Buf:
    __slots__ = ("w", "r")

    def __init__(self):
        self.w = None; self.r = {}


class View:
    __slots__ = ("buf", "ap")

    def __init__(self, buf, ap):
        self.buf = buf; self.ap = ap

    def __getitem__(self, idx):
        return View(self.buf, self.ap[idx])

    def re(self, s, **kw):
        return View(self.buf, self.ap.rearrange(s, **kw))

    def bc(self, shape):
        return View(self.buf, self.ap.to_broadcast(shape))

    def r32(self):
        return View(self.buf, self.ap.bitcast(F32R))


class Tile:
    def __init__(self, ap, buf=None):
        self.ap = ap; self.buf = buf or Buf()

    def __getitem__(self, idx):
        return View(self.buf, self.ap[idx])

    def sub(self, idx):
        return Tile(self.ap[idx])


class Eng:
    def __init__(self, fw, name, h, tl, self_sync):
        self.fw = fw; self.name = name; self.h = h; self.tl = tl
        self.self_sync = self_sync; self.seen = {}

    def wait_for(self, deps):
        for tl, cnt in deps.items():
            if tl is self.tl and not self.self_sync:
                continue
            if self.seen.get(tl, 0) >= cnt:
                continue
            self.h.wait_ge(tl.sem, cnt)
            self.seen[tl] = cnt

    def __getattr__(self, op):
        def f(inc=True, **kw):
            return self.fw._issue(self, op, kw, inc)
        return f


def _deps(reads, writes):
    deps = {}

    def add(tc):
        if tc is None:
            return
        tl, c = tc
        if deps.get(tl, 0) < c:
            deps[tl] = c
    for v in reads:
        add(v.buf.w)
    for v in writes:
        add(v.buf.w)
        for tl, c in v.buf.r.items():
            add((tl, c))
    return deps


class FW:
    def __init__(self, nc, es):
        self.nc = nc; self.es = es; self.nsem = 0; self.es_stack = [es]; self.uid = 0
        self.pe = Eng(self, "pe", nc.tensor, self.tl("pe"), False)
        self.act = Eng(self, "act", nc.scalar, self.tl("act"), True)
        self.dve = Eng(self, "dve", nc.vector, self.tl("dve"), True)
        self.pool = Eng(self, "pool", nc.gpsimd, self.tl("pool"), True)
        self.sp = Eng(self, "sp", nc.sync, self.tl("sp"), False)
        self.dma_tls = []

    def tl(self, name):
        self.nsem += 1
        return TL(self.es.enter_context(self.nc.semaphore(name)), name)

    def sb(self, name, shape, dtype=F32):
        self.uid += 1
        return Tile(self.es_stack[-1].enter_context(self.nc.sbuf_tensor("%s_%d" % (name, self.uid), list(shape), dtype))[:])

    def barrier(self):
        engs = (self.pe, self.act, self.dve, self.pool)
        tls = [e.tl for e in engs] + self.dma_tls
        for e in engs + (self.sp,):
            e.wait_for({tl: tl.count for tl in tls if tl.count and tl is not e.tl})

    def scope(self):
        fw = self

        class _S:
            def __enter__(self_):
                self_.les = ExitStack(); fw.es_stack.append(self_.les); return self_

            def __exit__(self_, *a):
                if fw.es_stack[-1] is not self_.les:
                    return False
                fw.barrier(); fw.es_stack.pop(); self_.les.close(); return False
        return _S()

    def ps(self, name, shape=(128, 512)):
        return Tile(self.es.enter_context(self.nc.psum_tensor(name, list(shape), F32))[:])

    def dram(self, name, shape, kind):
        return Tile(self.nc.dram_tensor(name, list(shape), F32, kind=kind).ap())

    def _issue(self, eng, op, kw, inc):
        reads = [v for k, v in kw.items() if isinstance(v, View) and k not in ("out", "accum_out", "ap")]
        writes = [v for k, v in kw.items() if isinstance(v, View) and k in ("out", "accum_out", "ap")]
        eng.wait_for(_deps(reads, writes))
        args = {k: (v.ap if isinstance(v, View) else v) for k, v in kw.items()}
        inst = getattr(eng.h, op)(**args)
        if inc:
            eng.tl.count += 1
            inst.then_inc(eng.tl.sem, 1)
            stamp = eng.tl.count
        else:
            stamp = eng.tl.count + 1
        for v in reads:
            if v.buf.r.get(eng.tl, 0) < stamp:
                v.buf.r[eng.tl] = stamp
        for v in writes:
            v.buf.w = (eng.tl, stamp); v.buf.r = {}
        return inst

    def dma(self, q, tl, out, in_, **kw):
        eng = {"sp": self.sp, "pool": self.pool, "act": self.act}[q]
        deps = _deps([in_], [out])
        if tl.count:
            deps[tl] = max(deps.get(tl, 0), tl.count)
        eng.wait_for(deps)
        inst = eng.h.dma_start(out=out.ap, in_=in_.ap, **kw)
        tl.count += 16
        inst.then_inc(tl.sem, 16)
        in_.buf.r[tl] = tl.count
        out.buf.w = (tl, tl.count); out.buf.r = {}
        if tl not in self.dma_tls:
            self.dma_tls.append(tl)

    def finish(self):
        for tl in self.dma_tls:
            self.sp.h.wait_ge(tl.sem, tl.count)
        for e in (self.pe, self.act, self.dve, self.pool):
            if e.tl.count:
                self.sp.h.wait_ge(e.tl.sem, e.tl.count)


def build(cfg, dbg=False):
    nc = bass.Bass("TRN2", target_bir_lowering=False)
    es = ExitStack()
    fw = FW(nc, es)
    pe, act, dve, pool = fw.pe, fw.act, fw.dve, fw.pool
    C = cfg
    D, KC, TP, NSC, TW = C.D, C.KC, C.TP, C.NSC, C.TW
    NS2 = 2 * NSC
    NSB, NUB, NHP, NCH, NRB, GB = C.NSB, C.NUB, C.NHP, C.NCH, C.NRB, C.GB
    P = 128

    def din(name, shape):
        return fw.dram(name, shape, "ExternalInput")

    def dout(name, shape):
        return fw.dram(name, shape, "ExternalOutput")

    xb = din("xb", [TP * C.NPASS, D])
    c17 = din("c17", [NSC + 2, D]); xs = din("xs", [NSC, D]); sshift = din("sshift", [NSC, D])
    s5re0 = din("s5re0", [NSC, NSB * 128]); s5im0 = din("s5im0", [NSC, NSB * 128])
    wkv0 = din("wkv0", [NSC * C.H, 4096])
    w_ada = din("w_ada", [3 * KC, P, KC, 128])
    w_in = din("w_in", [C.NIN // 128, P, KC, 128])
    w_glu = din("w_glu", [NUB, P, NUB, 128])
    w_out = din("w_out", [KC, P, KC, 128])
    b_ada = din("b_ada", [P, 3 * KC]); norm_g = din("norm_g", [P, KC]); mu = din("mu", [P, NRB])
    A_re = din("A_re", [P, NSB]); A_im = din("A_im", [P, NSB]); lstep = din("lstep", [P, NSB])
    B_re = din("B_re", [P, NSB, 16]); B_im = din("B_im", [P, NSB, 16])
    C_re = din("C_re", [P, NSB, 16]); C_im = din("C_im", [P, NSB, 16])
    Dsk = din("Dsk", [P, NUB]); b_glu = din("b_glu", [P, NUB])
    w0 = din("w0", [P, NHP]); a0 = din("a0", [P, NHP]); k_k = din("k_k", [P, NHP]); k_a = din("k_a", [P, NHP])
    r_k = din("r_k", [P, NHP]); gn_w = din("gn_w", [P, NHP]); gn_b = din("gn_b", [P, NHP])
    w2 = din("w2", [64, C.DR]); a2 = din("a2", [64, C.DR])
    fing = din("fing", [P, D])
    gnw_bh = din("gnw_bh", [P, 64]); gnb_bh = din("gnb_bh", [P, 64]); rk_bh = din("rk_bh", [P, 64])
    ident_d = din("ident", [P, P]); maskgt_d = din("maskgt", [P, P]); onesbd_d = din("onesbd", [P, P])
    reset_d = din("resetm", [P, TP]); iota_d = din("iota", [P, 128]); rowmask_d = din("rowmask", [P, 4])

    y_o = dout("y", [TP * C.NPASS, D]); ys_o = dout("ys", [NSC, D])
    ps5re_o = dout("ps5re", [NSB, P]); ps5im_o = dout("ps5im", [NSB, P])
    pwkv_o = dout("pwkv", [C.H, 64, 64]); pshift_o = dout("pshift", [KC, P])
    ss5re_o = dout("ss5re", [NSC, NSB * 128]); ss5im_o = dout("ss5im", [NSC, NSB * 128])
    swkv_o = dout("swkv", [NSC * C.H, 4096]); sshift_o = dout("sshift_o", [NSC, D])
    scr_a = fw.dram("scr_a", [NSC, C.H, 6, 64], "Internal")
    scr_b = fw.dram("scr_b", [NSC, C.DR], "Internal")
    dbg_o = {}

    tl_misc = [fw.tl("m%d" % i) for i in range(6)]
    mi = [0]

    def ld(out, in_, q="sp"):
        tl = tl_misc[mi[0] % len(tl_misc)]; mi[0] += 1
        fw.dma(q, tl, out, in_)

    tl_out = [fw.tl("o%d" % i) for i in range(4)]
    oi = [0]

    def st(out, in_, **kw):
        tl = tl_out[oi[0] % len(tl_out)]; oi[0] += 1
        fw.dma("sp", tl, out, in_, **kw)

    def const(name, d, shape):
        t = fw.sb(name, shape); ld(t[:], d[:]); return t
    ident = const("ident_s", ident_d, [P, P]); maskgt = const("maskgt_s", maskgt_d, [P, P])
    onesbd = const("onesbd_s", onesbd_d, [P, P]); resetm = const("reset_s", reset_d, [P, TP])
    iota = const("iota_s", iota_d, [P, 128]); rowmask = const("rowmask_s", rowmask_d, [P, 4])
    b_ada_s = const("b_ada_s", b_ada, [P, 3 * KC]); norm_g_s = const("norm_g_s", norm_g, [P, KC])
    mu_s = const("mu_s", mu, [P, NRB])
    Dsk_s = const("Dsk_s", Dsk, [P, NUB]); b_glu_s = const("b_glu_s", b_glu, [P, NUB])
    w0_s = const("w0_s", w0, [P, NHP]); a0_s = const("a0_s", a0, [P, NHP]); kk_s = const("kk_s", k_k, [P, NHP])
    ka_s = const("ka_s", k_a, [P, NHP]); rk_s = const("rk_s", r_k, [P, NHP])
    gnw_s = const("gnw_s", gn_w, [P, NHP]); gnb_s = const("gnb_s", gn_b, [P, NHP])
    w2_s = fw.sb("w2a2_s", [P, C.DR]); ld(w2_s[0:64, :], w2[:])
    a2_s = w2_s; ld(a2_s[64:128, :], a2[:])
    omu_s = fw.sb("omu_s", [P, NRB])
    dve.tensor_scalar(out=omu_s[:], in0=mu_s[:], scalar1=-1.0, scalar2=1.0, op0=ALU.mult, op1=ALU.add)

    PS = [fw.ps("psb%d" % i) for i in range(8)]

    NRING = 2
    ring = [fw.sb("wring%d" % i, [P, KC, 128], F32R) for i in range(NRING)]
    NSTG = 1
    wstg = [fw.sb("wstg%d" % i, [P, KC, 128]) for i in range(NSTG)]
    stg_tl = [fw.tl("ws%d" % i) for i in range(NSTG)]
    ring_tl = [fw.tl("wr%d" % i) for i in range(NRING)]
    ri = [0]

    def load_w(wd, blk, nk=KC):
        i = ri[0] % NRING; ri[0] += 1
        j = (ri[0] - 1) % NSTG
        fw.dma("sp", stg_tl[j], wstg[j][:, 0:nk, :], View(wd.buf, wd.ap[blk, :, 0:nk, :]))
        pool.tensor_copy(out=ring[i][:, 0:nk, :], in_=wstg[j][:, 0:nk, :])
        return ring[i]

    def mm_acc(psv, wslot, nk, rhs_fn, ncols=128):
        for kc in range(nk):
            pe.matmul(out=psv, lhsT=wslot[:, kc, 0:ncols].r32(), rhs=rhs_fn(kc),
                      start=(kc == 0), stop=(kc == nk - 1), inc=(kc == nk - 1))

    s5 = {}
    tA = const("A_re_s", A_re, [P, NSB]); tAi = const("A_im_s", A_im, [P, NSB]); tls = const("lstep_s", lstep, [P, NSB])
    step = fw.sb("s5step", [P, NSB]); act.activation(out=step[:], in_=tls[:], func=AF.Exp)
    lam = fw.sb("s5lam", [P, NSB]); dve.tensor_scalar_min(out=lam[:], in0=tA[:], scalar1=-1e-4)
    mag = fw.sb("s5mag", [P, NSB]); tmpc = fw.sb("s5tmp", [P, NSB]); theta = fw.sb("s5theta", [P, NSB])
    dve.tensor_tensor(out=tmpc[:], in0=lam[:], in1=step[:], op=ALU.mult)
    act.activation(out=mag[:], in_=tmpc[:], func=AF.Exp)
    dve.tensor_tensor(out=theta[:], in0=tAi[:], in1=step[:], op=ALU.mult)
    TWO_PI = 2.0 * math.pi

    I32 = mybir.dt.int32

    def sin_into(dst, x, shape, phase):
        with fw.scope():
            ki = fw.sb("sr_ki", shape, I32); kf = fw.sb("sr_kf", shape)
            dve.tensor_scalar_add(out=dst, in0=x, scalar1=float(phase))
            dve.tensor_scalar_mul(out=ki[:], in0=dst, scalar1=1.0 / TWO_PI)
            dve.tensor_copy(out=kf[:], in_=ki[:])
            dve.scalar_tensor_tensor(out=dst, in0=kf[:], scalar=-TWO_PI, in1=dst, op0=ALU.mult, op1=ALU.add)
            dve.tensor_scalar(out=kf[:], in0=dst, scalar1=math.pi, scalar2=-TWO_PI, op0=ALU.is_gt, op1=ALU.mult)
            dve.tensor_tensor(out=dst, in0=dst, in1=kf[:], op=ALU.add)
            dve.tensor_scalar(out=kf[:], in0=dst, scalar1=-math.pi, scalar2=TWO_PI, op0=ALU.is_lt, op1=ALU.mult)
            dve.tensor_tensor(out=dst, in0=dst, in1=kf[:], op=ALU.add)
            dve.tensor_scalar(out=dst, in0=dst, scalar1=math.pi, scalar2=-math.pi, op0=ALU.min, op1=ALU.max)
            act.activation(out=dst, in_=dst, func=AF.Sin)

    def sincos(name, ang_view, shape):
        sn = fw.sb(name + "_sin", shape); cs = fw.sb(name + "_cos", shape)
        sin_into(sn[:], ang_view, shape, 0.0)
        sin_into(cs[:], ang_view, shape, 0.5 * math.pi)
        return sn, cs
    thp = theta
    sn1, cs1 = sincos("s5t1", thp[:], [P, NSB])
    abr = fw.sb("s5abr", [P, NSB]); abi = fw.sb("s5abi", [P, NSB])
    dve.tensor_tensor(out=abr[:], in0=mag[:], in1=cs1[:], op=ALU.mult)
    dve.tensor_tensor(out=abi[:], in0=mag[:], in1=sn1[:], op=ALU.mult)
    den = fw.sb("s5den", [P, NSB]); t2 = fw.sb("s5t2", [P, NSB]); fre = fw.sb("s5fre", [P, NSB]); fim = fw.sb("s5fim", [P, NSB])
    abm1 = fw.sb("s5abm1", [P, NSB])
    dve.tensor_tensor(out=den[:], in0=lam[:], in1=lam[:], op=ALU.mult)
    dve.tensor_tensor(out=t2[:], in0=tAi[:], in1=tAi[:], op=ALU.mult)
    dve.tensor_tensor(out=den[:], in0=den[:], in1=t2[:], op=ALU.add)
    dve.reciprocal(out=den[:], in_=den[:])
    dve.tensor_scalar_add(out=abm1[:], in0=abr[:], scalar1=-1.0)
    dve.tensor_tensor(out=fre[:], in0=abm1[:], in1=lam[:], op=ALU.mult)
    dve.tensor_tensor(out=t2[:], in0=abi[:], in1=tAi[:], op=ALU.mult)
    dve.tensor_tensor(out=fre[:], in0=fre[:], in1=t2[:], op=ALU.add)
    dve.tensor_tensor(out=fre[:], in0=fre[:], in1=den[:], op=ALU.mult)
    dve.tensor_tensor(out=fim[:], in0=abi[:], in1=lam[:], op=ALU.mult)
    dve.tensor_tensor(out=t2[:], in0=abm1[:], in1=tAi[:], op=ALU.mult)
    dve.tensor_tensor(out=fim[:], in0=fim[:], in1=t2[:], op=ALU.subtract)
    dve.tensor_tensor(out=fim[:], in0=fim[:], in1=den[:], op=ALU.mult)
    LB = [fw.sb("s5LB" + nm, [P, NUB, 128]) for nm in ("re", "im")]
    LC = [fw.sb("s5ZC" + nm, [P, NSB, 2, 16]) for nm in ("re", "im")]
    with fw.scope():
        Br = const("B_re_s", B_re, [P, NSB, 16]); Bi = const("B_im_s", B_im, [P, NSB, 16])
        bbr = fw.sb("s5bbr", [P, NSB, 16]); bbi = fw.sb("s5bbi", [P, NSB, 16]); bt_ = fw.sb("s5bt", [P, NSB, 16])
        fre_b = fre[:, :, None].bc([P, NSB, 16]); fim_b = fim[:, :, None].bc([P, NSB, 16])
        dve.tensor_tensor(out=bbr[:], in0=Br[:], in1=fre_b, op=ALU.mult)
        dve.tensor_tensor(out=bt_[:], in0=Bi[:], in1=fim_b, op=ALU.mult)
        dve.tensor_tensor(out=bbr[:], in0=bbr[:], in1=bt_[:], op=ALU.subtract)
        dve.tensor_tensor(out=bbi[:], in0=Bi[:], in1=fre_b, op=ALU.mult)
        dve.tensor_tensor(out=bt_[:], in0=Br[:], in1=fim_b, op=ALU.mult)
        dve.tensor_tensor(out=bbi[:], in0=bbi[:], in1=bt_[:], op=ALU.add)
        for li, (nm, src) in enumerate((("re", bbr), ("im", bbi))):
            Z = fw.sb("s5ZB" + nm, [P, NSB, 2, 16])
            dve.memset(ap=Z[:], constant=0.0)
            dve.tensor_copy(out=Z[0:64, :, 0, :], in_=src[0:64, :, :])
            dve.tensor_copy(out=Z[64:128, :, 1, :], in_=src[64:128, :, :])
            L = LB[li]
            for q4 in range(NUB):
                pe.transpose(out=PS[0][:, 0:128], in_=Z[:, 4 * q4:4 * q4 + 4, :, :].re("p a b c -> p (a b c)"), identity=ident[:])
                act.copy(out=L[:, q4, :], in_=PS[0][:, 0:128])
        Cr = const("C_re_s", C_re, [P, NSB, 16]); Ci = const("C_im_s", C_im, [P, NSB, 16])
        for li, (nm, src, sgn) in enumerate((("re", Cr, 1.0), ("im", Ci, -1.0))):
            Z = LC[li]
            dve.memset(ap=Z[:], constant=0.0)
            dve.tensor_scalar_mul(out=Z[0:64, :, 0, :], in0=src[0:64, :, :], scalar1=sgn)
            dve.tensor_scalar_mul(out=Z[64:128, :, 1, :], in0=src[64:128, :, :], scalar1=sgn)
    def make_tables():
      sinT = fw.sb("s5sinT", [P, NSB, 128]); cosT = fw.sb("s5cosT", [P, NSB, 128])
      CH = min(8, NSB)
      with fw.scope():
        ang = fw.sb("s5ang", [P, CH, 128])
        for c0 in range(0, NSB, CH):
            dve.tensor_tensor(out=ang[:], in0=iota[:, None, :].bc([P, CH, 128]), in1=thp[:, c0:c0 + CH, None].bc([P, CH, 128]), op=ALU.mult)
            sin_into(sinT[:, c0:c0 + CH, :], ang[:], [P, CH, 128], 0.0)
            sin_into(cosT[:, c0:c0 + CH, :], ang[:], [P, CH, 128], 0.5 * math.pi)
      return sinT, cosT
    angL = fw.sb("s5angL", [P, NSB]); dve.tensor_scalar_mul(out=angL[:], in0=thp[:], scalar1=128.0)
    snL, csL = sincos("s5tL", angL[:], [P, NSB])
    s5car_r = fw.sb("s5car_r", [P, NSB]); s5car_i = fw.sb("s5car_i", [P, NSB])
    dve.memset(ap=s5car_r[:], constant=0.0); dve.memset(ap=s5car_i[:], constant=0.0)

    NM = NSC + 2
    scT = fw.sb("scT", [P, KC, NM], F32R)
    with fw.scope():
        c_s = fw.sb("c_s", [NM, D]); ld(c_s[:], c17[:])
        csl = fw.sb("csl", [NM, D]); act.activation(out=csl[:], in_=c_s[:], func=AF.Silu)
        for kc in range(KC):
            pe.transpose(out=PS[1][:, kc * NM:(kc + 1) * NM], in_=csl[:, kc * 128:(kc + 1) * 128], identity=ident[0:NM, 0:NM])
        act.copy(out=scT[:].re("p k n -> p (k n)"), in_=PS[1][:, 0:KC * NM])
    modT = fw.sb("modT", [P, 3 * KC, NM])
    for fb in range(3 * KC):
        w = load_w(w_ada, fb)
        psb = PS[2 + fb % 2]
        mm_acc(psb[:, 0:NM], w, KC, lambda kc: scT[:, kc, :])
        act.activation(out=modT[:, fb, :], in_=psb[:, 0:NM], func=AF.Identity, bias=b_ada_s[:, fb:fb + 1], scale=1.0)
    sceff = fw.sb("sceff", [P, KC, NM])
    dve.tensor_scalar_add(out=sceff[:], in0=modT[:, KC:2 * KC, :], scalar1=1.0)
    dve.tensor_tensor(out=sceff[:], in0=sceff[:], in1=norm_g_s[:, :, None].bc([P, KC, NM]), op=ALU.mult)
    sc_p = fw.sb("sc_p", [P, KC]); sh_p = fw.sb("sh_p", [P, KC]); gt_p = fw.sb("gt_p", [P, KC])
    dve.tensor_copy(out=sc_p[:], in_=sceff[:, :, NSC]); dve.tensor_copy(out=sh_p[:], in_=modT[:, 0:KC, NSC])
    dve.tensor_copy(out=gt_p[:], in_=modT[:, 2 * KC:3 * KC, NSC])

    hT = fw.sb("hT", [P, KC, TW], F32R)
    rw_car = fw.sb("rw_car", [P, NRB]); dve.memset(ap=rw_car[:], constant=0.0)
    Hst = [fw.sb("Hst%d" % h, [P, 64]) for h in range(C.H)]
    for h in range(C.H):
        dve.memset(ap=Hst[h][:], constant=0.0)

    eps_rms = fw.sb("eps_rms", [P, 1]); dve.memset(ap=eps_rms[:], constant=1e-6)
    eps_gn = fw.sb("eps_gn", [P, 1]); dve.memset(ap=eps_gn[:], constant=64e-5)

    def rstd_of(ss, n):
        act.activation(out=ss, in_=ss, func=AF.Sqrt, bias=eps_rms[0:n, :], scale=1.0 / D)
        dve.reciprocal(out=ss, in_=ss)

    xt_tl = [fw.tl("xt%d" % i) for i in range(2)]
    ssq = fw.sb("ssq", [P, 2])
    hlast = fw.sb("hlast", [P, KC])

    def phase_x(xd, is_b):
      with fw.scope():
        xt = [fw.sb("xt%d" % i, [P, D]) for i in range(2)]
        xsq = fw.sb("xsq", [P, D], mybir.dt.bfloat16)
        for tt in range(TP // 128):
            i = tt % 2
            fw.dma("sp", xt_tl[i], xt[i][:], xd[tt * 128:(tt + 1) * 128, :])
            s = ssq[:, i:i + 1]
            act.activation(out=xsq[:], in_=xt[i][:], func=AF.Square, accum_out=s)
            rstd_of(s, 128)
            act.activation(out=xt[i][:], in_=xt[i][:], func=AF.Copy, scale=s)
            for kc in range(KC):
                psb = PS[(kc // 4) % 2]
                pe.transpose(out=psb[:, (kc % 4) * 128:(kc % 4 + 1) * 128], in_=xt[i][:, kc * 128:(kc + 1) * 128], identity=ident[:])
                act.activation(out=hT[:, kc, tt * 128:(tt + 1) * 128], in_=psb[:, (kc % 4) * 128:(kc % 4 + 1) * 128],
                               func=AF.Identity, scale=sc_p[:, kc:kc + 1], bias=sh_p[:, kc:kc + 1])
                if is_b and tt == TP // 128 - 1:
                    act.activation(out=hlast[:, kc:kc + 1], in_=psb[:, (kc % 4) * 128 + 127:(kc % 4) * 128 + 128],
                                   func=AF.Identity, scale=sc_p[:, kc:kc + 1], bias=sh_p[:, kc:kc + 1])

    NPASS = C.NPASS
    W1 = TP + NSC
    UID = [0]

    def tiles(is_last):
        t = list(C.TT)
        if is_last:
            t.append((TP, NS2))
        return t

    barrier = fw.barrier
    STOP = float(os.environ.get("K_STOP", "99"))

    class StopBuild(Exception):
        pass

    def stop(n):
        return STOP <= n

    def chk(n):
        if STOP <= n:
            while len(fw.es_stack) > 1:
                fw.es_stack.pop().close()
            raise StopBuild()

    osT = fw.sb("osT", [P, NUB, TW], F32R)
    orT = fw.sb("orT", [P, NHP, TW], F32R)
    x0r = fw.sb("x0r", [P, NSB, NSC]); x0i = fw.sb("x0i", [P, NSB, NSC])
    x1r = fw.sb("x1r", [P, NSB, NSC]); x1i = fw.sb("x1i", [P, NSB, NSC])

    def gelu_inplace(v, tmp_v):
        dve.tensor_tensor(out=tmp_v, in0=v, in1=v, op=ALU.mult)
        dve.tensor_scalar(out=tmp_v, in0=tmp_v, scalar1=0.044715, scalar2=1.0, op0=ALU.mult, op1=ALU.add)
        dve.tensor_tensor(out=tmp_v, in0=tmp_v, in1=v, op=ALU.mult)
        act.activation(out=tmp_v, in_=tmp_v, func=AF.Sigmoid, scale=2.0 * math.sqrt(2.0 / math.pi))
        dve.tensor_tensor(out=v, in0=v, in1=tmp_v, op=ALU.mult)

    def s5_phase(is_last):
        Wd = TW if is_last else TP
        with fw.scope():
            lsb = fw.sb
            sinT, cosT = make_tables()
            chk(2.1)
            uT = lsb("uT", [P, TW]); yT = lsb("yT", [P, NUB, TW], F32R)
            scanscope = fw.scope(); scanscope.__enter__()
            zr = lsb("s5zr", [P, 4, 128]); zi = lsb("s5zi", [P, 4, 128])
            q1 = lsb("s5q1", [P, 4, 128])
            sr = lsb("s5sr", [P, 4, 128]); si = lsb("s5si", [P, 4, 128])
            xr = zr; xi = zi
            cq = [lsb("s5cq%d" % i, [P, 4]) for i in range(4)]
            LBz = [lsb("s5LBz%d" % i, [P, 4, 128]) for i in range(2)]
            sq = [lsb("s5sq%d" % i, [P, 4, NSC]) for i in range(2)]
            for blk in range(NUB):
                chk(2.12)
                w = load_w(w_in, C.OFF_U // 128 + blk)
                chk(2.15)
                for (t0, tn) in tiles(is_last):
                    psb = PS[2 + (t0 // 512) % 2]
                    mm_acc(psb[:, 0:tn], w, KC, lambda kc: hT[:, kc, t0:t0 + tn].r32())
                    chk(2.17)
                    act.copy(out=uT[:, t0:t0 + tn], in_=psb[:, 0:tn])
                chk(2.2)
                for li in range(2):
                    for j in range(4):
                        dve.tensor_scalar_mul(out=LBz[li][:, j, :], in0=LB[li][:, blk, :], scalar1=rowmask[:, j:j + 1])
                s0 = blk * 4; sl = slice(s0, s0 + 4)
                cT = cosT[:, sl, :]; sT = sinT[:, sl, :]
                for ts in range(TP // 128):
                    bur = PS[4 + (ts % 2) * 2]; bui = PS[5 + (ts % 2) * 2]
                    for j in range(4):
                        for (L, psb) in ((LBz[0], bur), (LBz[1], bui)):
                            pe.matmul(out=psb[:, j * 128:(j + 1) * 128], lhsT=L[:, j, :],
                                      rhs=uT[:, ts * 128:(ts + 1) * 128],
                                      start=True, stop=True, inc=(j == 3))
                    chk(2.4)
                    br = bur[:, :].re("p (a b) -> p a b", a=4); bi = bui[:, :].re("p (a b) -> p a b", a=4)
                    dve.tensor_tensor(out=zr[:], in0=br, in1=cT, op=ALU.mult)
                    dve.tensor_tensor(out=q1[:], in0=bi, in1=sT, op=ALU.mult)
                    dve.tensor_tensor(out=zr[:], in0=zr[:], in1=q1[:], op=ALU.add)
                    dve.tensor_tensor(out=zi[:], in0=bi, in1=cT, op=ALU.mult)
                    dve.tensor_tensor(out=q1[:], in0=br, in1=sT, op=ALU.mult)
                    dve.tensor_tensor(out=zi[:], in0=zi[:], in1=q1[:], op=ALU.subtract)
                    chk(2.5)
                    for j in range(4):
                        s = s0 + j
                        dve.tensor_tensor_scan(out=sr[:, j, :], data0=mag[:, s:s + 1].bc([P, 128]), data1=zr[:, j, :],
                                               initial=s5car_r[:, s:s + 1], op0=ALU.mult, op1=ALU.add)
                        dve.tensor_tensor_scan(out=si[:, j, :], data0=mag[:, s:s + 1].bc([P, 128]), data1=zi[:, j, :],
                                               initial=s5car_i[:, s:s + 1], op0=ALU.mult, op1=ALU.add)
                    chk(2.6)
                    la = sr[:, :, 127]; lb = si[:, :, 127]
                    dve.tensor_tensor(out=cq[0][:], in0=csL[:, sl], in1=la, op=ALU.mult)
                    dve.tensor_tensor(out=cq[1][:], in0=snL[:, sl], in1=lb, op=ALU.mult)
                    dve.tensor_tensor(out=cq[2][:], in0=snL[:, sl], in1=la, op=ALU.mult)
                    dve.tensor_tensor(out=cq[3][:], in0=csL[:, sl], in1=lb, op=ALU.mult)
                    dve.tensor_tensor(out=s5car_r[:, sl], in0=cq[0][:], in1=cq[1][:], op=ALU.subtract)
                    dve.tensor_tensor(out=s5car_i[:, sl], in0=cq[2][:], in1=cq[3][:], op=ALU.add)
                    dve.tensor_tensor(out=xr[:], in0=sr[:], in1=cT, op=ALU.mult)
                    dve.tensor_tensor(out=q1[:], in0=si[:], in1=sT, op=ALU.mult)
                    dve.tensor_tensor(out=xr[:], in0=xr[:], in1=q1[:], op=ALU.subtract)
                    dve.tensor_tensor(out=xi[:], in0=sr[:], in1=sT, op=ALU.mult)
                    dve.tensor_tensor(out=q1[:], in0=si[:], in1=cT, op=ALU.mult)
                    dve.tensor_tensor(out=xi[:], in0=xi[:], in1=q1[:], op=ALU.add)
                    chk(2.7)
                    psy = PS[0]
                    for j in range(4):
                        s = s0 + j
                        pe.matmul(out=psy[32 * j:32 * j + 32, 0:128], lhsT=LC[0][:, s, :, :].re("p a b -> p (a b)"),
                                  rhs=xr[:, j, :], start=True, stop=False, tile_position=(0, 32 * j), inc=False)
                        pe.matmul(out=psy[32 * j:32 * j + 32, 0:128], lhsT=LC[1][:, s, :, :].re("p a b -> p (a b)"),
                                  rhs=xi[:, j, :], start=False, stop=True, tile_position=(0, 32 * j), inc=(j == 3))
                    dve.scalar_tensor_tensor(out=yT[:, blk, ts * 128:(ts + 1) * 128], in0=uT[:, ts * 128:(ts + 1) * 128],
                                             scalar=Dsk_s[:, blk:blk + 1], in1=psy[:, 0:128], op0=ALU.mult, op1=ALU.add)
                if is_last:
                    psr = PS[6]; psi = PS[7]
                    for j in range(4):
                        for (L, psb) in ((LBz[0], psr), (LBz[1], psi)):
                            pe.matmul(out=psb[:, j * NSC:(j + 1) * NSC], lhsT=L[:, j, :],
                                      rhs=uT[:, TP:TP + NSC],
                                      start=True, stop=True, inc=(j == 3))
                    abr_b = abr[:, sl, None].bc([P, 4, NSC]); abi_b = abi[:, sl, None].bc([P, 4, NSC])
                    dve.tensor_tensor(out=sq[0][:], in0=x0r[:, sl, :], in1=abr_b, op=ALU.mult)
                    dve.tensor_tensor(out=sq[1][:], in0=x0i[:, sl, :], in1=abi_b, op=ALU.mult)
                    dve.tensor_tensor(out=sq[0][:], in0=sq[0][:], in1=sq[1][:], op=ALU.subtract)
                    dve.tensor_tensor(out=x1r[:, sl, :], in0=sq[0][:], in1=psr[:, 0:4 * NSC].re("p (a b) -> p a b", a=4), op=ALU.add)
                    dve.tensor_tensor(out=sq[0][:], in0=x0i[:, sl, :], in1=abr_b, op=ALU.mult)
                    dve.tensor_tensor(out=sq[1][:], in0=x0r[:, sl, :], in1=abi_b, op=ALU.mult)
                    dve.tensor_tensor(out=sq[0][:], in0=sq[0][:], in1=sq[1][:], op=ALU.add)
                    dve.tensor_tensor(out=x1i[:, sl, :], in0=sq[0][:], in1=psi[:, 0:4 * NSC].re("p (a b) -> p a b", a=4), op=ALU.add)
                    psy = PS[0]
                    for j in range(4):
                        s = s0 + j
                        pe.matmul(out=psy[32 * j:32 * j + 32, 0:NSC], lhsT=LC[0][:, s, :, :].re("p a b -> p (a b)"),
                                  rhs=x1r[:, s, :], start=True, stop=False, tile_position=(0, 32 * j), inc=False)
                        pe.matmul(out=psy[32 * j:32 * j + 32, 0:NSC], lhsT=LC[1][:, s, :, :].re("p a b -> p (a b)"),
                                  rhs=x1i[:, s, :], start=False, stop=True, tile_position=(0, 32 * j), inc=(j == 3))
                    dve.scalar_tensor_tensor(out=yT[:, blk, TP:TP + NSC], in0=uT[:, TP:TP + NSC],
                                             scalar=Dsk_s[:, blk:blk + 1], in1=psy[:, 0:NSC], op0=ALU.mult, op1=ALU.add)
            chk(2.8)
            scanscope.__exit__(None, None, None)
            gtmp = lsb("gtmp", [P, TW]); gs = lsb("gs", [P, TW]); zs = lsb("zs", [P, TW])
            Wy = W1 if is_last else TP
            for blk in range(NUB):
                gelu_inplace(yT[:, blk, 0:Wy], gtmp[:, 0:Wy])
            gt_tiles = list(C.TT) + ([(TP, NSC)] if is_last else [])
            for jb in range(NUB):
                w = load_w(w_glu, jb, nk=NUB)
                w2_ = load_w(w_in, C.OFF_Z // 128 + jb)
                for (t0, tn) in gt_tiles:
                    psb = PS[2]; psz = PS[3]
                    mm_acc(psb[:, 0:tn], w, NUB, lambda kc: yT[:, kc, t0:t0 + tn])
                    act.activation(out=gs[:, t0:t0 + tn], in_=psb[:, 0:tn], func=AF.Sigmoid, bias=b_glu_s[:, jb:jb + 1], scale=1.0)
                    mm_acc(psz[:, 0:tn], w2_, KC, lambda kc: hT[:, kc, t0:t0 + tn].r32())
                    act.activation(out=zs[:, t0:t0 + tn], in_=psz[:, 0:tn], func=AF.Silu)
                    dve.tensor_tensor(out=gs[:, t0:t0 + tn], in0=gs[:, t0:t0 + tn], in1=zs[:, t0:t0 + tn], op=ALU.mult)
                    dve.tensor_tensor(out=osT[:, jb, t0:t0 + tn], in0=gs[:, t0:t0 + tn], in1=yT[:, jb, t0:t0 + tn], op=ALU.mult)
            barrier()

    EM05 = math.exp(-0.5)

    def rwkv_phase(is_last):
        Wd = W1 if is_last else TP
        with fw.scope():
            lsb = fw.sb
            Pb = lsb("Pb", [P, 1 + TW]); tmpA = lsb("tmpA", [P, W1])
            lor = lsb("lor", [P, W1])
            rr = lsb("rr", [P, W1]); kr = lsb("kr", [P, W1]); vv = lsb("vv", [P, W1])
            ldc = lsb("ldc", [P, W1]); alr = lsb("alr", [P, W1]); kkn = lsb("kkn", [P, W1]); kmod = lsb("kmod", [P, W1])
            bb = lsb("bb", [P, W1]); cl = lsb("cl", [P, TP]); e1 = lsb("e1", [P, TP]); e2 = lsb("e2", [P, TP])
            cend = lsb("cend", [P, NCH]); PC = lsb("PC", [P, NCH])
            AR = lsb("AR", [P, NCH, 2, 64]); BK = lsb("BK", [P, NCH, 2, 64]); AV = lsb("AV", [P, NCH, 2, 64]); BKH = lsb("BKH", [P, NCH, 2, 64])
            bonus = lsb("bonus", [P, W1]); oT = lsb("oT", [P, W1]); osq = lsb("osq", [P, W1])
            GTs = [lsb("GTs%d" % e, [P, P]) for e in range(2)]
            X = [lsb("X%d" % e, [P, 64]) for e in range(2)]
            BKHt = [lsb("BKHt%d" % e, [P, 64]) for e in range(2)]
            PW = [lsb("PW%d" % e, [64, 3, 64]) for e in range(2)]
            Qs = [lsb("Qs%d" % e, [64, 64]) for e in range(2)]
            Ys = [lsb("Ys%d" % e, [64, 64]) for e in range(2)]
            WTs = [lsb("WTs%d" % e, [P, 64]) for e in range(2)]

            def proj_shift(rb, dst):
                w = load_w(w_in, C.OFF_RW // 128 + rb)
                for (t0, tn) in tiles(is_last):
                    psb = PS[6 + (t0 // 512) % 2]
                    mm_acc(psb[:, 0:tn], w, KC, lambda kc: hT[:, kc, t0:t0 + tn].r32())
                    act.copy(out=Pb[:, 1 + t0:1 + t0 + tn], in_=psb[:, 0:tn])
                act.copy(out=Pb[:, 0:1], in_=rw_car[:, rb:rb + 1])
                dve.tensor_scalar_mul(out=tmpA[:, 0:TP], in0=Pb[:, 0:TP], scalar1=mu_s[:, rb:rb + 1])
                dve.scalar_tensor_tensor(out=dst[:, 0:TP], in0=Pb[:, 1:TP + 1], scalar=omu_s[:, rb:rb + 1], in1=tmpA[:, 0:TP],
                                         op0=ALU.mult, op1=ALU.add)
                dve.tensor_copy(out=rw_car[:, rb:rb + 1], in_=Pb[:, TP:TP + 1])
                if is_last:
                    dve.tensor_scalar_mul(out=tmpA[:, TP:W1], in0=Pb[:, 1 + TP + NSC:1 + TP + NS2], scalar1=mu_s[:, rb:rb + 1])
                    dve.scalar_tensor_tensor(out=dst[:, TP:W1], in0=Pb[:, 1 + TP:1 + TP + NSC], scalar=omu_s[:, rb:rb + 1],
                                             in1=tmpA[:, TP:W1], op0=ALU.mult, op1=ALU.add)

            ct_tiles = list(C.TT) + ([(TP, NSC)] if is_last else [])
            proj_shift(3 * NHP, lor)
            act.activation(out=lor[0:64, 0:Wd], in_=lor[0:64, 0:Wd], func=AF.Tanh)
            for hp in range(NHP):
                proj_shift(hp, rr); proj_shift(NHP + hp, kr); proj_shift(2 * NHP + hp, vv)
                for (t0, tn) in ct_tiles:
                    psb = PS[6]
                    pe.matmul(out=psb[:, 0:tn], lhsT=w2_s[0:64, hp * 128:(hp + 1) * 128], rhs=lor[0:64, t0:t0 + tn], start=True, stop=True)
                    act.activation(out=ldc[:, t0:t0 + tn], in_=psb[:, 0:tn], func=AF.Sigmoid, bias=w0_s[:, hp:hp + 1], scale=1.0)
                    psb = PS[7]
                    pe.matmul(out=psb[:, 0:tn], lhsT=a2_s[64:128, hp * 128:(hp + 1) * 128], rhs=lor[64:128, t0:t0 + tn], start=True, stop=True)
                    act.activation(out=alr[:, t0:t0 + tn], in_=psb[:, 0:tn], func=AF.Sigmoid, bias=a0_s[:, hp:hp + 1], scale=1.0)
                dve.tensor_scalar_mul(out=ldc[:, 0:Wd], in0=ldc[:, 0:Wd], scalar1=-EM05)
                dve.tensor_scalar_mul(out=kkn[:, 0:Wd], in0=kr[:, 0:Wd], scalar1=kk_s[:, hp:hp + 1])
                dve.tensor_tensor(out=tmpA[:, 0:Wd], in0=kkn[:, 0:Wd], in1=kkn[:, 0:Wd], op=ALU.mult)
                for (t0, tn) in ct_tiles:
                    psb = PS[6]
                    pe.matmul(out=psb[:, 0:tn], lhsT=onesbd[:], rhs=tmpA[:, t0:t0 + tn], start=True, stop=True)
                    act.activation(out=bb[:, t0:t0 + tn], in_=psb[:, 0:tn], func=AF.Sqrt)
                dve.tensor_scalar_max(out=bb[:, 0:Wd], in0=bb[:, 0:Wd], scalar1=1e-12)
                dve.reciprocal(out=bb[:, 0:Wd], in_=bb[:, 0:Wd])
                dve.tensor_tensor(out=kkn[:, 0:Wd], in0=kkn[:, 0:Wd], in1=bb[:, 0:Wd], op=ALU.mult)
                dve.tensor_scalar(out=kmod[:, 0:Wd], in0=alr[:, 0:Wd], scalar1=-1.0, scalar2=ka_s[:, hp:hp + 1], op0=ALU.add, op1=ALU.mult)
                dve.scalar_tensor_tensor(out=kmod[:, 0:Wd], in0=kmod[:, 0:Wd], scalar=1.0, in1=kr[:, 0:Wd], op0=ALU.add, op1=ALU.mult)
                dve.tensor_tensor(out=bb[:, 0:Wd], in0=kkn[:, 0:Wd], in1=alr[:, 0:Wd], op=ALU.mult)
                dve.scalar_tensor_tensor(out=tmpA[:, 0:Wd], in0=rr[:, 0:Wd], scalar=rk_s[:, hp:hp + 1], in1=kmod[:, 0:Wd], op0=ALU.mult, op1=ALU.mult)
                for (t0, tn) in ct_tiles:
                    psb = PS[6]
                    pe.matmul(out=psb[:, 0:tn], lhsT=onesbd[:], rhs=tmpA[:, t0:t0 + tn], start=True, stop=True)
                    dve.tensor_tensor(out=bonus[:, t0:t0 + tn], in0=psb[:, 0:tn], in1=vv[:, t0:t0 + tn], op=ALU.mult)
                dve.tensor_tensor_scan(out=cl[:], data0=resetm[:], data1=ldc[:, 0:TP], initial=0.0, op0=ALU.mult, op1=ALU.add)
                dve.tensor_copy(out=cend[:], in_=cl[:].re("p (c t) -> p c t", t=64)[:, :, 63])
                act.activation(out=PC[:], in_=cend[:], func=AF.Exp)
                c4 = lambda t: t[:, 0:TP].re("p (c t) -> p c t", t=64)
                act.activation(out=e1[:], in_=cl[:], func=AF.Exp)
                dve.tensor_tensor(out=c4(AR)[:, :, :] if False else AR[:, :, 1, :], in0=c4(rr), in1=c4(e1), op=ALU.mult)
                dve.tensor_tensor(out=e2[:], in0=cl[:], in1=ldc[:, 0:TP], op=ALU.subtract)
                act.activation(out=e2[:], in_=e2[:], func=AF.Exp)
                dve.scalar_tensor_tensor(out=AR[:, :, 0, :], in0=c4(kkn), scalar=-1.0, in1=c4(e2), op0=ALU.mult, op1=ALU.mult)
                dve.tensor_copy(out=AV[:, :, 0, :], in_=AR[:, :, 0, :])
                dve.tensor_copy(out=AV[:, :, 1, :], in_=c4(vv))
                act.activation(out=e1[:], in_=cl[:], func=AF.Exp, scale=-1.0)
                dve.tensor_tensor(out=BK[:, :, 0, :], in0=c4(bb), in1=c4(e1), op=ALU.mult)
                dve.tensor_tensor(out=BK[:, :, 1, :], in0=c4(kmod), in1=c4(e1), op=ALU.mult)
                dve.tensor_tensor(out=c4(e2), in0=cend[:, :, None].bc([P, NCH, 64]), in1=c4(cl), op=ALU.subtract)
                act.activation(out=e2[:], in_=e2[:], func=AF.Exp)
                dve.tensor_tensor(out=BKH[:, :, 0, :], in0=c4(bb), in1=c4(e2), op=ALU.mult)
                dve.tensor_tensor(out=BKH[:, :, 1, :], in0=c4(kmod), in1=c4(e2), op=ALU.mult)
                for c in range(NCH):
                    for e in range(2):
                        pb = 64 * e
                        Hs = Hst[2 * hp + e]
                        B0 = PS[4 * e + 0]; B1 = PS[4 * e + 1]; B2 = PS[4 * e + 2]; B3 = PS[4 * e + 3]
                        pe.matmul(out=B0[:, 0:128], lhsT=BK[pb:pb + 64, c, :, :].re("p a b -> p (a b)"),
                                  rhs=AR[pb:pb + 64, c, :, :].re("p a b -> p (a b)"), start=True, stop=True, tile_position=(pb, 0))
                        dve.tensor_tensor(out=GTs[e][:], in0=B0[:, 0:128], in1=maskgt[:], op=ALU.mult)
                        pe.transpose(out=B1[:, 0:64], in_=AV[pb:pb + 64, c, :, :].re("p a b -> p (a b)"), identity=ident[pb:pb + 64, pb:pb + 64],
                                     tile_position=(pb, 0))
                        pe.transpose(out=B1[:, 64:128], in_=BKH[pb:pb + 64, c, :, :].re("p a b -> p (a b)"), identity=ident[pb:pb + 64, pb:pb + 64],
                                     tile_position=(pb, 0))
                        act.copy(out=X[e][:], in_=B1[:, 0:64])
                        act.copy(out=BKHt[e][:], in_=B1[:, 64:128])
                        pw = PW[e]
                        pe.transpose(out=B2[0:64, 0:64], in_=GTs[e][0:64, 0:64], identity=ident[0:64, 0:64])
                        act.copy(out=pw[:, 1, :], in_=B2[0:64, 0:64])
                        dve.tensor_copy(out=pw[:, 0, :], in_=GTs[e][0:64, 0:64])
                        dve.tensor_tensor(out=pw[:, 2, :], in0=GTs[e][0:64, 0:64], in1=ident[0:64, 0:64], op=ALU.add)
                        for lvl in range(1, 6):
                            pe.matmul(out=B2[0:64, 0:64], lhsT=pw[:, 1, :], rhs=pw[:, 0, :], start=True, stop=True)
                            pe.matmul(out=B2[0:64, 64:128], lhsT=pw[:, 0, :], rhs=pw[:, 1, :], start=True, stop=True)
                            act.copy(out=pw[:, 0:2, :].re("p a b -> p (a b)"), in_=B2[0:64, 0:128])
                            pe.matmul(out=B2[0:64, 128:192], lhsT=pw[:, 1, :], rhs=pw[:, 2, :], start=True, stop=True)
                            dve.tensor_tensor(out=pw[:, 2, :], in0=pw[:, 2, :], in1=B2[0:64, 128:192], op=ALU.add)
                        TTv = pw[:, 2, :]
                        pe.matmul(out=B3[0:64, 0:64], lhsT=GTs[e][64:128, 0:64], rhs=X[e][64:128, :], start=True, stop=True, tile_position=(64, 0))
                        act.copy(out=Qs[e][:], in_=B3[0:64, 0:64])
                        pe.matmul(out=B3[0:64, 64:128], lhsT=TTv, rhs=Qs[e][:], start=True, stop=True)
                        act.copy(out=Ys[e][:], in_=B3[0:64, 64:128])
                        pe.matmul(out=B3[pb:pb + 64, 128:192], lhsT=X[e][0:64, :], rhs=TTv, start=True, stop=True, tile_position=(0, pb))
                        act.copy(out=WTs[e][pb:pb + 64, :], in_=B3[pb:pb + 64, 128:192])
                        pe.matmul(out=B0[0:64, 128:192], lhsT=WTs[e][pb:pb + 64, :], rhs=Hs[pb:pb + 64, :], start=True, stop=True, tile_position=(pb, 0))
                        dve.tensor_tensor(out=X[e][0:64, :], in0=B0[0:64, 128:192], in1=Ys[e][:], op=ALU.add)
                        pe.matmul(out=B0[pb:pb + 64, 192:256], lhsT=Hs[pb:pb + 64, :], rhs=AR[pb:pb + 64, c, 1, :], start=True, stop=True,
                                  tile_position=(pb, pb))
                        act.copy(out=oT[pb:pb + 64, c * 64:(c + 1) * 64], in_=B0[pb:pb + 64, 192:256])
                        pe.matmul(out=B0[pb:pb + 64, 256:320], lhsT=X[e][:], rhs=GTs[e][:, 64:128], start=True, stop=True, tile_position=(0, pb))
                        dve.tensor_tensor(out=oT[pb:pb + 64, c * 64:(c + 1) * 64], in0=oT[pb:pb + 64, c * 64:(c + 1) * 64],
                                          in1=B0[pb:pb + 64, 256:320], op=ALU.add)
                        pe.matmul(out=B0[pb:pb + 64, 320:384], lhsT=BKHt[e][:], rhs=X[e][:], start=True, stop=True, tile_position=(0, pb))
                        dve.scalar_tensor_tensor(out=Hs[pb:pb + 64, :], in0=Hs[pb:pb + 64, :], scalar=PC[pb:pb + 64, c:c + 1],
                                                 in1=B0[pb:pb + 64, 320:384], op0=ALU.mult, op1=ALU.add)
                if is_last:
                    rwkv_sample(hp, rr, ldc, kmod, vv, kkn, bb, oT, tmpA)
                dve.tensor_tensor(out=osq[:, 0:TP], in0=oT[:, 0:TP], in1=oT[:, 0:TP], op=ALU.mult)
                for (t0, tn) in C.TT:
                    psm = PS[6]; psq = PS[7]
                    pe.matmul(out=psm[:, 0:tn], lhsT=onesbd[:], rhs=oT[:, t0:t0 + tn], start=True, stop=True)
                    pe.matmul(out=psq[:, 0:tn], lhsT=onesbd[:], rhs=osq[:, t0:t0 + tn], start=True, stop=True)
                    mean = tmpA[:, t0:t0 + tn]; var = osq[:, t0:t0 + tn]
                    act.mul(out=mean, in_=psm[:, 0:tn], mul=1.0 / 64)
                    dve.tensor_tensor(out=e1[:, 0:tn], in0=mean, in1=mean, op=ALU.mult)
                    dve.scalar_tensor_tensor(out=var, in0=psq[:, 0:tn], scalar=1.0 / 64, in1=e1[:, 0:tn], op0=ALU.mult, op1=ALU.subtract)
                    act.activation(out=var, in_=var, func=AF.Sqrt, bias=eps_gn[:], scale=1.0)
                    dve.reciprocal(out=var, in_=var)
                    dve.tensor_tensor(out=oT[:, t0:t0 + tn], in0=oT[:, t0:t0 + tn], in1=mean, op=ALU.subtract)
                    dve.tensor_tensor(out=oT[:, t0:t0 + tn], in0=oT[:, t0:t0 + tn], in1=var, op=ALU.mult)
                dve.tensor_scalar(out=oT[:, 0:TP], in0=oT[:, 0:TP], scalar1=gnw_s[:, hp:hp + 1], scalar2=gnb_s[:, hp:hp + 1], op0=ALU.mult, op1=ALU.add)
                dve.tensor_tensor(out=oT[:, 0:TP], in0=oT[:, 0:TP], in1=bonus[:, 0:TP], op=ALU.add)
                w = load_w(w_in, C.OFF_RWZ // 128 + hp)
                for (t0, tn) in C.TT:
                    psz = PS[6]
                    mm_acc(psz[:, 0:tn], w, KC, lambda kc: hT[:, kc, t0:t0 + tn].r32())
                    act.activation(out=tmpA[:, t0:t0 + tn], in_=psz[:, 0:tn], func=AF.Silu)
                    dve.tensor_tensor(out=orT[:, hp, t0:t0 + tn], in0=oT[:, t0:t0 + tn], in1=tmpA[:, t0:t0 + tn], op=ALU.mult)
            barrier()

    NPT = (NSC * C.H) // 128 if NSC * C.H >= 128 else 1
    NPP = min(128, NSC * C.H)
    BPT = NPP // C.H
    tokm = fw.sb("tokm", [NSC, 2, 6, 64])
    o_bh_all = fw.sb("o_bh_all", [NPP, NPT, 64])

    def rwkv_sample(hp, rr, ldc, kmod, vv, kkn, bb, oT, tmpA):
        act.activation(out=tmpA[:, TP:W1], in_=ldc[:, TP:W1], func=AF.Exp)
        vecs = [rr, tmpA, kmod, vv, kkn, bb]
        for vi, t in enumerate(vecs):
            pe.transpose(out=PS[6 + vi // 3][0:NSC, (vi % 3) * 128:(vi % 3 + 1) * 128], in_=t[:, TP:W1], identity=ident[:])
        act.copy(out=tokm[:, :, 0:3, :], in_=PS[6][0:NSC, 0:384].re("p (v h n) -> p h v n", v=3, h=2))
        act.copy(out=tokm[:, :, 3:6, :], in_=PS[7][0:NSC, 0:384].re("p (v h n) -> p h v n", v=3, h=2))
        st(scr_a[:, 2 * hp:2 * hp + 2, :, :], tokm[:])

    def rwkv_sample_core():
        with fw.scope():
            lsb = fw.sb
            S = lsb("S_s", [NPP, 64, 64]); T1 = lsb("T1_s", [NPP, 64, 64]); vec = lsb("vec_s", [NPP, 6, 64])
            sa = lsb("sa_s", [NPP, 64]); ob = lsb("ob_s", [NPP, 64]); st1 = lsb("st1_s", [NPP, 4])
            gw = lsb("gw_s", [NPP, 64]); gb = lsb("gb_s", [NPP, 64]); rkb = lsb("rkb_s", [NPP, 64])
            ld(gw[:], gnw_bh[0:NPP, :]); ld(gb[:], gnb_bh[0:NPP, :]); ld(rkb[:], rk_bh[0:NPP, :])
            for pt in range(NPT):
                ld(S[:].re("p a b -> p (a b)"), wkv0[pt * NPP:(pt + 1) * NPP, :])
                ld(vec[:], View(scr_a.buf, scr_a.ap[pt * BPT:(pt + 1) * BPT].rearrange("b h v n -> (b h) v n")))
                R = vec[:, 0, :]; Dc = vec[:, 1, :]; K = vec[:, 2, :]; V = vec[:, 3, :]; KK = vec[:, 4, :]; Bv = vec[:, 5, :]
                bj = lambda v: v[:, None, :].bc([NPP, 64, 64])
                bi_ = lambda v: v[:, :, None].bc([NPP, 64, 64])
                dve.tensor_tensor(out=T1[:], in0=S[:], in1=bj(KK), op=ALU.mult)
                dve.tensor_reduce(out=sa[:], in_=T1[:], axis=AX.X, op=ALU.add)
                dve.tensor_scalar_mul(out=sa[:], in0=sa[:], scalar1=-1.0)
                dve.tensor_tensor(out=S[:], in0=S[:], in1=bj(Dc), op=ALU.mult)
                dve.tensor_tensor(out=T1[:], in0=bi_(sa[:]), in1=bj(Bv), op=ALU.mult)
                dve.tensor_tensor(out=S[:], in0=S[:], in1=T1[:], op=ALU.add)
                dve.tensor_tensor(out=T1[:], in0=bi_(V), in1=bj(K), op=ALU.mult)
                dve.tensor_tensor(out=S[:], in0=S[:], in1=T1[:], op=ALU.add)
                st(swkv_o[pt * NPP:(pt + 1) * NPP, :], S[:].re("p a b -> p (a b)"))
                dve.tensor_tensor(out=T1[:], in0=S[:], in1=bj(R), op=ALU.mult)
                dve.tensor_reduce(out=ob[:], in_=T1[:], axis=AX.X, op=ALU.add)
                dve.tensor_reduce(out=st1[:, 0:1], in_=ob[:], axis=AX.X, op=ALU.add)
                dve.tensor_scalar_mul(out=st1[:, 0:1], in0=st1[:, 0:1], scalar1=1.0 / 64)
                dve.tensor_scalar(out=ob[:], in0=ob[:], scalar1=st1[:, 0:1], scalar2=None, op0=ALU.subtract)
                dve.tensor_tensor(out=sa[:], in0=ob[:], in1=ob[:], op=ALU.mult)
                dve.tensor_reduce(out=st1[:, 1:2], in_=sa[:], axis=AX.X, op=ALU.add)
                act.activation(out=st1[:, 1:2], in_=st1[:, 1:2], func=AF.Sqrt, bias=eps_gn[0:NPP, :], scale=1.0 / 64)
                dve.reciprocal(out=st1[:, 1:2], in_=st1[:, 1:2])
                dve.tensor_scalar(out=ob[:], in0=ob[:], scalar1=st1[:, 1:2], scalar2=None, op0=ALU.mult)
                dve.tensor_tensor(out=ob[:], in0=ob[:], in1=gw[:], op=ALU.mult)
                dve.tensor_tensor(out=ob[:], in0=ob[:], in1=gb[:], op=ALU.add)
                dve.tensor_tensor(out=sa[:], in0=R, in1=K, op=ALU.mult)
                dve.tensor_tensor(out=sa[:], in0=sa[:], in1=rkb[:], op=ALU.mult)
                dve.tensor_reduce(out=st1[:, 2:3], in_=sa[:], axis=AX.X, op=ALU.add)
                dve.scalar_tensor_tensor(out=o_bh_all[:, pt, :], in0=V, scalar=st1[:, 2:3], in1=ob[:], op0=ALU.mult, op1=ALU.add)
                st(View(scr_b.buf, scr_b.ap[pt * BPT:(pt + 1) * BPT].rearrange("b (h n) -> (b h) n", n=64)), o_bh_all[:, pt, :])
            otok = lsb("otok_s", [NSC, C.DR]); ld(otok[:], scr_b[:])
            zt_ = lsb("zt_s", [P, NSC]); of_ = lsb("of_s", [P, NSC])
            for hp in range(NHP):
                pe.transpose(out=PS[6][:, 0:NSC], in_=otok[:, hp * 128:(hp + 1) * 128], identity=ident[0:NSC, 0:NSC])
                act.copy(out=of_[:], in_=PS[6][:, 0:NSC])
                w = load_w(w_in, C.OFF_RWZ // 128 + hp)
                mm_acc(PS[7][:, 0:NSC], w, KC, lambda kc: hT[:, kc, TP:TP + NSC].r32())
                act.activation(out=zt_[:], in_=PS[7][:, 0:NSC], func=AF.Silu)
                dve.tensor_tensor(out=orT[:, hp, TP:TP + NSC], in0=of_[:], in1=zt_[:], op=ALU.mult)
            barrier()

    def post_phase(is_last, p):
        with fw.scope():
            lsb = fw.sb
            NTT = TP // 128
            xn = [lsb("xn%d" % i, [P, D]) for i in range(NTT)]
            xns = lsb("xns", [NSC, D]) if is_last else None
            g1 = lsb("g1", [P, 512]); g2 = lsb("g2", [P, 512]); MT = lsb("MT", [P, 512])
            sq = lsb("sqp", [P, D], mybir.dt.bfloat16); ssp = lsb("ssp", [P, NTT + 1])
            fg = lsb("fg", [P, D]); ld(fg[:], fing[:])
            pt_tiles = list(C.TT) + ([(TP, NSC)] if is_last else [])
            for tt in range(NTT):
                ld(xn[tt][:], xb[p * TP + tt * 128:p * TP + (tt + 1) * 128, :])
            if is_last:
                ld(xns[:], xs[:])
            for db in range(KC):
                for (t0, tn) in pt_tiles:
                    wg1 = load_w(w_in, C.OFF_G1 // 128 + db)
                    psg = PS[0]
                    mm_acc(psg[:, 0:tn], wg1, KC, lambda kc: hT[:, kc, t0:t0 + tn].r32())
                    act.activation(out=g1[:, 0:tn], in_=psg[:, 0:tn], func=AF.Sigmoid)
                    wo = load_w(w_out, db)
                    ps1 = PS[1]
                    mm_acc(ps1[:, 0:tn], wo, NUB, lambda kc: osT[:, kc, t0:t0 + tn])
                    dve.tensor_tensor(out=MT[:, 0:tn], in0=g1[:, 0:tn], in1=ps1[:, 0:tn], op=ALU.mult)
                    wg2 = load_w(w_in, C.OFF_G2 // 128 + db)
                    psg2 = PS[2]
                    mm_acc(psg2[:, 0:tn], wg2, KC, lambda kc: hT[:, kc, t0:t0 + tn].r32())
                    act.activation(out=g2[:, 0:tn], in_=psg2[:, 0:tn], func=AF.Sigmoid)
                    ps2 = PS[3]
                    for kc in range(NHP):
                        pe.matmul(out=ps2[:, 0:tn], lhsT=wo[:, NUB + kc, :].r32(), rhs=orT[:, kc, t0:t0 + tn],
                                  start=(kc == 0), stop=(kc == NHP - 1), inc=(kc == NHP - 1))
                    dve.tensor_tensor(out=g2[:, 0:tn], in0=g2[:, 0:tn], in1=ps2[:, 0:tn], op=ALU.mult)
                    dve.tensor_tensor(out=MT[:, 0:tn], in0=MT[:, 0:tn], in1=g2[:, 0:tn], op=ALU.add)
                    if t0 < TP:
                        dve.tensor_scalar_mul(out=MT[:, 0:tn], in0=MT[:, 0:tn], scalar1=gt_p[:, db:db + 1])
                        for q in range(tn // 128):
                            tt = t0 // 128 + q
                            pst = PS[4 + q % 2]
                            pe.transpose(out=pst[:, 0:128], in_=MT[:, q * 128:(q + 1) * 128], identity=ident[:])
                            dve.tensor_tensor(out=xn[tt][:, db * 128:(db + 1) * 128], in0=xn[tt][:, db * 128:(db + 1) * 128],
                                              in1=pst[:, 0:128], op=ALU.add)
                    else:
                        dve.tensor_tensor(out=MT[:, 0:NSC], in0=MT[:, 0:NSC], in1=modT[:, 2 * KC + db, 0:NSC], op=ALU.mult)
                        pst = PS[6]
                        pe.transpose(out=pst[0:NSC, 0:128], in_=MT[:, 0:NSC], identity=ident[:])
                        dve.tensor_tensor(out=xns[:, db * 128:(db + 1) * 128], in0=xns[:, db * 128:(db + 1) * 128],
                                          in1=pst[0:NSC, 0:128], op=ALU.add)
            for tt in range(NTT):
                s = ssp[:, tt:tt + 1]
                act.activation(out=sq[:], in_=xn[tt][:], func=AF.Square, accum_out=s)
                rstd_of(s, 128)
                dve.scalar_tensor_tensor(out=xn[tt][:], in0=xn[tt][:], scalar=s, in1=fg[:], op0=ALU.mult, op1=ALU.mult)
                st(y_o[p * TP + tt * 128:p * TP + (tt + 1) * 128, :], xn[tt][:])
            if is_last:
                s = ssp[0:NSC, NTT:NTT + 1]
                act.activation(out=sq[0:NSC, :], in_=xns[:], func=AF.Square, accum_out=s)
                rstd_of(s, NSC)
                dve.scalar_tensor_tensor(out=xns[:], in0=xns[:], scalar=s, in1=fg[0:NSC, :], op0=ALU.mult, op1=ALU.mult)
                st(ys_o[:], xns[:])
            barrier()

    def sample_prep():
        with fw.scope():
            lsb = fw.sb
            xs_t = lsb("xs_t", [NSC, D]); sh_t = lsb("sh_t", [NSC, D]); sqs = lsb("sqs", [NSC, D]); s1 = lsb("s1s", [NSC, 1])
            xsT = lsb("xsT", [P, KC, NSC]); hs_tok = lsb("hs_tok", [NSC, D]); stg = lsb("stg", [NSC, NSB * 128])
            ld(xs_t[:], xs[:]); ld(sh_t[:], sshift[:])
            act.activation(out=sqs[:], in_=xs_t[:], func=AF.Square, accum_out=s1[:])
            rstd_of(s1[:], NSC)
            act.activation(out=xs_t[:], in_=xs_t[:], func=AF.Copy, scale=s1[:])
            for kc in range(KC):
                pe.transpose(out=PS[0][:, kc * NSC:(kc + 1) * NSC], in_=xs_t[:, kc * 128:(kc + 1) * 128], identity=ident[0:NSC, 0:NSC])
                pe.transpose(out=PS[1][:, kc * NSC:(kc + 1) * NSC], in_=sh_t[:, kc * 128:(kc + 1) * 128], identity=ident[0:NSC, 0:NSC])
            dve.tensor_tensor(out=xsT[:], in0=PS[0][:, 0:KC * NSC].re("p (k n) -> p k n", n=NSC), in1=sceff[:, :, 0:NSC], op=ALU.mult)
            dve.tensor_tensor(out=xsT[:], in0=xsT[:], in1=modT[:, 0:KC, 0:NSC], op=ALU.add)
            dve.tensor_copy(out=hT[:, :, TP:TP + NSC], in_=xsT[:])
            act.copy(out=hT[:, :, TP + NSC:TP + NS2], in_=PS[1][:, 0:KC * NSC].re("p (k n) -> p k n", n=NSC))
            for kc in range(KC):
                pe.transpose(out=PS[2 + kc // 4 % 2][0:NSC, (kc % 4) * 128:(kc % 4 + 1) * 128], in_=xsT[:, kc, :], identity=ident[:])
                act.copy(out=hs_tok[:, kc * 128:(kc + 1) * 128], in_=PS[2 + kc // 4 % 2][0:NSC, (kc % 4) * 128:(kc % 4 + 1) * 128])
            st(sshift_o[:], hs_tok[:])
            for (src, dst) in ((s5re0, x0r), (s5im0, x0i)):
                ld(stg[:], src[:])
                for s in range(NSB):
                    pe.transpose(out=PS[4][:, s * NSC:(s + 1) * NSC], in_=stg[:, s * 128:(s + 1) * 128], identity=ident[0:NSC, 0:NSC])
                act.copy(out=dst[:].re("p s n -> p (s n)"), in_=PS[4][:, 0:NSB * NSC])
            barrier()

    def finals():
        with fw.scope():
            lsb = fw.sb
            xf = [lsb("xf%d" % i, [P, NSB]) for i in range(4)]
            dve.tensor_tensor(out=xf[0][:], in0=cs1[:], in1=s5car_r[:], op=ALU.mult)
            dve.tensor_tensor(out=xf[1][:], in0=sn1[:], in1=s5car_i[:], op=ALU.mult)
            dve.tensor_tensor(out=xf[0][:], in0=xf[0][:], in1=xf[1][:], op=ALU.add)
            dve.tensor_tensor(out=xf[2][:], in0=cs1[:], in1=s5car_i[:], op=ALU.mult)
            dve.tensor_tensor(out=xf[3][:], in0=sn1[:], in1=s5car_r[:], op=ALU.mult)
            dve.tensor_tensor(out=xf[2][:], in0=xf[2][:], in1=xf[3][:], op=ALU.subtract)
            xo = lsb("xfo", [NSB, 2, P])
            pe.transpose(out=PS[0][0:NSB, 0:128], in_=xf[0][:], identity=ident[:])
            pe.transpose(out=PS[0][0:NSB, 128:256], in_=xf[2][:], identity=ident[:])
            act.copy(out=xo[:].re("p a b -> p (a b)"), in_=PS[0][0:NSB, 0:256])
            st(ps5re_o[:], xo[:, 0, :]); st(ps5im_o[:], xo[:, 1, :])
            ho = lsb("hlo", [KC, P])
            pe.transpose(out=PS[1][0:KC, 0:128], in_=hlast[:], identity=ident[:])
            act.copy(out=ho[:], in_=PS[1][0:KC, 0:128])
            st(pshift_o[:], ho[:])
            so = lsb("wkvo", [64, C.H, 64])
            for h in range(C.H):
                pb = 64 * (h % 2)
                pe.transpose(out=PS[2][0:64, (h % 8) * 64:(h % 8 + 1) * 64], in_=Hst[h][pb:pb + 64, :], identity=ident[pb:pb + 64, pb:pb + 64],
                             tile_position=(pb, 0))
                act.copy(out=so[:, h, :], in_=PS[2][0:64, (h % 8) * 64:(h % 8 + 1) * 64])
            st(View(pwkv_o.buf, pwkv_o.ap.rearrange("h i j -> i h j")), so[:])
            so2 = lsb("s5o", [NSC, 512])
            for (src, dsto) in ((x1r, ss5re_o), (x1i, ss5im_o)):
                for g4 in range(NSB // 4):
                    for j in range(4):
                        pe.transpose(out=PS[3][0:NSC, j * 128:(j + 1) * 128], in_=src[:, g4 * 4 + j, :], identity=ident[:])
                    act.copy(out=so2[:], in_=PS[3][0:NSC, :])
                    st(dsto[:, g4 * 512:(g4 + 1) * 512], so2[:])

    try:
      for p in range(NPASS):
          if stop(1):
              break
          is_last = (p == NPASS - 1)
          if is_last:
              sample_prep()
          phase_x(View(xb.buf, xb.ap[p * TP:(p + 1) * TP, :]), is_last)
          barrier()
          if stop(2):
              break
          s5_phase(is_last)
          if stop(3):
              break
          rwkv_phase(is_last)
          if stop(4):
              break
          if is_last:
              rwkv_sample_core()
          post_phase(is_last, p)
          if stop(5):
              break
      if not stop(6):
          finals()
    except StopBuild:
        pass
    fw.finish()
    return nc, fw, es


def _blk(w, kcn):
    K, N = w.shape
    return np.ascontiguousarray(w.reshape(kcn, 128, N // 128, 128).transpose(2, 1, 0, 3))


def _fm(v):
    v = np.asarray(v).reshape(-1)
    return np.ascontiguousarray(v.reshape(-1, 128).T)


def make_in_maps(cfg, inp, n_cores, n_seq):
    C = cfg
    f = np.float32
    H = C.H; NSB = C.NSB
    sh = {}
    sh["w_ada"] = _blk(inp["w_ada"][0], C.KC); sh["w_in"] = _blk(inp["w_in"][0], C.KC)
    sh["w_glu"] = _blk(inp["w_glu"][0], C.NUB); sh["w_out"] = _blk(inp["w_out"][0], C.KC)
    sh["b_ada"] = _fm(inp["b_ada"][0]); sh["norm_g"] = _fm(inp["norm_g"][0]); sh["mu"] = _fm(inp["mu_rw"][0])
    sh["A_re"] = _fm(inp["A_re"][0]); sh["A_im"] = _fm(inp["A_im"][0])
    sh["lstep"] = np.ascontiguousarray(np.repeat(inp["log_step"][0].reshape(NSB, 2), 64, axis=1).T)
    for k in ("B_re", "B_im"):
        sh[k] = np.ascontiguousarray(inp[k][0].reshape(NSB, 2, 64, 16).transpose(1, 2, 0, 3).reshape(128, NSB, 16))
    for k in ("C_re", "C_im"):
        sh[k] = np.ascontiguousarray(inp[k][0].reshape(NSB, 2, 16, 64).transpose(1, 3, 0, 2).reshape(128, NSB, 16))
    sh["Dsk"] = _fm(inp["D_skip"][0]); sh["b_glu"] = _fm(inp["b_glu"][0])
    for k in ("w0", "a0", "k_k", "k_a", "r_k", "gn_w", "gn_b"):
        sh[k] = _fm(inp[k][0])
    sh["w2"] = np.ascontiguousarray(inp["w2"][0]); sh["a2"] = np.ascontiguousarray(inp["a2"][0])
    sh["fing"] = np.ascontiguousarray(np.broadcast_to(inp["final_g"].reshape(1, -1), (128, C.D)))
    rep = max(1, 128 // H)
    sh["gnw_bh"] = np.ascontiguousarray(np.tile(inp["gn_w"][0].reshape(H, 64), (rep, 1))[:128])
    sh["gnb_bh"] = np.ascontiguousarray(np.tile(inp["gn_b"][0].reshape(H, 64), (rep, 1))[:128])
    sh["rk_bh"] = np.ascontiguousarray(np.tile(inp["r_k"][0].reshape(H, 64), (rep, 1))[:128])
    if sh["gnw_bh"].shape[0] < 128:
        for k in ("gnw_bh", "gnb_bh", "rk_bh"):
            sh[k] = np.ascontiguousarray(np.concatenate([sh[k], np.zeros((128 - sh[k].shape[0], 64), f)], 0))
    sh["ident"] = np.eye(128, dtype=f)
    ms = np.triu(np.ones((64, 64), f), 1); mi_ = np.triu(np.ones((64, 64), f), 0)
    sh["maskgt"] = np.block([[ms, mi_], [ms, mi_]]).astype(f)
    ob = np.zeros((128, 128), f); ob[:64, :64] = 1; ob[64:, 64:] = 1
    sh["onesbd"] = ob
    rm = np.ones((128, C.TP), f); rm[:, ::64] = 0
    sh["resetm"] = rm
    rmk = np.zeros((128, 4), f)
    for j in range(4):
        rmk[32 * j:32 * j + 32, j] = 1
    sh["rowmask"] = rmk
    sh["iota"] = np.ascontiguousarray(np.broadcast_to(np.arange(128, dtype=f).reshape(1, -1), (128, 128)))
    sh = {k: np.ascontiguousarray(v, dtype=f) for k, v in sh.items()}
    maps = []
    NSC = C.NSC
    for c in range(n_cores):
        b = c % n_seq
        rows = slice(c * NSC, (c + 1) * NSC)
        m = dict(sh)
        m["xb"] = np.ascontiguousarray(inp["x_prompt"][b], dtype=f)
        cp = inp["c_prompt"][b:b + 1]
        m["c17"] = np.ascontiguousarray(np.concatenate([inp["c_sample"][rows], cp, cp], 0), dtype=f)
        m["xs"] = np.ascontiguousarray(inp["x_sample"][rows, 0, :], dtype=f)
        m["sshift"] = np.ascontiguousarray(inp["state_shift"][0, rows], dtype=f)
        m["s5re0"] = np.ascontiguousarray(inp["state_s5_re"][0, rows].reshape(NSC, -1), dtype=f)
        m["s5im0"] = np.ascontiguousarray(inp["state_s5_im"][0, rows].reshape(NSC, -1), dtype=f)
        m["wkv0"] = np.ascontiguousarray(inp["state_wkv"][0, rows].reshape(NSC * H, 4096), dtype=f)
        maps.append(m)
    return maps


def assemble(cfg, res, n_cores, n_seq):
    C = cfg; f = np.float32
    G = C.G; H = C.H; NSC = C.NSC
    R = lambda c, k: np.asarray(res[c][k], dtype=f)
    y_p = np.stack([R(b, "y") for b in range(n_seq)], 0)
    y_s = np.concatenate([R(c, "ys") for c in range(n_cores)], 0)[:, None, :]
    re_p = np.stack([R(b, "ps5re").reshape(G, 64) for b in range(n_seq)], 0)[None]
    im_p = np.stack([R(b, "ps5im").reshape(G, 64) for b in range(n_seq)], 0)[None]
    wkv_p = np.stack([R(b, "pwkv") for b in range(n_seq)], 0)[None]
    sh_p = np.stack([R(b, "pshift").reshape(-1) for b in range(n_seq)], 0)[None]
    re_s = np.concatenate([R(c, "ss5re").reshape(NSC, G, 64) for c in range(n_cores)], 0)[None]
    im_s = np.concatenate([R(c, "ss5im").reshape(NSC, G, 64) for c in range(n_cores)], 0)[None]
    wkv_s = np.concatenate([R(c, "swkv").reshape(NSC, H, 64, 64) for c in range(n_cores)], 0)[None]
    sh_s = np.concatenate([R(c, "sshift_o") for c in range(n_cores)], 0)[None]
    return (y_p, y_s, re_p, im_p, wkv_p, sh_p, re_s, im_s, wkv_s, sh_s)


def kernel(**inputs):
    cfg = Cfg()
    inp = {k: np.asarray(v) for k, v in inputs.items()}
    nc, fw, es = build(cfg)
    maps = make_in_maps(cfg, inp, 8, 4)
    res = run_bass_kernel_spmd(nc, maps, core_ids=list(range(8)))
    return assemble(cfg, res.results, 8, 4)
```

```python
import math
import os
from contextlib import ExitStack
import numpy as np
import concourse.bass as bass
import concourse.mybir as mybir
from concourse.bass_utils import run_bass_kernel_spmd

F32 = mybir.dt.float32
F32R = mybir.dt.float32r
AF = mybir.ActivationFunctionType
ALU = mybir.AluOpType
AX = mybir.AxisListType


class Cfg:
    def __init__(self, D=2048, TP=512, NSC=16, NPASS=4):
        self.NPASS = NPASS
        self.D = D; self.TP = TP; self.NSC = NSC
        self.KC = D // 128
        self.DS = D // 2; self.DR = D // 2
        self.G = self.DS // 16; self.NSB = self.G // 2; self.NUB = self.DS // 128
        self.H = self.DR // 64; self.NHP = self.DR // 128
        self.NCH = TP // 64
        self.OFF_U = 0; self.OFF_Z = self.DS; self.OFF_RW = 2 * self.DS
        self.NSH = 3 * self.DR + 128
        self.OFF_RWZ = self.OFF_RW + self.NSH
        self.OFF_G1 = self.OFF_RWZ + self.DR
        self.OFF_G2 = self.OFF_G1 + D
        self.NIN = self.OFF_G2 + D
        self.NRB = self.NSH // 128
        self.GB = min(2, self.NUB)
        self.TT = [(i, min(512, TP - i)) for i in range(0, TP, 512)]
        self.TW = TP + 2 * NSC


class TL:
    def __init__(self, sem, name):
        self.sem = sem; self.count = 0; self.name = name


class Buf:
    __slots__ = ("w", "r", "excl")

    def __init__(self):
        self.w = None; self.r = {}; self.excl = False


class View:
    __slots__ = ("buf", "ap")

    def __init__(self, buf, ap):
        self.buf = buf; self.ap = ap

    def __getitem__(self, idx):
        return View(self.buf, self.ap[idx])

    def re(self, s, **kw):
        return View(self.buf, self.ap.rearrange(s, **kw))

    def bc(self, shape):
        return View(self.buf, self.ap.to_broadcast(shape))

    def r32(self):
        return View(self.buf, self.ap.bitcast(F32R))


class Tile:
    def __init__(self, ap, buf=None):
        self.ap = ap; self.buf = buf or Buf()

    def __getitem__(self, idx):
        return View(self.buf, self.ap[idx])

    def sub(self, idx):
        return Tile(self.ap[idx])


class Eng:
    def __init__(self, fw, name, h, tl, self_sync):
        self.fw = fw; self.name = name; self.h = h; self.tl = tl
        self.self_sync = self_sync; self.seen = {}

    def wait_for(self, deps):
        for tl, cnt in deps.items():
            if tl is self.tl and not self.self_sync:
                continue
            if self.seen.get(tl, 0) >= cnt:
                continue
            self.h.wait_ge(tl.sem, cnt)
            self.seen[tl] = cnt

    def __getattr__(self, op):
        def f(inc=True, **kw):
            return self.fw._issue(self, op, kw, inc)
        return f


def _deps(reads, writes):
    deps = {}

    def add(tc):
        if tc is None:
            return
        tl, c = tc
        if deps.get(tl, 0) < c:
            deps[tl] = c
    for v in reads:
        add(v.buf.w)
        if v.buf.excl:
            for tl, c in v.buf.r.items():
                add((tl, c))
    for v in writes:
        add(v.buf.w)
        for tl, c in v.buf.r.items():
            add((tl, c))
    return deps


class FW:
    def __init__(self, nc, es):
        self.nc = nc; self.es = es; self.nsem = 0; self.es_stack = [es]; self.uid = 0
        self.pe = Eng(self, "pe", nc.tensor, self.tl("pe"), False)
        self.act = Eng(self, "act", nc.scalar, self.tl("act"), True)
        self.dve = Eng(self, "dve", nc.vector, self.tl("dve"), True)
        self.pool = Eng(self, "pool", nc.gpsimd, self.tl("pool"), True)
        self.sp = Eng(self, "sp", nc.sync, self.tl("sp"), False)
        self.dma_tls = []

    def tl(self, name):
        self.nsem += 1
        return TL(self.es.enter_context(self.nc.semaphore(name)), name)

    def sb(self, name, shape, dtype=F32):
        self.uid += 1
        return Tile(self.es_stack[-1].enter_context(self.nc.sbuf_tensor("%s_%d" % (name, self.uid), list(shape), dtype))[:])

    def barrier(self):
        engs = (self.pe, self.act, self.dve, self.pool)
        tls = [e.tl for e in engs] + self.dma_tls
        for e in engs + (self.sp,):
            e.wait_for({tl: tl.count for tl in tls if tl.count and tl is not e.tl})

    def scope(self):
        fw = self

        class _S:
            def __enter__(self_):
                self_.les = ExitStack(); fw.es_stack.append(self_.les); return self_

            def __exit__(self_, *a):
                if fw.es_stack[-1] is not self_.les:
                    return False
                fw.barrier(); fw.es_stack.pop(); self_.les.close(); return False
        return _S()

    def ps(self, name, shape=(128, 512)):
        t = Tile(self.es.enter_context(self.nc.psum_tensor(name, list(shape), F32))[:])
        t.buf.excl = True
        return t

    def dram(self, name, shape, kind):
        return Tile(self.nc.dram_tensor(name, list(shape), F32, kind=kind).ap())

    def _issue(self, eng, op, kw, inc):
        reads = [v for k, v in kw.items() if isinstance(v, View) and k not in ("out", "accum_out", "ap")]
        writes = [v for k, v in kw.items() if isinstance(v, View) and k in ("out", "accum_out", "ap")]
        eng.wait_for(_deps(reads, writes))
        args = {k: (v.ap if isinstance(v, View) else v) for k, v in kw.items()}
        inst = getattr(eng.h, op)(**args)
        if inc:
            eng.tl.count += 1
            inst.then_inc(eng.tl.sem, 1)
            stamp = eng.tl.count
        else:
            stamp = eng.tl.count + 1
        for v in reads:
            if v.buf.r.get(eng.tl, 0) < stamp:
                v.buf.r[eng.tl] = stamp
        for v in writes:
            v.buf.w = (eng.tl, stamp); v.buf.r = {}
        return inst

    def dma(self, q, tl, out, in_, **kw):
        eng = {"sp": self.sp, "pool": self.pool, "act": self.act}[q]
        deps = _deps([in_], [out])
        if tl.count:
            deps[tl] = max(deps.get(tl, 0), tl.count)
        eng.wait_for(deps)
        inst = eng.h.dma_start(out=out.ap, in_=in_.ap, **kw)
        tl.count += 16
        inst.then_inc(tl.sem, 16)
        in_.buf.r[tl] = tl.count
        out.buf.w = (tl, tl.count); out.buf.r = {}
        if tl not in self.dma_tls:
            self.dma_tls.append(tl)

    def finish(self):
        for tl in self.dma_tls:
            self.sp.h.wait_ge(tl.sem, tl.count)
        for e in (self.pe, self.act, self.dve, self.pool):
            if e.tl.count:
                self.sp.h.wait_ge(e.tl.sem, e.tl.count)


def build(cfg, dbg=False):
    nc = bass.Bass("TRN2", target_bir_lowering=False)
    es = ExitStack()
    fw = FW(nc, es)
    pe, act, dve, pool = fw.pe, fw.act, fw.dve, fw.pool
    C = cfg
    D, KC, TP, NSC, TW = C.D, C.KC, C.TP, C.NSC, C.TW
    NS2 = 2 * NSC
    NSB, NUB, NHP, NCH, NRB, GB = C.NSB, C.NUB, C.NHP, C.NCH, C.NRB, C.GB
    P = 128

    def din(name, shape):
        return fw.dram(name, shape, "ExternalInput")

    def dout(name, shape):
        return fw.dram(name, shape, "ExternalOutput")

    xb = din("xb", [TP * C.NPASS, D])
    c17 = din("c17", [NSC + 2, D]); xs = din("xs", [NSC, D]); sshift = din("sshift", [NSC, D])
    s5re0 = din("s5re0", [NSC, NSB * 128]); s5im0 = din("s5im0", [NSC, NSB * 128])
    wkv0 = din("wkv0", [NSC * C.H, 4096])
    w_ada = din("w_ada", [3 * KC, P, KC, 128])
    w_in = din("w_in", [C.NIN // 128, P, KC, 128])
    w_glu = din("w_glu", [NUB, P, NUB, 128])
    w_out = din("w_out", [KC, P, KC, 128])
    b_ada = din("b_ada", [P, 3 * KC]); norm_g = din("norm_g", [P, KC]); mu = din("mu", [P, NRB])
    A_re = din("A_re", [P, NSB]); A_im = din("A_im", [P, NSB]); lstep = din("lstep", [P, NSB])
    B_re = din("B_re", [P, NSB, 16]); B_im = din("B_im", [P, NSB, 16])
    C_re = din("C_re", [P, NSB, 16]); C_im = din("C_im", [P, NSB, 16])
    Dsk = din("Dsk", [P, NUB]); b_glu = din("b_glu", [P, NUB])
    w0 = din("w0", [P, NHP]); a0 = din("a0", [P, NHP]); k_k = din("k_k", [P, NHP]); k_a = din("k_a", [P, NHP])
    r_k = din("r_k", [P, NHP]); gn_w = din("gn_w", [P, NHP]); gn_b = din("gn_b", [P, NHP])
    w2 = din("w2", [64, C.DR]); a2 = din("a2", [64, C.DR])
    fing = din("fing", [P, D])
    gnw_bh = din("gnw_bh", [P, 64]); gnb_bh = din("gnb_bh", [P, 64]); rk_bh = din("rk_bh", [P, 64])
    ident_d = din("ident", [P, P]); maskgt_d = din("maskgt", [P, P]); onesbd_d = din("onesbd", [P, P])
    reset_d = din("resetm", [P, TP]); iota_d = din("iota", [P, 128]); rowmask_d = din("rowmask", [P, 4])

    y_o = dout("y", [TP * C.NPASS, D]); ys_o = dout("ys", [NSC, D])
    ps5re_o = dout("ps5re", [NSB, P]); ps5im_o = dout("ps5im", [NSB, P])
    pwkv_o = dout("pwkv", [C.H, 64, 64]); pshift_o = dout("pshift", [KC, P])
    ss5re_o = dout("ss5re", [NSC, NSB * 128]); ss5im_o = dout("ss5im", [NSC, NSB * 128])
    swkv_o = dout("swkv", [NSC * C.H, 4096]); sshift_o = dout("sshift_o", [NSC, D])
    scr_a = fw.dram("scr_a", [NSC, C.H, 6, 64], "Internal")
    scr_b = fw.dram("scr_b", [NSC, C.DR], "Internal")
    dbg_o = {}

    tl_misc = [fw.tl("m%d" % i) for i in range(6)]
    mi = [0]

    def ld(out, in_, q="sp"):
        tl = tl_misc[mi[0] % len(tl_misc)]; mi[0] += 1
        fw.dma(q, tl, out, in_)

    tl_out = [fw.tl("o%d" % i) for i in range(4)]
    oi = [0]

    def st(out, in_, **kw):
        tl = tl_out[oi[0] % len(tl_out)]; oi[0] += 1
        fw.dma("sp", tl, out, in_, **kw)

    def const(name, d, shape):
        t = fw.sb(name, shape); ld(t[:], d[:]); return t
    ident = const("ident_s", ident_d, [P, P]); maskgt = const("maskgt_s", maskgt_d, [P, P])
    onesbd = const("onesbd_s", onesbd_d, [P, P]); resetm = const("reset_s", reset_d, [P, TP])
    iota = const("iota_s", iota_d, [P, 128]); rowmask = const("rowmask_s", rowmask_d, [P, 4])
    b_ada_s = const("b_ada_s", b_ada, [P, 3 * KC]); norm_g_s = const("norm_g_s", norm_g, [P, KC])
    mu_s = const("mu_s", mu, [P, NRB])
    Dsk_s = const("Dsk_s", Dsk, [P, NUB]); b_glu_s = const("b_glu_s", b_glu, [P, NUB])
    w0_s = const("w0_s", w0, [P, NHP]); a0_s = const("a0_s", a0, [P, NHP]); kk_s = const("kk_s", k_k, [P, NHP])
    ka_s = const("ka_s", k_a, [P, NHP]); rk_s = const("rk_s", r_k, [P, NHP])
    gnw_s = const("gnw_s", gn_w, [P, NHP]); gnb_s = const("gnb_s", gn_b, [P, NHP])
    w2_s = fw.sb("w2a2_s", [P, C.DR]); ld(w2_s[0:64, :], w2[:])
    a2_s = w2_s; ld(a2_s[64:128, :], a2[:])
    omu_s = fw.sb("omu_s", [P, NRB])
    dve.tensor_scalar(out=omu_s[:], in0=mu_s[:], scalar1=-1.0, scalar2=1.0, op0=ALU.mult, op1=ALU.add)

    PS = [fw.ps("psb%d" % i) for i in range(8)]

    NRING = 2
    ring = [fw.sb("wring%d" % i, [P, KC, 128], F32R) for i in range(NRING)]
    NSTG = 1
    wstg = [fw.sb("wstg%d" % i, [P, KC, 128]) for i in range(NSTG)]
    stg_tl = [fw.tl("ws%d" % i) for i in range(NSTG)]
    ring_tl = [fw.tl("wr%d" % i) for i in range(NRING)]
    ri = [0]

    def load_w(wd, blk, nk=KC):
        i = ri[0] % NRING; ri[0] += 1
        j = (ri[0] - 1) % NSTG
        fw.dma("sp", stg_tl[j], wstg[j][:, 0:nk, :], View(wd.buf, wd.ap[blk, :, 0:nk, :]))
        pool.tensor_copy(out=ring[i][:, 0:nk, :], in_=wstg[j][:, 0:nk, :])
        return ring[i]

    def mm_acc(psv, wslot, nk, rhs_fn, ncols=128):
        for kc in range(nk):
            pe.matmul(out=psv, lhsT=wslot[:, kc, 0:ncols].r32(), rhs=rhs_fn(kc),
                      start=(kc == 0), stop=(kc == nk - 1), inc=(kc == nk - 1))

    s5 = {}
    tA = const("A_re_s", A_re, [P, NSB]); tAi = const("A_im_s", A_im, [P, NSB]); tls = const("lstep_s", lstep, [P, NSB])
    step = fw.sb("s5step", [P, NSB]); act.activation(out=step[:], in_=tls[:], func=AF.Exp)
    lam = fw.sb("s5lam", [P, NSB]); dve.tensor_scalar_min(out=lam[:], in0=tA[:], scalar1=-1e-4)
    mag = fw.sb("s5mag", [P, NSB]); tmpc = fw.sb("s5tmp", [P, NSB]); theta = fw.sb("s5theta", [P, NSB])
    dve.tensor_tensor(out=tmpc[:], in0=lam[:], in1=step[:], op=ALU.mult)
    act.activation(out=mag[:], in_=tmpc[:], func=AF.Exp)
    dve.tensor_tensor(out=theta[:], in0=tAi[:], in1=step[:], op=ALU.mult)
    TWO_PI = 2.0 * math.pi

    I32 = mybir.dt.int32

    def sin_into(dst, x, shape, phase):
        with fw.scope():
            ki = fw.sb("sr_ki", shape, I32); kf = fw.sb("sr_kf", shape)
            dve.tensor_scalar_add(out=dst, in0=x, scalar1=float(phase))
            dve.tensor_scalar_mul(out=ki[:], in0=dst, scalar1=1.0 / TWO_PI)
            dve.tensor_copy(out=kf[:], in_=ki[:])
            dve.scalar_tensor_tensor(out=dst, in0=kf[:], scalar=-TWO_PI, in1=dst, op0=ALU.mult, op1=ALU.add)
            dve.tensor_scalar(out=kf[:], in0=dst, scalar1=math.pi, scalar2=-TWO_PI, op0=ALU.is_gt, op1=ALU.mult)
            dve.tensor_tensor(out=dst, in0=dst, in1=kf[:], op=ALU.add)
            dve.tensor_scalar(out=kf[:], in0=dst, scalar1=-math.pi, scalar2=TWO_PI, op0=ALU.is_lt, op1=ALU.mult)
            dve.tensor_tensor(out=dst, in0=dst, in1=kf[:], op=ALU.add)
            dve.tensor_scalar(out=dst, in0=dst, scalar1=math.pi, scalar2=-math.pi, op0=ALU.min, op1=ALU.max)
            act.activation(out=dst, in_=dst, func=AF.Sin)

    def sincos(name, ang_view, shape):
        sn = fw.sb(name + "_sin", shape); cs = fw.sb(name + "_cos", shape)
        sin_into(sn[:], ang_view, shape, 0.0)
        sin_into(cs[:], ang_view, shape, 0.5 * math.pi)
        return sn, cs
    thp = theta
    sn1, cs1 = sincos("s5t1", thp[:], [P, NSB])
    abr = fw.sb("s5abr", [P, NSB]); abi = fw.sb("s5abi", [P, NSB])
    dve.tensor_tensor(out=abr[:], in0=mag[:], in1=cs1[:], op=ALU.mult)
    dve.tensor_tensor(out=abi[:], in0=mag[:], in1=sn1[:], op=ALU.mult)
    den = fw.sb("s5den", [P, NSB]); t2 = fw.sb("s5t2", [P, NSB]); fre = fw.sb("s5fre", [P, NSB]); fim = fw.sb("s5fim", [P, NSB])
    abm1 = fw.sb("s5abm1", [P, NSB])
    dve.tensor_tensor(out=den[:], in0=lam[:], in1=lam[:], op=ALU.mult)
    dve.tensor_tensor(out=t2[:], in0=tAi[:], in1=tAi[:], op=ALU.mult)
    dve.tensor_tensor(out=den[:], in0=den[:], in1=t2[:], op=ALU.add)
    dve.reciprocal(out=den[:], in_=den[:])
    dve.tensor_scalar_add(out=abm1[:], in0=abr[:], scalar1=-1.0)
    dve.tensor_tensor(out=fre[:], in0=abm1[:], in1=lam[:], op=ALU.mult)
    dve.tensor_tensor(out=t2[:], in0=abi[:], in1=tAi[:], op=ALU.mult)
    dve.tensor_tensor(out=fre[:], in0=fre[:], in1=t2[:], op=ALU.add)
    dve.tensor_tensor(out=fre[:], in0=fre[:], in1=den[:], op=ALU.mult)
    dve.tensor_tensor(out=fim[:], in0=abi[:], in1=lam[:], op=ALU.mult)
    dve.tensor_tensor(out=t2[:], in0=abm1[:], in1=tAi[:], op=ALU.mult)
    dve.tensor_tensor(out=fim[:], in0=fim[:], in1=t2[:], op=ALU.subtract)
    dve.tensor_tensor(out=fim[:], in0=fim[:], in1=den[:], op=ALU.mult)
    LB = [fw.sb("s5LB" + nm, [P, NUB, 128]) for nm in ("re", "im")]
    LC = [fw.sb("s5ZC" + nm, [P, NSB, 2, 16]) for nm in ("re", "im")]
    with fw.scope():
        Br = const("B_re_s", B_re, [P, NSB, 16]); Bi = const("B_im_s", B_im, [P, NSB, 16])
        bbr = fw.sb("s5bbr", [P, NSB, 16]); bbi = fw.sb("s5bbi", [P, NSB, 16]); bt_ = fw.sb("s5bt", [P, NSB, 16])
        fre_b = fre[:, :, None].bc([P, NSB, 16]); fim_b = fim[:, :, None].bc([P, NSB, 16])
        dve.tensor_tensor(out=bbr[:], in0=Br[:], in1=fre_b, op=ALU.mult)
        dve.tensor_tensor(out=bt_[:], in0=Bi[:], in1=fim_b, op=ALU.mult)
        dve.tensor_tensor(out=bbr[:], in0=bbr[:], in1=bt_[:], op=ALU.subtract)
        dve.tensor_tensor(out=bbi[:], in0=Bi[:], in1=fre_b, op=ALU.mult)
        dve.tensor_tensor(out=bt_[:], in0=Br[:], in1=fim_b, op=ALU.mult)
        dve.tensor_tensor(out=bbi[:], in0=bbi[:], in1=bt_[:], op=ALU.add)
        for li, (nm, src) in enumerate((("re", bbr), ("im", bbi))):
            Z = fw.sb("s5ZB" + nm, [P, NSB, 2, 16])
            dve.memset(ap=Z[:], constant=0.0)
            dve.tensor_copy(out=Z[0:64, :, 0, :], in_=src[0:64, :, :])
            dve.tensor_copy(out=Z[64:128, :, 1, :], in_=src[64:128, :, :])
            L = LB[li]
            for q4 in range(NUB):
                pe.transpose(out=PS[0][:, 0:128], in_=Z[:, 4 * q4:4 * q4 + 4, :, :].re("p a b c -> p (a b c)"), identity=ident[:])
                act.copy(out=L[:, q4, :], in_=PS[0][:, 0:128])
        Cr = const("C_re_s", C_re, [P, NSB, 16]); Ci = const("C_im_s", C_im, [P, NSB, 16])
        for li, (nm, src, sgn) in enumerate((("re", Cr, 1.0), ("im", Ci, -1.0))):
            Z = LC[li]
            dve.memset(ap=Z[:], constant=0.0)
            dve.tensor_scalar_mul(out=Z[0:64, :, 0, :], in0=src[0:64, :, :], scalar1=sgn)
            dve.tensor_scalar_mul(out=Z[64:128, :, 1, :], in0=src[64:128, :, :], scalar1=sgn)
    def make_tables():
      sinT = fw.sb("s5sinT", [P, NSB, 128]); cosT = fw.sb("s5cosT", [P, NSB, 128])
      CH = min(8, NSB)
      with fw.scope():
        ang = fw.sb("s5ang", [P, CH, 128])
        for c0 in range(0, NSB, CH):
            dve.tensor_tensor(out=ang[:], in0=iota[:, None, :].bc([P, CH, 128]), in1=thp[:, c0:c0 + CH, None].bc([P, CH, 128]), op=ALU.mult)
            sin_into(sinT[:, c0:c0 + CH, :], ang[:], [P, CH, 128], 0.0)
            sin_into(cosT[:, c0:c0 + CH, :], ang[:], [P, CH, 128], 0.5 * math.pi)
      return sinT, cosT
    angL = fw.sb("s5angL", [P, NSB]); dve.tensor_scalar_mul(out=angL[:], in0=thp[:], scalar1=128.0)
    snL, csL = sincos("s5tL", angL[:], [P, NSB])
    s5car_r = fw.sb("s5car_r", [P, NSB]); s5car_i = fw.sb("s5car_i", [P, NSB])
    dve.memset(ap=s5car_r[:], constant=0.0); dve.memset(ap=s5car_i[:], constant=0.0)

    NM = NSC + 2
    scT = fw.sb("scT", [P, KC, NM], F32R)
    with fw.scope():
        c_s = fw.sb("c_s", [NM, D]); ld(c_s[:], c17[:])
        csl = fw.sb("csl", [NM, D]); act.activation(out=csl[:], in_=c_s[:], func=AF.Silu)
        for kc in range(KC):
            pe.transpose(out=PS[1][:, kc * NM:(kc + 1) * NM], in_=csl[:, kc * 128:(kc + 1) * 128], identity=ident[0:NM, 0:NM])
        act.copy(out=scT[:].re("p k n -> p (k n)"), in_=PS[1][:, 0:KC * NM])
    modT = fw.sb("modT", [P, 3 * KC, NM])
    for fb in range(3 * KC):
        w = load_w(w_ada, fb)
        psb = PS[2 + fb % 2]
        mm_acc(psb[:, 0:NM], w, KC, lambda kc: scT[:, kc, :])
        act.activation(out=modT[:, fb, :], in_=psb[:, 0:NM], func=AF.Identity, bias=b_ada_s[:, fb:fb + 1], scale=1.0)
    sceff = fw.sb("sceff", [P, KC, NM])
    dve.tensor_scalar_add(out=sceff[:], in0=modT[:, KC:2 * KC, :], scalar1=1.0)
    dve.tensor_tensor(out=sceff[:], in0=sceff[:], in1=norm_g_s[:, :, None].bc([P, KC, NM]), op=ALU.mult)
    sc_p = fw.sb("sc_p", [P, KC]); sh_p = fw.sb("sh_p", [P, KC]); gt_p = fw.sb("gt_p", [P, KC])
    dve.tensor_copy(out=sc_p[:], in_=sceff[:, :, NSC]); dve.tensor_copy(out=sh_p[:], in_=modT[:, 0:KC, NSC])
    dve.tensor_copy(out=gt_p[:], in_=modT[:, 2 * KC:3 * KC, NSC])

    hT = fw.sb("hT", [P, KC, TW], F32R)
    rw_car = fw.sb("rw_car", [P, NRB]); dve.memset(ap=rw_car[:], constant=0.0)
    Hst = [fw.sb("Hst%d" % h, [P, 64]) for h in range(C.H)]
    for h in range(C.H):
        dve.memset(ap=Hst[h][:], constant=0.0)

    eps_rms = fw.sb("eps_rms", [P, 1]); dve.memset(ap=eps_rms[:], constant=1e-6)
    eps_gn = fw.sb("eps_gn", [P, 1]); dve.memset(ap=eps_gn[:], constant=64e-5)

    def rstd_of(ss, n):
        act.activation(out=ss, in_=ss, func=AF.Sqrt, bias=eps_rms[0:n, :], scale=1.0 / D)
        dve.reciprocal(out=ss, in_=ss)

    xt_tl = [fw.tl("xt%d" % i) for i in range(2)]
    ssq = fw.sb("ssq", [P, 2])
    hlast = fw.sb("hlast", [P, KC])

    def phase_x(xd, is_b):
      with fw.scope():
        xt = [fw.sb("xt%d" % i, [P, D]) for i in range(2)]
        xsq = fw.sb("xsq", [P, D], mybir.dt.bfloat16)
        for tt in range(TP // 128):
            i = tt % 2
            fw.dma("sp", xt_tl[i], xt[i][:], xd[tt * 128:(tt + 1) * 128, :])
            s = ssq[:, i:i + 1]
            act.activation(out=xsq[:], in_=xt[i][:], func=AF.Square, accum_out=s)
            rstd_of(s, 128)
            act.activation(out=xt[i][:], in_=xt[i][:], func=AF.Copy, scale=s)
            for kc in range(KC):
                psb = PS[(kc // 4) % 2]
                pe.transpose(out=psb[:, (kc % 4) * 128:(kc % 4 + 1) * 128], in_=xt[i][:, kc * 128:(kc + 1) * 128], identity=ident[:])
                act.activation(out=hT[:, kc, tt * 128:(tt + 1) * 128], in_=psb[:, (kc % 4) * 128:(kc % 4 + 1) * 128],
                               func=AF.Identity, scale=sc_p[:, kc:kc + 1], bias=sh_p[:, kc:kc + 1])
                if is_b and tt == TP // 128 - 1:
                    act.activation(out=hlast[:, kc:kc + 1], in_=psb[:, (kc % 4) * 128 + 127:(kc % 4) * 128 + 128],
                                   func=AF.Identity, scale=sc_p[:, kc:kc + 1], bias=sh_p[:, kc:kc + 1])

    NPASS = C.NPASS
    W1 = TP + NSC
    UID = [0]

    def tiles(is_last):
        t = list(C.TT)
        if is_last:
            t.append((TP, NS2))
        return t

    barrier = fw.barrier
    STOP = float(os.environ.get("K_STOP", "99"))

    class StopBuild(Exception):
        pass

    def stop(n):
        return STOP <= n

    def chk(n):
        if STOP <= n:
            while len(fw.es_stack) > 1:
                fw.es_stack.pop().close()
            raise StopBuild()

    osT = fw.sb("osT", [P, NUB, TW], F32R)
    orT = fw.sb("orT", [P, NHP, TW], F32R)
    x0r = fw.sb("x0r", [P, NSB, NSC]); x0i = fw.sb("x0i", [P, NSB, NSC])
    x1r = fw.sb("x1r", [P, NSB, NSC]); x1i = fw.sb("x1i", [P, NSB, NSC])

    def gelu_inplace(v, tmp_v):
        dve.tensor_tensor(out=tmp_v, in0=v, in1=v, op=ALU.mult)
        dve.tensor_scalar(out=tmp_v, in0=tmp_v, scalar1=0.044715, scalar2=1.0, op0=ALU.mult, op1=ALU.add)
        dve.tensor_tensor(out=tmp_v, in0=tmp_v, in1=v, op=ALU.mult)
        act.activation(out=tmp_v, in_=tmp_v, func=AF.Sigmoid, scale=2.0 * math.sqrt(2.0 / math.pi))
        dve.tensor_tensor(out=v, in0=v, in1=tmp_v, op=ALU.mult)

    def s5_phase(is_last):
        Wd = TW if is_last else TP
        with fw.scope():
            lsb = fw.sb
            sinT, cosT = make_tables()
            chk(2.1)
            uT = lsb("uT", [P, TW]); yT = lsb("yT", [P, NUB, TW], F32R)
            scanscope = fw.scope(); scanscope.__enter__()
            zr = lsb("s5zr", [P, 4, 128]); zi = lsb("s5zi", [P, 4, 128])
            q1 = lsb("s5q1", [P, 4, 128])
            sr = lsb("s5sr", [P, 4, 128]); si = lsb("s5si", [P, 4, 128])
            xr = zr; xi = zi
            cq = [lsb("s5cq%d" % i, [P, 4]) for i in range(4)]
            LBz = [lsb("s5LBz%d" % i, [P, 4, 128]) for i in range(2)]
            sq = [lsb("s5sq%d" % i, [P, 4, NSC]) for i in range(2)]
            for blk in range(NUB):
                chk(2.12)
                w = load_w(w_in, C.OFF_U // 128 + blk)
                chk(2.15)
                for (t0, tn) in tiles(is_last):
                    psb = PS[2 + (t0 // 512) % 2]
                    mm_acc(psb[:, 0:tn], w, KC, lambda kc: hT[:, kc, t0:t0 + tn].r32())
                    chk(2.17)
                    act.copy(out=uT[:, t0:t0 + tn], in_=psb[:, 0:tn])
                chk(2.2)
                for li in range(2):
                    for j in range(4):
                        dve.tensor_scalar_mul(out=LBz[li][:, j, :], in0=LB[li][:, blk, :], scalar1=rowmask[:, j:j + 1])
                s0 = blk * 4; sl = slice(s0, s0 + 4)
                cT = cosT[:, sl, :]; sT = sinT[:, sl, :]
                for ts in range(TP // 128):
                    bur = PS[4 + (ts % 2) * 2]; bui = PS[5 + (ts % 2) * 2]
                    for j in range(4):
                        for (L, psb) in ((LBz[0], bur), (LBz[1], bui)):
                            pe.matmul(out=psb[:, j * 128:(j + 1) * 128], lhsT=L[:, j, :],
                                      rhs=uT[:, ts * 128:(ts + 1) * 128],
                                      start=True, stop=True, inc=(j == 3))
                    chk(2.4)
                    br = bur[:, :].re("p (a b) -> p a b", a=4); bi = bui[:, :].re("p (a b) -> p a b", a=4)
                    dve.tensor_tensor(out=zr[:], in0=br, in1=cT, op=ALU.mult)
                    dve.tensor_tensor(out=q1[:], in0=bi, in1=sT, op=ALU.mult)
                    dve.tensor_tensor(out=zr[:], in0=zr[:], in1=q1[:], op=ALU.add)
                    dve.tensor_tensor(out=zi[:], in0=bi, in1=cT, op=ALU.mult)
                    dve.tensor_tensor(out=q1[:], in0=br, in1=sT, op=ALU.mult)
                    dve.tensor_tensor(out=zi[:], in0=zi[:], in1=q1[:], op=ALU.subtract)
                    chk(2.5)
                    for j in range(4):
                        s = s0 + j
                        dve.tensor_tensor_scan(out=sr[:, j, :], data0=mag[:, s:s + 1].bc([P, 128]), data1=zr[:, j, :],
                                               initial=s5car_r[:, s:s + 1], op0=ALU.mult, op1=ALU.add)
                        dve.tensor_tensor_scan(out=si[:, j, :], data0=mag[:, s:s + 1].bc([P, 128]), data1=zi[:, j, :],
                                               initial=s5car_i[:, s:s + 1], op0=ALU.mult, op1=ALU.add)
                    chk(2.6)
                    la = sr[:, :, 127]; lb = si[:, :, 127]
                    dve.tensor_tensor(out=cq[0][:], in0=csL[:, sl], in1=la, op=ALU.mult)
                    dve.tensor_tensor(out=cq[1][:], in0=snL[:, sl], in1=lb, op=ALU.mult)
                    dve.tensor_tensor(out=cq[2][:], in0=snL[:, sl], in1=la, op=ALU.mult)
                    dve.tensor_tensor(out=cq[3][:], in0=csL[:, sl], in1=lb, op=ALU.mult)
                    dve.tensor_tensor(out=s5car_r[:, sl], in0=cq[0][:], in1=cq[1][:], op=ALU.subtract)
                    dve.tensor_tensor(out=s5car_i[:, sl], in0=cq[2][:], in1=cq[3][:], op=ALU.add)
                    dve.tensor_tensor(out=xr[:], in0=sr[:], in1=cT, op=ALU.mult)
                    dve.tensor_tensor(out=q1[:], in0=si[:], in1=sT, op=ALU.mult)
                    dve.tensor_tensor(out=xr[:], in0=xr[:], in1=q1[:], op=ALU.subtract)
                    dve.tensor_tensor(out=xi[:], in0=sr[:], in1=sT, op=ALU.mult)
                    dve.tensor_tensor(out=q1[:], in0=si[:], in1=cT, op=ALU.mult)
                    dve.tensor_tensor(out=xi[:], in0=xi[:], in1=q1[:], op=ALU.add)
                    chk(2.7)
                    psy = PS[0]
                    for j in range(4):
                        s = s0 + j
                        pe.matmul(out=psy[32 * j:32 * j + 32, 0:128], lhsT=LC[0][:, s, :, :].re("p a b -> p (a b)"),
                                  rhs=xr[:, j, :], start=True, stop=False, tile_position=(0, 32 * j), inc=False)
                        pe.matmul(out=psy[32 * j:32 * j + 32, 0:128], lhsT=LC[1][:, s, :, :].re("p a b -> p (a b)"),
                                  rhs=xi[:, j, :], start=False, stop=True, tile_position=(0, 32 * j), inc=(j == 3))
                    dve.scalar_tensor_tensor(out=yT[:, blk, ts * 128:(ts + 1) * 128], in0=uT[:, ts * 128:(ts + 1) * 128],
                                             scalar=Dsk_s[:, blk:blk + 1], in1=psy[:, 0:128], op0=ALU.mult, op1=ALU.add)
                if is_last:
                    psr = PS[6]; psi = PS[7]
                    for j in range(4):
                        for (L, psb) in ((LBz[0], psr), (LBz[1], psi)):
                            pe.matmul(out=psb[:, j * NSC:(j + 1) * NSC], lhsT=L[:, j, :],
                                      rhs=uT[:, TP:TP + NSC],
                                      start=True, stop=True, inc=(j == 3))
                    abr_b = abr[:, sl, None].bc([P, 4, NSC]); abi_b = abi[:, sl, None].bc([P, 4, NSC])
                    dve.tensor_tensor(out=sq[0][:], in0=x0r[:, sl, :], in1=abr_b, op=ALU.mult)
                    dve.tensor_tensor(out=sq[1][:], in0=x0i[:, sl, :], in1=abi_b, op=ALU.mult)
                    dve.tensor_tensor(out=sq[0][:], in0=sq[0][:], in1=sq[1][:], op=ALU.subtract)
                    dve.tensor_tensor(out=x1r[:, sl, :], in0=sq[0][:], in1=psr[:, 0:4 * NSC].re("p (a b) -> p a b", a=4), op=ALU.add)
                    dve.tensor_tensor(out=sq[0][:], in0=x0i[:, sl, :], in1=abr_b, op=ALU.mult)
                    dve.tensor_tensor(out=sq[1][:], in0=x0r[:, sl, :], in1=abi_b, op=ALU.mult)
                    dve.tensor_tensor(out=sq[0][:], in0=sq[0][:], in1=sq[1][:], op=ALU.add)
                    dve.tensor_tensor(out=x1i[:, sl, :], in0=sq[0][:], in1=psi[:, 0:4 * NSC].re("p (a b) -> p a b", a=4), op=ALU.add)
                    psy = PS[0]
                    for j in range(4):
                        s = s0 + j
                        pe.matmul(out=psy[32 * j:32 * j + 32, 0:NSC], lhsT=LC[0][:, s, :, :].re("p a b -> p (a b)"),
                                  rhs=x1r[:, s, :], start=True, stop=False, tile_position=(0, 32 * j), inc=False)
                        pe.matmul(out=psy[32 * j:32 * j + 32, 0:NSC], lhsT=LC[1][:, s, :, :].re("p a b -> p (a b)"),
                                  rhs=x1i[:, s, :], start=False, stop=True, tile_position=(0, 32 * j), inc=(j == 3))
                    dve.scalar_tensor_tensor(out=yT[:, blk, TP:TP + NSC], in0=uT[:, TP:TP + NSC],
                                             scalar=Dsk_s[:, blk:blk + 1], in1=psy[:, 0:NSC], op0=ALU.mult, op1=ALU.add)
            chk(2.8)
            scanscope.__exit__(None, None, None)
            gtmp = lsb("gtmp", [P, TW]); gs = lsb("gs", [P, TW]); zs = lsb("zs", [P, TW])
            Wy = W1 if is_last else TP
            for blk in range(NUB):
                gelu_inplace(yT[:, blk, 0:Wy], gtmp[:, 0:Wy])
            gt_tiles = list(C.TT) + ([(TP, NSC)] if is_last else [])
            for jb in range(NUB):
                w = load_w(w_glu, jb, nk=NUB)
                w2_ = load_w(w_in, C.OFF_Z // 128 + jb)
                for (t0, tn) in gt_tiles:
                    psb = PS[2]; psz = PS[3]
                    mm_acc(psb[:, 0:tn], w, NUB, lambda kc: yT[:, kc, t0:t0 + tn])
                    act.activation(out=gs[:, t0:t0 + tn], in_=psb[:, 0:tn], func=AF.Sigmoid, bias=b_glu_s[:, jb:jb + 1], scale=1.0)
                    mm_acc(psz[:, 0:tn], w2_, KC, lambda kc: hT[:, kc, t0:t0 + tn].r32())
                    act.activation(out=zs[:, t0:t0 + tn], in_=psz[:, 0:tn], func=AF.Silu)
                    dve.tensor_tensor(out=gs[:, t0:t0 + tn], in0=gs[:, t0:t0 + tn], in1=zs[:, t0:t0 + tn], op=ALU.mult)
                    dve.tensor_tensor(out=osT[:, jb, t0:t0 + tn], in0=gs[:, t0:t0 + tn], in1=yT[:, jb, t0:t0 + tn], op=ALU.mult)
            barrier()

    EM05 = math.exp(-0.5)

    def rwkv_phase(is_last):
        Wd = W1 if is_last else TP
        with fw.scope():
            lsb = fw.sb
            Pb = lsb("Pb", [P, 1 + TW]); tmpA = lsb("tmpA", [P, W1])
            lor = lsb("lor", [P, W1])
            rr = lsb("rr", [P, W1]); kr = lsb("kr", [P, W1]); vv = lsb("vv", [P, W1])
            ldc = lsb("ldc", [P, W1]); alr = lsb("alr", [P, W1]); kkn = lsb("kkn", [P, W1]); kmod = lsb("kmod", [P, W1])
            bb = lsb("bb", [P, W1]); cl = lsb("cl", [P, TP]); e1 = lsb("e1", [P, TP]); e2 = Tile(kr.ap[:, 0:TP], kr.buf)
            cend = lsb("cend", [P, NCH]); PC = lsb("PC", [P, NCH])
            AR = lsb("AR", [P, NCH, 2, 64]); BK = lsb("BK", [P, NCH, 2, 64]); AV = lsb("AV", [P, NCH, 2, 64]); BKH = lsb("BKH", [P, NCH, 2, 64])
            bonus = alr; oT = Pb; osq = kkn
            CB = 3; NSL = 2 * CB
            GTs = [lsb("GTs%d" % e, [P, P]) for e in range(2 * NSL)]
            X = [lsb("X%d" % e, [P, 64]) for e in range(2 * NSL)]
            BKHt = [lsb("BKHt%d" % e, [P, 64]) for e in range(2 * NSL)]
            PW = [lsb("PW%d" % e, [64, 3, 64]) for e in range(NSL)]
            Ys = [lsb("Ys%d" % e, [64, 64]) for e in range(2 * NSL)]
            WTs = [lsb("WTs%d" % e, [P, 64]) for e in range(2 * NSL)]

            def proj_shift(rb, dst):
                w = load_w(w_in, C.OFF_RW // 128 + rb)
                for (t0, tn) in tiles(is_last):
                    psb = PS[6 + (t0 // 512) % 2]
                    mm_acc(psb[:, 0:tn], w, KC, lambda kc: hT[:, kc, t0:t0 + tn].r32())
                    act.copy(out=Pb[:, 1 + t0:1 + t0 + tn], in_=psb[:, 0:tn])
                act.copy(out=Pb[:, 0:1], in_=rw_car[:, rb:rb + 1])
                dve.tensor_scalar_mul(out=tmpA[:, 0:TP], in0=Pb[:, 0:TP], scalar1=mu_s[:, rb:rb + 1])
                dve.scalar_tensor_tensor(out=dst[:, 0:TP], in0=Pb[:, 1:TP + 1], scalar=omu_s[:, rb:rb + 1], in1=tmpA[:, 0:TP],
                                         op0=ALU.mult, op1=ALU.add)
                dve.tensor_copy(out=rw_car[:, rb:rb + 1], in_=Pb[:, TP:TP + 1])
                if is_last:
                    dve.tensor_scalar_mul(out=tmpA[:, TP:W1], in0=Pb[:, 1 + TP + NSC:1 + TP + NS2], scalar1=mu_s[:, rb:rb + 1])
                    dve.scalar_tensor_tensor(out=dst[:, TP:W1], in0=Pb[:, 1 + TP:1 + TP + NSC], scalar=omu_s[:, rb:rb + 1],
                                             in1=tmpA[:, TP:W1], op0=ALU.mult, op1=ALU.add)

            ct_tiles = list(C.TT) + ([(TP, NSC)] if is_last else [])
            proj_shift(3 * NHP, lor)
            act.activation(out=lor[0:64, 0:Wd], in_=lor[0:64, 0:Wd], func=AF.Tanh)
            for hp in range(NHP):
                proj_shift(hp, rr); proj_shift(NHP + hp, kr); proj_shift(2 * NHP + hp, vv)
                for (t0, tn) in ct_tiles:
                    psb = PS[6]
                    pe.matmul(out=psb[:, 0:tn], lhsT=w2_s[0:64, hp * 128:(hp + 1) * 128], rhs=lor[0:64, t0:t0 + tn], start=True, stop=True)
                    act.activation(out=ldc[:, t0:t0 + tn], in_=psb[:, 0:tn], func=AF.Sigmoid, bias=w0_s[:, hp:hp + 1], scale=1.0)
                    psb = PS[7]
                    pe.matmul(out=psb[:, 0:tn], lhsT=a2_s[64:128, hp * 128:(hp + 1) * 128], rhs=lor[64:128, t0:t0 + tn], start=True, stop=True)
                    act.activation(out=alr[:, t0:t0 + tn], in_=psb[:, 0:tn], func=AF.Sigmoid, bias=a0_s[:, hp:hp + 1], scale=1.0)
                dve.tensor_scalar_mul(out=ldc[:, 0:Wd], in0=ldc[:, 0:Wd], scalar1=-EM05)
                dve.tensor_scalar_mul(out=kkn[:, 0:Wd], in0=kr[:, 0:Wd], scalar1=kk_s[:, hp:hp + 1])
                dve.tensor_tensor(out=tmpA[:, 0:Wd], in0=kkn[:, 0:Wd], in1=kkn[:, 0:Wd], op=ALU.mult)
                for (t0, tn) in ct_tiles:
                    psb = PS[6]
                    pe.matmul(out=psb[:, 0:tn], lhsT=onesbd[:], rhs=tmpA[:, t0:t0 + tn], start=True, stop=True)
                    act.activation(out=bb[:, t0:t0 + tn], in_=psb[:, 0:tn], func=AF.Sqrt)
                dve.tensor_scalar_max(out=bb[:, 0:Wd], in0=bb[:, 0:Wd], scalar1=1e-12)
                dve.reciprocal(out=bb[:, 0:Wd], in_=bb[:, 0:Wd])
                dve.tensor_tensor(out=kkn[:, 0:Wd], in0=kkn[:, 0:Wd], in1=bb[:, 0:Wd], op=ALU.mult)
                dve.tensor_scalar(out=kmod[:, 0:Wd], in0=alr[:, 0:Wd], scalar1=-1.0, scalar2=ka_s[:, hp:hp + 1], op0=ALU.add, op1=ALU.mult)
                dve.scalar_tensor_tensor(out=kmod[:, 0:Wd], in0=kmod[:, 0:Wd], scalar=1.0, in1=kr[:, 0:Wd], op0=ALU.add, op1=ALU.mult)
                dve.tensor_tensor(out=bb[:, 0:Wd], in0=kkn[:, 0:Wd], in1=alr[:, 0:Wd], op=ALU.mult)
                dve.scalar_tensor_tensor(out=tmpA[:, 0:Wd], in0=rr[:, 0:Wd], scalar=rk_s[:, hp:hp + 1], in1=kmod[:, 0:Wd], op0=ALU.mult, op1=ALU.mult)
                for (t0, tn) in ct_tiles:
                    psb = PS[6]
                    pe.matmul(out=psb[:, 0:tn], lhsT=onesbd[:], rhs=tmpA[:, t0:t0 + tn], start=True, stop=True)
                    dve.tensor_tensor(out=bonus[:, t0:t0 + tn], in0=psb[:, 0:tn], in1=vv[:, t0:t0 + tn], op=ALU.mult)
                dve.tensor_tensor_scan(out=cl[:], data0=resetm[:], data1=ldc[:, 0:TP], initial=0.0, op0=ALU.mult, op1=ALU.add)
                dve.tensor_copy(out=cend[:], in_=cl[:].re("p (c t) -> p c t", t=64)[:, :, 63])
                act.activation(out=PC[:], in_=cend[:], func=AF.Exp)
                c4 = lambda t: t[:, 0:TP].re("p (c t) -> p c t", t=64)
                act.activation(out=e1[:], in_=cl[:], func=AF.Exp)
                dve.tensor_tensor(out=c4(AR)[:, :, :] if False else AR[:, :, 1, :], in0=c4(rr), in1=c4(e1), op=ALU.mult)
                dve.tensor_tensor(out=e2[:], in0=cl[:], in1=ldc[:, 0:TP], op=ALU.subtract)
                act.activation(out=e2[:], in_=e2[:], func=AF.Exp)
                dve.scalar_tensor_tensor(out=AR[:, :, 0, :], in0=c4(kkn), scalar=-1.0, in1=c4(e2), op0=ALU.mult, op1=ALU.mult)
                dve.tensor_copy(out=AV[:, :, 0, :], in_=AR[:, :, 0, :])
                dve.tensor_copy(out=AV[:, :, 1, :], in_=c4(vv))
                act.activation(out=e1[:], in_=cl[:], func=AF.Exp, scale=-1.0)
                dve.tensor_tensor(out=BK[:, :, 0, :], in0=c4(bb), in1=c4(e1), op=ALU.mult)
                dve.tensor_tensor(out=BK[:, :, 1, :], in0=c4(kmod), in1=c4(e1), op=ALU.mult)
                dve.tensor_tensor(out=c4(e2), in0=cend[:, :, None].bc([P, NCH, 64]), in1=c4(cl), op=ALU.subtract)
                act.activation(out=e2[:], in_=e2[:], func=AF.Exp)
                dve.tensor_tensor(out=BKH[:, :, 0, :], in0=c4(bb), in1=c4(e2), op=ALU.mult)
                dve.tensor_tensor(out=BKH[:, :, 1, :], in0=c4(kmod), in1=c4(e2), op=ALU.mult)
                def fl(v):
                    return v.re("p a b -> p (a b)")

                def stage1(pairs, sb):
                    n = len(pairs)
                    for i, (c, e) in enumerate(pairs):
                        pb = 64 * e; B = PS[i]
                        pe.matmul(out=B[:, 0:128], lhsT=fl(BK[pb:pb + 64, c, :, :]), rhs=fl(AR[pb:pb + 64, c, :, :]),
                                  start=True, stop=True, tile_position=(pb, 0))
                        pe.transpose(out=B[:, 128:192], in_=fl(AV[pb:pb + 64, c, :, :]), identity=ident[pb:pb + 64, pb:pb + 64],
                                     tile_position=(pb, 0))
                        pe.transpose(out=B[:, 192:256], in_=fl(BKH[pb:pb + 64, c, :, :]), identity=ident[pb:pb + 64, pb:pb + 64],
                                     tile_position=(pb, 0))
                    for i, (c, e) in enumerate(pairs):
                        B = PS[i]; r = sb + i
                        dve.tensor_tensor(out=GTs[r][:], in0=B[:, 0:128], in1=maskgt[:], op=ALU.mult)
                        act.copy(out=X[r][:], in_=B[:, 128:192])
                        act.copy(out=BKHt[r][:], in_=B[:, 192:256])
                    yield
                    for i, (c, e) in enumerate(pairs):
                        pe.transpose(out=PS[i][0:64, 0:64], in_=GTs[sb + i][0:64, 0:64], identity=ident[0:64, 0:64])
                    for i, (c, e) in enumerate(pairs):
                        r = sb + i; pw = PW[i]
                        act.copy(out=pw[:, 1, :], in_=PS[i][0:64, 0:64])
                        dve.tensor_copy(out=pw[:, 0, :], in_=GTs[r][0:64, 0:64])
                        dve.tensor_tensor(out=pw[:, 2, :], in0=GTs[r][0:64, 0:64], in1=ident[0:64, 0:64], op=ALU.add)
                    yield
                    for lvl in range(1, 6):
                        for i in range(n):
                            pw = PW[i]; B = PS[i]
                            pe.matmul(out=B[0:64, 0:64], lhsT=pw[:, 1, :], rhs=pw[:, 0, :], start=True, stop=True)
                            pe.matmul(out=B[0:64, 64:128], lhsT=pw[:, 0, :], rhs=pw[:, 1, :], start=True, stop=True)
                        for i in range(n):
                            act.copy(out=fl(PW[i][:, 0:2, :]), in_=PS[i][0:64, 0:128])
                        yield
                        for i in range(n):
                            pw = PW[i]
                            pe.matmul(out=PS[i][0:64, 128:192], lhsT=pw[:, 1, :], rhs=pw[:, 2, :], start=True, stop=True)
                        for i in range(n):
                            pw = PW[i]
                            dve.tensor_tensor(out=pw[:, 2, :], in0=pw[:, 2, :], in1=PS[i][0:64, 128:192], op=ALU.add)
                        yield
                    for i, (c, e) in enumerate(pairs):
                        pb = 64 * e; r = sb + i; B = PS[i]
                        pe.matmul(out=B[0:64, 256:320], lhsT=GTs[r][64:128, 0:64], rhs=X[r][64:128, :], start=True, stop=True,
                                  tile_position=(64, 0))
                    for i, (c, e) in enumerate(pairs):
                        pb = 64 * e; r = sb + i; B = PS[i]
                        pe.matmul(out=B[pb:pb + 64, 320:384], lhsT=X[r][0:64, :], rhs=PW[i][:, 2, :], start=True, stop=True,
                                  tile_position=(0, pb))
                    for i, (c, e) in enumerate(pairs):
                        pb = 64 * e; r = sb + i; B = PS[i]
                        act.copy(out=PW[i][:, 0, :], in_=B[0:64, 256:320])
                        act.copy(out=WTs[r][pb:pb + 64, :], in_=B[pb:pb + 64, 320:384])
                    yield
                    for i in range(n):
                        pe.matmul(out=PS[i][0:64, 384:448], lhsT=PW[i][:, 2, :], rhs=PW[i][:, 0, :], start=True, stop=True)
                    for i in range(n):
                        act.copy(out=Ys[sb + i][:], in_=PS[i][0:64, 384:448])
                    yield

                def stage2(chs, sb):
                    for ci, c in enumerate(chs):
                        for e in range(2):
                            pb = 64 * e; r = sb + 2 * ci + e; Hs = Hst[2 * hp + e]
                            pe.matmul(out=PS[6 + e][0:64, 0:64], lhsT=WTs[r][pb:pb + 64, :], rhs=Hs[pb:pb + 64, :], start=True, stop=True,
                                      tile_position=(pb, 0))
                        for e in range(2):
                            r = sb + 2 * ci + e
                            dve.tensor_tensor(out=X[r][0:64, :], in0=PS[6 + e][0:64, 0:64], in1=Ys[r][:], op=ALU.add)
                        yield
                        for e in range(2):
                            pb = 64 * e; r = sb + 2 * ci + e; Hs = Hst[2 * hp + e]; B = PS[6 + e]
                            pe.matmul(out=B[pb:pb + 64, 64:128], lhsT=Hs[pb:pb + 64, :], rhs=AR[pb:pb + 64, c, 1, :], start=True, stop=True,
                                      tile_position=(pb, pb))
                            pe.matmul(out=B[pb:pb + 64, 128:192], lhsT=X[r][:], rhs=GTs[r][:, 64:128], start=True, stop=True,
                                      tile_position=(0, pb))
                            pe.matmul(out=B[pb:pb + 64, 192:256], lhsT=BKHt[r][:], rhs=X[r][:], start=True, stop=True, tile_position=(0, pb))
                        for e in range(2):
                            pb = 64 * e; Hs = Hst[2 * hp + e]; B = PS[6 + e]
                            act.copy(out=oT[pb:pb + 64, c * 64:(c + 1) * 64], in_=B[pb:pb + 64, 64:128])
                            dve.tensor_tensor(out=oT[pb:pb + 64, c * 64:(c + 1) * 64], in0=oT[pb:pb + 64, c * 64:(c + 1) * 64],
                                              in1=B[pb:pb + 64, 128:192], op=ALU.add)
                            dve.scalar_tensor_tensor(out=Hs[pb:pb + 64, :], in0=Hs[pb:pb + 64, :], scalar=PC[pb:pb + 64, c:c + 1],
                                                     in1=B[pb:pb + 64, 192:256], op0=ALU.mult, op1=ALU.add)
                        yield

                batches = [list(range(c0, min(c0 + CB, NCH))) for c0 in range(0, NCH, CB)]
                prev = None
                for bi in range(len(batches) + 1):
                    g1 = None; g2 = None
                    if bi < len(batches):
                        chs = batches[bi]
                        g1 = stage1([(c, e) for c in chs for e in range(2)], (bi % 2) * NSL)
                    if prev is not None:
                        g2 = stage2(prev[0], prev[1])
                    while g1 is not None or g2 is not None:
                        for _ in range(2):
                            if g1 is not None:
                                try:
                                    next(g1)
                                except StopIteration:
                                    g1 = None
                        if g2 is not None:
                            try:
                                next(g2)
                            except StopIteration:
                                g2 = None
                    prev = (batches[bi], (bi % 2) * NSL) if bi < len(batches) else None
                if is_last:
                    rwkv_sample(hp, rr, ldc, kmod, vv, kkn, bb, oT, tmpA)
                dve.tensor_tensor(out=osq[:, 0:TP], in0=oT[:, 0:TP], in1=oT[:, 0:TP], op=ALU.mult)
                for (t0, tn) in C.TT:
                    psm = PS[6]; psq = PS[7]
                    pe.matmul(out=psm[:, 0:tn], lhsT=onesbd[:], rhs=oT[:, t0:t0 + tn], start=True, stop=True)
                    pe.matmul(out=psq[:, 0:tn], lhsT=onesbd[:], rhs=osq[:, t0:t0 + tn], start=True, stop=True)
                    mean = tmpA[:, t0:t0 + tn]; var = osq[:, t0:t0 + tn]
                    act.mul(out=mean, in_=psm[:, 0:tn], mul=1.0 / 64)
                    dve.tensor_tensor(out=e1[:, 0:tn], in0=mean, in1=mean, op=ALU.mult)
                    dve.scalar_tensor_tensor(out=var, in0=psq[:, 0:tn], scalar=1.0 / 64, in1=e1[:, 0:tn], op0=ALU.mult, op1=ALU.subtract)
                    act.activation(out=var, in_=var, func=AF.Sqrt, bias=eps_gn[:], scale=1.0)
                    dve.reciprocal(out=var, in_=var)
                    dve.tensor_tensor(out=oT[:, t0:t0 + tn], in0=oT[:, t0:t0 + tn], in1=mean, op=ALU.subtract)
                    dve.tensor_tensor(out=oT[:, t0:t0 + tn], in0=oT[:, t0:t0 + tn], in1=var, op=ALU.mult)
                dve.tensor_scalar(out=oT[:, 0:TP], in0=oT[:, 0:TP], scalar1=gnw_s[:, hp:hp + 1], scalar2=gnb_s[:, hp:hp + 1], op0=ALU.mult, op1=ALU.add)
                dve.tensor_tensor(out=oT[:, 0:TP], in0=oT[:, 0:TP], in1=bonus[:, 0:TP], op=ALU.add)
                w = load_w(w_in, C.OFF_RWZ // 128 + hp)
                for (t0, tn) in C.TT:
                    psz = PS[6]
                    mm_acc(psz[:, 0:tn], w, KC, lambda kc: hT[:, kc, t0:t0 + tn].r32())
                    act.activation(out=tmpA[:, t0:t0 + tn], in_=psz[:, 0:tn], func=AF.Silu)
                    dve.tensor_tensor(out=orT[:, hp, t0:t0 + tn], in0=oT[:, t0:t0 + tn], in1=tmpA[:, t0:t0 + tn], op=ALU.mult)
            barrier()

    NPT = (NSC * C.H) // 128 if NSC * C.H >= 128 else 1
    NPP = min(128, NSC * C.H)
    BPT = NPP // C.H
    tokm = fw.sb("tokm", [NSC, 2, 6, 64])
    o_bh_all = fw.sb("o_bh_all", [NPP, NPT, 64])

    def rwkv_sample(hp, rr, ldc, kmod, vv, kkn, bb, oT, tmpA):
        act.activation(out=tmpA[:, TP:W1], in_=ldc[:, TP:W1], func=AF.Exp)
        vecs = [rr, tmpA, kmod, vv, kkn, bb]
        for vi, t in enumerate(vecs):
            pe.transpose(out=PS[6 + vi // 3][0:NSC, (vi % 3) * 128:(vi % 3 + 1) * 128], in_=t[:, TP:W1], identity=ident[:])
        act.copy(out=tokm[:, :, 0:3, :], in_=PS[6][0:NSC, 0:384].re("p (v h n) -> p h v n", v=3, h=2))
        act.copy(out=tokm[:, :, 3:6, :], in_=PS[7][0:NSC, 0:384].re("p (v h n) -> p h v n", v=3, h=2))
        st(scr_a[:, 2 * hp:2 * hp + 2, :, :], tokm[:])

    def rwkv_sample_core():
        with fw.scope():
            lsb = fw.sb
            S = lsb("S_s", [NPP, 64, 64]); T1 = lsb("T1_s", [NPP, 64, 64]); vec = lsb("vec_s", [NPP, 6, 64])
            sa = lsb("sa_s", [NPP, 64]); ob = lsb("ob_s", [NPP, 64]); st1 = lsb("st1_s", [NPP, 4])
            gw = lsb("gw_s", [NPP, 64]); gb = lsb("gb_s", [NPP, 64]); rkb = lsb("rkb_s", [NPP, 64])
            ld(gw[:], gnw_bh[0:NPP, :]); ld(gb[:], gnb_bh[0:NPP, :]); ld(rkb[:], rk_bh[0:NPP, :])
            for pt in range(NPT):
                ld(S[:].re("p a b -> p (a b)"), wkv0[pt * NPP:(pt + 1) * NPP, :])
                ld(vec[:], View(scr_a.buf, scr_a.ap[pt * BPT:(pt + 1) * BPT].rearrange("b h v n -> (b h) v n")))
                R = vec[:, 0, :]; Dc = vec[:, 1, :]; K = vec[:, 2, :]; V = vec[:, 3, :]; KK = vec[:, 4, :]; Bv = vec[:, 5, :]
                bj = lambda v: v[:, None, :].bc([NPP, 64, 64])
                bi_ = lambda v: v[:, :, None].bc([NPP, 64, 64])
                dve.tensor_tensor(out=T1[:], in0=S[:], in1=bj(KK), op=ALU.mult)
                dve.tensor_reduce(out=sa[:], in_=T1[:], axis=AX.X, op=ALU.add)
                dve.tensor_scalar_mul(out=sa[:], in0=sa[:], scalar1=-1.0)
                dve.tensor_tensor(out=S[:], in0=S[:], in1=bj(Dc), op=ALU.mult)
                dve.tensor_tensor(out=T1[:], in0=bi_(sa[:]), in1=bj(Bv), op=ALU.mult)
                dve.tensor_tensor(out=S[:], in0=S[:], in1=T1[:], op=ALU.add)
                dve.tensor_tensor(out=T1[:], in0=bi_(V), in1=bj(K), op=ALU.mult)
                dve.tensor_tensor(out=S[:], in0=S[:], in1=T1[:], op=ALU.add)
                st(swkv_o[pt * NPP:(pt + 1) * NPP, :], S[:].re("p a b -> p (a b)"))
                dve.tensor_tensor(out=T1[:], in0=S[:], in1=bj(R), op=ALU.mult)
                dve.tensor_reduce(out=ob[:], in_=T1[:], axis=AX.X, op=ALU.add)
                dve.tensor_reduce(out=st1[:, 0:1], in_=ob[:], axis=AX.X, op=ALU.add)
                dve.tensor_scalar_mul(out=st1[:, 0:1], in0=st1[:, 0:1], scalar1=1.0 / 64)
                dve.tensor_scalar(out=ob[:], in0=ob[:], scalar1=st1[:, 0:1], scalar2=None, op0=ALU.subtract)
                dve.tensor_tensor(out=sa[:], in0=ob[:], in1=ob[:], op=ALU.mult)
                dve.tensor_reduce(out=st1[:, 1:2], in_=sa[:], axis=AX.X, op=ALU.add)
                act.activation(out=st1[:, 1:2], in_=st1[:, 1:2], func=AF.Sqrt, bias=eps_gn[0:NPP, :], scale=1.0 / 64)
                dve.reciprocal(out=st1[:, 1:2], in_=st1[:, 1:2])
                dve.tensor_scalar(out=ob[:], in0=ob[:], scalar1=st1[:, 1:2], scalar2=None, op0=ALU.mult)
                dve.tensor_tensor(out=ob[:], in0=ob[:], in1=gw[:], op=ALU.mult)
                dve.tensor_tensor(out=ob[:], in0=ob[:], in1=gb[:], op=ALU.add)
                dve.tensor_tensor(out=sa[:], in0=R, in1=K, op=ALU.mult)
                dve.tensor_tensor(out=sa[:], in0=sa[:], in1=rkb[:], op=ALU.mult)
                dve.tensor_reduce(out=st1[:, 2:3], in_=sa[:], axis=AX.X, op=ALU.add)
                dve.scalar_tensor_tensor(out=o_bh_all[:, pt, :], in0=V, scalar=st1[:, 2:3], in1=ob[:], op0=ALU.mult, op1=ALU.add)
                st(View(scr_b.buf, scr_b.ap[pt * BPT:(pt + 1) * BPT].rearrange("b (h n) -> (b h) n", n=64)), o_bh_all[:, pt, :])
            otok = lsb("otok_s", [NSC, C.DR]); ld(otok[:], scr_b[:])
            zt_ = lsb("zt_s", [P, NSC]); of_ = lsb("of_s", [P, NSC])
            for hp in range(NHP):
                pe.transpose(out=PS[6][:, 0:NSC], in_=otok[:, hp * 128:(hp + 1) * 128], identity=ident[0:NSC, 0:NSC])
                act.copy(out=of_[:], in_=PS[6][:, 0:NSC])
                w = load_w(w_in, C.OFF_RWZ // 128 + hp)
                mm_acc(PS[7][:, 0:NSC], w, KC, lambda kc: hT[:, kc, TP:TP + NSC].r32())
                act.activation(out=zt_[:], in_=PS[7][:, 0:NSC], func=AF.Silu)
                dve.tensor_tensor(out=orT[:, hp, TP:TP + NSC], in0=of_[:], in1=zt_[:], op=ALU.mult)
            barrier()

    def post_phase(is_last, p):
        with fw.scope():
            lsb = fw.sb
            NTT = TP // 128
            xn = [lsb("xn%d" % i, [P, D]) for i in range(NTT)]
            xns = lsb("xns", [NSC, D]) if is_last else None
            g1 = lsb("g1", [P, 512]); g2 = lsb("g2", [P, 512]); MT = lsb("MT", [P, 512])
            sq = lsb("sqp", [P, D], mybir.dt.bfloat16); ssp = lsb("ssp", [P, NTT + 1])
            fg = lsb("fg", [P, D]); ld(fg[:], fing[:])
            pt_tiles = list(C.TT) + ([(TP, NSC)] if is_last else [])
            for tt in range(NTT):
                ld(xn[tt][:], xb[p * TP + tt * 128:p * TP + (tt + 1) * 128, :])
            if is_last:
                ld(xns[:], xs[:])
            for db in range(KC):
                for (t0, tn) in pt_tiles:
                    wg1 = load_w(w_in, C.OFF_G1 // 128 + db)
                    psg = PS[0]
                    mm_acc(psg[:, 0:tn], wg1, KC, lambda kc: hT[:, kc, t0:t0 + tn].r32())
                    act.activation(out=g1[:, 0:tn], in_=psg[:, 0:tn], func=AF.Sigmoid)
                    wo = load_w(w_out, db)
                    ps1 = PS[1]
                    mm_acc(ps1[:, 0:tn], wo, NUB, lambda kc: osT[:, kc, t0:t0 + tn])
                    dve.tensor_tensor(out=MT[:, 0:tn], in0=g1[:, 0:tn], in1=ps1[:, 0:tn], op=ALU.mult)
                    wg2 = load_w(w_in, C.OFF_G2 // 128 + db)
                    psg2 = PS[2]
                    mm_acc(psg2[:, 0:tn], wg2, KC, lambda kc: hT[:, kc, t0:t0 + tn].r32())
                    act.activation(out=g2[:, 0:tn], in_=psg2[:, 0:tn], func=AF.Sigmoid)
                    ps2 = PS[3]
                    for kc in range(NHP):
                        pe.matmul(out=ps2[:, 0:tn], lhsT=wo[:, NUB + kc, :].r32(), rhs=orT[:, kc, t0:t0 + tn],
                                  start=(kc == 0), stop=(kc == NHP - 1), inc=(kc == NHP - 1))
                    dve.tensor_tensor(out=g2[:, 0:tn], in0=g2[:, 0:tn], in1=ps2[:, 0:tn], op=ALU.mult)
                    dve.tensor_tensor(out=MT[:, 0:tn], in0=MT[:, 0:tn], in1=g2[:, 0:tn], op=ALU.add)
                    if t0 < TP:
                        dve.tensor_scalar_mul(out=MT[:, 0:tn], in0=MT[:, 0:tn], scalar1=gt_p[:, db:db + 1])
                        for q in range(tn // 128):
                            tt = t0 // 128 + q
                            pst = PS[4 + q % 2]
                            pe.transpose(out=pst[:, 0:128], in_=MT[:, q * 128:(q + 1) * 128], identity=ident[:])
                            dve.tensor_tensor(out=xn[tt][:, db * 128:(db + 1) * 128], in0=xn[tt][:, db * 128:(db + 1) * 128],
                                              in1=pst[:, 0:128], op=ALU.add)
                    else:
                        dve.tensor_tensor(out=MT[:, 0:NSC], in0=MT[:, 0:NSC], in1=modT[:, 2 * KC + db, 0:NSC], op=ALU.mult)
                        pst = PS[6]
                        pe.transpose(out=pst[0:NSC, 0:128], in_=MT[:, 0:NSC], identity=ident[:])
                        dve.tensor_tensor(out=xns[:, db * 128:(db + 1) * 128], in0=xns[:, db * 128:(db + 1) * 128],
                                          in1=pst[0:NSC, 0:128], op=ALU.add)
            for tt in range(NTT):
                s = ssp[:, tt:tt + 1]
                act.activation(out=sq[:], in_=xn[tt][:], func=AF.Square, accum_out=s)
                rstd_of(s, 128)
                dve.scalar_tensor_tensor(out=xn[tt][:], in0=xn[tt][:], scalar=s, in1=fg[:], op0=ALU.mult, op1=ALU.mult)
                st(y_o[p * TP + tt * 128:p * TP + (tt + 1) * 128, :], xn[tt][:])
            if is_last:
                s = ssp[0:NSC, NTT:NTT + 1]
                act.activation(out=sq[0:NSC, :], in_=xns[:], func=AF.Square, accum_out=s)
                rstd_of(s, NSC)
                dve.scalar_tensor_tensor(out=xns[:], in0=xns[:], scalar=s, in1=fg[0:NSC, :], op0=ALU.mult, op1=ALU.mult)
                st(ys_o[:], xns[:])
            barrier()

    def sample_prep():
        with fw.scope():
            lsb = fw.sb
            xs_t = lsb("xs_t", [NSC, D]); sh_t = lsb("sh_t", [NSC, D]); sqs = lsb("sqs", [NSC, D]); s1 = lsb("s1s", [NSC, 1])
            xsT = lsb("xsT", [P, KC, NSC]); hs_tok = lsb("hs_tok", [NSC, D]); stg = lsb("stg", [NSC, NSB * 128])
            ld(xs_t[:], xs[:]); ld(sh_t[:], sshift[:])
            act.activation(out=sqs[:], in_=xs_t[:], func=AF.Square, accum_out=s1[:])
            rstd_of(s1[:], NSC)
            act.activation(out=xs_t[:], in_=xs_t[:], func=AF.Copy, scale=s1[:])
            for kc in range(KC):
                pe.transpose(out=PS[0][:, kc * NSC:(kc + 1) * NSC], in_=xs_t[:, kc * 128:(kc + 1) * 128], identity=ident[0:NSC, 0:NSC])
                pe.transpose(out=PS[1][:, kc * NSC:(kc + 1) * NSC], in_=sh_t[:, kc * 128:(kc + 1) * 128], identity=ident[0:NSC, 0:NSC])
            dve.tensor_tensor(out=xsT[:], in0=PS[0][:, 0:KC * NSC].re("p (k n) -> p k n", n=NSC), in1=sceff[:, :, 0:NSC], op=ALU.mult)
            dve.tensor_tensor(out=xsT[:], in0=xsT[:], in1=modT[:, 0:KC, 0:NSC], op=ALU.add)
            dve.tensor_copy(out=hT[:, :, TP:TP + NSC], in_=xsT[:])
            act.copy(out=hT[:, :, TP + NSC:TP + NS2], in_=PS[1][:, 0:KC * NSC].re("p (k n) -> p k n", n=NSC))
            for kc in range(KC):
                pe.transpose(out=PS[2 + kc // 4 % 2][0:NSC, (kc % 4) * 128:(kc % 4 + 1) * 128], in_=xsT[:, kc, :], identity=ident[:])
                act.copy(out=hs_tok[:, kc * 128:(kc + 1) * 128], in_=PS[2 + kc // 4 % 2][0:NSC, (kc % 4) * 128:(kc % 4 + 1) * 128])
            st(sshift_o[:], hs_tok[:])
            for (src, dst) in ((s5re0, x0r), (s5im0, x0i)):
                ld(stg[:], src[:])
                for s in range(NSB):
                    pe.transpose(out=PS[4][:, s * NSC:(s + 1) * NSC], in_=stg[:, s * 128:(s + 1) * 128], identity=ident[0:NSC, 0:NSC])
                act.copy(out=dst[:].re("p s n -> p (s n)"), in_=PS[4][:, 0:NSB * NSC])
            barrier()

    def finals():
        with fw.scope():
            lsb = fw.sb
            xf = [lsb("xf%d" % i, [P, NSB]) for i in range(4)]
            dve.tensor_tensor(out=xf[0][:], in0=cs1[:], in1=s5car_r[:], op=ALU.mult)
            dve.tensor_tensor(out=xf[1][:], in0=sn1[:], in1=s5car_i[:], op=ALU.mult)
            dve.tensor_tensor(out=xf[0][:], in0=xf[0][:], in1=xf[1][:], op=ALU.add)
            dve.tensor_tensor(out=xf[2][:], in0=cs1[:], in1=s5car_i[:], op=ALU.mult)
            dve.tensor_tensor(out=xf[3][:], in0=sn1[:], in1=s5car_r[:], op=ALU.mult)
            dve.tensor_tensor(out=xf[2][:], in0=xf[2][:], in1=xf[3][:], op=ALU.subtract)
            xo = lsb("xfo", [NSB, 2, P])
            pe.transpose(out=PS[0][0:NSB, 0:128], in_=xf[0][:], identity=ident[:])
            pe.transpose(out=PS[0][0:NSB, 128:256], in_=xf[2][:], identity=ident[:])
            act.copy(out=xo[:].re("p a b -> p (a b)"), in_=PS[0][0:NSB, 0:256])
            st(ps5re_o[:], xo[:, 0, :]); st(ps5im_o[:], xo[:, 1, :])
            ho = lsb("hlo", [KC, P])
            pe.transpose(out=PS[1][0:KC, 0:128], in_=hlast[:], identity=ident[:])
            act.copy(out=ho[:], in_=PS[1][0:KC, 0:128])
            st(pshift_o[:], ho[:])
            so = lsb("wkvo", [64, C.H, 64])
            for h in range(C.H):
                pb = 64 * (h % 2)
                pe.transpose(out=PS[2][0:64, (h % 8) * 64:(h % 8 + 1) * 64], in_=Hst[h][pb:pb + 64, :], identity=ident[pb:pb + 64, pb:pb + 64],
                             tile_position=(pb, 0))
                act.copy(out=so[:, h, :], in_=PS[2][0:64, (h % 8) * 64:(h % 8 + 1) * 64])
            st(View(pwkv_o.buf, pwkv_o.ap.rearrange("h i j -> i h j")), so[:])
            so2 = lsb("s5o", [NSC, 512])
            for (src, dsto) in ((x1r, ss5re_o), (x1i, ss5im_o)):
                for g4 in range(NSB // 4):
                    for j in range(4):
                        pe.transpose(out=PS[3][0:NSC, j * 128:(j + 1) * 128], in_=src[:, g4 * 4 + j, :], identity=ident[:])
                    act.copy(out=so2[:], in_=PS[3][0:NSC, :])
                    st(dsto[:, g4 * 512:(g4 + 1) * 512], so2[:])

    try:
      for p in range(NPASS):
          if stop(1):
              break
          is_last = (p == NPASS - 1)
          if is_last:
              sample_prep()
          phase_x(View(xb.buf, xb.ap[p * TP:(p + 1) * TP, :]), is_last)
          barrier()
          if stop(2):
              break
          s5_phase(is_last)
          if stop(3):
              break
          rwkv_phase(is_last)
          if stop(4):
              break
          if is_last:
              rwkv_sample_core()
          post_phase(is_last, p)
          if stop(5):
              break
      if not stop(6):
          finals()
    except StopBuild:
        pass
    fw.finish()
    return nc, fw, es


def _blk(w, kcn):
    K, N = w.shape
    return np.ascontiguousarray(w.reshape(kcn, 128, N // 128, 128).transpose(2, 1, 0, 3))


def _fm(v):
    v = np.asarray(v).reshape(-1)
    return np.ascontiguousarray(v.reshape(-1, 128).T)


def make_in_maps(cfg, inp, n_cores, n_seq):
    C = cfg
    f = np.float32
    H = C.H; NSB = C.NSB
    sh = {}
    sh["w_ada"] = _blk(inp["w_ada"][0], C.KC); sh["w_in"] = _blk(inp["w_in"][0], C.KC)
    sh["w_glu"] = _blk(inp["w_glu"][0], C.NUB); sh["w_out"] = _blk(inp["w_out"][0], C.KC)
    sh["b_ada"] = _fm(inp["b_ada"][0]); sh["norm_g"] = _fm(inp["norm_g"][0]); sh["mu"] = _fm(inp["mu_rw"][0])
    sh["A_re"] = _fm(inp["A_re"][0]); sh["A_im"] = _fm(inp["A_im"][0])
    sh["lstep"] = np.ascontiguousarray(np.repeat(inp["log_step"][0].reshape(NSB, 2), 64, axis=1).T)
    for k in ("B_re", "B_im"):
        sh[k] = np.ascontiguousarray(inp[k][0].reshape(NSB, 2, 64, 16).transpose(1, 2, 0, 3).reshape(128, NSB, 16))
    for k in ("C_re", "C_im"):
        sh[k] = np.ascontiguousarray(inp[k][0].reshape(NSB, 2, 16, 64).transpose(1, 3, 0, 2).reshape(128, NSB, 16))
    sh["Dsk"] = _fm(inp["D_skip"][0]); sh["b_glu"] = _fm(inp["b_glu"][0])
    for k in ("w0", "a0", "k_k", "k_a", "r_k", "gn_w", "gn_b"):
        sh[k] = _fm(inp[k][0])
    sh["w2"] = np.ascontiguousarray(inp["w2"][0]); sh["a2"] = np.ascontiguousarray(inp["a2"][0])
    sh["fing"] = np.ascontiguousarray(np.broadcast_to(inp["final_g"].reshape(1, -1), (128, C.D)))
    rep = max(1, 128 // H)
    sh["gnw_bh"] = np.ascontiguousarray(np.tile(inp["gn_w"][0].reshape(H, 64), (rep, 1))[:128])
    sh["gnb_bh"] = np.ascontiguousarray(np.tile(inp["gn_b"][0].reshape(H, 64), (rep, 1))[:128])
    sh["rk_bh"] = np.ascontiguousarray(np.tile(inp["r_k"][0].reshape(H, 64), (rep, 1))[:128])
    if sh["gnw_bh"].shape[0] < 128:
        for k in ("gnw_bh", "gnb_bh", "rk_bh"):
            sh[k] = np.ascontiguousarray(np.concatenate([sh[k], np.zeros((128 - sh[k].shape[0], 64), f)], 0))
    sh["ident"] = np.eye(128, dtype=f)
    ms = np.triu(np.ones((64, 64), f), 1); mi_ = np.triu(np.ones((64, 64), f), 0)
    sh["maskgt"] = np.block([[ms, mi_], [ms, mi_]]).astype(f)
    ob = np.zeros((128, 128), f); ob[:64, :64] = 1; ob[64:, 64:] = 1
    sh["onesbd"] = ob
    rm = np.ones((128, C.TP), f); rm[:, ::64] = 0
    sh["resetm"] = rm
    rmk = np.zeros((128, 4), f)
    for j in range(4):
        rmk[32 * j:32 * j + 32, j] = 1
    sh["rowmask"] = rmk
    sh["iota"] = np.ascontiguousarray(np.broadcast_to(np.arange(128, dtype=f).reshape(1, -1), (128, 128)))
    sh = {k: np.ascontiguousarray(v, dtype=f) for k, v in sh.items()}
    maps = []
    NSC = C.NSC
    for c in range(n_cores):
        b = c % n_seq
        rows = slice(c * NSC, (c + 1) * NSC)
        m = dict(sh)
        m["xb"] = np.ascontiguousarray(inp["x_prompt"][b], dtype=f)
        cp = inp["c_prompt"][b:b + 1]
        m["c17"] = np.ascontiguousarray(np.concatenate([inp["c_sample"][rows], cp, cp], 0), dtype=f)
        m["xs"] = np.ascontiguousarray(inp["x_sample"][rows, 0, :], dtype=f)
        m["sshift"] = np.ascontiguousarray(inp["state_shift"][0, rows], dtype=f)
        m["s5re0"] = np.ascontiguousarray(inp["state_s5_re"][0, rows].reshape(NSC, -1), dtype=f)
        m["s5im0"] = np.ascontiguousarray(inp["state_s5_im"][0, rows].reshape(NSC, -1), dtype=f)
        m["wkv0"] = np.ascontiguousarray(inp["state_wkv"][0, rows].reshape(NSC * H, 4096), dtype=f)
        maps.append(m)
    return maps


def assemble(cfg, res, n_cores, n_seq):
    C = cfg; f = np.float32
    G = C.G; H = C.H; NSC = C.NSC
    R = lambda c, k: np.asarray(res[c][k], dtype=f)
    y_p = np.stack([R(b, "y") for b in range(n_seq)], 0)
    y_s = np.concatenate([R(c, "ys") for c in range(n_cores)], 0)[:, None, :]
    re_p = np.stack([R(b, "ps5re").reshape(G, 64) for b in range(n_seq)], 0)[None]
    im_p = np.stack([R(b, "ps5im").reshape(G, 64) for b in range(n_seq)], 0)[None]
    wkv_p = np.stack([R(b, "pwkv") for b in range(n_seq)], 0)[None]
    sh_p = np.stack([R(b, "pshift").reshape(-1) for b in range(n_seq)], 0)[None]
    re_s = np.concatenate([R(c, "ss5re").reshape(NSC, G, 64) for c in range(n_cores)], 0)[None]
    im_s = np.concatenate([R(c, "ss5im").reshape(NSC, G, 64) for c in range(n_cores)], 0)[None]
    wkv_s = np.concatenate([R(c, "swkv").reshape(NSC, H, 64, 64) for c in range(n_cores)], 0)[None]
    sh_s = np.concatenate([R(c, "sshift_o") for c in range(n_cores)], 0)[None]
    return (y_p, y_s, re_p, im_p, wkv_p, sh_p, re_s, im_s, wkv_s, sh_s)


def kernel(**inputs):
    cfg = Cfg()
    inp = {k: np.asarray(v) for k, v in inputs.items()}
    nc, fw, es = build(cfg)
    maps = make_in_maps(cfg, inp, 8, 4)
    res = run_bass_kernel_spmd(nc, maps, core_ids=list(range(8)))
    return assemble(cfg, res.results, 8, 4)
```

```python
import math
import os
from contextlib import ExitStack
import numpy as np
import concourse.bass as bass
import concourse.mybir as mybir
from concourse.bass_utils import run_bass_kernel_spmd

F32 = mybir.dt.float32
F32R = mybir.dt.float32r
AF = mybir.ActivationFunctionType
ALU = mybir.AluOpType
AX = mybir.AxisListType


class Cfg:
    def __init__(self, D=2048, TP=512, NSC=16, NPASS=4):
        self.NPASS = NPASS
        self.D = D; self.TP = TP; self.NSC = NSC
        self.KC = D // 128
        self.DS = D // 2; self.DR = D // 2
        self.G = self.DS // 16; self.NSB = self.G // 2; self.NUB = self.DS // 128
        self.H = self.DR // 64; self.NHP = self.DR // 128
        self.NCH = TP // 64
        self.OFF_U = 0; self.OFF_Z = self.DS; self.OFF_RW = 2 * self.DS
        self.NSH = 3 * self.DR + 128
        self.OFF_RWZ = self.OFF_RW + self.NSH
        self.OFF_G1 = self.OFF_RWZ + self.DR
        self.OFF_G2 = self.OFF_G1 + D
        self.NIN = self.OFF_G2 + D
        self.NRB = self.NSH // 128
        self.GB = min(2, self.NUB)
        self.TT = [(i, min(512, TP - i)) for i in range(0, TP, 512)]
        self.TW = TP + 2 * NSC


class TL:
    def __init__(self, sem, name):
        self.sem = sem; self.count = 0; self.name = name


class Buf:
    __slots__ = ("w", "r", "excl")

    def __init__(self):
        self.w = None; self.r = {}; self.excl = False


class View:
    __slots__ = ("buf", "ap")

    def __init__(self, buf, ap):
        self.buf = buf; self.ap = ap

    def __getitem__(self, idx):
        return View(self.buf, self.ap[idx])

    def re(self, s, **kw):
        return View(self.buf, self.ap.rearrange(s, **kw))

    def bc(self, shape):
        return View(self.buf, self.ap.to_broadcast(shape))

    def r32(self):
        return View(self.buf, self.ap.bitcast(F32R))


class Tile:
    def __init__(self, ap, buf=None):
        self.ap = ap; self.buf = buf or Buf()

    def __getitem__(self, idx):
        return View(self.buf, self.ap[idx])

    def sub(self, idx):
        return Tile(self.ap[idx])


class Eng:
    def __init__(self, fw, name, h, tl, self_sync):
        self.fw = fw; self.name = name; self.h = h; self.tl = tl
        self.self_sync = self_sync; self.seen = {}

    def wait_for(self, deps):
        for tl, cnt in deps.items():
            if tl is self.tl and not self.self_sync:
                continue
            if self.seen.get(tl, 0) >= cnt:
                continue
            self.h.wait_ge(tl.sem, cnt)
            self.seen[tl] = cnt

    def __getattr__(self, op):
        def f(inc=True, **kw):
            return self.fw._issue(self, op, kw, inc)
        return f


def _deps(reads, writes):
    deps = {}

    def add(tc):
        if tc is None:
            return
        tl, c = tc
        if deps.get(tl, 0) < c:
            deps[tl] = c
    for v in reads:
        add(v.buf.w)
        if v.buf.excl:
            for tl, c in v.buf.r.items():
                add((tl, c))
    for v in writes:
        add(v.buf.w)
        for tl, c in v.buf.r.items():
            add((tl, c))
    return deps


class FW:
    def __init__(self, nc, es):
        self.nc = nc; self.es = es; self.nsem = 0; self.es_stack = [es]; self.uid = 0
        self.pe = Eng(self, "pe", nc.tensor, self.tl("pe"), False)
        self.act = Eng(self, "act", nc.scalar, self.tl("act"), True)
        self.dve = Eng(self, "dve", nc.vector, self.tl("dve"), True)
        self.pool = Eng(self, "pool", nc.gpsimd, self.tl("pool"), True)
        self.sp = Eng(self, "sp", nc.sync, self.tl("sp"), False)
        self.dma_tls = []

    def tl(self, name):
        self.nsem += 1
        return TL(self.es.enter_context(self.nc.semaphore(name)), name)

    def sb(self, name, shape, dtype=F32):
        self.uid += 1
        return Tile(self.es_stack[-1].enter_context(self.nc.sbuf_tensor("%s_%d" % (name, self.uid), list(shape), dtype))[:])

    def barrier(self):
        engs = (self.pe, self.act, self.dve, self.pool)
        tls = [e.tl for e in engs] + self.dma_tls
        for e in engs + (self.sp,):
            e.wait_for({tl: tl.count for tl in tls if tl.count and tl is not e.tl})

    def scope(self):
        fw = self

        class _S:
            def __enter__(self_):
                self_.les = ExitStack(); fw.es_stack.append(self_.les); return self_

            def __exit__(self_, *a):
                if fw.es_stack[-1] is not self_.les:
                    return False
                fw.barrier(); fw.es_stack.pop(); self_.les.close(); return False
        return _S()

    def ps(self, name, shape=(128, 512)):
        t = Tile(self.es.enter_context(self.nc.psum_tensor(name, list(shape), F32))[:])
        t.buf.excl = True
        return t

    def dram(self, name, shape, kind):
        return Tile(self.nc.dram_tensor(name, list(shape), F32, kind=kind).ap())

    def _issue(self, eng, op, kw, inc):
        reads = [v for k, v in kw.items() if isinstance(v, View) and k not in ("out", "accum_out", "ap")]
        writes = [v for k, v in kw.items() if isinstance(v, View) and k in ("out", "accum_out", "ap")]
        eng.wait_for(_deps(reads, writes))
        args = {k: (v.ap if isinstance(v, View) else v) for k, v in kw.items()}
        inst = getattr(eng.h, op)(**args)
        if inc:
            eng.tl.count += 1
            inst.then_inc(eng.tl.sem, 1)
            stamp = eng.tl.count
        else:
            stamp = eng.tl.count + 1
        for v in reads:
            if v.buf.r.get(eng.tl, 0) < stamp:
                v.buf.r[eng.tl] = stamp
        for v in writes:
            v.buf.w = (eng.tl, stamp); v.buf.r = {}
        return inst

    def dma(self, q, tl, out, in_, **kw):
        eng = {"sp": self.sp, "pool": self.pool, "act": self.act}[q]
        deps = _deps([in_], [out])
        if tl.count:
            deps[tl] = max(deps.get(tl, 0), tl.count)
        eng.wait_for(deps)
        inst = eng.h.dma_start(out=out.ap, in_=in_.ap, **kw)
        tl.count += 16
        inst.then_inc(tl.sem, 16)
        in_.buf.r[tl] = tl.count
        out.buf.w = (tl, tl.count); out.buf.r = {}
        if tl not in self.dma_tls:
            self.dma_tls.append(tl)

    def finish(self):
        for tl in self.dma_tls:
            self.sp.h.wait_ge(tl.sem, tl.count)
        for e in (self.pe, self.act, self.dve, self.pool):
            if e.tl.count:
                self.sp.h.wait_ge(e.tl.sem, e.tl.count)


def build(cfg, dbg=False):
    nc = bass.Bass("TRN2", target_bir_lowering=False)
    es = ExitStack()
    fw = FW(nc, es)
    pe, act, dve, pool = fw.pe, fw.act, fw.dve, fw.pool
    C = cfg
    D, KC, TP, NSC, TW = C.D, C.KC, C.TP, C.NSC, C.TW
    NS2 = 2 * NSC
    NSB, NUB, NHP, NCH, NRB, GB = C.NSB, C.NUB, C.NHP, C.NCH, C.NRB, C.GB
    P = 128

    def din(name, shape):
        return fw.dram(name, shape, "ExternalInput")

    def dout(name, shape):
        return fw.dram(name, shape, "ExternalOutput")

    xb = din("xb", [TP * C.NPASS, D])
    c17 = din("c17", [NSC + 2, D]); xs = din("xs", [NSC, D]); sshift = din("sshift", [NSC, D])
    s5re0 = din("s5re0", [NSC, NSB * 128]); s5im0 = din("s5im0", [NSC, NSB * 128])
    wkv0 = din("wkv0", [NSC * C.H, 4096])
    w_ada = din("w_ada", [3 * KC, P, KC, 128])
    w_in = din("w_in", [C.NIN // 128, P, KC, 128])
    w_glu = din("w_glu", [NUB, P, NUB, 128])
    w_out = din("w_out", [KC, P, KC, 128])
    b_ada = din("b_ada", [P, 3 * KC]); norm_g = din("norm_g", [P, KC]); mu = din("mu", [P, NRB])
    A_re = din("A_re", [P, NSB]); A_im = din("A_im", [P, NSB]); lstep = din("lstep", [P, NSB])
    B_re = din("B_re", [P, NSB, 16]); B_im = din("B_im", [P, NSB, 16])
    C_re = din("C_re", [P, NSB, 16]); C_im = din("C_im", [P, NSB, 16])
    Dsk = din("Dsk", [P, NUB]); b_glu = din("b_glu", [P, NUB])
    w0 = din("w0", [P, NHP]); a0 = din("a0", [P, NHP]); k_k = din("k_k", [P, NHP]); k_a = din("k_a", [P, NHP])
    r_k = din("r_k", [P, NHP]); gn_w = din("gn_w", [P, NHP]); gn_b = din("gn_b", [P, NHP])
    w2 = din("w2", [64, C.DR]); a2 = din("a2", [64, C.DR])
    fing = din("fing", [P, D])
    gnw_bh = din("gnw_bh", [P, 64]); gnb_bh = din("gnb_bh", [P, 64]); rk_bh = din("rk_bh", [P, 64])
    ident_d = din("ident", [P, P]); maskgt_d = din("maskgt", [P, P]); onesbd_d = din("onesbd", [P, P])
    reset_d = din("resetm", [P, TP]); iota_d = din("iota", [P, 128]); rowmask_d = din("rowmask", [P, 4])

    y_o = dout("y", [TP * C.NPASS, D]); ys_o = dout("ys", [NSC, D])
    ps5re_o = dout("ps5re", [NSB, P]); ps5im_o = dout("ps5im", [NSB, P])
    pwkv_o = dout("pwkv", [C.H, 64, 64]); pshift_o = dout("pshift", [KC, P])
    ss5re_o = dout("ss5re", [NSC, NSB * 128]); ss5im_o = dout("ss5im", [NSC, NSB * 128])
    swkv_o = dout("swkv", [NSC * C.H, 4096]); sshift_o = dout("sshift_o", [NSC, D])
    scr_a = fw.dram("scr_a", [NSC, C.H, 6, 64], "Internal")
    scr_b = fw.dram("scr_b", [NSC, C.DR], "Internal")
    dbg_o = {}

    tl_misc = [fw.tl("m%d" % i) for i in range(6)]
    mi = [0]

    def ld(out, in_, q="sp"):
        tl = tl_misc[mi[0] % len(tl_misc)]; mi[0] += 1
        fw.dma(q, tl, out, in_)

    tl_out = [fw.tl("o%d" % i) for i in range(4)]
    oi = [0]

    def st(out, in_, **kw):
        tl = tl_out[oi[0] % len(tl_out)]; oi[0] += 1
        fw.dma("sp", tl, out, in_, **kw)

    def const(name, d, shape):
        t = fw.sb(name, shape); ld(t[:], d[:]); return t
    ident = const("ident_s", ident_d, [P, P]); maskgt = const("maskgt_s", maskgt_d, [P, P])
    onesbd = const("onesbd_s", onesbd_d, [P, P]); resetm = const("reset_s", reset_d, [P, TP])
    iota = const("iota_s", iota_d, [P, 128]); rowmask = const("rowmask_s", rowmask_d, [P, 4])
    b_ada_s = const("b_ada_s", b_ada, [P, 3 * KC]); norm_g_s = const("norm_g_s", norm_g, [P, KC])
    mu_s = const("mu_s", mu, [P, NRB])
    Dsk_s = const("Dsk_s", Dsk, [P, NUB]); b_glu_s = const("b_glu_s", b_glu, [P, NUB])
    w0_s = const("w0_s", w0, [P, NHP]); a0_s = const("a0_s", a0, [P, NHP]); kk_s = const("kk_s", k_k, [P, NHP])
    ka_s = const("ka_s", k_a, [P, NHP]); rk_s = const("rk_s", r_k, [P, NHP])
    gnw_s = const("gnw_s", gn_w, [P, NHP]); gnb_s = const("gnb_s", gn_b, [P, NHP])
    w2_s = fw.sb("w2a2_s", [P, C.DR]); ld(w2_s[0:64, :], w2[:])
    a2_s = w2_s; ld(a2_s[64:128, :], a2[:])
    omu_s = fw.sb("omu_s", [P, NRB])
    dve.tensor_scalar(out=omu_s[:], in0=mu_s[:], scalar1=-1.0, scalar2=1.0, op0=ALU.mult, op1=ALU.add)

    PS = [fw.ps("psb%d" % i) for i in range(8)]

    NRING = 2
    KH = KC // 2
    ring = [[fw.sb("wring%d_%d" % (i, h), [P, KH, 128], F32R) for h in range(2)] for i in range(NRING)]
    NSTG = 2
    wstg = [fw.sb("wstg%d" % i, [P, KH, 128]) for i in range(NSTG)]
    stg_tl = [fw.tl("ws%d" % i) for i in range(NSTG)]
    ri = [0]; si_ = [0]

    class WSlot:
        def __init__(self, halves):
            self.halves = halves

        def __getitem__(self, idx):
            p, kc, c = idx
            return self.halves[kc // KH][p, kc % KH, c]

    def load_w(wd, blk, nk=KC):
        i = ri[0] % NRING; ri[0] += 1
        for h in range(2):
            k0 = h * KH; k1 = min(nk, (h + 1) * KH)
            if k1 <= k0:
                continue
            j = si_[0] % NSTG; si_[0] += 1
            fw.dma("sp", stg_tl[j], wstg[j][:, 0:k1 - k0, :], View(wd.buf, wd.ap[blk, :, k0:k1, :]))
            if h == 0:
                pool.tensor_copy(out=ring[i][h][:, 0:k1 - k0, :], in_=wstg[j][:, 0:k1 - k0, :])
            else:
                act.copy(out=ring[i][h][:, 0:k1 - k0, :], in_=wstg[j][:, 0:k1 - k0, :])
        return WSlot(ring[i])

    def mm_acc(psv, wslot, nk, rhs_fn, ncols=128):
        for kc in range(nk):
            pe.matmul(out=psv, lhsT=wslot[:, kc, 0:ncols].r32(), rhs=rhs_fn(kc),
                      start=(kc == 0), stop=(kc == nk - 1), inc=(kc == nk - 1))

    s5 = {}
    tA = const("A_re_s", A_re, [P, NSB]); tAi = const("A_im_s", A_im, [P, NSB]); tls = const("lstep_s", lstep, [P, NSB])
    step = fw.sb("s5step", [P, NSB]); act.activation(out=step[:], in_=tls[:], func=AF.Exp)
    lam = fw.sb("s5lam", [P, NSB]); dve.tensor_scalar_min(out=lam[:], in0=tA[:], scalar1=-1e-4)
    mag = fw.sb("s5mag", [P, NSB]); tmpc = fw.sb("s5tmp", [P, NSB]); theta = fw.sb("s5theta", [P, NSB])
    dve.tensor_tensor(out=tmpc[:], in0=lam[:], in1=step[:], op=ALU.mult)
    act.activation(out=mag[:], in_=tmpc[:], func=AF.Exp)
    dve.tensor_tensor(out=theta[:], in0=tAi[:], in1=step[:], op=ALU.mult)
    TWO_PI = 2.0 * math.pi

    I32 = mybir.dt.int32

    def sin_into(dst, x, shape, phase):
        with fw.scope():
            ki = fw.sb("sr_ki", shape, I32); kf = fw.sb("sr_kf", shape)
            dve.tensor_scalar_add(out=dst, in0=x, scalar1=float(phase))
            dve.tensor_scalar_mul(out=ki[:], in0=dst, scalar1=1.0 / TWO_PI)
            dve.tensor_copy(out=kf[:], in_=ki[:])
            dve.scalar_tensor_tensor(out=dst, in0=kf[:], scalar=-TWO_PI, in1=dst, op0=ALU.mult, op1=ALU.add)
            dve.tensor_scalar(out=kf[:], in0=dst, scalar1=math.pi, scalar2=-TWO_PI, op0=ALU.is_gt, op1=ALU.mult)
            dve.tensor_tensor(out=dst, in0=dst, in1=kf[:], op=ALU.add)
            dve.tensor_scalar(out=kf[:], in0=dst, scalar1=-math.pi, scalar2=TWO_PI, op0=ALU.is_lt, op1=ALU.mult)
            dve.tensor_tensor(out=dst, in0=dst, in1=kf[:], op=ALU.add)
            dve.tensor_scalar(out=dst, in0=dst, scalar1=math.pi, scalar2=-math.pi, op0=ALU.min, op1=ALU.max)
            act.activation(out=dst, in_=dst, func=AF.Sin)

    def sincos(name, ang_view, shape):
        sn = fw.sb(name + "_sin", shape); cs = fw.sb(name + "_cos", shape)
        sin_into(sn[:], ang_view, shape, 0.0)
        sin_into(cs[:], ang_view, shape, 0.5 * math.pi)
        return sn, cs
    thp = theta
    sn1, cs1 = sincos("s5t1", thp[:], [P, NSB])
    abr = fw.sb("s5abr", [P, NSB]); abi = fw.sb("s5abi", [P, NSB])
    dve.tensor_tensor(out=abr[:], in0=mag[:], in1=cs1[:], op=ALU.mult)
    dve.tensor_tensor(out=abi[:], in0=mag[:], in1=sn1[:], op=ALU.mult)
    den = fw.sb("s5den", [P, NSB]); t2 = fw.sb("s5t2", [P, NSB]); fre = fw.sb("s5fre", [P, NSB]); fim = fw.sb("s5fim", [P, NSB])
    abm1 = fw.sb("s5abm1", [P, NSB])
    dve.tensor_tensor(out=den[:], in0=lam[:], in1=lam[:], op=ALU.mult)
    dve.tensor_tensor(out=t2[:], in0=tAi[:], in1=tAi[:], op=ALU.mult)
    dve.tensor_tensor(out=den[:], in0=den[:], in1=t2[:], op=ALU.add)
    dve.reciprocal(out=den[:], in_=den[:])
    dve.tensor_scalar_add(out=abm1[:], in0=abr[:], scalar1=-1.0)
    dve.tensor_tensor(out=fre[:], in0=abm1[:], in1=lam[:], op=ALU.mult)
    dve.tensor_tensor(out=t2[:], in0=abi[:], in1=tAi[:], op=ALU.mult)
    dve.tensor_tensor(out=fre[:], in0=fre[:], in1=t2[:], op=ALU.add)
    dve.tensor_tensor(out=fre[:], in0=fre[:], in1=den[:], op=ALU.mult)
    dve.tensor_tensor(out=fim[:], in0=abi[:], in1=lam[:], op=ALU.mult)
    dve.tensor_tensor(out=t2[:], in0=abm1[:], in1=tAi[:], op=ALU.mult)
    dve.tensor_tensor(out=fim[:], in0=fim[:], in1=t2[:], op=ALU.subtract)
    dve.tensor_tensor(out=fim[:], in0=fim[:], in1=den[:], op=ALU.mult)
    LB = [fw.sb("s5LB" + nm, [P, NUB, 128]) for nm in ("re", "im")]
    LC = [fw.sb("s5ZC" + nm, [P, NSB, 2, 16]) for nm in ("re", "im")]
    with fw.scope():
        Br = const("B_re_s", B_re, [P, NSB, 16]); Bi = const("B_im_s", B_im, [P, NSB, 16])
        bbr = fw.sb("s5bbr", [P, NSB, 16]); bbi = fw.sb("s5bbi", [P, NSB, 16]); bt_ = fw.sb("s5bt", [P, NSB, 16])
        fre_b = fre[:, :, None].bc([P, NSB, 16]); fim_b = fim[:, :, None].bc([P, NSB, 16])
        dve.tensor_tensor(out=bbr[:], in0=Br[:], in1=fre_b, op=ALU.mult)
        dve.tensor_tensor(out=bt_[:], in0=Bi[:], in1=fim_b, op=ALU.mult)
        dve.tensor_tensor(out=bbr[:], in0=bbr[:], in1=bt_[:], op=ALU.subtract)
        dve.tensor_tensor(out=bbi[:], in0=Bi[:], in1=fre_b, op=ALU.mult)
        dve.tensor_tensor(out=bt_[:], in0=Br[:], in1=fim_b, op=ALU.mult)
        dve.tensor_tensor(out=bbi[:], in0=bbi[:], in1=bt_[:], op=ALU.add)
        for li, (nm, src) in enumerate((("re", bbr), ("im", bbi))):
            Z = fw.sb("s5ZB" + nm, [P, NSB, 2, 16])
            dve.memset(ap=Z[:], constant=0.0)
            dve.tensor_copy(out=Z[0:64, :, 0, :], in_=src[0:64, :, :])
            dve.tensor_copy(out=Z[64:128, :, 1, :], in_=src[64:128, :, :])
            L = LB[li]
            for q4 in range(NUB):
                pe.transpose(out=PS[0][:, 0:128], in_=Z[:, 4 * q4:4 * q4 + 4, :, :].re("p a b c -> p (a b c)"), identity=ident[:])
                act.copy(out=L[:, q4, :], in_=PS[0][:, 0:128])
        Cr = const("C_re_s", C_re, [P, NSB, 16]); Ci = const("C_im_s", C_im, [P, NSB, 16])
        for li, (nm, src, sgn) in enumerate((("re", Cr, 1.0), ("im", Ci, -1.0))):
            Z = LC[li]
            dve.memset(ap=Z[:], constant=0.0)
            dve.tensor_scalar_mul(out=Z[0:64, :, 0, :], in0=src[0:64, :, :], scalar1=sgn)
            dve.tensor_scalar_mul(out=Z[64:128, :, 1, :], in0=src[64:128, :, :], scalar1=sgn)
    def make_tables():
      sinT = fw.sb("s5sinT", [P, NSB, 128]); cosT = fw.sb("s5cosT", [P, NSB, 128])
      CH = min(8, NSB)
      with fw.scope():
        ang = fw.sb("s5ang", [P, CH, 128])
        for c0 in range(0, NSB, CH):
            dve.tensor_tensor(out=ang[:], in0=iota[:, None, :].bc([P, CH, 128]), in1=thp[:, c0:c0 + CH, None].bc([P, CH, 128]), op=ALU.mult)
            sin_into(sinT[:, c0:c0 + CH, :], ang[:], [P, CH, 128], 0.0)
            sin_into(cosT[:, c0:c0 + CH, :], ang[:], [P, CH, 128], 0.5 * math.pi)
      return sinT, cosT
    angL = fw.sb("s5angL", [P, NSB]); dve.tensor_scalar_mul(out=angL[:], in0=thp[:], scalar1=128.0)
    snL, csL = sincos("s5tL", angL[:], [P, NSB])
    s5car_r = fw.sb("s5car_r", [P, NSB]); s5car_i = fw.sb("s5car_i", [P, NSB])
    dve.memset(ap=s5car_r[:], constant=0.0); dve.memset(ap=s5car_i[:], constant=0.0)

    NM = NSC + 2
    scT = fw.sb("scT", [P, KC, NM], F32R)
    with fw.scope():
        c_s = fw.sb("c_s", [NM, D]); ld(c_s[:], c17[:])
        csl = fw.sb("csl", [NM, D]); act.activation(out=csl[:], in_=c_s[:], func=AF.Silu)
        for kc in range(KC):
            pe.transpose(out=PS[1][:, kc * NM:(kc + 1) * NM], in_=csl[:, kc * 128:(kc + 1) * 128], identity=ident[0:NM, 0:NM])
        act.copy(out=scT[:].re("p k n -> p (k n)"), in_=PS[1][:, 0:KC * NM])
    modT = fw.sb("modT", [P, 3 * KC, NM])
    for fb in range(3 * KC):
        w = load_w(w_ada, fb)
        psb = PS[2 + fb % 2]
        mm_acc(psb[:, 0:NM], w, KC, lambda kc: scT[:, kc, :])
        act.activation(out=modT[:, fb, :], in_=psb[:, 0:NM], func=AF.Identity, bias=b_ada_s[:, fb:fb + 1], scale=1.0)
    sceff = fw.sb("sceff", [P, KC, NM])
    dve.tensor_scalar_add(out=sceff[:], in0=modT[:, KC:2 * KC, :], scalar1=1.0)
    dve.tensor_tensor(out=sceff[:], in0=sceff[:], in1=norm_g_s[:, :, None].bc([P, KC, NM]), op=ALU.mult)
    sc_p = fw.sb("sc_p", [P, KC]); sh_p = fw.sb("sh_p", [P, KC]); gt_p = fw.sb("gt_p", [P, KC])
    dve.tensor_copy(out=sc_p[:], in_=sceff[:, :, NSC]); dve.tensor_copy(out=sh_p[:], in_=modT[:, 0:KC, NSC])
    dve.tensor_copy(out=gt_p[:], in_=modT[:, 2 * KC:3 * KC, NSC])

    hT = fw.sb("hT", [P, KC, TW], F32R)
    rw_car = fw.sb("rw_car", [P, NRB]); dve.memset(ap=rw_car[:], constant=0.0)
    Hst = [fw.sb("Hst%d" % h, [P, 64]) for h in range(C.H)]
    for h in range(C.H):
        dve.memset(ap=Hst[h][:], constant=0.0)

    eps_rms = fw.sb("eps_rms", [P, 1]); dve.memset(ap=eps_rms[:], constant=1e-6)
    eps_gn = fw.sb("eps_gn", [P, 1]); dve.memset(ap=eps_gn[:], constant=64e-5)

    def rstd_of(ss, n):
        act.activation(out=ss, in_=ss, func=AF.Sqrt, bias=eps_rms[0:n, :], scale=1.0 / D)
        dve.reciprocal(out=ss, in_=ss)

    xt_tl = [fw.tl("xt%d" % i) for i in range(2)]
    ssq = fw.sb("ssq", [P, 2])
    hlast = fw.sb("hlast", [P, KC])

    def phase_x(xd, is_b):
      with fw.scope():
        xt = [fw.sb("xt%d" % i, [P, D]) for i in range(2)]
        xsq = fw.sb("xsq", [P, D], mybir.dt.bfloat16)
        for tt in range(TP // 128):
            i = tt % 2
            fw.dma("sp", xt_tl[i], xt[i][:], xd[tt * 128:(tt + 1) * 128, :])
            s = ssq[:, i:i + 1]
            act.activation(out=xsq[:], in_=xt[i][:], func=AF.Square, accum_out=s)
            rstd_of(s, 128)
            act.activation(out=xt[i][:], in_=xt[i][:], func=AF.Copy, scale=s)
            for kc in range(KC):
                psb = PS[(kc // 4) % 2]
                pe.transpose(out=psb[:, (kc % 4) * 128:(kc % 4 + 1) * 128], in_=xt[i][:, kc * 128:(kc + 1) * 128], identity=ident[:])
                act.activation(out=hT[:, kc, tt * 128:(tt + 1) * 128], in_=psb[:, (kc % 4) * 128:(kc % 4 + 1) * 128],
                               func=AF.Identity, scale=sc_p[:, kc:kc + 1], bias=sh_p[:, kc:kc + 1])
                if is_b and tt == TP // 128 - 1:
                    act.activation(out=hlast[:, kc:kc + 1], in_=psb[:, (kc % 4) * 128 + 127:(kc % 4) * 128 + 128],
                                   func=AF.Identity, scale=sc_p[:, kc:kc + 1], bias=sh_p[:, kc:kc + 1])

    NPASS = C.NPASS
    W1 = TP + NSC
    UID = [0]

    def tiles(is_last):
        t = list(C.TT)
        if is_last:
            t.append((TP, NS2))
        return t

    barrier = fw.barrier
    STOP = float(os.environ.get("K_STOP", "99"))

    class StopBuild(Exception):
        pass

    def stop(n):
        return STOP <= n

    def chk(n):
        if STOP <= n:
            while len(fw.es_stack) > 1:
                fw.es_stack.pop().close()
            raise StopBuild()

    osT = fw.sb("osT", [P, NUB, TW], F32R)
    orT = fw.sb("orT", [P, NHP, TW], F32R)
    x0r = fw.sb("x0r", [P, NSB, NSC]); x0i = fw.sb("x0i", [P, NSB, NSC])
    x1r = fw.sb("x1r", [P, NSB, NSC]); x1i = fw.sb("x1i", [P, NSB, NSC])

    def gelu_inplace(v, tmp_v):
        dve.tensor_tensor(out=tmp_v, in0=v, in1=v, op=ALU.mult)
        dve.tensor_scalar(out=tmp_v, in0=tmp_v, scalar1=0.044715, scalar2=1.0, op0=ALU.mult, op1=ALU.add)
        dve.tensor_tensor(out=tmp_v, in0=tmp_v, in1=v, op=ALU.mult)
        act.activation(out=tmp_v, in_=tmp_v, func=AF.Sigmoid, scale=2.0 * math.sqrt(2.0 / math.pi))
        dve.tensor_tensor(out=v, in0=v, in1=tmp_v, op=ALU.mult)

    def s5_phase(is_last):
        Wd = TW if is_last else TP
        with fw.scope():
            lsb = fw.sb
            sinT, cosT = make_tables()
            chk(2.1)
            uT = lsb("uT", [P, TW]); yT = lsb("yT", [P, NUB, TW], F32R)
            scanscope = fw.scope(); scanscope.__enter__()
            zr = lsb("s5zr", [P, 4, 128]); zi = lsb("s5zi", [P, 4, 128])
            q1 = lsb("s5q1", [P, 4, 128])
            sr = lsb("s5sr", [P, 4, 128]); si = lsb("s5si", [P, 4, 128])
            xr = zr; xi = zi
            cq = [lsb("s5cq%d" % i, [P, 4]) for i in range(4)]
            LBz = [lsb("s5LBz%d" % i, [P, 4, 128]) for i in range(2)]
            sq = [lsb("s5sq%d" % i, [P, 4, NSC]) for i in range(2)]
            for blk in range(NUB):
                chk(2.12)
                w = load_w(w_in, C.OFF_U // 128 + blk)
                chk(2.15)
                for (t0, tn) in tiles(is_last):
                    psb = PS[2 + (t0 // 512) % 2]
                    mm_acc(psb[:, 0:tn], w, KC, lambda kc: hT[:, kc, t0:t0 + tn].r32())
                    chk(2.17)
                    act.copy(out=uT[:, t0:t0 + tn], in_=psb[:, 0:tn])
                chk(2.2)
                for li in range(2):
                    for j in range(4):
                        dve.tensor_scalar_mul(out=LBz[li][:, j, :], in0=LB[li][:, blk, :], scalar1=rowmask[:, j:j + 1])
                s0 = blk * 4; sl = slice(s0, s0 + 4)
                cT = cosT[:, sl, :]; sT = sinT[:, sl, :]
                for ts in range(TP // 128):
                    bur = PS[4 + (ts % 2) * 2]; bui = PS[5 + (ts % 2) * 2]
                    for j in range(4):
                        for (L, psb) in ((LBz[0], bur), (LBz[1], bui)):
                            pe.matmul(out=psb[:, j * 128:(j + 1) * 128], lhsT=L[:, j, :],
                                      rhs=uT[:, ts * 128:(ts + 1) * 128],
                                      start=True, stop=True, inc=(j == 3))
                    chk(2.4)
                    br = bur[:, :].re("p (a b) -> p a b", a=4); bi = bui[:, :].re("p (a b) -> p a b", a=4)
                    dve.tensor_tensor(out=zr[:], in0=br, in1=cT, op=ALU.mult)
                    dve.tensor_tensor(out=q1[:], in0=bi, in1=sT, op=ALU.mult)
                    dve.tensor_tensor(out=zr[:], in0=zr[:], in1=q1[:], op=ALU.add)
                    dve.tensor_tensor(out=zi[:], in0=bi, in1=cT, op=ALU.mult)
                    dve.tensor_tensor(out=q1[:], in0=br, in1=sT, op=ALU.mult)
                    dve.tensor_tensor(out=zi[:], in0=zi[:], in1=q1[:], op=ALU.subtract)
                    chk(2.5)
                    for j in range(4):
                        s = s0 + j
                        dve.tensor_tensor_scan(out=sr[:, j, :], data0=mag[:, s:s + 1].bc([P, 128]), data1=zr[:, j, :],
                                               initial=s5car_r[:, s:s + 1], op0=ALU.mult, op1=ALU.add)
                        dve.tensor_tensor_scan(out=si[:, j, :], data0=mag[:, s:s + 1].bc([P, 128]), data1=zi[:, j, :],
                                               initial=s5car_i[:, s:s + 1], op0=ALU.mult, op1=ALU.add)
                    chk(2.6)
                    la = sr[:, :, 127]; lb = si[:, :, 127]
                    dve.tensor_tensor(out=cq[0][:], in0=csL[:, sl], in1=la, op=ALU.mult)
                    dve.tensor_tensor(out=cq[1][:], in0=snL[:, sl], in1=lb, op=ALU.mult)
                    dve.tensor_tensor(out=cq[2][:], in0=snL[:, sl], in1=la, op=ALU.mult)
                    dve.tensor_tensor(out=cq[3][:], in0=csL[:, sl], in1=lb, op=ALU.mult)
                    dve.tensor_tensor(out=s5car_r[:, sl], in0=cq[0][:], in1=cq[1][:], op=ALU.subtract)
                    dve.tensor_tensor(out=s5car_i[:, sl], in0=cq[2][:], in1=cq[3][:], op=ALU.add)
                    dve.tensor_tensor(out=xr[:], in0=sr[:], in1=cT, op=ALU.mult)
                    dve.tensor_tensor(out=q1[:], in0=si[:], in1=sT, op=ALU.mult)
                    dve.tensor_tensor(out=xr[:], in0=xr[:], in1=q1[:], op=ALU.subtract)
                    dve.tensor_tensor(out=xi[:], in0=sr[:], in1=sT, op=ALU.mult)
                    dve.tensor_tensor(out=q1[:], in0=si[:], in1=cT, op=ALU.mult)
                    dve.tensor_tensor(out=xi[:], in0=xi[:], in1=q1[:], op=ALU.add)
                    chk(2.7)
                    psy = PS[0]
                    for j in range(4):
                        s = s0 + j
                        pe.matmul(out=psy[32 * j:32 * j + 32, 0:128], lhsT=LC[0][:, s, :, :].re("p a b -> p (a b)"),
                                  rhs=xr[:, j, :], start=True, stop=False, tile_position=(0, 32 * j), inc=False)
                        pe.matmul(out=psy[32 * j:32 * j + 32, 0:128], lhsT=LC[1][:, s, :, :].re("p a b -> p (a b)"),
                                  rhs=xi[:, j, :], start=False, stop=True, tile_position=(0, 32 * j), inc=(j == 3))
                    dve.scalar_tensor_tensor(out=yT[:, blk, ts * 128:(ts + 1) * 128], in0=uT[:, ts * 128:(ts + 1) * 128],
                                             scalar=Dsk_s[:, blk:blk + 1], in1=psy[:, 0:128], op0=ALU.mult, op1=ALU.add)
                if is_last:
                    psr = PS[6]; psi = PS[7]
                    for j in range(4):
                        for (L, psb) in ((LBz[0], psr), (LBz[1], psi)):
                            pe.matmul(out=psb[:, j * NSC:(j + 1) * NSC], lhsT=L[:, j, :],
                                      rhs=uT[:, TP:TP + NSC],
                                      start=True, stop=True, inc=(j == 3))
                    abr_b = abr[:, sl, None].bc([P, 4, NSC]); abi_b = abi[:, sl, None].bc([P, 4, NSC])
                    dve.tensor_tensor(out=sq[0][:], in0=x0r[:, sl, :], in1=abr_b, op=ALU.mult)
                    dve.tensor_tensor(out=sq[1][:], in0=x0i[:, sl, :], in1=abi_b, op=ALU.mult)
                    dve.tensor_tensor(out=sq[0][:], in0=sq[0][:], in1=sq[1][:], op=ALU.subtract)
                    dve.tensor_tensor(out=x1r[:, sl, :], in0=sq[0][:], in1=psr[:, 0:4 * NSC].re("p (a b) -> p a b", a=4), op=ALU.add)
                    dve.tensor_tensor(out=sq[0][:], in0=x0i[:, sl, :], in1=abr_b, op=ALU.mult)
                    dve.tensor_tensor(out=sq[1][:], in0=x0r[:, sl, :], in1=abi_b, op=ALU.mult)
                    dve.tensor_tensor(out=sq[0][:], in0=sq[0][:], in1=sq[1][:], op=ALU.add)
                    dve.tensor_tensor(out=x1i[:, sl, :], in0=sq[0][:], in1=psi[:, 0:4 * NSC].re("p (a b) -> p a b", a=4), op=ALU.add)
                    psy = PS[0]
                    for j in range(4):
                        s = s0 + j
                        pe.matmul(out=psy[32 * j:32 * j + 32, 0:NSC], lhsT=LC[0][:, s, :, :].re("p a b -> p (a b)"),
                                  rhs=x1r[:, s, :], start=True, stop=False, tile_position=(0, 32 * j), inc=False)
                        pe.matmul(out=psy[32 * j:32 * j + 32, 0:NSC], lhsT=LC[1][:, s, :, :].re("p a b -> p (a b)"),
                                  rhs=x1i[:, s, :], start=False, stop=True, tile_position=(0, 32 * j), inc=(j == 3))
                    dve.scalar_tensor_tensor(out=yT[:, blk, TP:TP + NSC], in0=uT[:, TP:TP + NSC],
                                             scalar=Dsk_s[:, blk:blk + 1], in1=psy[:, 0:NSC], op0=ALU.mult, op1=ALU.add)
            chk(2.8)
            scanscope.__exit__(None, None, None)
            gtmp = lsb("gtmp", [P, TW]); gs = lsb("gs", [P, TW]); zs = lsb("zs", [P, TW])
            Wy = W1 if is_last else TP
            for blk in range(NUB):
                gelu_inplace(yT[:, blk, 0:Wy], gtmp[:, 0:Wy])
            gt_tiles = list(C.TT) + ([(TP, NSC)] if is_last else [])
            for jb in range(NUB):
                w = load_w(w_glu, jb, nk=NUB)
                w2_ = load_w(w_in, C.OFF_Z // 128 + jb)
                for (t0, tn) in gt_tiles:
                    psb = PS[2]; psz = PS[3]
                    mm_acc(psb[:, 0:tn], w, NUB, lambda kc: yT[:, kc, t0:t0 + tn])
                    act.activation(out=gs[:, t0:t0 + tn], in_=psb[:, 0:tn], func=AF.Sigmoid, bias=b_glu_s[:, jb:jb + 1], scale=1.0)
                    mm_acc(psz[:, 0:tn], w2_, KC, lambda kc: hT[:, kc, t0:t0 + tn].r32())
                    act.activation(out=zs[:, t0:t0 + tn], in_=psz[:, 0:tn], func=AF.Silu)
                    dve.tensor_tensor(out=gs[:, t0:t0 + tn], in0=gs[:, t0:t0 + tn], in1=zs[:, t0:t0 + tn], op=ALU.mult)
                    dve.tensor_tensor(out=osT[:, jb, t0:t0 + tn], in0=gs[:, t0:t0 + tn], in1=yT[:, jb, t0:t0 + tn], op=ALU.mult)
            barrier()

    EM05 = math.exp(-0.5)

    def rwkv_phase(is_last):
        Wd = W1 if is_last else TP
        with fw.scope():
            lsb = fw.sb
            Pb = lsb("Pb", [P, 1 + TW]); tmpA = lsb("tmpA", [P, W1])
            lor = lsb("lor", [P, W1])
            rr = lsb("rr", [P, W1]); kr = lsb("kr", [P, W1]); vv = lsb("vv", [P, W1])
            ldc = lsb("ldc", [P, W1]); alr = lsb("alr", [P, W1]); kkn = lsb("kkn", [P, W1]); kmod = lsb("kmod", [P, W1])
            bb = lsb("bb", [P, W1]); cl = lsb("cl", [P, TP]); e1 = lsb("e1", [P, TP]); e2 = Tile(kr.ap[:, 0:TP], kr.buf)
            cend = lsb("cend", [P, NCH]); PC = lsb("PC", [P, NCH])
            AR = lsb("AR", [P, NCH, 2, 64]); BK = lsb("BK", [P, NCH, 2, 64]); AV = lsb("AV", [P, NCH, 2, 64]); BKH = lsb("BKH", [P, NCH, 2, 64])
            bonus = alr; oT = Pb; osq = kkn
            CB = 3; NSL = 2 * CB
            GTs = [lsb("GTs%d" % e, [P, P]) for e in range(2 * NSL)]
            X = [lsb("X%d" % e, [P, 64]) for e in range(2 * NSL)]
            BKHt = [lsb("BKHt%d" % e, [P, 64]) for e in range(2 * NSL)]
            PW = [lsb("PW%d" % e, [64, 3, 64]) for e in range(NSL)]
            Ys = [lsb("Ys%d" % e, [64, 64]) for e in range(2 * NSL)]
            WTs = [lsb("WTs%d" % e, [P, 64]) for e in range(2 * NSL)]

            def proj_shift(rb, dst):
                w = load_w(w_in, C.OFF_RW // 128 + rb)
                for (t0, tn) in tiles(is_last):
                    psb = PS[6 + (t0 // 512) % 2]
                    mm_acc(psb[:, 0:tn], w, KC, lambda kc: hT[:, kc, t0:t0 + tn].r32())
                    act.copy(out=Pb[:, 1 + t0:1 + t0 + tn], in_=psb[:, 0:tn])
                act.copy(out=Pb[:, 0:1], in_=rw_car[:, rb:rb + 1])
                dve.tensor_scalar_mul(out=tmpA[:, 0:TP], in0=Pb[:, 0:TP], scalar1=mu_s[:, rb:rb + 1])
                dve.scalar_tensor_tensor(out=dst[:, 0:TP], in0=Pb[:, 1:TP + 1], scalar=omu_s[:, rb:rb + 1], in1=tmpA[:, 0:TP],
                                         op0=ALU.mult, op1=ALU.add)
                dve.tensor_copy(out=rw_car[:, rb:rb + 1], in_=Pb[:, TP:TP + 1])
                if is_last:
                    dve.tensor_scalar_mul(out=tmpA[:, TP:W1], in0=Pb[:, 1 + TP + NSC:1 + TP + NS2], scalar1=mu_s[:, rb:rb + 1])
                    dve.scalar_tensor_tensor(out=dst[:, TP:W1], in0=Pb[:, 1 + TP:1 + TP + NSC], scalar=omu_s[:, rb:rb + 1],
                                             in1=tmpA[:, TP:W1], op0=ALU.mult, op1=ALU.add)

            ct_tiles = list(C.TT) + ([(TP, NSC)] if is_last else [])
            proj_shift(3 * NHP, lor)
            act.activation(out=lor[0:64, 0:Wd], in_=lor[0:64, 0:Wd], func=AF.Tanh)
            for hp in range(NHP):
                proj_shift(hp, rr); proj_shift(NHP + hp, kr); proj_shift(2 * NHP + hp, vv)
                for (t0, tn) in ct_tiles:
                    psb = PS[6]
                    pe.matmul(out=psb[:, 0:tn], lhsT=w2_s[0:64, hp * 128:(hp + 1) * 128], rhs=lor[0:64, t0:t0 + tn], start=True, stop=True)
                    act.activation(out=ldc[:, t0:t0 + tn], in_=psb[:, 0:tn], func=AF.Sigmoid, bias=w0_s[:, hp:hp + 1], scale=1.0)
                    psb = PS[7]
                    pe.matmul(out=psb[:, 0:tn], lhsT=a2_s[64:128, hp * 128:(hp + 1) * 128], rhs=lor[64:128, t0:t0 + tn], start=True, stop=True)
                    act.activation(out=alr[:, t0:t0 + tn], in_=psb[:, 0:tn], func=AF.Sigmoid, bias=a0_s[:, hp:hp + 1], scale=1.0)
                dve.tensor_scalar_mul(out=ldc[:, 0:Wd], in0=ldc[:, 0:Wd], scalar1=-EM05)
                dve.tensor_scalar_mul(out=kkn[:, 0:Wd], in0=kr[:, 0:Wd], scalar1=kk_s[:, hp:hp + 1])
                dve.tensor_tensor(out=tmpA[:, 0:Wd], in0=kkn[:, 0:Wd], in1=kkn[:, 0:Wd], op=ALU.mult)
                for (t0, tn) in ct_tiles:
                    psb = PS[6]
                    pe.matmul(out=psb[:, 0:tn], lhsT=onesbd[:], rhs=tmpA[:, t0:t0 + tn], start=True, stop=True)
                    act.activation(out=bb[:, t0:t0 + tn], in_=psb[:, 0:tn], func=AF.Sqrt)
                dve.tensor_scalar_max(out=bb[:, 0:Wd], in0=bb[:, 0:Wd], scalar1=1e-12)
                dve.reciprocal(out=bb[:, 0:Wd], in_=bb[:, 0:Wd])
                dve.tensor_tensor(out=kkn[:, 0:Wd], in0=kkn[:, 0:Wd], in1=bb[:, 0:Wd], op=ALU.mult)
                dve.tensor_scalar(out=kmod[:, 0:Wd], in0=alr[:, 0:Wd], scalar1=-1.0, scalar2=ka_s[:, hp:hp + 1], op0=ALU.add, op1=ALU.mult)
                dve.scalar_tensor_tensor(out=kmod[:, 0:Wd], in0=kmod[:, 0:Wd], scalar=1.0, in1=kr[:, 0:Wd], op0=ALU.add, op1=ALU.mult)
                dve.tensor_tensor(out=bb[:, 0:Wd], in0=kkn[:, 0:Wd], in1=alr[:, 0:Wd], op=ALU.mult)
                dve.scalar_tensor_tensor(out=tmpA[:, 0:Wd], in0=rr[:, 0:Wd], scalar=rk_s[:, hp:hp + 1], in1=kmod[:, 0:Wd], op0=ALU.mult, op1=ALU.mult)
                for (t0, tn) in ct_tiles:
                    psb = PS[6]
                    pe.matmul(out=psb[:, 0:tn], lhsT=onesbd[:], rhs=tmpA[:, t0:t0 + tn], start=True, stop=True)
                    dve.tensor_tensor(out=bonus[:, t0:t0 + tn], in0=psb[:, 0:tn], in1=vv[:, t0:t0 + tn], op=ALU.mult)
                dve.tensor_tensor_scan(out=cl[:], data0=resetm[:], data1=ldc[:, 0:TP], initial=0.0, op0=ALU.mult, op1=ALU.add)
                dve.tensor_copy(out=cend[:], in_=cl[:].re("p (c t) -> p c t", t=64)[:, :, 63])
                act.activation(out=PC[:], in_=cend[:], func=AF.Exp)
                c4 = lambda t: t[:, 0:TP].re("p (c t) -> p c t", t=64)
                act.activation(out=e1[:], in_=cl[:], func=AF.Exp)
                dve.tensor_tensor(out=c4(AR)[:, :, :] if False else AR[:, :, 1, :], in0=c4(rr), in1=c4(e1), op=ALU.mult)
                dve.tensor_tensor(out=e2[:], in0=cl[:], in1=ldc[:, 0:TP], op=ALU.subtract)
                act.activation(out=e2[:], in_=e2[:], func=AF.Exp)
                dve.scalar_tensor_tensor(out=AR[:, :, 0, :], in0=c4(kkn), scalar=-1.0, in1=c4(e2), op0=ALU.mult, op1=ALU.mult)
                dve.tensor_copy(out=AV[:, :, 0, :], in_=AR[:, :, 0, :])
                dve.tensor_copy(out=AV[:, :, 1, :], in_=c4(vv))
                act.activation(out=e1[:], in_=cl[:], func=AF.Exp, scale=-1.0)
                dve.tensor_tensor(out=BK[:, :, 0, :], in0=c4(bb), in1=c4(e1), op=ALU.mult)
                dve.tensor_tensor(out=BK[:, :, 1, :], in0=c4(kmod), in1=c4(e1), op=ALU.mult)
                dve.tensor_tensor(out=c4(e2), in0=cend[:, :, None].bc([P, NCH, 64]), in1=c4(cl), op=ALU.subtract)
                act.activation(out=e2[:], in_=e2[:], func=AF.Exp)
                dve.tensor_tensor(out=BKH[:, :, 0, :], in0=c4(bb), in1=c4(e2), op=ALU.mult)
                dve.tensor_tensor(out=BKH[:, :, 1, :], in0=c4(kmod), in1=c4(e2), op=ALU.mult)
                def fl(v):
                    return v.re("p a b -> p (a b)")

                def stage1(pairs, sb):
                    n = len(pairs)
                    for i, (c, e) in enumerate(pairs):
                        pb = 64 * e; B = PS[i]
                        pe.matmul(out=B[:, 0:128], lhsT=fl(BK[pb:pb + 64, c, :, :]), rhs=fl(AR[pb:pb + 64, c, :, :]),
                                  start=True, stop=True, tile_position=(pb, 0))
                        pe.transpose(out=B[:, 128:192], in_=fl(AV[pb:pb + 64, c, :, :]), identity=ident[pb:pb + 64, pb:pb + 64],
                                     tile_position=(pb, 0))
                        pe.transpose(out=B[:, 192:256], in_=fl(BKH[pb:pb + 64, c, :, :]), identity=ident[pb:pb + 64, pb:pb + 64],
                                     tile_position=(pb, 0))
                    for i, (c, e) in enumerate(pairs):
                        B = PS[i]; r = sb + i
                        dve.tensor_tensor(out=GTs[r][:], in0=B[:, 0:128], in1=maskgt[:], op=ALU.mult)
                        act.copy(out=X[r][:], in_=B[:, 128:192])
                        act.copy(out=BKHt[r][:], in_=B[:, 192:256])
                    yield
                    for i, (c, e) in enumerate(pairs):
                        pe.transpose(out=PS[i][0:64, 0:64], in_=GTs[sb + i][0:64, 0:64], identity=ident[0:64, 0:64])
                    for i, (c, e) in enumerate(pairs):
                        r = sb + i; pw = PW[i]
                        act.copy(out=pw[:, 1, :], in_=PS[i][0:64, 0:64])
                        dve.tensor_copy(out=pw[:, 0, :], in_=GTs[r][0:64, 0:64])
                        dve.tensor_tensor(out=pw[:, 2, :], in0=GTs[r][0:64, 0:64], in1=ident[0:64, 0:64], op=ALU.add)
                    yield
                    for lvl in range(1, 6):
                        for i in range(n):
                            pw = PW[i]; B = PS[i]
                            pe.matmul(out=B[0:64, 0:64], lhsT=pw[:, 1, :], rhs=pw[:, 0, :], start=True, stop=True)
                            pe.matmul(out=B[0:64, 64:128], lhsT=pw[:, 0, :], rhs=pw[:, 1, :], start=True, stop=True)
                        for i in range(n):
                            act.copy(out=fl(PW[i][:, 0:2, :]), in_=PS[i][0:64, 0:128])
                        yield
                        for i in range(n):
                            pw = PW[i]
                            pe.matmul(out=PS[i][0:64, 128:192], lhsT=pw[:, 1, :], rhs=pw[:, 2, :], start=True, stop=True)
                        for i in range(n):
                            pw = PW[i]
                            dve.tensor_tensor(out=pw[:, 2, :], in0=pw[:, 2, :], in1=PS[i][0:64, 128:192], op=ALU.add)
                        yield
                    for i, (c, e) in enumerate(pairs):
                        pb = 64 * e; r = sb + i; B = PS[i]
                        pe.matmul(out=B[0:64, 256:320], lhsT=GTs[r][64:128, 0:64], rhs=X[r][64:128, :], start=True, stop=True,
                                  tile_position=(64, 0))
                    for i, (c, e) in enumerate(pairs):
                        pb = 64 * e; r = sb + i; B = PS[i]
                        pe.matmul(out=B[pb:pb + 64, 320:384], lhsT=X[r][0:64, :], rhs=PW[i][:, 2, :], start=True, stop=True,
                                  tile_position=(0, pb))
                    for i, (c, e) in enumerate(pairs):
                        pb = 64 * e; r = sb + i; B = PS[i]
                        act.copy(out=PW[i][:, 0, :], in_=B[0:64, 256:320])
                        act.copy(out=WTs[r][pb:pb + 64, :], in_=B[pb:pb + 64, 320:384])
                    yield
                    for i in range(n):
                        pe.matmul(out=PS[i][0:64, 384:448], lhsT=PW[i][:, 2, :], rhs=PW[i][:, 0, :], start=True, stop=True)
                    for i in range(n):
                        act.copy(out=Ys[sb + i][:], in_=PS[i][0:64, 384:448])
                    yield

                def stage2(chs, sb):
                    for ci, c in enumerate(chs):
                        for e in range(2):
                            pb = 64 * e; r = sb + 2 * ci + e; Hs = Hst[2 * hp + e]
                            pe.matmul(out=PS[6 + e][0:64, 0:64], lhsT=WTs[r][pb:pb + 64, :], rhs=Hs[pb:pb + 64, :], start=True, stop=True,
                                      tile_position=(pb, 0))
                        for e in range(2):
                            r = sb + 2 * ci + e
                            dve.tensor_tensor(out=X[r][0:64, :], in0=PS[6 + e][0:64, 0:64], in1=Ys[r][:], op=ALU.add)
                        yield
                        for e in range(2):
                            pb = 64 * e; r = sb + 2 * ci + e; Hs = Hst[2 * hp + e]; B = PS[6 + e]
                            pe.matmul(out=B[pb:pb + 64, 64:128], lhsT=Hs[pb:pb + 64, :], rhs=AR[pb:pb + 64, c, 1, :], start=True, stop=True,
                                      tile_position=(pb, pb))
                            pe.matmul(out=B[pb:pb + 64, 128:192], lhsT=X[r][:], rhs=GTs[r][:, 64:128], start=True, stop=True,
                                      tile_position=(0, pb))
                            pe.matmul(out=B[pb:pb + 64, 192:256], lhsT=BKHt[r][:], rhs=X[r][:], start=True, stop=True, tile_position=(0, pb))
                        for e in range(2):
                            pb = 64 * e; Hs = Hst[2 * hp + e]; B = PS[6 + e]
                            act.copy(out=oT[pb:pb + 64, c * 64:(c + 1) * 64], in_=B[pb:pb + 64, 64:128])
                            dve.tensor_tensor(out=oT[pb:pb + 64, c * 64:(c + 1) * 64], in0=oT[pb:pb + 64, c * 64:(c + 1) * 64],
                                              in1=B[pb:pb + 64, 128:192], op=ALU.add)
                            dve.scalar_tensor_tensor(out=Hs[pb:pb + 64, :], in0=Hs[pb:pb + 64, :], scalar=PC[pb:pb + 64, c:c + 1],
                                                     in1=B[pb:pb + 64, 192:256], op0=ALU.mult, op1=ALU.add)
                        yield

                batches = [list(range(c0, min(c0 + CB, NCH))) for c0 in range(0, NCH, CB)]
                prev = None
                for bi in range(len(batches) + 1):
                    g1 = None; g2 = None
                    if bi < len(batches):
                        chs = batches[bi]
                        g1 = stage1([(c, e) for c in chs for e in range(2)], (bi % 2) * NSL)
                    if prev is not None:
                        g2 = stage2(prev[0], prev[1])
                    while g1 is not None or g2 is not None:
                        for _ in range(2):
                            if g1 is not None:
                                try:
                                    next(g1)
                                except StopIteration:
                                    g1 = None
                        if g2 is not None:
                            try:
                                next(g2)
                            except StopIteration:
                                g2 = None
                    prev = (batches[bi], (bi % 2) * NSL) if bi < len(batches) else None
                if is_last:
                    rwkv_sample(hp, rr, ldc, kmod, vv, kkn, bb, oT, tmpA)
                dve.tensor_tensor(out=osq[:, 0:TP], in0=oT[:, 0:TP], in1=oT[:, 0:TP], op=ALU.mult)
                for (t0, tn) in C.TT:
                    psm = PS[6]; psq = PS[7]
                    pe.matmul(out=psm[:, 0:tn], lhsT=onesbd[:], rhs=oT[:, t0:t0 + tn], start=True, stop=True)
                    pe.matmul(out=psq[:, 0:tn], lhsT=onesbd[:], rhs=osq[:, t0:t0 + tn], start=True, stop=True)
                    mean = tmpA[:, t0:t0 + tn]; var = osq[:, t0:t0 + tn]
                    act.mul(out=mean, in_=psm[:, 0:tn], mul=1.0 / 64)
                    dve.tensor_tensor(out=e1[:, 0:tn], in0=mean, in1=mean, op=ALU.mult)
                    dve.scalar_tensor_tensor(out=var, in0=psq[:, 0:tn], scalar=1.0 / 64, in1=e1[:, 0:tn], op0=ALU.mult, op1=ALU.subtract)
                    act.activation(out=var, in_=var, func=AF.Sqrt, bias=eps_gn[:], scale=1.0)
                    dve.reciprocal(out=var, in_=var)
                    dve.tensor_tensor(out=oT[:, t0:t0 + tn], in0=oT[:, t0:t0 + tn], in1=mean, op=ALU.subtract)
                    dve.tensor_tensor(out=oT[:, t0:t0 + tn], in0=oT[:, t0:t0 + tn], in1=var, op=ALU.mult)
                dve.tensor_scalar(out=oT[:, 0:TP], in0=oT[:, 0:TP], scalar1=gnw_s[:, hp:hp + 1], scalar2=gnb_s[:, hp:hp + 1], op0=ALU.mult, op1=ALU.add)
                dve.tensor_tensor(out=oT[:, 0:TP], in0=oT[:, 0:TP], in1=bonus[:, 0:TP], op=ALU.add)
                w = load_w(w_in, C.OFF_RWZ // 128 + hp)
                for (t0, tn) in C.TT:
                    psz = PS[6]
                    mm_acc(psz[:, 0:tn], w, KC, lambda kc: hT[:, kc, t0:t0 + tn].r32())
                    act.activation(out=tmpA[:, t0:t0 + tn], in_=psz[:, 0:tn], func=AF.Silu)
                    dve.tensor_tensor(out=orT[:, hp, t0:t0 + tn], in0=oT[:, t0:t0 + tn], in1=tmpA[:, t0:t0 + tn], op=ALU.mult)
            barrier()

    NPT = (NSC * C.H) // 128 if NSC * C.H >= 128 else 1
    NPP = min(128, NSC * C.H)
    BPT = NPP // C.H
    tokm = fw.sb("tokm", [NSC, 2, 6, 64])
    o_bh_all = fw.sb("o_bh_all", [NPP, NPT, 64])

    def rwkv_sample(hp, rr, ldc, kmod, vv, kkn, bb, oT, tmpA):
        act.activation(out=tmpA[:, TP:W1], in_=ldc[:, TP:W1], func=AF.Exp)
        vecs = [rr, tmpA, kmod, vv, kkn, bb]
        for vi, t in enumerate(vecs):
            pe.transpose(out=PS[6 + vi // 3][0:NSC, (vi % 3) * 128:(vi % 3 + 1) * 128], in_=t[:, TP:W1], identity=ident[:])
        act.copy(out=tokm[:, :, 0:3, :], in_=PS[6][0:NSC, 0:384].re("p (v h n) -> p h v n", v=3, h=2))
        act.copy(out=tokm[:, :, 3:6, :], in_=PS[7][0:NSC, 0:384].re("p (v h n) -> p h v n", v=3, h=2))
        st(scr_a[:, 2 * hp:2 * hp + 2, :, :], tokm[:])

    def rwkv_sample_core():
        with fw.scope():
            lsb = fw.sb
            S = lsb("S_s", [NPP, 64, 64]); T1 = lsb("T1_s", [NPP, 64, 64]); vec = lsb("vec_s", [NPP, 6, 64])
            sa = lsb("sa_s", [NPP, 64]); ob = lsb("ob_s", [NPP, 64]); st1 = lsb("st1_s", [NPP, 4])
            gw = lsb("gw_s", [NPP, 64]); gb = lsb("gb_s", [NPP, 64]); rkb = lsb("rkb_s", [NPP, 64])
            ld(gw[:], gnw_bh[0:NPP, :]); ld(gb[:], gnb_bh[0:NPP, :]); ld(rkb[:], rk_bh[0:NPP, :])
            for pt in range(NPT):
                ld(S[:].re("p a b -> p (a b)"), wkv0[pt * NPP:(pt + 1) * NPP, :])
                ld(vec[:], View(scr_a.buf, scr_a.ap[pt * BPT:(pt + 1) * BPT].rearrange("b h v n -> (b h) v n")))
                R = vec[:, 0, :]; Dc = vec[:, 1, :]; K = vec[:, 2, :]; V = vec[:, 3, :]; KK = vec[:, 4, :]; Bv = vec[:, 5, :]
                bj = lambda v: v[:, None, :].bc([NPP, 64, 64])
                bi_ = lambda v: v[:, :, None].bc([NPP, 64, 64])
                dve.tensor_tensor(out=T1[:], in0=S[:], in1=bj(KK), op=ALU.mult)
                dve.tensor_reduce(out=sa[:], in_=T1[:], axis=AX.X, op=ALU.add)
                dve.tensor_scalar_mul(out=sa[:], in0=sa[:], scalar1=-1.0)
                dve.tensor_tensor(out=S[:], in0=S[:], in1=bj(Dc), op=ALU.mult)
                dve.tensor_tensor(out=T1[:], in0=bi_(sa[:]), in1=bj(Bv), op=ALU.mult)
                dve.tensor_tensor(out=S[:], in0=S[:], in1=T1[:], op=ALU.add)
                dve.tensor_tensor(out=T1[:], in0=bi_(V), in1=bj(K), op=ALU.mult)
                dve.tensor_tensor(out=S[:], in0=S[:], in1=T1[:], op=ALU.add)
                st(swkv_o[pt * NPP:(pt + 1) * NPP, :], S[:].re("p a b -> p (a b)"))
                dve.tensor_tensor(out=T1[:], in0=S[:], in1=bj(R), op=ALU.mult)
                dve.tensor_reduce(out=ob[:], in_=T1[:], axis=AX.X, op=ALU.add)
                dve.tensor_reduce(out=st1[:, 0:1], in_=ob[:], axis=AX.X, op=ALU.add)
                dve.tensor_scalar_mul(out=st1[:, 0:1], in0=st1[:, 0:1], scalar1=1.0 / 64)
                dve.tensor_scalar(out=ob[:], in0=ob[:], scalar1=st1[:, 0:1], scalar2=None, op0=ALU.subtract)
                dve.tensor_tensor(out=sa[:], in0=ob[:], in1=ob[:], op=ALU.mult)
                dve.tensor_reduce(out=st1[:, 1:2], in_=sa[:], axis=AX.X, op=ALU.add)
                act.activation(out=st1[:, 1:2], in_=st1[:, 1:2], func=AF.Sqrt, bias=eps_gn[0:NPP, :], scale=1.0 / 64)
                dve.reciprocal(out=st1[:, 1:2], in_=st1[:, 1:2])
                dve.tensor_scalar(out=ob[:], in0=ob[:], scalar1=st1[:, 1:2], scalar2=None, op0=ALU.mult)
                dve.tensor_tensor(out=ob[:], in0=ob[:], in1=gw[:], op=ALU.mult)
                dve.tensor_tensor(out=ob[:], in0=ob[:], in1=gb[:], op=ALU.add)
                dve.tensor_tensor(out=sa[:], in0=R, in1=K, op=ALU.mult)
                dve.tensor_tensor(out=sa[:], in0=sa[:], in1=rkb[:], op=ALU.mult)
                dve.tensor_reduce(out=st1[:, 2:3], in_=sa[:], axis=AX.X, op=ALU.add)
                dve.scalar_tensor_tensor(out=o_bh_all[:, pt, :], in0=V, scalar=st1[:, 2:3], in1=ob[:], op0=ALU.mult, op1=ALU.add)
                st(View(scr_b.buf, scr_b.ap[pt * BPT:(pt + 1) * BPT].rearrange("b (h n) -> (b h) n", n=64)), o_bh_all[:, pt, :])
            otok = lsb("otok_s", [NSC, C.DR]); ld(otok[:], scr_b[:])
            zt_ = lsb("zt_s", [P, NSC]); of_ = lsb("of_s", [P, NSC])
            for hp in range(NHP):
                pe.transpose(out=PS[6][:, 0:NSC], in_=otok[:, hp * 128:(hp + 1) * 128], identity=ident[0:NSC, 0:NSC])
                act.copy(out=of_[:], in_=PS[6][:, 0:NSC])
                w = load_w(w_in, C.OFF_RWZ // 128 + hp)
                mm_acc(PS[7][:, 0:NSC], w, KC, lambda kc: hT[:, kc, TP:TP + NSC].r32())
                act.activation(out=zt_[:], in_=PS[7][:, 0:NSC], func=AF.Silu)
                dve.tensor_tensor(out=orT[:, hp, TP:TP + NSC], in0=of_[:], in1=zt_[:], op=ALU.mult)
            barrier()

    def post_phase(is_last, p):
        with fw.scope():
            lsb = fw.sb
            NTT = TP // 128
            xn = [lsb("xn%d" % i, [P, D]) for i in range(NTT)]
            xns = lsb("xns", [NSC, D]) if is_last else None
            g1 = lsb("g1", [P, TW]); g2 = lsb("g2", [P, TW]); MT = lsb("MT", [P, TW])
            sq = lsb("sqp", [P, D], mybir.dt.bfloat16); ssp = lsb("ssp", [P, NTT + 1])
            fg = lsb("fg", [P, D]); ld(fg[:], fing[:])
            pt_tiles = list(C.TT) + ([(TP, NSC)] if is_last else [])
            for tt in range(NTT):
                ld(xn[tt][:], xb[p * TP + tt * 128:p * TP + (tt + 1) * 128, :])
            if is_last:
                ld(xns[:], xs[:])
            for db in range(KC):
                wg1 = load_w(w_in, C.OFF_G1 // 128 + db)
                for (t0, tn) in pt_tiles:
                    psg = PS[0 + (t0 // 512) % 2]
                    mm_acc(psg[:, 0:tn], wg1, KC, lambda kc: hT[:, kc, t0:t0 + tn].r32())
                    act.activation(out=g1[:, t0:t0 + tn], in_=psg[:, 0:tn], func=AF.Sigmoid)
                wo = load_w(w_out, db)
                for (t0, tn) in pt_tiles:
                    ps1 = PS[2 + (t0 // 512) % 2]
                    mm_acc(ps1[:, 0:tn], wo, NUB, lambda kc: osT[:, kc, t0:t0 + tn])
                    dve.tensor_tensor(out=MT[:, t0:t0 + tn], in0=g1[:, t0:t0 + tn], in1=ps1[:, 0:tn], op=ALU.mult)
                wg2 = load_w(w_in, C.OFF_G2 // 128 + db)
                for (t0, tn) in pt_tiles:
                    psg2 = PS[0 + (t0 // 512) % 2]
                    mm_acc(psg2[:, 0:tn], wg2, KC, lambda kc: hT[:, kc, t0:t0 + tn].r32())
                    act.activation(out=g2[:, t0:t0 + tn], in_=psg2[:, 0:tn], func=AF.Sigmoid)
                    ps2 = PS[2 + (t0 // 512) % 2]
                    for kc in range(NHP):
                        pe.matmul(out=ps2[:, 0:tn], lhsT=wo[:, NUB + kc, :].r32(), rhs=orT[:, kc, t0:t0 + tn],
                                  start=(kc == 0), stop=(kc == NHP - 1), inc=(kc == NHP - 1))
                    dve.tensor_tensor(out=g2[:, t0:t0 + tn], in0=g2[:, t0:t0 + tn], in1=ps2[:, 0:tn], op=ALU.mult)
                    dve.tensor_tensor(out=MT[:, t0:t0 + tn], in0=MT[:, t0:t0 + tn], in1=g2[:, t0:t0 + tn], op=ALU.add)
                    if t0 < TP:
                        dve.tensor_scalar_mul(out=MT[:, t0:t0 + tn], in0=MT[:, t0:t0 + tn], scalar1=gt_p[:, db:db + 1])
                        for q in range(tn // 128):
                            tt = t0 // 128 + q
                            pst = PS[4 + q % 2]
                            pe.transpose(out=pst[:, 0:128], in_=MT[:, t0 + q * 128:t0 + (q + 1) * 128], identity=ident[:])
                            dve.tensor_tensor(out=xn[tt][:, db * 128:(db + 1) * 128], in0=xn[tt][:, db * 128:(db + 1) * 128],
                                              in1=pst[:, 0:128], op=ALU.add)
                    else:
                        dve.tensor_tensor(out=MT[:, TP:TP + NSC], in0=MT[:, TP:TP + NSC], in1=modT[:, 2 * KC + db, 0:NSC], op=ALU.mult)
                        pst = PS[6]
                        pe.transpose(out=pst[0:NSC, 0:128], in_=MT[:, TP:TP + NSC], identity=ident[:])
                        dve.tensor_tensor(out=xns[:, db * 128:(db + 1) * 128], in0=xns[:, db * 128:(db + 1) * 128],
                                          in1=pst[0:NSC, 0:128], op=ALU.add)
            for tt in range(NTT):
                s = ssp[:, tt:tt + 1]
                act.activation(out=sq[:], in_=xn[tt][:], func=AF.Square, accum_out=s)
                rstd_of(s, 128)
                dve.scalar_tensor_tensor(out=xn[tt][:], in0=xn[tt][:], scalar=s, in1=fg[:], op0=ALU.mult, op1=ALU.mult)
                st(y_o[p * TP + tt * 128:p * TP + (tt + 1) * 128, :], xn[tt][:])
            if is_last:
                s = ssp[0:NSC, NTT:NTT + 1]
                act.activation(out=sq[0:NSC, :], in_=xns[:], func=AF.Square, accum_out=s)
                rstd_of(s, NSC)
                dve.scalar_tensor_tensor(out=xns[:], in0=xns[:], scalar=s, in1=fg[0:NSC, :], op0=ALU.mult, op1=ALU.mult)
                st(ys_o[:], xns[:])
            barrier()

    def sample_prep():
        with fw.scope():
            lsb = fw.sb
            xs_t = lsb("xs_t", [NSC, D]); sh_t = lsb("sh_t", [NSC, D]); sqs = lsb("sqs", [NSC, D]); s1 = lsb("s1s", [NSC, 1])
            xsT = lsb("xsT", [P, KC, NSC]); hs_tok = lsb("hs_tok", [NSC, D]); stg = lsb("stg", [NSC, NSB * 128])
            ld(xs_t[:], xs[:]); ld(sh_t[:], sshift[:])
            act.activation(out=sqs[:], in_=xs_t[:], func=AF.Square, accum_out=s1[:])
            rstd_of(s1[:], NSC)
            act.activation(out=xs_t[:], in_=xs_t[:], func=AF.Copy, scale=s1[:])
            for kc in range(KC):
                pe.transpose(out=PS[0][:, kc * NSC:(kc + 1) * NSC], in_=xs_t[:, kc * 128:(kc + 1) * 128], identity=ident[0:NSC, 0:NSC])
                pe.transpose(out=PS[1][:, kc * NSC:(kc + 1) * NSC], in_=sh_t[:, kc * 128:(kc + 1) * 128], identity=ident[0:NSC, 0:NSC])
            dve.tensor_tensor(out=xsT[:], in0=PS[0][:, 0:KC * NSC].re("p (k n) -> p k n", n=NSC), in1=sceff[:, :, 0:NSC], op=ALU.mult)
            dve.tensor_tensor(out=xsT[:], in0=xsT[:], in1=modT[:, 0:KC, 0:NSC], op=ALU.add)
            dve.tensor_copy(out=hT[:, :, TP:TP + NSC], in_=xsT[:])
            act.copy(out=hT[:, :, TP + NSC:TP + NS2], in_=PS[1][:, 0:KC * NSC].re("p (k n) -> p k n", n=NSC))
            for kc in range(KC):
                pe.transpose(out=PS[2 + kc // 4 % 2][0:NSC, (kc % 4) * 128:(kc % 4 + 1) * 128], in_=xsT[:, kc, :], identity=ident[:])
                act.copy(out=hs_tok[:, kc * 128:(kc + 1) * 128], in_=PS[2 + kc // 4 % 2][0:NSC, (kc % 4) * 128:(kc % 4 + 1) * 128])
            st(sshift_o[:], hs_tok[:])
            for (src, dst) in ((s5re0, x0r), (s5im0, x0i)):
                ld(stg[:], src[:])
                for s in range(NSB):
                    pe.transpose(out=PS[4][:, s * NSC:(s + 1) * NSC], in_=stg[:, s * 128:(s + 1) * 128], identity=ident[0:NSC, 0:NSC])
                act.copy(out=dst[:].re("p s n -> p (s n)"), in_=PS[4][:, 0:NSB * NSC])
            barrier()

    def finals():
        with fw.scope():
            lsb = fw.sb
            xf = [lsb("xf%d" % i, [P, NSB]) for i in range(4)]
            dve.tensor_tensor(out=xf[0][:], in0=cs1[:], in1=s5car_r[:], op=ALU.mult)
            dve.tensor_tensor(out=xf[1][:], in0=sn1[:], in1=s5car_i[:], op=ALU.mult)
            dve.tensor_tensor(out=xf[0][:], in0=xf[0][:], in1=xf[1][:], op=ALU.add)
            dve.tensor_tensor(out=xf[2][:], in0=cs1[:], in1=s5car_i[:], op=ALU.mult)
            dve.tensor_tensor(out=xf[3][:], in0=sn1[:], in1=s5car_r[:], op=ALU.mult)
            dve.tensor_tensor(out=xf[2][:], in0=xf[2][:], in1=xf[3][:], op=ALU.subtract)
            xo = lsb("xfo", [NSB, 2, P])
            pe.transpose(out=PS[0][0:NSB, 0:128], in_=xf[0][:], identity=ident[:])
            pe.transpose(out=PS[0][0:NSB, 128:256], in_=xf[2][:], identity=ident[:])
            act.copy(out=xo[:].re("p a b -> p (a b)"), in_=PS[0][0:NSB, 0:256])
            st(ps5re_o[:], xo[:, 0, :]); st(ps5im_o[:], xo[:, 1, :])
            ho = lsb("hlo", [KC, P])
            pe.transpose(out=PS[1][0:KC, 0:128], in_=hlast[:], identity=ident[:])
            act.copy(out=ho[:], in_=PS[1][0:KC, 0:128])
            st(pshift_o[:], ho[:])
            so = lsb("wkvo", [64, C.H, 64])
            for h in range(C.H):
                pb = 64 * (h % 2)
                pe.transpose(out=PS[2][0:64, (h % 8) * 64:(h % 8 + 1) * 64], in_=Hst[h][pb:pb + 64, :], identity=ident[pb:pb + 64, pb:pb + 64],
                             tile_position=(pb, 0))
                act.copy(out=so[:, h, :], in_=PS[2][0:64, (h % 8) * 64:(h % 8 + 1) * 64])
            st(View(pwkv_o.buf, pwkv_o.ap.rearrange("h i j -> i h j")), so[:])
            so2 = lsb("s5o", [NSC, 512])
            for (src, dsto) in ((x1r, ss5re_o), (x1i, ss5im_o)):
                for g4 in range(NSB // 4):
                    for j in range(4):
                        pe.transpose(out=PS[3][0:NSC, j * 128:(j + 1) * 128], in_=src[:, g4 * 4 + j, :], identity=ident[:])
                    act.copy(out=so2[:], in_=PS[3][0:NSC, :])
                    st(dsto[:, g4 * 512:(g4 + 1) * 512], so2[:])

    try:
      for p in range(NPASS):
          if stop(1):
              break
          is_last = (p == NPASS - 1)
          if is_last:
              sample_prep()
          phase_x(View(xb.buf, xb.ap[p * TP:(p + 1) * TP, :]), is_last)
          barrier()
          if stop(2):
              break
          s5_phase(is_last)
          if stop(3):
              break
          rwkv_phase(is_last)
          if stop(4):
              break
          if is_last:
              rwkv_sample_core()
          post_phase(is_last, p)
          if stop(5):
              break
      if not stop(6):
          finals()
    except StopBuild:
        pass
    fw.finish()
    return nc, fw, es


def _blk(w, kcn):
    K, N = w.shape
    return np.ascontiguousarray(w.reshape(kcn, 128, N // 128, 128).transpose(2, 1, 0, 3))


def _fm(v):
    v = np.asarray(v).reshape(-1)
    return np.ascontiguousarray(v.reshape(-1, 128).T)


def make_in_maps(cfg, inp, n_cores, n_seq):
    C = cfg
    f = np.float32
    H = C.H; NSB = C.NSB
    sh = {}
    sh["w_ada"] = _blk(inp["w_ada"][0], C.KC); sh["w_in"] = _blk(inp["w_in"][0], C.KC)
    sh["w_glu"] = _blk(inp["w_glu"][0], C.NUB); sh["w_out"] = _blk(inp["w_out"][0], C.KC)
    sh["b_ada"] = _fm(inp["b_ada"][0]); sh["norm_g"] = _fm(inp["norm_g"][0]); sh["mu"] = _fm(inp["mu_rw"][0])
    sh["A_re"] = _fm(inp["A_re"][0]); sh["A_im"] = _fm(inp["A_im"][0])
    sh["lstep"] = np.ascontiguousarray(np.repeat(inp["log_step"][0].reshape(NSB, 2), 64, axis=1).T)
    for k in ("B_re", "B_im"):
        sh[k] = np.ascontiguousarray(inp[k][0].reshape(NSB, 2, 64, 16).transpose(1, 2, 0, 3).reshape(128, NSB, 16))
    for k in ("C_re", "C_im"):
        sh[k] = np.ascontiguousarray(inp[k][0].reshape(NSB, 2, 16, 64).transpose(1, 3, 0, 2).reshape(128, NSB, 16))
    sh["Dsk"] = _fm(inp["D_skip"][0]); sh["b_glu"] = _fm(inp["b_glu"][0])
    for k in ("w0", "a0", "k_k", "k_a", "r_k", "gn_w", "gn_b"):
        sh[k] = _fm(inp[k][0])
    sh["w2"] = np.ascontiguousarray(inp["w2"][0]); sh["a2"] = np.ascontiguousarray(inp["a2"][0])
    sh["fing"] = np.ascontiguousarray(np.broadcast_to(inp["final_g"].reshape(1, -1), (128, C.D)))
    rep = max(1, 128 // H)
    sh["gnw_bh"] = np.ascontiguousarray(np.tile(inp["gn_w"][0].reshape(H, 64), (rep, 1))[:128])
    sh["gnb_bh"] = np.ascontiguousarray(np.tile(inp["gn_b"][0].reshape(H, 64), (rep, 1))[:128])
    sh["rk_bh"] = np.ascontiguousarray(np.tile(inp["r_k"][0].reshape(H, 64), (rep, 1))[:128])
    if sh["gnw_bh"].shape[0] < 128:
        for k in ("gnw_bh", "gnb_bh", "rk_bh"):
            sh[k] = np.ascontiguousarray(np.concatenate([sh[k], np.zeros((128 - sh[k].shape[0], 64), f)], 0))
    sh["ident"] = np.eye(128, dtype=f)
    ms = np.triu(np.ones((64, 64), f), 1); mi_ = np.triu(np.ones((64, 64), f), 0)
    sh["maskgt"] = np.block([[ms, mi_], [ms, mi_]]).astype(f)
    ob = np.zeros((128, 128), f); ob[:64, :64] = 1; ob[64:, 64:] = 1
    sh["onesbd"] = ob
    rm = np.ones((128, C.TP), f); rm[:, ::64] = 0
    sh["resetm"] = rm
    rmk = np.zeros((128, 4), f)
    for j in range(4):
        rmk[32 * j:32 * j + 32, j] = 1
    sh["rowmask"] = rmk
    sh["iota"] = np.ascontiguousarray(np.broadcast_to(np.arange(128, dtype=f).reshape(1, -1), (128, 128)))
    sh = {k: np.ascontiguousarray(v, dtype=f) for k, v in sh.items()}
    maps = []
    NSC = C.NSC
    for c in range(n_cores):
        b = c % n_seq
        rows = slice(c * NSC, (c + 1) * NSC)
        m = dict(sh)
        m["xb"] = np.ascontiguousarray(inp["x_prompt"][b], dtype=f)
        cp = inp["c_prompt"][b:b + 1]
        m["c17"] = np.ascontiguousarray(np.concatenate([inp["c_sample"][rows], cp, cp], 0), dtype=f)
        m["xs"] = np.ascontiguousarray(inp["x_sample"][rows, 0, :], dtype=f)
        m["sshift"] = np.ascontiguousarray(inp["state_shift"][0, rows], dtype=f)
        m["s5re0"] = np.ascontiguousarray(inp["state_s5_re"][0, rows].reshape(NSC, -1), dtype=f)
        m["s5im0"] = np.ascontiguousarray(inp["state_s5_im"][0, rows].reshape(NSC, -1), dtype=f)
        m["wkv0"] = np.ascontiguousarray(inp["state_wkv"][0, rows].reshape(NSC * H, 4096), dtype=f)
        maps.append(m)
    return maps


def assemble(cfg, res, n_cores, n_seq):
    C = cfg; f = np.float32
    G = C.G; H = C.H; NSC = C.NSC
    R = lambda c, k: np.asarray(res[c][k], dtype=f)
    y_p = np.stack([R(b, "y") for b in range(n_seq)], 0)
    y_s = np.concatenate([R(c, "ys") for c in range(n_cores)], 0)[:, None, :]
    re_p = np.stack([R(b, "ps5re").reshape(G, 64) for b in range(n_seq)], 0)[None]
    im_p = np.stack([R(b, "ps5im").reshape(G, 64) for b in range(n_seq)], 0)[None]
    wkv_p = np.stack([R(b, "pwkv") for b in range(n_seq)], 0)[None]
    sh_p = np.stack([R(b, "pshift").reshape(-1) for b in range(n_seq)], 0)[None]
    re_s = np.concatenate([R(c, "ss5re").reshape(NSC, G, 64) for c in range(n_cores)], 0)[None]
    im_s = np.concatenate([R(c, "ss5im").reshape(NSC, G, 64) for c in range(n_cores)], 0)[None]
    wkv_s = np.concatenate([R(c, "swkv").reshape(NSC, H, 64, 64) for c in range(n_cores)], 0)[None]
    sh_s = np.concatenate([R(c, "sshift_o") for c in range(n_cores)], 0)[None]
    return (y_p, y_s, re_p, im_p, wkv_p, sh_p, re_s, im_s, wkv_s, sh_s)


def kernel(**inputs):
    cfg = Cfg()
    inp = {k: np.asarray(v) for k, v in inputs.items()}
    nc, fw, es = build(cfg)
    maps = make_in_maps(cfg, inp, 8, 4)
    res = run_bass_kernel_spmd(nc, maps, core_ids=list(range(8)))
    return assemble(cfg, res.results, 8, 4)
```

```python
import math
import os
from contextlib import ExitStack
import numpy as np
import concourse.bass as bass
import concourse.mybir as mybir
from concourse.bass_utils import run_bass_kernel_spmd

F32 = mybir.dt.float32
F32R = mybir.dt.float32r
AF = mybir.ActivationFunctionType
ALU = mybir.AluOpType
AX = mybir.AxisListType


class Cfg:
    def __init__(self, D=2048, TP=512, NSC=16, NPASS=4):
        self.NPASS = NPASS
        self.D = D; self.TP = TP; self.NSC = NSC
        self.KC = D // 128
        self.DS = D // 2; self.DR = D // 2
        self.G = self.DS // 16; self.NSB = self.G // 2; self.NUB = self.DS // 128
        self.H = self.DR // 64; self.NHP = self.DR // 128
        self.NCH = TP // 64
        self.OFF_U = 0; self.OFF_Z = self.DS; self.OFF_RW = 2 * self.DS
        self.NSH = 3 * self.DR + 128
        self.OFF_RWZ = self.OFF_RW + self.NSH
        self.OFF_G1 = self.OFF_RWZ + self.DR
        self.OFF_G2 = self.OFF_G1 + D
        self.NIN = self.OFF_G2 + D
        self.NRB = self.NSH // 128
        self.GB = min(2, self.NUB)
        self.TT = [(i, min(512, TP - i)) for i in range(0, TP, 512)]
        self.TW = TP + 2 * NSC


class TL:
    def __init__(self, sem, name):
        self.sem = sem; self.count = 0; self.name = name


class Buf:
    __slots__ = ("w", "r", "excl")

    def __init__(self):
        self.w = None; self.r = {}; self.excl = False


class View:
    __slots__ = ("buf", "ap")

    def __init__(self, buf, ap):
        self.buf = buf; self.ap = ap

    def __getitem__(self, idx):
        return View(self.buf, self.ap[idx])

    def re(self, s, **kw):
        return View(self.buf, self.ap.rearrange(s, **kw))

    def bc(self, shape):
        return View(self.buf, self.ap.to_broadcast(shape))

    def r32(self):
        return View(self.buf, self.ap.bitcast(F32R))


class Tile:
    def __init__(self, ap, buf=None):
        self.ap = ap; self.buf = buf or Buf()

    def __getitem__(self, idx):
        return View(self.buf, self.ap[idx])

    def sub(self, idx):
        return Tile(self.ap[idx])


class Eng:
    def __init__(self, fw, name, h, tl, self_sync):
        self.fw = fw; self.name = name; self.h = h; self.tl = tl
        self.self_sync = self_sync; self.seen = {}

    def wait_for(self, deps):
        for tl, cnt in deps.items():
            if tl is self.tl and not self.self_sync:
                continue
            if self.seen.get(tl, 0) >= cnt:
                continue
            self.h.wait_ge(tl.sem, cnt)
            self.seen[tl] = cnt

    def __getattr__(self, op):
        def f(inc=True, **kw):
            return self.fw._issue(self, op, kw, inc)
        return f


def _deps(reads, writes):
    deps = {}

    def add(tc):
        if tc is None:
            return
        tl, c = tc
        if deps.get(tl, 0) < c:
            deps[tl] = c
    for v in reads:
        add(v.buf.w)
        if v.buf.excl:
            for tl, c in v.buf.r.items():
                add((tl, c))
    for v in writes:
        add(v.buf.w)
        for tl, c in v.buf.r.items():
            add((tl, c))
    return deps


class FW:
    def __init__(self, nc, es):
        self.nc = nc; self.es = es; self.nsem = 0; self.es_stack = [es]; self.uid = 0
        self.pe = Eng(self, "pe", nc.tensor, self.tl("pe"), False)
        self.act = Eng(self, "act", nc.scalar, self.tl("act"), True)
        self.dve = Eng(self, "dve", nc.vector, self.tl("dve"), True)
        self.pool = Eng(self, "pool", nc.gpsimd, self.tl("pool"), True)
        self.sp = Eng(self, "sp", nc.sync, self.tl("sp"), False)
        self.dma_tls = []

    def tl(self, name):
        self.nsem += 1
        return TL(self.es.enter_context(self.nc.semaphore(name)), name)

    def sb(self, name, shape, dtype=F32):
        self.uid += 1
        return Tile(self.es_stack[-1].enter_context(self.nc.sbuf_tensor("%s_%d" % (name, self.uid), list(shape), dtype))[:])

    def barrier(self):
        engs = (self.pe, self.act, self.dve, self.pool)
        tls = [e.tl for e in engs] + self.dma_tls
        for e in engs + (self.sp,):
            e.wait_for({tl: tl.count for tl in tls if tl.count and tl is not e.tl})

    def scope(self):
        fw = self

        class _S:
            def __enter__(self_):
                self_.les = ExitStack(); fw.es_stack.append(self_.les); return self_

            def __exit__(self_, *a):
                if fw.es_stack[-1] is not self_.les:
                    return False
                fw.barrier(); fw.es_stack.pop(); self_.les.close(); return False
        return _S()

    def ps(self, name, shape=(128, 512)):
        t = Tile(self.es.enter_context(self.nc.psum_tensor(name, list(shape), F32))[:])
        t.buf.excl = True
        return t

    def dram(self, name, shape, kind):
        return Tile(self.nc.dram_tensor(name, list(shape), F32, kind=kind).ap())

    def _issue(self, eng, op, kw, inc):
        reads = [v for k, v in kw.items() if isinstance(v, View) and k not in ("out", "accum_out", "ap")]
        writes = [v for k, v in kw.items() if isinstance(v, View) and k in ("out", "accum_out", "ap")]
        eng.wait_for(_deps(reads, writes))
        args = {k: (v.ap if isinstance(v, View) else v) for k, v in kw.items()}
        inst = getattr(eng.h, op)(**args)
        if inc:
            eng.tl.count += 1
            inst.then_inc(eng.tl.sem, 1)
            stamp = eng.tl.count
        else:
            stamp = eng.tl.count + 1
        for v in reads:
            if v.buf.r.get(eng.tl, 0) < stamp:
                v.buf.r[eng.tl] = stamp
        for v in writes:
            v.buf.w = (eng.tl, stamp); v.buf.r = {}
        return inst

    def dma(self, q, tl, out, in_, **kw):
        eng = {"sp": self.sp, "pool": self.pool, "act": self.act}[q]
        deps = _deps([in_], [out])
        if tl.count:
            deps[tl] = max(deps.get(tl, 0), tl.count)
        eng.wait_for(deps)
        inst = eng.h.dma_start(out=out.ap, in_=in_.ap, **kw)
        tl.count += 16
        inst.then_inc(tl.sem, 16)
        in_.buf.r[tl] = tl.count
        out.buf.w = (tl, tl.count); out.buf.r = {}
        if tl not in self.dma_tls:
            self.dma_tls.append(tl)

    def finish(self):
        for tl in self.dma_tls:
            self.sp.h.wait_ge(tl.sem, tl.count)
        for e in (self.pe, self.act, self.dve, self.pool):
            if e.tl.count:
                self.sp.h.wait_ge(e.tl.sem, e.tl.count)


def build(cfg, dbg=False):
    nc = bass.Bass("TRN2", target_bir_lowering=False)
    es = ExitStack()
    fw = FW(nc, es)
    pe, act, dve, pool = fw.pe, fw.act, fw.dve, fw.pool
    C = cfg
    D, KC, TP, NSC, TW = C.D, C.KC, C.TP, C.NSC, C.TW
    NS2 = 2 * NSC
    NSB, NUB, NHP, NCH, NRB, GB = C.NSB, C.NUB, C.NHP, C.NCH, C.NRB, C.GB
    P = 128

    def din(name, shape):
        return fw.dram(name, shape, "ExternalInput")

    def dout(name, shape):
        return fw.dram(name, shape, "ExternalOutput")

    xb = din("xb", [TP * C.NPASS, D])
    c17 = din("c17", [NSC + 2, D]); xs = din("xs", [NSC, D]); sshift = din("sshift", [NSC, D])
    s5re0 = din("s5re0", [NSC, NSB * 128]); s5im0 = din("s5im0", [NSC, NSB * 128])
    wkv0 = din("wkv0", [NSC * C.H, 4096])
    w_ada = din("w_ada", [3 * KC, P, KC, 128])
    w_in = din("w_in", [C.NIN // 128, P, KC, 128])
    w_glu = din("w_glu", [NUB, P, NUB, 128])
    w_out = din("w_out", [KC, P, KC, 128])
    b_ada = din("b_ada", [P, 3 * KC]); norm_g = din("norm_g", [P, KC]); mu = din("mu", [P, NRB])
    A_re = din("A_re", [P, NSB]); A_im = din("A_im", [P, NSB]); lstep = din("lstep", [P, NSB])
    B_re = din("B_re", [P, NSB, 16]); B_im = din("B_im", [P, NSB, 16])
    C_re = din("C_re", [P, NSB, 16]); C_im = din("C_im", [P, NSB, 16])
    Dsk = din("Dsk", [P, NUB]); b_glu = din("b_glu", [P, NUB])
    w0 = din("w0", [P, NHP]); a0 = din("a0", [P, NHP]); k_k = din("k_k", [P, NHP]); k_a = din("k_a", [P, NHP])
    r_k = din("r_k", [P, NHP]); gn_w = din("gn_w", [P, NHP]); gn_b = din("gn_b", [P, NHP])
    w2 = din("w2", [64, C.DR]); a2 = din("a2", [64, C.DR])
    fing = din("fing", [P, D])
    gnw_bh = din("gnw_bh", [P, 64]); gnb_bh = din("gnb_bh", [P, 64]); rk_bh = din("rk_bh", [P, 64])
    ident_d = din("ident", [P, P]); maskgt_d = din("maskgt", [P, P]); onesbd_d = din("onesbd", [P, P])
    reset_d = din("resetm", [P, TP]); iota_d = din("iota", [P, 128]); rowmask_d = din("rowmask", [P, 4])

    y_o = dout("y", [TP * C.NPASS, D]); ys_o = dout("ys", [NSC, D])
    ps5re_o = dout("ps5re", [NSB, P]); ps5im_o = dout("ps5im", [NSB, P])
    pwkv_o = dout("pwkv", [C.H, 64, 64]); pshift_o = dout("pshift", [KC, P])
    ss5re_o = dout("ss5re", [NSC, NSB * 128]); ss5im_o = dout("ss5im", [NSC, NSB * 128])
    swkv_o = dout("swkv", [NSC * C.H, 4096]); sshift_o = dout("sshift_o", [NSC, D])
    scr_a = fw.dram("scr_a", [NSC, C.H, 6, 64], "Internal")
    scr_b = fw.dram("scr_b", [NSC, C.DR], "Internal")
    dbg_o = {}

    tl_misc = [fw.tl("m%d" % i) for i in range(6)]
    mi = [0]

    def ld(out, in_, q="sp"):
        tl = tl_misc[mi[0] % len(tl_misc)]; mi[0] += 1
        fw.dma(q, tl, out, in_)

    tl_out = [fw.tl("o%d" % i) for i in range(4)]
    oi = [0]

    def st(out, in_, **kw):
        tl = tl_out[oi[0] % len(tl_out)]; oi[0] += 1
        fw.dma("sp", tl, out, in_, **kw)

    def const(name, d, shape):
        t = fw.sb(name, shape); ld(t[:], d[:]); return t
    ident = const("ident_s", ident_d, [P, P]); maskgt = const("maskgt_s", maskgt_d, [P, P])
    onesbd = const("onesbd_s", onesbd_d, [P, P]); resetm = const("reset_s", reset_d, [P, TP])
    iota = const("iota_s", iota_d, [P, 128]); rowmask = const("rowmask_s", rowmask_d, [P, 4])
    b_ada_s = const("b_ada_s", b_ada, [P, 3 * KC]); norm_g_s = const("norm_g_s", norm_g, [P, KC])
    mu_s = const("mu_s", mu, [P, NRB])
    Dsk_s = const("Dsk_s", Dsk, [P, NUB]); b_glu_s = const("b_glu_s", b_glu, [P, NUB])
    w0_s = const("w0_s", w0, [P, NHP]); a0_s = const("a0_s", a0, [P, NHP]); kk_s = const("kk_s", k_k, [P, NHP])
    ka_s = const("ka_s", k_a, [P, NHP]); rk_s = const("rk_s", r_k, [P, NHP])
    gnw_s = const("gnw_s", gn_w, [P, NHP]); gnb_s = const("gnb_s", gn_b, [P, NHP])
    w2_s = fw.sb("w2a2_s", [P, C.DR]); ld(w2_s[0:64, :], w2[:])
    a2_s = w2_s; ld(a2_s[64:128, :], a2[:])
    omu_s = fw.sb("omu_s", [P, NRB])
    dve.tensor_scalar(out=omu_s[:], in0=mu_s[:], scalar1=-1.0, scalar2=1.0, op0=ALU.mult, op1=ALU.add)

    PS = [fw.ps("psb%d" % i) for i in range(8)]

    NRING = 3
    ring = [fw.sb("wring%d" % i, [P, KC, 128], F32R) for i in range(NRING)]
    ring_tl = [fw.tl("wr%d" % i) for i in range(NRING)]
    ri = [0]

    def load_w(wd, blk, nk=KC):
        i = ri[0] % NRING; ri[0] += 1
        fw.dma("pool", ring_tl[i], ring[i][:, 0:nk, :], View(wd.buf, wd.ap[blk, :, 0:nk, :]))
        return ring[i]

    def mm_acc(psv, wslot, nk, rhs_fn, ncols=128):
        for kc in range(nk):
            pe.matmul(out=psv, lhsT=wslot[:, kc, 0:ncols].r32(), rhs=rhs_fn(kc),
                      start=(kc == 0), stop=(kc == nk - 1), inc=(kc == nk - 1))

    s5 = {}
    tA = const("A_re_s", A_re, [P, NSB]); tAi = const("A_im_s", A_im, [P, NSB]); tls = const("lstep_s", lstep, [P, NSB])
    step = fw.sb("s5step", [P, NSB]); act.activation(out=step[:], in_=tls[:], func=AF.Exp)
    lam = fw.sb("s5lam", [P, NSB]); dve.tensor_scalar_min(out=lam[:], in0=tA[:], scalar1=-1e-4)
    mag = fw.sb("s5mag", [P, NSB]); tmpc = fw.sb("s5tmp", [P, NSB]); theta = fw.sb("s5theta", [P, NSB])
    dve.tensor_tensor(out=tmpc[:], in0=lam[:], in1=step[:], op=ALU.mult)
    act.activation(out=mag[:], in_=tmpc[:], func=AF.Exp)
    dve.tensor_tensor(out=theta[:], in0=tAi[:], in1=step[:], op=ALU.mult)
    TWO_PI = 2.0 * math.pi

    I32 = mybir.dt.int32

    def sin_into(dst, x, shape, phase):
        with fw.scope():
            ki = fw.sb("sr_ki", shape, I32); kf = fw.sb("sr_kf", shape)
            dve.tensor_scalar_add(out=dst, in0=x, scalar1=float(phase))
            dve.tensor_scalar_mul(out=ki[:], in0=dst, scalar1=1.0 / TWO_PI)
            dve.tensor_copy(out=kf[:], in_=ki[:])
            dve.scalar_tensor_tensor(out=dst, in0=kf[:], scalar=-TWO_PI, in1=dst, op0=ALU.mult, op1=ALU.add)
            dve.tensor_scalar(out=kf[:], in0=dst, scalar1=math.pi, scalar2=-TWO_PI, op0=ALU.is_gt, op1=ALU.mult)
            dve.tensor_tensor(out=dst, in0=dst, in1=kf[:], op=ALU.add)
            dve.tensor_scalar(out=kf[:], in0=dst, scalar1=-math.pi, scalar2=TWO_PI, op0=ALU.is_lt, op1=ALU.mult)
            dve.tensor_tensor(out=dst, in0=dst, in1=kf[:], op=ALU.add)
            dve.tensor_scalar(out=dst, in0=dst, scalar1=math.pi, scalar2=-math.pi, op0=ALU.min, op1=ALU.max)
            act.activation(out=dst, in_=dst, func=AF.Sin)

    def sincos(name, ang_view, shape):
        sn = fw.sb(name + "_sin", shape); cs = fw.sb(name + "_cos", shape)
        sin_into(sn[:], ang_view, shape, 0.0)
        sin_into(cs[:], ang_view, shape, 0.5 * math.pi)
        return sn, cs
    thp = theta
    sn1, cs1 = sincos("s5t1", thp[:], [P, NSB])
    abr = fw.sb("s5abr", [P, NSB]); abi = fw.sb("s5abi", [P, NSB])
    dve.tensor_tensor(out=abr[:], in0=mag[:], in1=cs1[:], op=ALU.mult)
    dve.tensor_tensor(out=abi[:], in0=mag[:], in1=sn1[:], op=ALU.mult)
    den = fw.sb("s5den", [P, NSB]); t2 = fw.sb("s5t2", [P, NSB]); fre = fw.sb("s5fre", [P, NSB]); fim = fw.sb("s5fim", [P, NSB])
    abm1 = fw.sb("s5abm1", [P, NSB])
    dve.tensor_tensor(out=den[:], in0=lam[:], in1=lam[:], op=ALU.mult)
    dve.tensor_tensor(out=t2[:], in0=tAi[:], in1=tAi[:], op=ALU.mult)
    dve.tensor_tensor(out=den[:], in0=den[:], in1=t2[:], op=ALU.add)
    dve.reciprocal(out=den[:], in_=den[:])
    dve.tensor_scalar_add(out=abm1[:], in0=abr[:], scalar1=-1.0)
    dve.tensor_tensor(out=fre[:], in0=abm1[:], in1=lam[:], op=ALU.mult)
    dve.tensor_tensor(out=t2[:], in0=abi[:], in1=tAi[:], op=ALU.mult)
    dve.tensor_tensor(out=fre[:], in0=fre[:], in1=t2[:], op=ALU.add)
    dve.tensor_tensor(out=fre[:], in0=fre[:], in1=den[:], op=ALU.mult)
    dve.tensor_tensor(out=fim[:], in0=abi[:], in1=lam[:], op=ALU.mult)
    dve.tensor_tensor(out=t2[:], in0=abm1[:], in1=tAi[:], op=ALU.mult)
    dve.tensor_tensor(out=fim[:], in0=fim[:], in1=t2[:], op=ALU.subtract)
    dve.tensor_tensor(out=fim[:], in0=fim[:], in1=den[:], op=ALU.mult)
    LB = [fw.sb("s5LB" + nm, [P, NUB, 128]) for nm in ("re", "im")]
    LC = [fw.sb("s5ZC" + nm, [P, NSB, 2, 16]) for nm in ("re", "im")]
    with fw.scope():
        Br = const("B_re_s", B_re, [P, NSB, 16]); Bi = const("B_im_s", B_im, [P, NSB, 16])
        bbr = fw.sb("s5bbr", [P, NSB, 16]); bbi = fw.sb("s5bbi", [P, NSB, 16]); bt_ = fw.sb("s5bt", [P, NSB, 16])
        fre_b = fre[:, :, None].bc([P, NSB, 16]); fim_b = fim[:, :, None].bc([P, NSB, 16])
        dve.tensor_tensor(out=bbr[:], in0=Br[:], in1=fre_b, op=ALU.mult)
        dve.tensor_tensor(out=bt_[:], in0=Bi[:], in1=fim_b, op=ALU.mult)
        dve.tensor_tensor(out=bbr[:], in0=bbr[:], in1=bt_[:], op=ALU.subtract)
        dve.tensor_tensor(out=bbi[:], in0=Bi[:], in1=fre_b, op=ALU.mult)
        dve.tensor_tensor(out=bt_[:], in0=Br[:], in1=fim_b, op=ALU.mult)
        dve.tensor_tensor(out=bbi[:], in0=bbi[:], in1=bt_[:], op=ALU.add)
        for li, (nm, src) in enumerate((("re", bbr), ("im", bbi))):
            Z = fw.sb("s5ZB" + nm, [P, NSB, 2, 16])
            dve.memset(ap=Z[:], constant=0.0)
            dve.tensor_copy(out=Z[0:64, :, 0, :], in_=src[0:64, :, :])
            dve.tensor_copy(out=Z[64:128, :, 1, :], in_=src[64:128, :, :])
            L = LB[li]
            for q4 in range(NUB):
                pe.transpose(out=PS[0][:, 0:128], in_=Z[:, 4 * q4:4 * q4 + 4, :, :].re("p a b c -> p (a b c)"), identity=ident[:])
                act.copy(out=L[:, q4, :], in_=PS[0][:, 0:128])
        Cr = const("C_re_s", C_re, [P, NSB, 16]); Ci = const("C_im_s", C_im, [P, NSB, 16])
        for li, (nm, src, sgn) in enumerate((("re", Cr, 1.0), ("im", Ci, -1.0))):
            Z = LC[li]
            dve.memset(ap=Z[:], constant=0.0)
            dve.tensor_scalar_mul(out=Z[0:64, :, 0, :], in0=src[0:64, :, :], scalar1=sgn)
            dve.tensor_scalar_mul(out=Z[64:128, :, 1, :], in0=src[64:128, :, :], scalar1=sgn)
    def make_tables():
      sinT = fw.sb("s5sinT", [P, NSB, 128]); cosT = fw.sb("s5cosT", [P, NSB, 128])
      CH = min(8, NSB)
      with fw.scope():
        ang = fw.sb("s5ang", [P, CH, 128])
        for c0 in range(0, NSB, CH):
            dve.tensor_tensor(out=ang[:], in0=iota[:, None, :].bc([P, CH, 128]), in1=thp[:, c0:c0 + CH, None].bc([P, CH, 128]), op=ALU.mult)
            sin_into(sinT[:, c0:c0 + CH, :], ang[:], [P, CH, 128], 0.0)
            sin_into(cosT[:, c0:c0 + CH, :], ang[:], [P, CH, 128], 0.5 * math.pi)
      return sinT, cosT
    angL = fw.sb("s5angL", [P, NSB]); dve.tensor_scalar_mul(out=angL[:], in0=thp[:], scalar1=128.0)
    snL, csL = sincos("s5tL", angL[:], [P, NSB])
    s5car_r = fw.sb("s5car_r", [P, NSB]); s5car_i = fw.sb("s5car_i", [P, NSB])
    dve.memset(ap=s5car_r[:], constant=0.0); dve.memset(ap=s5car_i[:], constant=0.0)

    NM = NSC + 2
    scT = fw.sb("scT", [P, KC, NM], F32R)
    with fw.scope():
        c_s = fw.sb("c_s", [NM, D]); ld(c_s[:], c17[:])
        csl = fw.sb("csl", [NM, D]); act.activation(out=csl[:], in_=c_s[:], func=AF.Silu)
        for kc in range(KC):
            pe.transpose(out=PS[1][:, kc * NM:(kc + 1) * NM], in_=csl[:, kc * 128:(kc + 1) * 128], identity=ident[0:NM, 0:NM])
        act.copy(out=scT[:].re("p k n -> p (k n)"), in_=PS[1][:, 0:KC * NM])
    modT = fw.sb("modT", [P, 3 * KC, NM])
    for fb in range(3 * KC):
        w = load_w(w_ada, fb)
        psb = PS[2 + fb % 2]
        mm_acc(psb[:, 0:NM], w, KC, lambda kc: scT[:, kc, :])
        act.activation(out=modT[:, fb, :], in_=psb[:, 0:NM], func=AF.Identity, bias=b_ada_s[:, fb:fb + 1], scale=1.0)
    sceff = fw.sb("sceff", [P, KC, NM])
    dve.tensor_scalar_add(out=sceff[:], in0=modT[:, KC:2 * KC, :], scalar1=1.0)
    dve.tensor_tensor(out=sceff[:], in0=sceff[:], in1=norm_g_s[:, :, None].bc([P, KC, NM]), op=ALU.mult)
    sc_p = fw.sb("sc_p", [P, KC]); sh_p = fw.sb("sh_p", [P, KC]); gt_p = fw.sb("gt_p", [P, KC])
    dve.tensor_copy(out=sc_p[:], in_=sceff[:, :, NSC]); dve.tensor_copy(out=sh_p[:], in_=modT[:, 0:KC, NSC])
    dve.tensor_copy(out=gt_p[:], in_=modT[:, 2 * KC:3 * KC, NSC])

    hT = fw.sb("hT", [P, KC, TW], F32R)
    rw_car = fw.sb("rw_car", [P, NRB]); dve.memset(ap=rw_car[:], constant=0.0)
    Hst = [fw.sb("Hst%d" % h, [P, 64]) for h in range(C.H)]
    for h in range(C.H):
        dve.memset(ap=Hst[h][:], constant=0.0)

    eps_rms = fw.sb("eps_rms", [P, 1]); dve.memset(ap=eps_rms[:], constant=1e-6)
    eps_gn = fw.sb("eps_gn", [P, 1]); dve.memset(ap=eps_gn[:], constant=64e-5)

    def rstd_of(ss, n):
        act.activation(out=ss, in_=ss, func=AF.Sqrt, bias=eps_rms[0:n, :], scale=1.0 / D)
        dve.reciprocal(out=ss, in_=ss)

    xt_tl = [fw.tl("xt%d" % i) for i in range(2)]
    ssq = fw.sb("ssq", [P, 2])
    hlast = fw.sb("hlast", [P, KC])

    def phase_x(xd, is_b):
      with fw.scope():
        xt = [fw.sb("xt%d" % i, [P, D]) for i in range(2)]
        xsq = fw.sb("xsq", [P, D], mybir.dt.bfloat16)
        for tt in range(TP // 128):
            i = tt % 2
            fw.dma("sp", xt_tl[i], xt[i][:], xd[tt * 128:(tt + 1) * 128, :])
            s = ssq[:, i:i + 1]
            act.activation(out=xsq[:], in_=xt[i][:], func=AF.Square, accum_out=s)
            rstd_of(s, 128)
            act.activation(out=xt[i][:], in_=xt[i][:], func=AF.Copy, scale=s)
            for kc in range(KC):
                psb = PS[(kc // 4) % 2]
                pe.transpose(out=psb[:, (kc % 4) * 128:(kc % 4 + 1) * 128], in_=xt[i][:, kc * 128:(kc + 1) * 128], identity=ident[:])
                act.activation(out=hT[:, kc, tt * 128:(tt + 1) * 128], in_=psb[:, (kc % 4) * 128:(kc % 4 + 1) * 128],
                               func=AF.Identity, scale=sc_p[:, kc:kc + 1], bias=sh_p[:, kc:kc + 1])
                if is_b and tt == TP // 128 - 1:
                    act.activation(out=hlast[:, kc:kc + 1], in_=psb[:, (kc % 4) * 128 + 127:(kc % 4) * 128 + 128],
                                   func=AF.Identity, scale=sc_p[:, kc:kc + 1], bias=sh_p[:, kc:kc + 1])

    NPASS = C.NPASS
    W1 = TP + NSC
    UID = [0]

    def tiles(is_last):
        t = list(C.TT)
        if is_last:
            t.append((TP, NS2))
        return t

    barrier = fw.barrier
    STOP = float(os.environ.get("K_STOP", "99"))

    class StopBuild(Exception):
        pass

    def stop(n):
        return STOP <= n

    def chk(n):
        if STOP <= n:
            while len(fw.es_stack) > 1:
                fw.es_stack.pop().close()
            raise StopBuild()

    osT = fw.sb("osT", [P, NUB, TW], F32R)
    orT = fw.sb("orT", [P, NHP, TW], F32R)
    x0r = fw.sb("x0r", [P, NSB, NSC]); x0i = fw.sb("x0i", [P, NSB, NSC])
    x1r = fw.sb("x1r", [P, NSB, NSC]); x1i = fw.sb("x1i", [P, NSB, NSC])

    def gelu_inplace(v, tmp_v):
        dve.tensor_tensor(out=tmp_v, in0=v, in1=v, op=ALU.mult)
        dve.tensor_scalar(out=tmp_v, in0=tmp_v, scalar1=0.044715, scalar2=1.0, op0=ALU.mult, op1=ALU.add)
        dve.tensor_tensor(out=tmp_v, in0=tmp_v, in1=v, op=ALU.mult)
        act.activation(out=tmp_v, in_=tmp_v, func=AF.Sigmoid, scale=2.0 * math.sqrt(2.0 / math.pi))
        dve.tensor_tensor(out=v, in0=v, in1=tmp_v, op=ALU.mult)

    def s5_phase(is_last):
        Wd = TW if is_last else TP
        with fw.scope():
            lsb = fw.sb
            sinT, cosT = make_tables()
            chk(2.1)
            uT = lsb("uT", [P, TW]); yT = lsb("yT", [P, NUB, TW], F32R)
            scanscope = fw.scope(); scanscope.__enter__()
            zr = lsb("s5zr", [P, 4, 128]); zi = lsb("s5zi", [P, 4, 128])
            q1 = lsb("s5q1", [P, 4, 128])
            sr = lsb("s5sr", [P, 4, 128]); si = lsb("s5si", [P, 4, 128])
            xr = zr; xi = zi
            cq = [lsb("s5cq%d" % i, [P, 4]) for i in range(4)]
            LBz = [lsb("s5LBz%d" % i, [P, 4, 128]) for i in range(2)]
            sq = [lsb("s5sq%d" % i, [P, 4, NSC]) for i in range(2)]
            for blk in range(NUB):
                chk(2.12)
                w = load_w(w_in, C.OFF_U // 128 + blk)
                chk(2.15)
                for (t0, tn) in tiles(is_last):
                    psb = PS[2 + (t0 // 512) % 2]
                    mm_acc(psb[:, 0:tn], w, KC, lambda kc: hT[:, kc, t0:t0 + tn].r32())
                    chk(2.17)
                    act.copy(out=uT[:, t0:t0 + tn], in_=psb[:, 0:tn])
                chk(2.2)
                for li in range(2):
                    for j in range(4):
                        dve.tensor_scalar_mul(out=LBz[li][:, j, :], in0=LB[li][:, blk, :], scalar1=rowmask[:, j:j + 1])
                s0 = blk * 4; sl = slice(s0, s0 + 4)
                cT = cosT[:, sl, :]; sT = sinT[:, sl, :]
                for ts in range(TP // 128):
                    bur = PS[4 + (ts % 2) * 2]; bui = PS[5 + (ts % 2) * 2]
                    for j in range(4):
                        for (L, psb) in ((LBz[0], bur), (LBz[1], bui)):
                            pe.matmul(out=psb[:, j * 128:(j + 1) * 128], lhsT=L[:, j, :],
                                      rhs=uT[:, ts * 128:(ts + 1) * 128],
                                      start=True, stop=True, inc=(j == 3))
                    chk(2.4)
                    br = bur[:, :].re("p (a b) -> p a b", a=4); bi = bui[:, :].re("p (a b) -> p a b", a=4)
                    dve.tensor_tensor(out=zr[:], in0=br, in1=cT, op=ALU.mult)
                    dve.tensor_tensor(out=q1[:], in0=bi, in1=sT, op=ALU.mult)
                    dve.tensor_tensor(out=zr[:], in0=zr[:], in1=q1[:], op=ALU.add)
                    dve.tensor_tensor(out=zi[:], in0=bi, in1=cT, op=ALU.mult)
                    dve.tensor_tensor(out=q1[:], in0=br, in1=sT, op=ALU.mult)
                    dve.tensor_tensor(out=zi[:], in0=zi[:], in1=q1[:], op=ALU.subtract)
                    chk(2.5)
                    for j in range(4):
                        s = s0 + j
                        dve.tensor_tensor_scan(out=sr[:, j, :], data0=mag[:, s:s + 1].bc([P, 128]), data1=zr[:, j, :],
                                               initial=s5car_r[:, s:s + 1], op0=ALU.mult, op1=ALU.add)
                        dve.tensor_tensor_scan(out=si[:, j, :], data0=mag[:, s:s + 1].bc([P, 128]), data1=zi[:, j, :],
                                               initial=s5car_i[:, s:s + 1], op0=ALU.mult, op1=ALU.add)
                    chk(2.6)
                    la = sr[:, :, 127]; lb = si[:, :, 127]
                    dve.tensor_tensor(out=cq[0][:], in0=csL[:, sl], in1=la, op=ALU.mult)
                    dve.tensor_tensor(out=cq[1][:], in0=snL[:, sl], in1=lb, op=ALU.mult)
                    dve.tensor_tensor(out=cq[2][:], in0=snL[:, sl], in1=la, op=ALU.mult)
                    dve.tensor_tensor(out=cq[3][:], in0=csL[:, sl], in1=lb, op=ALU.mult)
                    dve.tensor_tensor(out=s5car_r[:, sl], in0=cq[0][:], in1=cq[1][:], op=ALU.subtract)
                    dve.tensor_tensor(out=s5car_i[:, sl], in0=cq[2][:], in1=cq[3][:], op=ALU.add)
                    dve.tensor_tensor(out=xr[:], in0=sr[:], in1=cT, op=ALU.mult)
                    dve.tensor_tensor(out=q1[:], in0=si[:], in1=sT, op=ALU.mult)
                    dve.tensor_tensor(out=xr[:], in0=xr[:], in1=q1[:], op=ALU.subtract)
                    dve.tensor_tensor(out=xi[:], in0=sr[:], in1=sT, op=ALU.mult)
                    dve.tensor_tensor(out=q1[:], in0=si[:], in1=cT, op=ALU.mult)
                    dve.tensor_tensor(out=xi[:], in0=xi[:], in1=q1[:], op=ALU.add)
                    chk(2.7)
                    psy = PS[0]
                    for j in range(4):
                        s = s0 + j
                        pe.matmul(out=psy[32 * j:32 * j + 32, 0:128], lhsT=LC[0][:, s, :, :].re("p a b -> p (a b)"),
                                  rhs=xr[:, j, :], start=True, stop=False, tile_position=(0, 32 * j), inc=False)
                        pe.matmul(out=psy[32 * j:32 * j + 32, 0:128], lhsT=LC[1][:, s, :, :].re("p a b -> p (a b)"),
                                  rhs=xi[:, j, :], start=False, stop=True, tile_position=(0, 32 * j), inc=(j == 3))
                    dve.scalar_tensor_tensor(out=yT[:, blk, ts * 128:(ts + 1) * 128], in0=uT[:, ts * 128:(ts + 1) * 128],
                                             scalar=Dsk_s[:, blk:blk + 1], in1=psy[:, 0:128], op0=ALU.mult, op1=ALU.add)
                if is_last:
                    psr = PS[6]; psi = PS[7]
                    for j in range(4):
                        for (L, psb) in ((LBz[0], psr), (LBz[1], psi)):
                            pe.matmul(out=psb[:, j * NSC:(j + 1) * NSC], lhsT=L[:, j, :],
                                      rhs=uT[:, TP:TP + NSC],
                                      start=True, stop=True, inc=(j == 3))
                    abr_b = abr[:, sl, None].bc([P, 4, NSC]); abi_b = abi[:, sl, None].bc([P, 4, NSC])
                    dve.tensor_tensor(out=sq[0][:], in0=x0r[:, sl, :], in1=abr_b, op=ALU.mult)
                    dve.tensor_tensor(out=sq[1][:], in0=x0i[:, sl, :], in1=abi_b, op=ALU.mult)
                    dve.tensor_tensor(out=sq[0][:], in0=sq[0][:], in1=sq[1][:], op=ALU.subtract)
                    dve.tensor_tensor(out=x1r[:, sl, :], in0=sq[0][:], in1=psr[:, 0:4 * NSC].re("p (a b) -> p a b", a=4), op=ALU.add)
                    dve.tensor_tensor(out=sq[0][:], in0=x0i[:, sl, :], in1=abr_b, op=ALU.mult)
                    dve.tensor_tensor(out=sq[1][:], in0=x0r[:, sl, :], in1=abi_b, op=ALU.mult)
                    dve.tensor_tensor(out=sq[0][:], in0=sq[0][:], in1=sq[1][:], op=ALU.add)
                    dve.tensor_tensor(out=x1i[:, sl, :], in0=sq[0][:], in1=psi[:, 0:4 * NSC].re("p (a b) -> p a b", a=4), op=ALU.add)
                    psy = PS[0]
                    for j in range(4):
                        s = s0 + j
                        pe.matmul(out=psy[32 * j:32 * j + 32, 0:NSC], lhsT=LC[0][:, s, :, :].re("p a b -> p (a b)"),
                                  rhs=x1r[:, s, :], start=True, stop=False, tile_position=(0, 32 * j), inc=False)
                        pe.matmul(out=psy[32 * j:32 * j + 32, 0:NSC], lhsT=LC[1][:, s, :, :].re("p a b -> p (a b)"),
                                  rhs=x1i[:, s, :], start=False, stop=True, tile_position=(0, 32 * j), inc=(j == 3))
                    dve.scalar_tensor_tensor(out=yT[:, blk, TP:TP + NSC], in0=uT[:, TP:TP + NSC],
                                             scalar=Dsk_s[:, blk:blk + 1], in1=psy[:, 0:NSC], op0=ALU.mult, op1=ALU.add)
            chk(2.8)
            scanscope.__exit__(None, None, None)
            gtmp = lsb("gtmp", [P, TW]); gs = lsb("gs", [P, TW]); zs = lsb("zs", [P, TW])
            Wy = W1 if is_last else TP
            for blk in range(NUB):
                gelu_inplace(yT[:, blk, 0:Wy], gtmp[:, 0:Wy])
            gt_tiles = list(C.TT) + ([(TP, NSC)] if is_last else [])
            for jb in range(NUB):
                w = load_w(w_glu, jb, nk=NUB)
                w2_ = load_w(w_in, C.OFF_Z // 128 + jb)
                for (t0, tn) in gt_tiles:
                    psb = PS[2]; psz = PS[3]
                    mm_acc(psb[:, 0:tn], w, NUB, lambda kc: yT[:, kc, t0:t0 + tn])
                    act.activation(out=gs[:, t0:t0 + tn], in_=psb[:, 0:tn], func=AF.Sigmoid, bias=b_glu_s[:, jb:jb + 1], scale=1.0)
                    mm_acc(psz[:, 0:tn], w2_, KC, lambda kc: hT[:, kc, t0:t0 + tn].r32())
                    act.activation(out=zs[:, t0:t0 + tn], in_=psz[:, 0:tn], func=AF.Silu)
                    dve.tensor_tensor(out=gs[:, t0:t0 + tn], in0=gs[:, t0:t0 + tn], in1=zs[:, t0:t0 + tn], op=ALU.mult)
                    dve.tensor_tensor(out=osT[:, jb, t0:t0 + tn], in0=gs[:, t0:t0 + tn], in1=yT[:, jb, t0:t0 + tn], op=ALU.mult)
            barrier()

    EM05 = math.exp(-0.5)

    def rwkv_phase(is_last):
        Wd = W1 if is_last else TP
        with fw.scope():
            lsb = fw.sb
            Pb = lsb("Pb", [P, 1 + TW]); tmpA = lsb("tmpA", [P, W1])
            lor = lsb("lor", [P, W1])
            rr = lsb("rr", [P, W1]); kr = lsb("kr", [P, W1]); vv = lsb("vv", [P, W1])
            ldc = lsb("ldc", [P, W1]); alr = lsb("alr", [P, W1]); kkn = lsb("kkn", [P, W1]); kmod = lsb("kmod", [P, W1])
            bb = lsb("bb", [P, W1]); cl = lsb("cl", [P, TP]); e1 = lsb("e1", [P, TP]); e2 = Tile(kr.ap[:, 0:TP], kr.buf)
            cend = lsb("cend", [P, NCH]); PC = lsb("PC", [P, NCH])
            AR = lsb("AR", [P, NCH, 2, 64]); BK = lsb("BK", [P, NCH, 2, 64]); AV = lsb("AV", [P, NCH, 2, 64]); BKH = lsb("BKH", [P, NCH, 2, 64])
            bonus = alr; oT = Pb; osq = kkn
            CB = 3; NSL = 2 * CB
            GTs = [lsb("GTs%d" % e, [P, P]) for e in range(2 * NSL)]
            X = [lsb("X%d" % e, [P, 64]) for e in range(2 * NSL)]
            BKHt = [lsb("BKHt%d" % e, [P, 64]) for e in range(2 * NSL)]
            PW = [lsb("PW%d" % e, [64, 3, 64]) for e in range(NSL)]
            Ys = [lsb("Ys%d" % e, [64, 64]) for e in range(2 * NSL)]
            WTs = [lsb("WTs%d" % e, [P, 64]) for e in range(2 * NSL)]

            def proj_shift(rb, dst):
                w = load_w(w_in, C.OFF_RW // 128 + rb)
                for (t0, tn) in tiles(is_last):
                    psb = PS[6 + (t0 // 512) % 2]
                    mm_acc(psb[:, 0:tn], w, KC, lambda kc: hT[:, kc, t0:t0 + tn].r32())
                    act.copy(out=Pb[:, 1 + t0:1 + t0 + tn], in_=psb[:, 0:tn])
                act.copy(out=Pb[:, 0:1], in_=rw_car[:, rb:rb + 1])
                dve.tensor_scalar_mul(out=tmpA[:, 0:TP], in0=Pb[:, 0:TP], scalar1=mu_s[:, rb:rb + 1])
                dve.scalar_tensor_tensor(out=dst[:, 0:TP], in0=Pb[:, 1:TP + 1], scalar=omu_s[:, rb:rb + 1], in1=tmpA[:, 0:TP],
                                         op0=ALU.mult, op1=ALU.add)
                dve.tensor_copy(out=rw_car[:, rb:rb + 1], in_=Pb[:, TP:TP + 1])
                if is_last:
                    dve.tensor_scalar_mul(out=tmpA[:, TP:W1], in0=Pb[:, 1 + TP + NSC:1 + TP + NS2], scalar1=mu_s[:, rb:rb + 1])
                    dve.scalar_tensor_tensor(out=dst[:, TP:W1], in0=Pb[:, 1 + TP:1 + TP + NSC], scalar=omu_s[:, rb:rb + 1],
                                             in1=tmpA[:, TP:W1], op0=ALU.mult, op1=ALU.add)

            ct_tiles = list(C.TT) + ([(TP, NSC)] if is_last else [])
            proj_shift(3 * NHP, lor)
            act.activation(out=lor[0:64, 0:Wd], in_=lor[0:64, 0:Wd], func=AF.Tanh)
            for hp in range(NHP):
                proj_shift(hp, rr); proj_shift(NHP + hp, kr); proj_shift(2 * NHP + hp, vv)
                for (t0, tn) in ct_tiles:
                    psb = PS[6]
                    pe.matmul(out=psb[:, 0:tn], lhsT=w2_s[0:64, hp * 128:(hp + 1) * 128], rhs=lor[0:64, t0:t0 + tn], start=True, stop=True)
                    act.activation(out=ldc[:, t0:t0 + tn], in_=psb[:, 0:tn], func=AF.Sigmoid, bias=w0_s[:, hp:hp + 1], scale=1.0)
                    psb = PS[7]
                    pe.matmul(out=psb[:, 0:tn], lhsT=a2_s[64:128, hp * 128:(hp + 1) * 128], rhs=lor[64:128, t0:t0 + tn], start=True, stop=True)
                    act.activation(out=alr[:, t0:t0 + tn], in_=psb[:, 0:tn], func=AF.Sigmoid, bias=a0_s[:, hp:hp + 1], scale=1.0)
                dve.tensor_scalar_mul(out=ldc[:, 0:Wd], in0=ldc[:, 0:Wd], scalar1=-EM05)
                dve.tensor_scalar_mul(out=kkn[:, 0:Wd], in0=kr[:, 0:Wd], scalar1=kk_s[:, hp:hp + 1])
                dve.tensor_tensor(out=tmpA[:, 0:Wd], in0=kkn[:, 0:Wd], in1=kkn[:, 0:Wd], op=ALU.mult)
                for (t0, tn) in ct_tiles:
                    psb = PS[6]
                    pe.matmul(out=psb[:, 0:tn], lhsT=onesbd[:], rhs=tmpA[:, t0:t0 + tn], start=True, stop=True)
                    act.activation(out=bb[:, t0:t0 + tn], in_=psb[:, 0:tn], func=AF.Sqrt)
                dve.tensor_scalar_max(out=bb[:, 0:Wd], in0=bb[:, 0:Wd], scalar1=1e-12)
                dve.reciprocal(out=bb[:, 0:Wd], in_=bb[:, 0:Wd])
                dve.tensor_tensor(out=kkn[:, 0:Wd], in0=kkn[:, 0:Wd], in1=bb[:, 0:Wd], op=ALU.mult)
                dve.tensor_scalar(out=kmod[:, 0:Wd], in0=alr[:, 0:Wd], scalar1=-1.0, scalar2=ka_s[:, hp:hp + 1], op0=ALU.add, op1=ALU.mult)
                dve.scalar_tensor_tensor(out=kmod[:, 0:Wd], in0=kmod[:, 0:Wd], scalar=1.0, in1=kr[:, 0:Wd], op0=ALU.add, op1=ALU.mult)
                dve.tensor_tensor(out=bb[:, 0:Wd], in0=kkn[:, 0:Wd], in1=alr[:, 0:Wd], op=ALU.mult)
                dve.scalar_tensor_tensor(out=tmpA[:, 0:Wd], in0=rr[:, 0:Wd], scalar=rk_s[:, hp:hp + 1], in1=kmod[:, 0:Wd], op0=ALU.mult, op1=ALU.mult)
                for (t0, tn) in ct_tiles:
                    psb = PS[6]
                    pe.matmul(out=psb[:, 0:tn], lhsT=onesbd[:], rhs=tmpA[:, t0:t0 + tn], start=True, stop=True)
                    dve.tensor_tensor(out=bonus[:, t0:t0 + tn], in0=psb[:, 0:tn], in1=vv[:, t0:t0 + tn], op=ALU.mult)
                dve.tensor_tensor_scan(out=cl[:], data0=resetm[:], data1=ldc[:, 0:TP], initial=0.0, op0=ALU.mult, op1=ALU.add)
                dve.tensor_copy(out=cend[:], in_=cl[:].re("p (c t) -> p c t", t=64)[:, :, 63])
                act.activation(out=PC[:], in_=cend[:], func=AF.Exp)
                c4 = lambda t: t[:, 0:TP].re("p (c t) -> p c t", t=64)
                act.activation(out=e1[:], in_=cl[:], func=AF.Exp)
                dve.tensor_tensor(out=c4(AR)[:, :, :] if False else AR[:, :, 1, :], in0=c4(rr), in1=c4(e1), op=ALU.mult)
                dve.tensor_tensor(out=e2[:], in0=cl[:], in1=ldc[:, 0:TP], op=ALU.subtract)
                act.activation(out=e2[:], in_=e2[:], func=AF.Exp)
                dve.scalar_tensor_tensor(out=AR[:, :, 0, :], in0=c4(kkn), scalar=-1.0, in1=c4(e2), op0=ALU.mult, op1=ALU.mult)
                dve.tensor_copy(out=AV[:, :, 0, :], in_=AR[:, :, 0, :])
                dve.tensor_copy(out=AV[:, :, 1, :], in_=c4(vv))
                act.activation(out=e1[:], in_=cl[:], func=AF.Exp, scale=-1.0)
                dve.tensor_tensor(out=BK[:, :, 0, :], in0=c4(bb), in1=c4(e1), op=ALU.mult)
                dve.tensor_tensor(out=BK[:, :, 1, :], in0=c4(kmod), in1=c4(e1), op=ALU.mult)
                dve.tensor_tensor(out=c4(e2), in0=cend[:, :, None].bc([P, NCH, 64]), in1=c4(cl), op=ALU.subtract)
                act.activation(out=e2[:], in_=e2[:], func=AF.Exp)
                dve.tensor_tensor(out=BKH[:, :, 0, :], in0=c4(bb), in1=c4(e2), op=ALU.mult)
                dve.tensor_tensor(out=BKH[:, :, 1, :], in0=c4(kmod), in1=c4(e2), op=ALU.mult)
                def fl(v):
                    return v.re("p a b -> p (a b)")

                def stage1(pairs, sb):
                    n = len(pairs)
                    for i, (c, e) in enumerate(pairs):
                        pb = 64 * e; B = PS[i]
                        pe.matmul(out=B[:, 0:128], lhsT=fl(BK[pb:pb + 64, c, :, :]), rhs=fl(AR[pb:pb + 64, c, :, :]),
                                  start=True, stop=True, tile_position=(pb, 0))
                        pe.transpose(out=B[:, 128:192], in_=fl(AV[pb:pb + 64, c, :, :]), identity=ident[pb:pb + 64, pb:pb + 64],
                                     tile_position=(pb, 0))
                        pe.transpose(out=B[:, 192:256], in_=fl(BKH[pb:pb + 64, c, :, :]), identity=ident[pb:pb + 64, pb:pb + 64],
                                     tile_position=(pb, 0))
                    for i, (c, e) in enumerate(pairs):
                        B = PS[i]; r = sb + i
                        dve.tensor_tensor(out=GTs[r][:], in0=B[:, 0:128], in1=maskgt[:], op=ALU.mult)
                        act.copy(out=X[r][:], in_=B[:, 128:192])
                        act.copy(out=BKHt[r][:], in_=B[:, 192:256])
                    yield
                    for i, (c, e) in enumerate(pairs):
                        pe.transpose(out=PS[i][0:64, 0:64], in_=GTs[sb + i][0:64, 0:64], identity=ident[0:64, 0:64])
                    for i, (c, e) in enumerate(pairs):
                        r = sb + i; pw = PW[i]
                        act.copy(out=pw[:, 1, :], in_=PS[i][0:64, 0:64])
                        dve.tensor_copy(out=pw[:, 0, :], in_=GTs[r][0:64, 0:64])
                        dve.tensor_tensor(out=pw[:, 2, :], in0=GTs[r][0:64, 0:64], in1=ident[0:64, 0:64], op=ALU.add)
                    yield
                    for lvl in range(1, 6):
                        for i in range(n):
                            pw = PW[i]; B = PS[i]
                            pe.matmul(out=B[0:64, 0:64], lhsT=pw[:, 1, :], rhs=pw[:, 0, :], start=True, stop=True)
                            pe.matmul(out=B[0:64, 64:128], lhsT=pw[:, 0, :], rhs=pw[:, 1, :], start=True, stop=True)
                        for i in range(n):
                            act.copy(out=fl(PW[i][:, 0:2, :]), in_=PS[i][0:64, 0:128])
                        yield
                        for i in range(n):
                            pw = PW[i]
                            pe.matmul(out=PS[i][0:64, 128:192], lhsT=pw[:, 1, :], rhs=pw[:, 2, :], start=True, stop=True)
                        for i in range(n):
                            pw = PW[i]
                            dve.tensor_tensor(out=pw[:, 2, :], in0=pw[:, 2, :], in1=PS[i][0:64, 128:192], op=ALU.add)
                        yield
                    for i, (c, e) in enumerate(pairs):
                        pb = 64 * e; r = sb + i; B = PS[i]
                        pe.matmul(out=B[0:64, 256:320], lhsT=GTs[r][64:128, 0:64], rhs=X[r][64:128, :], start=True, stop=True,
                                  tile_position=(64, 0))
                    for i, (c, e) in enumerate(pairs):
                        pb = 64 * e; r = sb + i; B = PS[i]
                        pe.matmul(out=B[pb:pb + 64, 320:384], lhsT=X[r][0:64, :], rhs=PW[i][:, 2, :], start=True, stop=True,
                                  tile_position=(0, pb))
                    for i, (c, e) in enumerate(pairs):
                        pb = 64 * e; r = sb + i; B = PS[i]
                        act.copy(out=PW[i][:, 0, :], in_=B[0:64, 256:320])
                        act.copy(out=WTs[r][pb:pb + 64, :], in_=B[pb:pb + 64, 320:384])
                    yield
                    for i in range(n):
                        pe.matmul(out=PS[i][0:64, 384:448], lhsT=PW[i][:, 2, :], rhs=PW[i][:, 0, :], start=True, stop=True)
                    for i in range(n):
                        act.copy(out=Ys[sb + i][:], in_=PS[i][0:64, 384:448])
                    yield

                def stage2(chs, sb):
                    for ci, c in enumerate(chs):
                        for e in range(2):
                            pb = 64 * e; r = sb + 2 * ci + e; Hs = Hst[2 * hp + e]
                            pe.matmul(out=PS[6 + e][0:64, 0:64], lhsT=WTs[r][pb:pb + 64, :], rhs=Hs[pb:pb + 64, :], start=True, stop=True,
                                      tile_position=(pb, 0))
                        for e in range(2):
                            r = sb + 2 * ci + e
                            dve.tensor_tensor(out=X[r][0:64, :], in0=PS[6 + e][0:64, 0:64], in1=Ys[r][:], op=ALU.add)
                        yield
                        for e in range(2):
                            pb = 64 * e; r = sb + 2 * ci + e; Hs = Hst[2 * hp + e]; B = PS[6 + e]
                            pe.matmul(out=B[pb:pb + 64, 64:128], lhsT=Hs[pb:pb + 64, :], rhs=AR[pb:pb + 64, c, 1, :], start=True, stop=True,
                                      tile_position=(pb, pb))
                            pe.matmul(out=B[pb:pb + 64, 128:192], lhsT=X[r][:], rhs=GTs[r][:, 64:128], start=True, stop=True,
                                      tile_position=(0, pb))
                            pe.matmul(out=B[pb:pb + 64, 192:256], lhsT=BKHt[r][:], rhs=X[r][:], start=True, stop=True, tile_position=(0, pb))
                        for e in range(2):
                            pb = 64 * e; Hs = Hst[2 * hp + e]; B = PS[6 + e]
                            act.copy(out=oT[pb:pb + 64, c * 64:(c + 1) * 64], in_=B[pb:pb + 64, 64:128])
                            dve.tensor_tensor(out=oT[pb:pb + 64, c * 64:(c + 1) * 64], in0=oT[pb:pb + 64, c * 64:(c + 1) * 64],
                                              in1=B[pb:pb + 64, 128:192], op=ALU.add)
                            dve.scalar_tensor_tensor(out=Hs[pb:pb + 64, :], in0=Hs[pb:pb + 64, :], scalar=PC[pb:pb + 64, c:c + 1],
                                                     in1=B[pb:pb + 64, 192:256], op0=ALU.mult, op1=ALU.add)
                        yield

                batches = [list(range(c0, min(c0 + CB, NCH))) for c0 in range(0, NCH, CB)]
                prev = None
                for bi in range(len(batches) + 1):
                    g1 = None; g2 = None
                    if bi < len(batches):
                        chs = batches[bi]
                        g1 = stage1([(c, e) for c in chs for e in range(2)], (bi % 2) * NSL)
                    if prev is not None:
                        g2 = stage2(prev[0], prev[1])
                    while g1 is not None or g2 is not None:
                        for _ in range(2):
                            if g1 is not None:
                                try:
                                    next(g1)
                                except StopIteration:
                                    g1 = None
                        if g2 is not None:
                            try:
                                next(g2)
                            except StopIteration:
                                g2 = None
                    prev = (batches[bi], (bi % 2) * NSL) if bi < len(batches) else None
                if is_last:
                    rwkv_sample(hp, rr, ldc, kmod, vv, kkn, bb, oT, tmpA)
                dve.tensor_tensor(out=osq[:, 0:TP], in0=oT[:, 0:TP], in1=oT[:, 0:TP], op=ALU.mult)
                for (t0, tn) in C.TT:
                    psm = PS[6]; psq = PS[7]
                    pe.matmul(out=psm[:, 0:tn], lhsT=onesbd[:], rhs=oT[:, t0:t0 + tn], start=True, stop=True)
                    pe.matmul(out=psq[:, 0:tn], lhsT=onesbd[:], rhs=osq[:, t0:t0 + tn], start=True, stop=True)
                    mean = tmpA[:, t0:t0 + tn]; var = osq[:, t0:t0 + tn]
                    act.mul(out=mean, in_=psm[:, 0:tn], mul=1.0 / 64)
                    dve.tensor_tensor(out=e1[:, 0:tn], in0=mean, in1=mean, op=ALU.mult)
                    dve.scalar_tensor_tensor(out=var, in0=psq[:, 0:tn], scalar=1.0 / 64, in1=e1[:, 0:tn], op0=ALU.mult, op1=ALU.subtract)
                    act.activation(out=var, in_=var, func=AF.Sqrt, bias=eps_gn[:], scale=1.0)
                    dve.reciprocal(out=var, in_=var)
                    dve.tensor_tensor(out=oT[:, t0:t0 + tn], in0=oT[:, t0:t0 + tn], in1=mean, op=ALU.subtract)
                    dve.tensor_tensor(out=oT[:, t0:t0 + tn], in0=oT[:, t0:t0 + tn], in1=var, op=ALU.mult)
                dve.tensor_scalar(out=oT[:, 0:TP], in0=oT[:, 0:TP], scalar1=gnw_s[:, hp:hp + 1], scalar2=gnb_s[:, hp:hp + 1], op0=ALU.mult, op1=ALU.add)
                dve.tensor_tensor(out=oT[:, 0:TP], in0=oT[:, 0:TP], in1=bonus[:, 0:TP], op=ALU.add)
                w = load_w(w_in, C.OFF_RWZ // 128 + hp)
                for (t0, tn) in C.TT:
                    psz = PS[6]
                    mm_acc(psz[:, 0:tn], w, KC, lambda kc: hT[:, kc, t0:t0 + tn].r32())
                    act.activation(out=tmpA[:, t0:t0 + tn], in_=psz[:, 0:tn], func=AF.Silu)
                    dve.tensor_tensor(out=orT[:, hp, t0:t0 + tn], in0=oT[:, t0:t0 + tn], in1=tmpA[:, t0:t0 + tn], op=ALU.mult)
            barrier()

    NPT = (NSC * C.H) // 128 if NSC * C.H >= 128 else 1
    NPP = min(128, NSC * C.H)
    BPT = NPP // C.H
    tokm = fw.sb("tokm", [NSC, 2, 6, 64])
    o_bh_all = fw.sb("o_bh_all", [NPP, NPT, 64])

    def rwkv_sample(hp, rr, ldc, kmod, vv, kkn, bb, oT, tmpA):
        act.activation(out=tmpA[:, TP:W1], in_=ldc[:, TP:W1], func=AF.Exp)
        vecs = [rr, tmpA, kmod, vv, kkn, bb]
        for vi, t in enumerate(vecs):
            pe.transpose(out=PS[6 + vi // 3][0:NSC, (vi % 3) * 128:(vi % 3 + 1) * 128], in_=t[:, TP:W1], identity=ident[:])
        act.copy(out=tokm[:, :, 0:3, :], in_=PS[6][0:NSC, 0:384].re("p (v h n) -> p h v n", v=3, h=2))
        act.copy(out=tokm[:, :, 3:6, :], in_=PS[7][0:NSC, 0:384].re("p (v h n) -> p h v n", v=3, h=2))
        st(scr_a[:, 2 * hp:2 * hp + 2, :, :], tokm[:])

    def rwkv_sample_core():
        with fw.scope():
            lsb = fw.sb
            S = lsb("S_s", [NPP, 64, 64]); T1 = lsb("T1_s", [NPP, 64, 64]); vec = lsb("vec_s", [NPP, 6, 64])
            sa = lsb("sa_s", [NPP, 64]); ob = lsb("ob_s", [NPP, 64]); st1 = lsb("st1_s", [NPP, 4])
            gw = lsb("gw_s", [NPP, 64]); gb = lsb("gb_s", [NPP, 64]); rkb = lsb("rkb_s", [NPP, 64])
            ld(gw[:], gnw_bh[0:NPP, :]); ld(gb[:], gnb_bh[0:NPP, :]); ld(rkb[:], rk_bh[0:NPP, :])
            for pt in range(NPT):
                ld(S[:].re("p a b -> p (a b)"), wkv0[pt * NPP:(pt + 1) * NPP, :])
                ld(vec[:], View(scr_a.buf, scr_a.ap[pt * BPT:(pt + 1) * BPT].rearrange("b h v n -> (b h) v n")))
                R = vec[:, 0, :]; Dc = vec[:, 1, :]; K = vec[:, 2, :]; V = vec[:, 3, :]; KK = vec[:, 4, :]; Bv = vec[:, 5, :]
                bj = lambda v: v[:, None, :].bc([NPP, 64, 64])
                bi_ = lambda v: v[:, :, None].bc([NPP, 64, 64])
                dve.tensor_tensor(out=T1[:], in0=S[:], in1=bj(KK), op=ALU.mult)
                dve.tensor_reduce(out=sa[:], in_=T1[:], axis=AX.X, op=ALU.add)
                dve.tensor_scalar_mul(out=sa[:], in0=sa[:], scalar1=-1.0)
                dve.tensor_tensor(out=S[:], in0=S[:], in1=bj(Dc), op=ALU.mult)
                dve.tensor_tensor(out=T1[:], in0=bi_(sa[:]), in1=bj(Bv), op=ALU.mult)
                dve.tensor_tensor(out=S[:], in0=S[:], in1=T1[:], op=ALU.add)
                dve.tensor_tensor(out=T1[:], in0=bi_(V), in1=bj(K), op=ALU.mult)
                dve.tensor_tensor(out=S[:], in0=S[:], in1=T1[:], op=ALU.add)
                st(swkv_o[pt * NPP:(pt + 1) * NPP, :], S[:].re("p a b -> p (a b)"))
                dve.tensor_tensor(out=T1[:], in0=S[:], in1=bj(R), op=ALU.mult)
                dve.tensor_reduce(out=ob[:], in_=T1[:], axis=AX.X, op=ALU.add)
                dve.tensor_reduce(out=st1[:, 0:1], in_=ob[:], axis=AX.X, op=ALU.add)
                dve.tensor_scalar_mul(out=st1[:, 0:1], in0=st1[:, 0:1], scalar1=1.0 / 64)
                dve.tensor_scalar(out=ob[:], in0=ob[:], scalar1=st1[:, 0:1], scalar2=None, op0=ALU.subtract)
                dve.tensor_tensor(out=sa[:], in0=ob[:], in1=ob[:], op=ALU.mult)
                dve.tensor_reduce(out=st1[:, 1:2], in_=sa[:], axis=AX.X, op=ALU.add)
                act.activation(out=st1[:, 1:2], in_=st1[:, 1:2], func=AF.Sqrt, bias=eps_gn[0:NPP, :], scale=1.0 / 64)
                dve.reciprocal(out=st1[:, 1:2], in_=st1[:, 1:2])
                dve.tensor_scalar(out=ob[:], in0=ob[:], scalar1=st1[:, 1:2], scalar2=None, op0=ALU.mult)
                dve.tensor_tensor(out=ob[:], in0=ob[:], in1=gw[:], op=ALU.mult)
                dve.tensor_tensor(out=ob[:], in0=ob[:], in1=gb[:], op=ALU.add)
                dve.tensor_tensor(out=sa[:], in0=R, in1=K, op=ALU.mult)
                dve.tensor_tensor(out=sa[:], in0=sa[:], in1=rkb[:], op=ALU.mult)
                dve.tensor_reduce(out=st1[:, 2:3], in_=sa[:], axis=AX.X, op=ALU.add)
                dve.scalar_tensor_tensor(out=o_bh_all[:, pt, :], in0=V, scalar=st1[:, 2:3], in1=ob[:], op0=ALU.mult, op1=ALU.add)
                st(View(scr_b.buf, scr_b.ap[pt * BPT:(pt + 1) * BPT].rearrange("b (h n) -> (b h) n", n=64)), o_bh_all[:, pt, :])
            otok = lsb("otok_s", [NSC, C.DR]); ld(otok[:], scr_b[:])
            zt_ = lsb("zt_s", [P, NSC]); of_ = lsb("of_s", [P, NSC])
            for hp in range(NHP):
                pe.transpose(out=PS[6][:, 0:NSC], in_=otok[:, hp * 128:(hp + 1) * 128], identity=ident[0:NSC, 0:NSC])
                act.copy(out=of_[:], in_=PS[6][:, 0:NSC])
                w = load_w(w_in, C.OFF_RWZ // 128 + hp)
                mm_acc(PS[7][:, 0:NSC], w, KC, lambda kc: hT[:, kc, TP:TP + NSC].r32())
                act.activation(out=zt_[:], in_=PS[7][:, 0:NSC], func=AF.Silu)
                dve.tensor_tensor(out=orT[:, hp, TP:TP + NSC], in0=of_[:], in1=zt_[:], op=ALU.mult)
            barrier()

    def post_phase(is_last, p):
        with fw.scope():
            lsb = fw.sb
            NTT = TP // 128
            xn = [lsb("xn%d" % i, [P, D]) for i in range(NTT)]
            xns = lsb("xns", [NSC, D]) if is_last else None
            g1 = lsb("g1", [P, TW]); g2 = lsb("g2", [P, TW]); MT = lsb("MT", [P, TW])
            sq = lsb("sqp", [P, D], mybir.dt.bfloat16); ssp = lsb("ssp", [P, NTT + 1])
            fg = lsb("fg", [P, D]); ld(fg[:], fing[:])
            pt_tiles = list(C.TT) + ([(TP, NSC)] if is_last else [])
            for tt in range(NTT):
                ld(xn[tt][:], xb[p * TP + tt * 128:p * TP + (tt + 1) * 128, :])
            if is_last:
                ld(xns[:], xs[:])
            for db in range(KC):
                wg1 = load_w(w_in, C.OFF_G1 // 128 + db)
                for (t0, tn) in pt_tiles:
                    psg = PS[0 + (t0 // 512) % 2]
                    mm_acc(psg[:, 0:tn], wg1, KC, lambda kc: hT[:, kc, t0:t0 + tn].r32())
                    act.activation(out=g1[:, t0:t0 + tn], in_=psg[:, 0:tn], func=AF.Sigmoid)
                wo = load_w(w_out, db)
                for (t0, tn) in pt_tiles:
                    ps1 = PS[2 + (t0 // 512) % 2]
                    mm_acc(ps1[:, 0:tn], wo, NUB, lambda kc: osT[:, kc, t0:t0 + tn])
                    dve.tensor_tensor(out=MT[:, t0:t0 + tn], in0=g1[:, t0:t0 + tn], in1=ps1[:, 0:tn], op=ALU.mult)
                wg2 = load_w(w_in, C.OFF_G2 // 128 + db)
                for (t0, tn) in pt_tiles:
                    psg2 = PS[0 + (t0 // 512) % 2]
                    mm_acc(psg2[:, 0:tn], wg2, KC, lambda kc: hT[:, kc, t0:t0 + tn].r32())
                    act.activation(out=g2[:, t0:t0 + tn], in_=psg2[:, 0:tn], func=AF.Sigmoid)
                    ps2 = PS[2 + (t0 // 512) % 2]
                    for kc in range(NHP):
                        pe.matmul(out=ps2[:, 0:tn], lhsT=wo[:, NUB + kc, :].r32(), rhs=orT[:, kc, t0:t0 + tn],
                                  start=(kc == 0), stop=(kc == NHP - 1), inc=(kc == NHP - 1))
                    dve.tensor_tensor(out=g2[:, t0:t0 + tn], in0=g2[:, t0:t0 + tn], in1=ps2[:, 0:tn], op=ALU.mult)
                    dve.tensor_tensor(out=MT[:, t0:t0 + tn], in0=MT[:, t0:t0 + tn], in1=g2[:, t0:t0 + tn], op=ALU.add)
                    if t0 < TP:
                        dve.tensor_scalar_mul(out=MT[:, t0:t0 + tn], in0=MT[:, t0:t0 + tn], scalar1=gt_p[:, db:db + 1])
                        for q in range(tn // 128):
                            tt = t0 // 128 + q
                            pst = PS[4 + q % 2]
                            pe.transpose(out=pst[:, 0:128], in_=MT[:, t0 + q * 128:t0 + (q + 1) * 128], identity=ident[:])
                            dve.tensor_tensor(out=xn[tt][:, db * 128:(db + 1) * 128], in0=xn[tt][:, db * 128:(db + 1) * 128],
                                              in1=pst[:, 0:128], op=ALU.add)
                    else:
                        dve.tensor_tensor(out=MT[:, TP:TP + NSC], in0=MT[:, TP:TP + NSC], in1=modT[:, 2 * KC + db, 0:NSC], op=ALU.mult)
                        pst = PS[6]
                        pe.transpose(out=pst[0:NSC, 0:128], in_=MT[:, TP:TP + NSC], identity=ident[:])
                        dve.tensor_tensor(out=xns[:, db * 128:(db + 1) * 128], in0=xns[:, db * 128:(db + 1) * 128],
                                          in1=pst[0:NSC, 0:128], op=ALU.add)
            for tt in range(NTT):
                s = ssp[:, tt:tt + 1]
                act.activation(out=sq[:], in_=xn[tt][:], func=AF.Square, accum_out=s)
                rstd_of(s, 128)
                dve.scalar_tensor_tensor(out=xn[tt][:], in0=xn[tt][:], scalar=s, in1=fg[:], op0=ALU.mult, op1=ALU.mult)
                st(y_o[p * TP + tt * 128:p * TP + (tt + 1) * 128, :], xn[tt][:])
            if is_last:
                s = ssp[0:NSC, NTT:NTT + 1]
                act.activation(out=sq[0:NSC, :], in_=xns[:], func=AF.Square, accum_out=s)
                rstd_of(s, NSC)
                dve.scalar_tensor_tensor(out=xns[:], in0=xns[:], scalar=s, in1=fg[0:NSC, :], op0=ALU.mult, op1=ALU.mult)
                st(ys_o[:], xns[:])
            barrier()

    def sample_prep():
        with fw.scope():
            lsb = fw.sb
            xs_t = lsb("xs_t", [NSC, D]); sh_t = lsb("sh_t", [NSC, D]); sqs = lsb("sqs", [NSC, D]); s1 = lsb("s1s", [NSC, 1])
            xsT = lsb("xsT", [P, KC, NSC]); hs_tok = lsb("hs_tok", [NSC, D]); stg = lsb("stg", [NSC, NSB * 128])
            ld(xs_t[:], xs[:]); ld(sh_t[:], sshift[:])
            act.activation(out=sqs[:], in_=xs_t[:], func=AF.Square, accum_out=s1[:])
            rstd_of(s1[:], NSC)
            act.activation(out=xs_t[:], in_=xs_t[:], func=AF.Copy, scale=s1[:])
            for kc in range(KC):
                pe.transpose(out=PS[0][:, kc * NSC:(kc + 1) * NSC], in_=xs_t[:, kc * 128:(kc + 1) * 128], identity=ident[0:NSC, 0:NSC])
                pe.transpose(out=PS[1][:, kc * NSC:(kc + 1) * NSC], in_=sh_t[:, kc * 128:(kc + 1) * 128], identity=ident[0:NSC, 0:NSC])
            dve.tensor_tensor(out=xsT[:], in0=PS[0][:, 0:KC * NSC].re("p (k n) -> p k n", n=NSC), in1=sceff[:, :, 0:NSC], op=ALU.mult)
            dve.tensor_tensor(out=xsT[:], in0=xsT[:], in1=modT[:, 0:KC, 0:NSC], op=ALU.add)
            dve.tensor_copy(out=hT[:, :, TP:TP + NSC], in_=xsT[:])
            act.copy(out=hT[:, :, TP + NSC:TP + NS2], in_=PS[1][:, 0:KC * NSC].re("p (k n) -> p k n", n=NSC))
            for kc in range(KC):
                pe.transpose(out=PS[2 + kc // 4 % 2][0:NSC, (kc % 4) * 128:(kc % 4 + 1) * 128], in_=xsT[:, kc, :], identity=ident[:])
                act.copy(out=hs_tok[:, kc * 128:(kc + 1) * 128], in_=PS[2 + kc // 4 % 2][0:NSC, (kc % 4) * 128:(kc % 4 + 1) * 128])
            st(sshift_o[:], hs_tok[:])
            for (src, dst) in ((s5re0, x0r), (s5im0, x0i)):
                ld(stg[:], src[:])
                for s in range(NSB):
                    pe.transpose(out=PS[4][:, s * NSC:(s + 1) * NSC], in_=stg[:, s * 128:(s + 1) * 128], identity=ident[0:NSC, 0:NSC])
                act.copy(out=dst[:].re("p s n -> p (s n)"), in_=PS[4][:, 0:NSB * NSC])
            barrier()

    def finals():
        with fw.scope():
            lsb = fw.sb
            xf = [lsb("xf%d" % i, [P, NSB]) for i in range(4)]
            dve.tensor_tensor(out=xf[0][:], in0=cs1[:], in1=s5car_r[:], op=ALU.mult)
            dve.tensor_tensor(out=xf[1][:], in0=sn1[:], in1=s5car_i[:], op=ALU.mult)
            dve.tensor_tensor(out=xf[0][:], in0=xf[0][:], in1=xf[1][:], op=ALU.add)
            dve.tensor_tensor(out=xf[2][:], in0=cs1[:], in1=s5car_i[:], op=ALU.mult)
            dve.tensor_tensor(out=xf[3][:], in0=sn1[:], in1=s5car_r[:], op=ALU.mult)
            dve.tensor_tensor(out=xf[2][:], in0=xf[2][:], in1=xf[3][:], op=ALU.subtract)
            xo = lsb("xfo", [NSB, 2, P])
            pe.transpose(out=PS[0][0:NSB, 0:128], in_=xf[0][:], identity=ident[:])
            pe.transpose(out=PS[0][0:NSB, 128:256], in_=xf[2][:], identity=ident[:])
            act.copy(out=xo[:].re("p a b -> p (a b)"), in_=PS[0][0:NSB, 0:256])
            st(ps5re_o[:], xo[:, 0, :]); st(ps5im_o[:], xo[:, 1, :])
            ho = lsb("hlo", [KC, P])
            pe.transpose(out=PS[1][0:KC, 0:128], in_=hlast[:], identity=ident[:])
            act.copy(out=ho[:], in_=PS[1][0:KC, 0:128])
            st(pshift_o[:], ho[:])
            so = lsb("wkvo", [64, C.H, 64])
            for h in range(C.H):
                pb = 64 * (h % 2)
                pe.transpose(out=PS[2][0:64, (h % 8) * 64:(h % 8 + 1) * 64], in_=Hst[h][pb:pb + 64, :], identity=ident[pb:pb + 64, pb:pb + 64],
                             tile_position=(pb, 0))
                act.copy(out=so[:, h, :], in_=PS[2][0:64, (h % 8) * 64:(h % 8 + 1) * 64])
            st(View(pwkv_o.buf, pwkv_o.ap.rearrange("h i j -> i h j")), so[:])
            so2 = lsb("s5o", [NSC, 512])
            for (src, dsto) in ((x1r, ss5re_o), (x1i, ss5im_o)):
                for g4 in range(NSB // 4):
                    for j in range(4):
                        pe.transpose(out=PS[3][0:NSC, j * 128:(j + 1) * 128], in_=src[:, g4 * 4 + j, :], identity=ident[:])
                    act.copy(out=so2[:], in_=PS[3][0:NSC, :])
                    st(dsto[:, g4 * 512:(g4 + 1) * 512], so2[:])

    try:
      for p in range(NPASS):
          if stop(1):
              break
          is_last = (p == NPASS - 1)
          if is_last:
              sample_prep()
          phase_x(View(xb.buf, xb.ap[p * TP:(p + 1) * TP, :]), is_last)
          barrier()
          if stop(2):
              break
          s5_phase(is_last)
          if stop(3):
              break
          rwkv_phase(is_last)
          if stop(4):
              break
          if is_last:
              rwkv_sample_core()
          post_phase(is_last, p)
          if stop(5):
              break
      if not stop(6):
          finals()
    except StopBuild:
        pass
    fw.finish()
    return nc, fw, es


def _blk(w, kcn):
    K, N = w.shape
    return np.ascontiguousarray(w.reshape(kcn, 128, N // 128, 128).transpose(2, 1, 0, 3))


def _fm(v):
    v = np.asarray(v).reshape(-1)
    return np.ascontiguousarray(v.reshape(-1, 128).T)


def make_in_maps(cfg, inp, n_cores, n_seq):
    C = cfg
    f = np.float32
    H = C.H; NSB = C.NSB
    sh = {}
    sh["w_ada"] = _blk(inp["w_ada"][0], C.KC); sh["w_in"] = _blk(inp["w_in"][0], C.KC)
    sh["w_glu"] = _blk(inp["w_glu"][0], C.NUB); sh["w_out"] = _blk(inp["w_out"][0], C.KC)
    sh["b_ada"] = _fm(inp["b_ada"][0]); sh["norm_g"] = _fm(inp["norm_g"][0]); sh["mu"] = _fm(inp["mu_rw"][0])
    sh["A_re"] = _fm(inp["A_re"][0]); sh["A_im"] = _fm(inp["A_im"][0])
    sh["lstep"] = np.ascontiguousarray(np.repeat(inp["log_step"][0].reshape(NSB, 2), 64, axis=1).T)
    for k in ("B_re", "B_im"):
        sh[k] = np.ascontiguousarray(inp[k][0].reshape(NSB, 2, 64, 16).transpose(1, 2, 0, 3).reshape(128, NSB, 16))
    for k in ("C_re", "C_im"):
        sh[k] = np.ascontiguousarray(inp[k][0].reshape(NSB, 2, 16, 64).transpose(1, 3, 0, 2).reshape(128, NSB, 16))
    sh["Dsk"] = _fm(inp["D_skip"][0]); sh["b_glu"] = _fm(inp["b_glu"][0])
    for k in ("w0", "a0", "k_k", "k_a", "r_k", "gn_w", "gn_b"):
        sh[k] = _fm(inp[k][0])
    sh["w2"] = np.ascontiguousarray(inp["w2"][0]); sh["a2"] = np.ascontiguousarray(inp["a2"][0])
    sh["fing"] = np.ascontiguousarray(np.broadcast_to(inp["final_g"].reshape(1, -1), (128, C.D)))
    rep = max(1, 128 // H)
    sh["gnw_bh"] = np.ascontiguousarray(np.tile(inp["gn_w"][0].reshape(H, 64), (rep, 1))[:128])
    sh["gnb_bh"] = np.ascontiguousarray(np.tile(inp["gn_b"][0].reshape(H, 64), (rep, 1))[:128])
    sh["rk_bh"] = np.ascontiguousarray(np.tile(inp["r_k"][0].reshape(H, 64), (rep, 1))[:128])
    if sh["gnw_bh"].shape[0] < 128:
        for k in ("gnw_bh", "gnb_bh", "rk_bh"):
            sh[k] = np.ascontiguousarray(np.concatenate([sh[k], np.zeros((128 - sh[k].shape[0], 64), f)], 0))
    sh["ident"] = np.eye(128, dtype=f)
    ms = np.triu(np.ones((64, 64), f), 1); mi_ = np.triu(np.ones((64, 64), f), 0)
    sh["maskgt"] = np.block([[ms, mi_], [ms, mi_]]).astype(f)
    ob = np.zeros((128, 128), f); ob[:64, :64] = 1; ob[64:, 64:] = 1
    sh["onesbd"] = ob
    rm = np.ones((128, C.TP), f); rm[:, ::64] = 0
    sh["resetm"] = rm
    rmk = np.zeros((128, 4), f)
    for j in range(4):
        rmk[32 * j:32 * j + 32, j] = 1
    sh["rowmask"] = rmk
    sh["iota"] = np.ascontiguousarray(np.broadcast_to(np.arange(128, dtype=f).reshape(1, -1), (128, 128)))
    sh = {k: np.ascontiguousarray(v, dtype=f) for k, v in sh.items()}
    maps = []
    NSC = C.NSC
    for c in range(n_cores):
        b = c % n_seq
        rows = slice(c * NSC, (c + 1) * NSC)
        m = dict(sh)
        m["xb"] = np.ascontiguousarray(inp["x_prompt"][b], dtype=f)
        cp = inp["c_prompt"][b:b + 1]
        m["c17"] = np.ascontiguousarray(np.concatenate([inp["c_sample"][rows], cp, cp], 0), dtype=f)
        m["xs"] = np.ascontiguousarray(inp["x_sample"][rows, 0, :], dtype=f)
        m["sshift"] = np.ascontiguousarray(inp["state_shift"][0, rows], dtype=f)
        m["s5re0"] = np.ascontiguousarray(inp["state_s5_re"][0, rows].reshape(NSC, -1), dtype=f)
        m["s5im0"] = np.ascontiguousarray(inp["state_s5_im"][0, rows].reshape(NSC, -1), dtype=f)
        m["wkv0"] = np.ascontiguousarray(inp["state_wkv"][0, rows].reshape(NSC * H, 4096), dtype=f)
        maps.append(m)
    return maps


def assemble(cfg, res, n_cores, n_seq):
    C = cfg; f = np.float32
    G = C.G; H = C.H; NSC = C.NSC
    R = lambda c, k: np.asarray(res[c][k], dtype=f)
    y_p = np.stack([R(b, "y") for b in range(n_seq)], 0)
    y_s = np.concatenate([R(c, "ys") for c in range(n_cores)], 0)[:, None, :]
    re_p = np.stack([R(b, "ps5re").reshape(G, 64) for b in range(n_seq)], 0)[None]
    im_p = np.stack([R(b, "ps5im").reshape(G, 64) for b in range(n_seq)], 0)[None]
    wkv_p = np.stack([R(b, "pwkv") for b in range(n_seq)], 0)[None]
    sh_p = np.stack([R(b, "pshift").reshape(-1) for b in range(n_seq)], 0)[None]
    re_s = np.concatenate([R(c, "ss5re").reshape(NSC, G, 64) for c in range(n_cores)], 0)[None]
    im_s = np.concatenate([R(c, "ss5im").reshape(NSC, G, 64) for c in range(n_cores)], 0)[None]
    wkv_s = np.concatenate([R(c, "swkv").reshape(NSC, H, 64, 64) for c in range(n_cores)], 0)[None]
    sh_s = np.concatenate([R(c, "sshift_o") for c in range(n_cores)], 0)[None]
    return (y_p, y_s, re_p, im_p, wkv_p, sh_p, re_s, im_s, wkv_s, sh_s)


def kernel(**inputs):
    cfg = Cfg()
    inp = {k: np.asarray(v) for k, v in inputs.items()}
    nc, fw, es = build(cfg)
    maps = make_in_maps(cfg, inp, 8, 4)
    res = run_bass_kernel_spmd(nc, maps, core_ids=list(range(8)))
    return assemble(cfg, res.results, 8, 4)
```

```python
import math
import os
from contextlib import ExitStack
import numpy as np
import concourse.bass as bass
import concourse.mybir as mybir
from concourse.bass_utils import run_bass_kernel_spmd

F32 = mybir.dt.float32
F32R = mybir.dt.float32r
AF = mybir.ActivationFunctionType
ALU = mybir.AluOpType
AX = mybir.AxisListType


class Cfg:
    def __init__(self, D=2048, TP=512, NSC=16, NPASS=4):
        self.NPASS = NPASS
        self.D = D; self.TP = TP; self.NSC = NSC
        self.KC = D // 128
        self.DS = D // 2; self.DR = D // 2
        self.G = self.DS // 16; self.NSB = self.G // 2; self.NUB = self.DS // 128
        self.H = self.DR // 64; self.NHP = self.DR // 128
        self.NCH = TP // 64
        self.OFF_U = 0; self.OFF_Z = self.DS; self.OFF_RW = 2 * self.DS
        self.NSH = 3 * self.DR + 128
        self.OFF_RWZ = self.OFF_RW + self.NSH
        self.OFF_G1 = self.OFF_RWZ + self.DR
        self.OFF_G2 = self.OFF_G1 + D
        self.NIN = self.OFF_G2 + D
        self.NRB = self.NSH // 128
        self.GB = min(2, self.NUB)
        self.TT = [(i, min(512, TP - i)) for i in range(0, TP, 512)]
        self.TW = TP + 2 * NSC


class TL:
    def __init__(self, sem, name):
        self.sem = sem; self.count = 0; self.name = name


class Buf:
    __slots__ = ("w", "r", "excl")

    def __init__(self):
        self.w = None; self.r = {}; self.excl = False


class View:
    __slots__ = ("buf", "ap")

    def __init__(self, buf, ap):
        self.buf = buf; self.ap = ap

    def __getitem__(self, idx):
        return View(self.buf, self.ap[idx])

    def re(self, s, **kw):
        return View(self.buf, self.ap.rearrange(s, **kw))

    def bc(self, shape):
        return View(self.buf, self.ap.to_broadcast(shape))

    def r32(self):
        return View(self.buf, self.ap.bitcast(F32R))


class Tile:
    def __init__(self, ap, buf=None):
        self.ap = ap; self.buf = buf or Buf()

    def __getitem__(self, idx):
        return View(self.buf, self.ap[idx])

    def sub(self, idx):
        return Tile(self.ap[idx])


class Eng:
    def __init__(self, fw, name, h, tl, self_sync):
        self.fw = fw; self.name = name; self.h = h; self.tl = tl
        self.self_sync = self_sync; self.seen = {}

    def wait_for(self, deps):
        for tl, cnt in deps.items():
            if tl is self.tl and not self.self_sync:
                continue
            if self.seen.get(tl, 0) >= cnt:
                continue
            self.h.wait_ge(tl.sem, cnt)
            self.seen[tl] = cnt

    def __getattr__(self, op):
        def f(inc=True, **kw):
            return self.fw._issue(self, op, kw, inc)
        return f


def _deps(reads, writes):
    deps = {}

    def add(tc):
        if tc is None:
            return
        tl, c = tc
        if deps.get(tl, 0) < c:
            deps[tl] = c
    for v in reads:
        add(v.buf.w)
        if v.buf.excl:
            for tl, c in v.buf.r.items():
                add((tl, c))
    for v in writes:
        add(v.buf.w)
        for tl, c in v.buf.r.items():
            add((tl, c))
    return deps


class FW:
    def __init__(self, nc, es):
        self.nc = nc; self.es = es; self.nsem = 0; self.es_stack = [es]; self.uid = 0
        self.pe = Eng(self, "pe", nc.tensor, self.tl("pe"), False)
        self.act = Eng(self, "act", nc.scalar, self.tl("act"), True)
        self.dve = Eng(self, "dve", nc.vector, self.tl("dve"), True)
        self.pool = Eng(self, "pool", nc.gpsimd, self.tl("pool"), True)
        self.sp = Eng(self, "sp", nc.sync, self.tl("sp"), False)
        self.dma_tls = []

    def tl(self, name):
        self.nsem += 1
        return TL(self.es.enter_context(self.nc.semaphore(name)), name)

    def sb(self, name, shape, dtype=F32):
        self.uid += 1
        return Tile(self.es_stack[-1].enter_context(self.nc.sbuf_tensor("%s_%d" % (name, self.uid), list(shape), dtype))[:])

    def barrier(self):
        engs = (self.pe, self.act, self.dve, self.pool)
        tls = [e.tl for e in engs] + self.dma_tls
        for e in engs + (self.sp,):
            e.wait_for({tl: tl.count for tl in tls if tl.count and tl is not e.tl})

    def scope(self):
        fw = self

        class _S:
            def __enter__(self_):
                self_.les = ExitStack(); fw.es_stack.append(self_.les); return self_

            def __exit__(self_, *a):
                if fw.es_stack[-1] is not self_.les:
                    return False
                fw.barrier(); fw.es_stack.pop(); self_.les.close(); return False
        return _S()

    def ps(self, name, shape=(128, 512)):
        t = Tile(self.es.enter_context(self.nc.psum_tensor(name, list(shape), F32))[:])
        t.buf.excl = True
        return t

    def dram(self, name, shape, kind):
        return Tile(self.nc.dram_tensor(name, list(shape), F32, kind=kind).ap())

    def _issue(self, eng, op, kw, inc):
        reads = [v for k, v in kw.items() if isinstance(v, View) and k not in ("out", "accum_out", "ap")]
        writes = [v for k, v in kw.items() if isinstance(v, View) and k in ("out", "accum_out", "ap")]
        eng.wait_for(_deps(reads, writes))
        args = {k: (v.ap if isinstance(v, View) else v) for k, v in kw.items()}
        inst = getattr(eng.h, op)(**args)
        if inc:
            eng.tl.count += 1
            inst.then_inc(eng.tl.sem, 1)
            stamp = eng.tl.count
        else:
            stamp = eng.tl.count + 1
        for v in reads:
            if v.buf.r.get(eng.tl, 0) < stamp:
                v.buf.r[eng.tl] = stamp
        for v in writes:
            v.buf.w = (eng.tl, stamp); v.buf.r = {}
        return inst

    def dma(self, q, tl, out, in_, **kw):
        eng = {"sp": self.sp, "pool": self.pool, "act": self.act}[q]
        deps = _deps([in_], [out])
        if tl.count:
            deps[tl] = max(deps.get(tl, 0), tl.count)
        eng.wait_for(deps)
        inst = eng.h.dma_start(out=out.ap, in_=in_.ap, **kw)
        tl.count += 16
        inst.then_inc(tl.sem, 16)
        in_.buf.r[tl] = tl.count
        out.buf.w = (tl, tl.count); out.buf.r = {}
        if tl not in self.dma_tls:
            self.dma_tls.append(tl)

    def finish(self):
        for tl in self.dma_tls:
            self.sp.h.wait_ge(tl.sem, tl.count)
        for e in (self.pe, self.act, self.dve, self.pool):
            if e.tl.count:
                self.sp.h.wait_ge(e.tl.sem, e.tl.count)


def build(cfg, dbg=False):
    nc = bass.Bass("TRN2", target_bir_lowering=False)
    es = ExitStack()
    fw = FW(nc, es)
    pe, act, dve, pool = fw.pe, fw.act, fw.dve, fw.pool
    C = cfg
    D, KC, TP, NSC, TW = C.D, C.KC, C.TP, C.NSC, C.TW
    NS2 = 2 * NSC
    NSB, NUB, NHP, NCH, NRB, GB = C.NSB, C.NUB, C.NHP, C.NCH, C.NRB, C.GB
    P = 128

    def din(name, shape):
        return fw.dram(name, shape, "ExternalInput")

    def dout(name, shape):
        return fw.dram(name, shape, "ExternalOutput")

    xb = din("xb", [TP * C.NPASS, D]); flag = din("flag", [P, 1])
    c17 = din("c17", [NSC + 2, D]); xs = din("xs", [NSC, D]); sshift = din("sshift", [NSC, D])
    s5re0 = din("s5re0", [NSC, NSB * 128]); s5im0 = din("s5im0", [NSC, NSB * 128])
    wkv0 = din("wkv0", [NSC * C.H, 4096])
    w_ada = din("w_ada", [3 * KC, P, KC, 128])
    w_in = din("w_in", [C.NIN // 128, P, KC, 128])
    w_glu = din("w_glu", [NUB, P, NUB, 128])
    w_out = din("w_out", [KC, P, KC, 128])
    b_ada = din("b_ada", [P, 3 * KC]); norm_g = din("norm_g", [P, KC]); mu = din("mu", [P, NRB])
    A_re = din("A_re", [P, NSB]); A_im = din("A_im", [P, NSB]); lstep = din("lstep", [P, NSB])
    B_re = din("B_re", [P, NSB, 16]); B_im = din("B_im", [P, NSB, 16])
    C_re = din("C_re", [P, NSB, 16]); C_im = din("C_im", [P, NSB, 16])
    Dsk = din("Dsk", [P, NUB]); b_glu = din("b_glu", [P, NUB])
    w0 = din("w0", [P, NHP]); a0 = din("a0", [P, NHP]); k_k = din("k_k", [P, NHP]); k_a = din("k_a", [P, NHP])
    r_k = din("r_k", [P, NHP]); gn_w = din("gn_w", [P, NHP]); gn_b = din("gn_b", [P, NHP])
    w2 = din("w2", [64, C.DR]); a2 = din("a2", [64, C.DR])
    fing = din("fing", [P, D])
    gnw_bh = din("gnw_bh", [P, 64]); gnb_bh = din("gnb_bh", [P, 64]); rk_bh = din("rk_bh", [P, 64])
    ident_d = din("ident", [P, P]); maskgt_d = din("maskgt", [P, P]); onesbd_d = din("onesbd", [P, P])
    reset_d = din("resetm", [P, TP]); iota_d = din("iota", [P, 128]); rowmask_d = din("rowmask", [P, 4])

    y_o = dout("y", [TP * (C.NPASS // 2), D]); ys_o = dout("ys", [NSC, D])
    ps5re_o = dout("ps5re", [NSB, P]); ps5im_o = dout("ps5im", [NSB, P])
    pwkv_o = dout("pwkv", [C.H, 64, 64]); pshift_o = dout("pshift", [KC, P])
    ss5re_o = dout("ss5re", [NSC, NSB * 128]); ss5im_o = dout("ss5im", [NSC, NSB * 128])
    swkv_o = dout("swkv", [NSC * C.H, 4096]); sshift_o = dout("sshift_o", [NSC, D])
    scr_a = fw.dram("scr_a", [NSC, C.H, 6, 64], "Internal")
    scr_b = fw.dram("scr_b", [NSC, C.DR], "Internal")
    dbg_o = {}

    tl_misc = [fw.tl("m%d" % i) for i in range(6)]
    mi = [0]

    def ld(out, in_, q="sp"):
        tl = tl_misc[mi[0] % len(tl_misc)]; mi[0] += 1
        fw.dma(q, tl, out, in_)

    tl_out = [fw.tl("o%d" % i) for i in range(4)]
    oi = [0]

    def st(out, in_, **kw):
        tl = tl_out[oi[0] % len(tl_out)]; oi[0] += 1
        fw.dma("sp", tl, out, in_, **kw)

    def const(name, d, shape):
        t = fw.sb(name, shape); ld(t[:], d[:]); return t
    ident = const("ident_s", ident_d, [P, P]); maskgt = const("maskgt_s", maskgt_d, [P, P])
    onesbd = const("onesbd_s", onesbd_d, [P, P]); resetm = const("reset_s", reset_d, [P, TP])
    flag_s = const("flag_s", flag, [P, 1]); iota = const("iota_s", iota_d, [P, 128]); rowmask = const("rowmask_s", rowmask_d, [P, 4])
    b_ada_s = const("b_ada_s", b_ada, [P, 3 * KC]); norm_g_s = const("norm_g_s", norm_g, [P, KC])
    mu_s = const("mu_s", mu, [P, NRB])
    Dsk_s = const("Dsk_s", Dsk, [P, NUB]); b_glu_s = const("b_glu_s", b_glu, [P, NUB])
    w0_s = const("w0_s", w0, [P, NHP]); a0_s = const("a0_s", a0, [P, NHP]); kk_s = const("kk_s", k_k, [P, NHP])
    ka_s = const("ka_s", k_a, [P, NHP]); rk_s = const("rk_s", r_k, [P, NHP])
    gnw_s = const("gnw_s", gn_w, [P, NHP]); gnb_s = const("gnb_s", gn_b, [P, NHP])
    w2_s = fw.sb("w2a2_s", [P, C.DR]); ld(w2_s[0:64, :], w2[:])
    a2_s = w2_s; ld(a2_s[64:128, :], a2[:])
    omu_s = fw.sb("omu_s", [P, NRB])
    dve.tensor_scalar(out=omu_s[:], in0=mu_s[:], scalar1=-1.0, scalar2=1.0, op0=ALU.mult, op1=ALU.add)

    PS = [fw.ps("psb%d" % i) for i in range(8)]

    NRING = 3
    ring = [fw.sb("wring%d" % i, [P, KC, 128], F32R) for i in range(NRING)]
    ring_tl = [fw.tl("wr%d" % i) for i in range(NRING)]
    ri = [0]

    def load_w(wd, blk, nk=KC):
        i = ri[0] % NRING; ri[0] += 1
        fw.dma("pool", ring_tl[i], ring[i][:, 0:nk, :], View(wd.buf, wd.ap[blk, :, 0:nk, :]))
        return ring[i]

    def mm_acc(psv, wslot, nk, rhs_fn, ncols=128):
        for kc in range(nk):
            pe.matmul(out=psv, lhsT=wslot[:, kc, 0:ncols].r32(), rhs=rhs_fn(kc),
                      start=(kc == 0), stop=(kc == nk - 1), inc=(kc == nk - 1))

    s5 = {}
    tA = const("A_re_s", A_re, [P, NSB]); tAi = const("A_im_s", A_im, [P, NSB]); tls = const("lstep_s", lstep, [P, NSB])
    step = fw.sb("s5step", [P, NSB]); act.activation(out=step[:], in_=tls[:], func=AF.Exp)
    lam = fw.sb("s5lam", [P, NSB]); dve.tensor_scalar_min(out=lam[:], in0=tA[:], scalar1=-1e-4)
    mag = fw.sb("s5mag", [P, NSB]); tmpc = fw.sb("s5tmp", [P, NSB]); theta = fw.sb("s5theta", [P, NSB])
    dve.tensor_tensor(out=tmpc[:], in0=lam[:], in1=step[:], op=ALU.mult)
    act.activation(out=mag[:], in_=tmpc[:], func=AF.Exp)
    dve.tensor_tensor(out=theta[:], in0=tAi[:], in1=step[:], op=ALU.mult)
    TWO_PI = 2.0 * math.pi

    I32 = mybir.dt.int32

    def sin_into(dst, x, shape, phase):
        with fw.scope():
            ki = fw.sb("sr_ki", shape, I32); kf = fw.sb("sr_kf", shape)
            dve.tensor_scalar_add(out=dst, in0=x, scalar1=float(phase))
            dve.tensor_scalar_mul(out=ki[:], in0=dst, scalar1=1.0 / TWO_PI)
            dve.tensor_copy(out=kf[:], in_=ki[:])
            dve.scalar_tensor_tensor(out=dst, in0=kf[:], scalar=-TWO_PI, in1=dst, op0=ALU.mult, op1=ALU.add)
            dve.tensor_scalar(out=kf[:], in0=dst, scalar1=math.pi, scalar2=-TWO_PI, op0=ALU.is_gt, op1=ALU.mult)
            dve.tensor_tensor(out=dst, in0=dst, in1=kf[:], op=ALU.add)
            dve.tensor_scalar(out=kf[:], in0=dst, scalar1=-math.pi, scalar2=TWO_PI, op0=ALU.is_lt, op1=ALU.mult)
            dve.tensor_tensor(out=dst, in0=dst, in1=kf[:], op=ALU.add)
            dve.tensor_scalar(out=dst, in0=dst, scalar1=math.pi, scalar2=-math.pi, op0=ALU.min, op1=ALU.max)
            act.activation(out=dst, in_=dst, func=AF.Sin)

    def sincos(name, ang_view, shape):
        sn = fw.sb(name + "_sin", shape); cs = fw.sb(name + "_cos", shape)
        sin_into(sn[:], ang_view, shape, 0.0)
        sin_into(cs[:], ang_view, shape, 0.5 * math.pi)
        return sn, cs
    thp = theta
    sn1, cs1 = sincos("s5t1", thp[:], [P, NSB])
    abr = fw.sb("s5abr", [P, NSB]); abi = fw.sb("s5abi", [P, NSB])
    dve.tensor_tensor(out=abr[:], in0=mag[:], in1=cs1[:], op=ALU.mult)
    dve.tensor_tensor(out=abi[:], in0=mag[:], in1=sn1[:], op=ALU.mult)
    den = fw.sb("s5den", [P, NSB]); t2 = fw.sb("s5t2", [P, NSB]); fre = fw.sb("s5fre", [P, NSB]); fim = fw.sb("s5fim", [P, NSB])
    abm1 = fw.sb("s5abm1", [P, NSB])
    dve.tensor_tensor(out=den[:], in0=lam[:], in1=lam[:], op=ALU.mult)
    dve.tensor_tensor(out=t2[:], in0=tAi[:], in1=tAi[:], op=ALU.mult)
    dve.tensor_tensor(out=den[:], in0=den[:], in1=t2[:], op=ALU.add)
    dve.reciprocal(out=den[:], in_=den[:])
    dve.tensor_scalar_add(out=abm1[:], in0=abr[:], scalar1=-1.0)
    dve.tensor_tensor(out=fre[:], in0=abm1[:], in1=lam[:], op=ALU.mult)
    dve.tensor_tensor(out=t2[:], in0=abi[:], in1=tAi[:], op=ALU.mult)
    dve.tensor_tensor(out=fre[:], in0=fre[:], in1=t2[:], op=ALU.add)
    dve.tensor_tensor(out=fre[:], in0=fre[:], in1=den[:], op=ALU.mult)
    dve.tensor_tensor(out=fim[:], in0=abi[:], in1=lam[:], op=ALU.mult)
    dve.tensor_tensor(out=t2[:], in0=abm1[:], in1=tAi[:], op=ALU.mult)
    dve.tensor_tensor(out=fim[:], in0=fim[:], in1=t2[:], op=ALU.subtract)
    dve.tensor_tensor(out=fim[:], in0=fim[:], in1=den[:], op=ALU.mult)
    LB = [fw.sb("s5LB" + nm, [P, NUB, 128]) for nm in ("re", "im")]
    LC = [fw.sb("s5ZC" + nm, [P, NSB, 2, 16]) for nm in ("re", "im")]
    with fw.scope():
        Br = const("B_re_s", B_re, [P, NSB, 16]); Bi = const("B_im_s", B_im, [P, NSB, 16])
        bbr = fw.sb("s5bbr", [P, NSB, 16]); bbi = fw.sb("s5bbi", [P, NSB, 16]); bt_ = fw.sb("s5bt", [P, NSB, 16])
        fre_b = fre[:, :, None].bc([P, NSB, 16]); fim_b = fim[:, :, None].bc([P, NSB, 16])
        dve.tensor_tensor(out=bbr[:], in0=Br[:], in1=fre_b, op=ALU.mult)
        dve.tensor_tensor(out=bt_[:], in0=Bi[:], in1=fim_b, op=ALU.mult)
        dve.tensor_tensor(out=bbr[:], in0=bbr[:], in1=bt_[:], op=ALU.subtract)
        dve.tensor_tensor(out=bbi[:], in0=Bi[:], in1=fre_b, op=ALU.mult)
        dve.tensor_tensor(out=bt_[:], in0=Br[:], in1=fim_b, op=ALU.mult)
        dve.tensor_tensor(out=bbi[:], in0=bbi[:], in1=bt_[:], op=ALU.add)
        for li, (nm, src) in enumerate((("re", bbr), ("im", bbi))):
            Z = fw.sb("s5ZB" + nm, [P, NSB, 2, 16])
            dve.memset(ap=Z[:], constant=0.0)
            dve.tensor_copy(out=Z[0:64, :, 0, :], in_=src[0:64, :, :])
            dve.tensor_copy(out=Z[64:128, :, 1, :], in_=src[64:128, :, :])
            L = LB[li]
            for q4 in range(NUB):
                pe.transpose(out=PS[0][:, 0:128], in_=Z[:, 4 * q4:4 * q4 + 4, :, :].re("p a b c -> p (a b c)"), identity=ident[:])
                act.copy(out=L[:, q4, :], in_=PS[0][:, 0:128])
        Cr = const("C_re_s", C_re, [P, NSB, 16]); Ci = const("C_im_s", C_im, [P, NSB, 16])
        for li, (nm, src, sgn) in enumerate((("re", Cr, 1.0), ("im", Ci, -1.0))):
            Z = LC[li]
            dve.memset(ap=Z[:], constant=0.0)
            dve.tensor_scalar_mul(out=Z[0:64, :, 0, :], in0=src[0:64, :, :], scalar1=sgn)
            dve.tensor_scalar_mul(out=Z[64:128, :, 1, :], in0=src[64:128, :, :], scalar1=sgn)
    def make_tables():
      sinT = fw.sb("s5sinT", [P, NSB, 128]); cosT = fw.sb("s5cosT", [P, NSB, 128])
      CH = min(8, NSB)
      with fw.scope():
        ang = fw.sb("s5ang", [P, CH, 128])
        for c0 in range(0, NSB, CH):
            dve.tensor_tensor(out=ang[:], in0=iota[:, None, :].bc([P, CH, 128]), in1=thp[:, c0:c0 + CH, None].bc([P, CH, 128]), op=ALU.mult)
            sin_into(sinT[:, c0:c0 + CH, :], ang[:], [P, CH, 128], 0.0)
            sin_into(cosT[:, c0:c0 + CH, :], ang[:], [P, CH, 128], 0.5 * math.pi)
      return sinT, cosT
    angL = fw.sb("s5angL", [P, NSB]); dve.tensor_scalar_mul(out=angL[:], in0=thp[:], scalar1=128.0)
    snL, csL = sincos("s5tL", angL[:], [P, NSB])
    s5car_r = fw.sb("s5car_r", [P, NSB]); s5car_i = fw.sb("s5car_i", [P, NSB])
    dve.memset(ap=s5car_r[:], constant=0.0); dve.memset(ap=s5car_i[:], constant=0.0)

    NM = NSC + 2
    scT = fw.sb("scT", [P, KC, NM], F32R)
    with fw.scope():
        c_s = fw.sb("c_s", [NM, D]); ld(c_s[:], c17[:])
        csl = fw.sb("csl", [NM, D]); act.activation(out=csl[:], in_=c_s[:], func=AF.Silu)
        for kc in range(KC):
            pe.transpose(out=PS[1][:, kc * NM:(kc + 1) * NM], in_=csl[:, kc * 128:(kc + 1) * 128], identity=ident[0:NM, 0:NM])
        act.copy(out=scT[:].re("p k n -> p (k n)"), in_=PS[1][:, 0:KC * NM])
    modT = fw.sb("modT", [P, 3 * KC, NM])
    for fb in range(3 * KC):
        w = load_w(w_ada, fb)
        psb = PS[2 + fb % 2]
        mm_acc(psb[:, 0:NM], w, KC, lambda kc: scT[:, kc, :])
        act.activation(out=modT[:, fb, :], in_=psb[:, 0:NM], func=AF.Identity, bias=b_ada_s[:, fb:fb + 1], scale=1.0)
    sceff = fw.sb("sceff", [P, KC, NM])
    dve.tensor_scalar_add(out=sceff[:], in0=modT[:, KC:2 * KC, :], scalar1=1.0)
    dve.tensor_tensor(out=sceff[:], in0=sceff[:], in1=norm_g_s[:, :, None].bc([P, KC, NM]), op=ALU.mult)
    sc_p = fw.sb("sc_p", [P, KC]); sh_p = fw.sb("sh_p", [P, KC]); gt_p = fw.sb("gt_p", [P, KC])
    dve.tensor_copy(out=sc_p[:], in_=sceff[:, :, NSC]); dve.tensor_copy(out=sh_p[:], in_=modT[:, 0:KC, NSC])
    dve.tensor_copy(out=gt_p[:], in_=modT[:, 2 * KC:3 * KC, NSC])

    hT = fw.sb("hT", [P, KC, TW], F32R)
    rw_car = fw.sb("rw_car", [P, NRB]); dve.memset(ap=rw_car[:], constant=0.0)
    Hst = [fw.sb("Hst%d" % h, [P, 64]) for h in range(C.H)]
    for h in range(C.H):
        dve.memset(ap=Hst[h][:], constant=0.0)

    eps_rms = fw.sb("eps_rms", [P, 1]); dve.memset(ap=eps_rms[:], constant=1e-6)
    eps_gn = fw.sb("eps_gn", [P, 1]); dve.memset(ap=eps_gn[:], constant=64e-5)

    def rstd_of(ss, n):
        act.activation(out=ss, in_=ss, func=AF.Sqrt, bias=eps_rms[0:n, :], scale=1.0 / D)
        dve.reciprocal(out=ss, in_=ss)

    xt_tl = [fw.tl("xt%d" % i) for i in range(2)]
    ssq = fw.sb("ssq", [P, 2])
    hlast = fw.sb("hlast", [P, KC])

    def phase_x(xd, is_b):
      with fw.scope():
        xt = [fw.sb("xt%d" % i, [P, D]) for i in range(2)]
        xsq = fw.sb("xsq", [P, D], mybir.dt.bfloat16)
        for tt in range(TP // 128):
            i = tt % 2
            fw.dma("sp", xt_tl[i], xt[i][:], xd[tt * 128:(tt + 1) * 128, :])
            s = ssq[:, i:i + 1]
            act.activation(out=xsq[:], in_=xt[i][:], func=AF.Square, accum_out=s)
            rstd_of(s, 128)
            act.activation(out=xt[i][:], in_=xt[i][:], func=AF.Copy, scale=s)
            for kc in range(KC):
                psb = PS[(kc // 4) % 2]
                pe.transpose(out=psb[:, (kc % 4) * 128:(kc % 4 + 1) * 128], in_=xt[i][:, kc * 128:(kc + 1) * 128], identity=ident[:])
                act.activation(out=hT[:, kc, tt * 128:(tt + 1) * 128], in_=psb[:, (kc % 4) * 128:(kc % 4 + 1) * 128],
                               func=AF.Identity, scale=sc_p[:, kc:kc + 1], bias=sh_p[:, kc:kc + 1])
                if is_b and tt == TP // 128 - 1:
                    act.activation(out=hlast[:, kc:kc + 1], in_=psb[:, (kc % 4) * 128 + 127:(kc % 4) * 128 + 128],
                                   func=AF.Identity, scale=sc_p[:, kc:kc + 1], bias=sh_p[:, kc:kc + 1])

    NPASS = C.NPASS
    W1 = TP + NSC
    UID = [0]

    def tiles(is_last):
        t = list(C.TT)
        if is_last:
            t.append((TP, NS2))
        return t

    barrier = fw.barrier
    STOP = float(os.environ.get("K_STOP", "99"))

    class StopBuild(Exception):
        pass

    def stop(n):
        return STOP <= n

    def chk(n):
        if STOP <= n:
            while len(fw.es_stack) > 1:
                fw.es_stack.pop().close()
            raise StopBuild()

    osT = fw.sb("osT", [P, NUB, TW], F32R)
    orT = fw.sb("orT", [P, NHP, TW], F32R)
    x0r = fw.sb("x0r", [P, NSB, NSC]); x0i = fw.sb("x0i", [P, NSB, NSC])
    x1r = fw.sb("x1r", [P, NSB, NSC]); x1i = fw.sb("x1i", [P, NSB, NSC])

    def gelu_inplace(v, tmp_v):
        dve.tensor_tensor(out=tmp_v, in0=v, in1=v, op=ALU.mult)
        dve.tensor_scalar(out=tmp_v, in0=tmp_v, scalar1=0.044715, scalar2=1.0, op0=ALU.mult, op1=ALU.add)
        dve.tensor_tensor(out=tmp_v, in0=tmp_v, in1=v, op=ALU.mult)
        act.activation(out=tmp_v, in_=tmp_v, func=AF.Sigmoid, scale=2.0 * math.sqrt(2.0 / math.pi))
        dve.tensor_tensor(out=v, in0=v, in1=tmp_v, op=ALU.mult)

    def s5_phase(is_last, state_only=False):
        Wd = TW if is_last else TP
        with fw.scope():
            lsb = fw.sb
            sinT, cosT = make_tables()
            chk(2.1)
            uT = lsb("uT", [P, TW]); yT = lsb("yT", [P, NUB, TW], F32R)
            scanscope = fw.scope(); scanscope.__enter__()
            zr = lsb("s5zr", [P, 4, 128]); zi = lsb("s5zi", [P, 4, 128])
            q1 = lsb("s5q1", [P, 4, 128])
            sr = lsb("s5sr", [P, 4, 128]); si = lsb("s5si", [P, 4, 128])
            xr = zr; xi = zi
            cq = [lsb("s5cq%d" % i, [P, 4]) for i in range(4)]
            LBz = [lsb("s5LBz%d" % i, [P, 4, 128]) for i in range(2)]
            sq = [lsb("s5sq%d" % i, [P, 4, NSC]) for i in range(2)]
            for blk in range(NUB):
                chk(2.12)
                w = load_w(w_in, C.OFF_U // 128 + blk)
                chk(2.15)
                for (t0, tn) in tiles(is_last):
                    psb = PS[2 + (t0 // 512) % 2]
                    mm_acc(psb[:, 0:tn], w, KC, lambda kc: hT[:, kc, t0:t0 + tn].r32())
                    chk(2.17)
                    act.copy(out=uT[:, t0:t0 + tn], in_=psb[:, 0:tn])
                chk(2.2)
                for li in range(2):
                    for j in range(4):
                        dve.tensor_scalar_mul(out=LBz[li][:, j, :], in0=LB[li][:, blk, :], scalar1=rowmask[:, j:j + 1])
                s0 = blk * 4; sl = slice(s0, s0 + 4)
                cT = cosT[:, sl, :]; sT = sinT[:, sl, :]
                for ts in range(TP // 128):
                    bur = PS[4 + (ts % 2) * 2]; bui = PS[5 + (ts % 2) * 2]
                    for j in range(4):
                        for (L, psb) in ((LBz[0], bur), (LBz[1], bui)):
                            pe.matmul(out=psb[:, j * 128:(j + 1) * 128], lhsT=L[:, j, :],
                                      rhs=uT[:, ts * 128:(ts + 1) * 128],
                                      start=True, stop=True, inc=(j == 3))
                    chk(2.4)
                    br = bur[:, :].re("p (a b) -> p a b", a=4); bi = bui[:, :].re("p (a b) -> p a b", a=4)
                    dve.tensor_tensor(out=zr[:], in0=br, in1=cT, op=ALU.mult)
                    dve.tensor_tensor(out=q1[:], in0=bi, in1=sT, op=ALU.mult)
                    dve.tensor_tensor(out=zr[:], in0=zr[:], in1=q1[:], op=ALU.add)
                    dve.tensor_tensor(out=zi[:], in0=bi, in1=cT, op=ALU.mult)
                    dve.tensor_tensor(out=q1[:], in0=br, in1=sT, op=ALU.mult)
                    dve.tensor_tensor(out=zi[:], in0=zi[:], in1=q1[:], op=ALU.subtract)
                    chk(2.5)
                    for j in range(4):
                        s = s0 + j
                        dve.tensor_tensor_scan(out=sr[:, j, :], data0=mag[:, s:s + 1].bc([P, 128]), data1=zr[:, j, :],
                                               initial=s5car_r[:, s:s + 1], op0=ALU.mult, op1=ALU.add)
                        dve.tensor_tensor_scan(out=si[:, j, :], data0=mag[:, s:s + 1].bc([P, 128]), data1=zi[:, j, :],
                                               initial=s5car_i[:, s:s + 1], op0=ALU.mult, op1=ALU.add)
                    chk(2.6)
                    la = sr[:, :, 127]; lb = si[:, :, 127]
                    dve.tensor_tensor(out=cq[0][:], in0=csL[:, sl], in1=la, op=ALU.mult)
                    dve.tensor_tensor(out=cq[1][:], in0=snL[:, sl], in1=lb, op=ALU.mult)
                    dve.tensor_tensor(out=cq[2][:], in0=snL[:, sl], in1=la, op=ALU.mult)
                    dve.tensor_tensor(out=cq[3][:], in0=csL[:, sl], in1=lb, op=ALU.mult)
                    dve.tensor_tensor(out=s5car_r[:, sl], in0=cq[0][:], in1=cq[1][:], op=ALU.subtract)
                    dve.tensor_tensor(out=s5car_i[:, sl], in0=cq[2][:], in1=cq[3][:], op=ALU.add)
                    if state_only:
                        continue
                    dve.tensor_tensor(out=xr[:], in0=sr[:], in1=cT, op=ALU.mult)
                    dve.tensor_tensor(out=q1[:], in0=si[:], in1=sT, op=ALU.mult)
                    dve.tensor_tensor(out=xr[:], in0=xr[:], in1=q1[:], op=ALU.subtract)
                    dve.tensor_tensor(out=xi[:], in0=sr[:], in1=sT, op=ALU.mult)
                    dve.tensor_tensor(out=q1[:], in0=si[:], in1=cT, op=ALU.mult)
                    dve.tensor_tensor(out=xi[:], in0=xi[:], in1=q1[:], op=ALU.add)
                    chk(2.7)
                    psy = PS[0]
                    for j in range(4):
                        s = s0 + j
                        pe.matmul(out=psy[32 * j:32 * j + 32, 0:128], lhsT=LC[0][:, s, :, :].re("p a b -> p (a b)"),
                                  rhs=xr[:, j, :], start=True, stop=False, tile_position=(0, 32 * j), inc=False)
                        pe.matmul(out=psy[32 * j:32 * j + 32, 0:128], lhsT=LC[1][:, s, :, :].re("p a b -> p (a b)"),
                                  rhs=xi[:, j, :], start=False, stop=True, tile_position=(0, 32 * j), inc=(j == 3))
                    dve.scalar_tensor_tensor(out=yT[:, blk, ts * 128:(ts + 1) * 128], in0=uT[:, ts * 128:(ts + 1) * 128],
                                             scalar=Dsk_s[:, blk:blk + 1], in1=psy[:, 0:128], op0=ALU.mult, op1=ALU.add)
                if is_last:
                    psr = PS[6]; psi = PS[7]
                    for j in range(4):
                        for (L, psb) in ((LBz[0], psr), (LBz[1], psi)):
                            pe.matmul(out=psb[:, j * NSC:(j + 1) * NSC], lhsT=L[:, j, :],
                                      rhs=uT[:, TP:TP + NSC],
                                      start=True, stop=True, inc=(j == 3))
                    abr_b = abr[:, sl, None].bc([P, 4, NSC]); abi_b = abi[:, sl, None].bc([P, 4, NSC])
                    dve.tensor_tensor(out=sq[0][:], in0=x0r[:, sl, :], in1=abr_b, op=ALU.mult)
                    dve.tensor_tensor(out=sq[1][:], in0=x0i[:, sl, :], in1=abi_b, op=ALU.mult)
                    dve.tensor_tensor(out=sq[0][:], in0=sq[0][:], in1=sq[1][:], op=ALU.subtract)
                    dve.tensor_tensor(out=x1r[:, sl, :], in0=sq[0][:], in1=psr[:, 0:4 * NSC].re("p (a b) -> p a b", a=4), op=ALU.add)
                    dve.tensor_tensor(out=sq[0][:], in0=x0i[:, sl, :], in1=abr_b, op=ALU.mult)
                    dve.tensor_tensor(out=sq[1][:], in0=x0r[:, sl, :], in1=abi_b, op=ALU.mult)
                    dve.tensor_tensor(out=sq[0][:], in0=sq[0][:], in1=sq[1][:], op=ALU.add)
                    dve.tensor_tensor(out=x1i[:, sl, :], in0=sq[0][:], in1=psi[:, 0:4 * NSC].re("p (a b) -> p a b", a=4), op=ALU.add)
                    psy = PS[0]
                    for j in range(4):
                        s = s0 + j
                        pe.matmul(out=psy[32 * j:32 * j + 32, 0:NSC], lhsT=LC[0][:, s, :, :].re("p a b -> p (a b)"),
                                  rhs=x1r[:, s, :], start=True, stop=False, tile_position=(0, 32 * j), inc=False)
                        pe.matmul(out=psy[32 * j:32 * j + 32, 0:NSC], lhsT=LC[1][:, s, :, :].re("p a b -> p (a b)"),
                                  rhs=x1i[:, s, :], start=False, stop=True, tile_position=(0, 32 * j), inc=(j == 3))
                    dve.scalar_tensor_tensor(out=yT[:, blk, TP:TP + NSC], in0=uT[:, TP:TP + NSC],
                                             scalar=Dsk_s[:, blk:blk + 1], in1=psy[:, 0:NSC], op0=ALU.mult, op1=ALU.add)
            chk(2.8)
            scanscope.__exit__(None, None, None)
            if state_only:
                return
            gtmp = lsb("gtmp", [P, TW]); gs = lsb("gs", [P, TW]); zs = lsb("zs", [P, TW])
            Wy = W1 if is_last else TP
            for blk in range(NUB):
                gelu_inplace(yT[:, blk, 0:Wy], gtmp[:, 0:Wy])
            gt_tiles = list(C.TT) + ([(TP, NSC)] if is_last else [])
            for jb in range(NUB):
                w = load_w(w_glu, jb, nk=NUB)
                w2_ = load_w(w_in, C.OFF_Z // 128 + jb)
                for (t0, tn) in gt_tiles:
                    psb = PS[2]; psz = PS[3]
                    mm_acc(psb[:, 0:tn], w, NUB, lambda kc: yT[:, kc, t0:t0 + tn])
                    act.activation(out=gs[:, t0:t0 + tn], in_=psb[:, 0:tn], func=AF.Sigmoid, bias=b_glu_s[:, jb:jb + 1], scale=1.0)
                    mm_acc(psz[:, 0:tn], w2_, KC, lambda kc: hT[:, kc, t0:t0 + tn].r32())
                    act.activation(out=zs[:, t0:t0 + tn], in_=psz[:, 0:tn], func=AF.Silu)
                    dve.tensor_tensor(out=gs[:, t0:t0 + tn], in0=gs[:, t0:t0 + tn], in1=zs[:, t0:t0 + tn], op=ALU.mult)
                    dve.tensor_tensor(out=osT[:, jb, t0:t0 + tn], in0=gs[:, t0:t0 + tn], in1=yT[:, jb, t0:t0 + tn], op=ALU.mult)
            barrier()

    EM05 = math.exp(-0.5)

    def rwkv_phase(is_last, state_only=False):
        Wd = W1 if is_last else TP
        with fw.scope():
            lsb = fw.sb
            Pb = lsb("Pb", [P, 1 + TW]); tmpA = lsb("tmpA", [P, W1])
            lor = lsb("lor", [P, W1])
            rr = lsb("rr", [P, W1]); kr = lsb("kr", [P, W1]); vv = lsb("vv", [P, W1])
            ldc = lsb("ldc", [P, W1]); alr = lsb("alr", [P, W1]); kkn = lsb("kkn", [P, W1]); kmod = lsb("kmod", [P, W1])
            bb = lsb("bb", [P, W1]); cl = lsb("cl", [P, TP]); e1 = lsb("e1", [P, TP]); e2 = Tile(kr.ap[:, 0:TP], kr.buf)
            cend = lsb("cend", [P, NCH]); PC = lsb("PC", [P, NCH])
            AR = lsb("AR", [P, NCH, 2, 64]); BK = lsb("BK", [P, NCH, 2, 64]); AV = lsb("AV", [P, NCH, 2, 64]); BKH = lsb("BKH", [P, NCH, 2, 64])
            bonus = alr; oT = Pb; osq = kkn
            CB = 3; NSL = 2 * CB
            GTs = [lsb("GTs%d" % e, [P, P]) for e in range(2 * NSL)]
            X = [lsb("X%d" % e, [P, 64]) for e in range(2 * NSL)]
            BKHt = [lsb("BKHt%d" % e, [P, 64]) for e in range(2 * NSL)]
            PW = [lsb("PW%d" % e, [64, 3, 64]) for e in range(NSL)]
            Ys = [lsb("Ys%d" % e, [64, 64]) for e in range(2 * NSL)]
            WTs = [lsb("WTs%d" % e, [P, 64]) for e in range(2 * NSL)]

            def proj_shift(rb, dst):
                w = load_w(w_in, C.OFF_RW // 128 + rb)
                for (t0, tn) in tiles(is_last):
                    psb = PS[6 + (t0 // 512) % 2]
                    mm_acc(psb[:, 0:tn], w, KC, lambda kc: hT[:, kc, t0:t0 + tn].r32())
                    act.copy(out=Pb[:, 1 + t0:1 + t0 + tn], in_=psb[:, 0:tn])
                act.copy(out=Pb[:, 0:1], in_=rw_car[:, rb:rb + 1])
                dve.tensor_scalar_mul(out=tmpA[:, 0:TP], in0=Pb[:, 0:TP], scalar1=mu_s[:, rb:rb + 1])
                dve.scalar_tensor_tensor(out=dst[:, 0:TP], in0=Pb[:, 1:TP + 1], scalar=omu_s[:, rb:rb + 1], in1=tmpA[:, 0:TP],
                                         op0=ALU.mult, op1=ALU.add)
                dve.tensor_copy(out=rw_car[:, rb:rb + 1], in_=Pb[:, TP:TP + 1])
                if is_last:
                    dve.tensor_scalar_mul(out=tmpA[:, TP:W1], in0=Pb[:, 1 + TP + NSC:1 + TP + NS2], scalar1=mu_s[:, rb:rb + 1])
                    dve.scalar_tensor_tensor(out=dst[:, TP:W1], in0=Pb[:, 1 + TP:1 + TP + NSC], scalar=omu_s[:, rb:rb + 1],
                                             in1=tmpA[:, TP:W1], op0=ALU.mult, op1=ALU.add)

            ct_tiles = list(C.TT) + ([(TP, NSC)] if is_last else [])
            proj_shift(3 * NHP, lor)
            act.activation(out=lor[0:64, 0:Wd], in_=lor[0:64, 0:Wd], func=AF.Tanh)
            for hp in range(NHP):
                proj_shift(hp, rr); proj_shift(NHP + hp, kr); proj_shift(2 * NHP + hp, vv)
                for (t0, tn) in ct_tiles:
                    psb = PS[6]
                    pe.matmul(out=psb[:, 0:tn], lhsT=w2_s[0:64, hp * 128:(hp + 1) * 128], rhs=lor[0:64, t0:t0 + tn], start=True, stop=True)
                    act.activation(out=ldc[:, t0:t0 + tn], in_=psb[:, 0:tn], func=AF.Sigmoid, bias=w0_s[:, hp:hp + 1], scale=1.0)
                    psb = PS[7]
                    pe.matmul(out=psb[:, 0:tn], lhsT=a2_s[64:128, hp * 128:(hp + 1) * 128], rhs=lor[64:128, t0:t0 + tn], start=True, stop=True)
                    act.activation(out=alr[:, t0:t0 + tn], in_=psb[:, 0:tn], func=AF.Sigmoid, bias=a0_s[:, hp:hp + 1], scale=1.0)
                dve.tensor_scalar_mul(out=ldc[:, 0:Wd], in0=ldc[:, 0:Wd], scalar1=-EM05)
                dve.tensor_scalar_mul(out=kkn[:, 0:Wd], in0=kr[:, 0:Wd], scalar1=kk_s[:, hp:hp + 1])
                dve.tensor_tensor(out=tmpA[:, 0:Wd], in0=kkn[:, 0:Wd], in1=kkn[:, 0:Wd], op=ALU.mult)
                for (t0, tn) in ct_tiles:
                    psb = PS[6]
                    pe.matmul(out=psb[:, 0:tn], lhsT=onesbd[:], rhs=tmpA[:, t0:t0 + tn], start=True, stop=True)
                    act.activation(out=bb[:, t0:t0 + tn], in_=psb[:, 0:tn], func=AF.Sqrt)
                dve.tensor_scalar_max(out=bb[:, 0:Wd], in0=bb[:, 0:Wd], scalar1=1e-12)
                dve.reciprocal(out=bb[:, 0:Wd], in_=bb[:, 0:Wd])
                dve.tensor_tensor(out=kkn[:, 0:Wd], in0=kkn[:, 0:Wd], in1=bb[:, 0:Wd], op=ALU.mult)
                dve.tensor_scalar(out=kmod[:, 0:Wd], in0=alr[:, 0:Wd], scalar1=-1.0, scalar2=ka_s[:, hp:hp + 1], op0=ALU.add, op1=ALU.mult)
                dve.scalar_tensor_tensor(out=kmod[:, 0:Wd], in0=kmod[:, 0:Wd], scalar=1.0, in1=kr[:, 0:Wd], op0=ALU.add, op1=ALU.mult)
                dve.tensor_tensor(out=bb[:, 0:Wd], in0=kkn[:, 0:Wd], in1=alr[:, 0:Wd], op=ALU.mult)
                if not state_only:
                    dve.scalar_tensor_tensor(out=tmpA[:, 0:Wd], in0=rr[:, 0:Wd], scalar=rk_s[:, hp:hp + 1], in1=kmod[:, 0:Wd], op0=ALU.mult, op1=ALU.mult)
                for (t0, tn) in ([] if state_only else ct_tiles):
                    psb = PS[6]
                    pe.matmul(out=psb[:, 0:tn], lhsT=onesbd[:], rhs=tmpA[:, t0:t0 + tn], start=True, stop=True)
                    dve.tensor_tensor(out=bonus[:, t0:t0 + tn], in0=psb[:, 0:tn], in1=vv[:, t0:t0 + tn], op=ALU.mult)
                dve.tensor_tensor_scan(out=cl[:], data0=resetm[:], data1=ldc[:, 0:TP], initial=0.0, op0=ALU.mult, op1=ALU.add)
                dve.tensor_copy(out=cend[:], in_=cl[:].re("p (c t) -> p c t", t=64)[:, :, 63])
                act.activation(out=PC[:], in_=cend[:], func=AF.Exp)
                c4 = lambda t: t[:, 0:TP].re("p (c t) -> p c t", t=64)
                act.activation(out=e1[:], in_=cl[:], func=AF.Exp)
                dve.tensor_tensor(out=c4(AR)[:, :, :] if False else AR[:, :, 1, :], in0=c4(rr), in1=c4(e1), op=ALU.mult)
                dve.tensor_tensor(out=e2[:], in0=cl[:], in1=ldc[:, 0:TP], op=ALU.subtract)
                act.activation(out=e2[:], in_=e2[:], func=AF.Exp)
                dve.scalar_tensor_tensor(out=AR[:, :, 0, :], in0=c4(kkn), scalar=-1.0, in1=c4(e2), op0=ALU.mult, op1=ALU.mult)
                dve.tensor_copy(out=AV[:, :, 0, :], in_=AR[:, :, 0, :])
                dve.tensor_copy(out=AV[:, :, 1, :], in_=c4(vv))
                act.activation(out=e1[:], in_=cl[:], func=AF.Exp, scale=-1.0)
                dve.tensor_tensor(out=BK[:, :, 0, :], in0=c4(bb), in1=c4(e1), op=ALU.mult)
                dve.tensor_tensor(out=BK[:, :, 1, :], in0=c4(kmod), in1=c4(e1), op=ALU.mult)
                dve.tensor_tensor(out=c4(e2), in0=cend[:, :, None].bc([P, NCH, 64]), in1=c4(cl), op=ALU.subtract)
                act.activation(out=e2[:], in_=e2[:], func=AF.Exp)
                dve.tensor_tensor(out=BKH[:, :, 0, :], in0=c4(bb), in1=c4(e2), op=ALU.mult)
                dve.tensor_tensor(out=BKH[:, :, 1, :], in0=c4(kmod), in1=c4(e2), op=ALU.mult)
                def fl(v):
                    return v.re("p a b -> p (a b)")

                def stage1(pairs, sb):
                    n = len(pairs)
                    for i, (c, e) in enumerate(pairs):
                        pb = 64 * e; B = PS[i]
                        pe.matmul(out=B[:, 0:128], lhsT=fl(BK[pb:pb + 64, c, :, :]), rhs=fl(AR[pb:pb + 64, c, :, :]),
                                  start=True, stop=True, tile_position=(pb, 0))
                        pe.transpose(out=B[:, 128:192], in_=fl(AV[pb:pb + 64, c, :, :]), identity=ident[pb:pb + 64, pb:pb + 64],
                                     tile_position=(pb, 0))
                        pe.transpose(out=B[:, 192:256], in_=fl(BKH[pb:pb + 64, c, :, :]), identity=ident[pb:pb + 64, pb:pb + 64],
                                     tile_position=(pb, 0))
                    for i, (c, e) in enumerate(pairs):
                        B = PS[i]; r = sb + i
                        dve.tensor_tensor(out=GTs[r][:], in0=B[:, 0:128], in1=maskgt[:], op=ALU.mult)
                        act.copy(out=X[r][:], in_=B[:, 128:192])
                        act.copy(out=BKHt[r][:], in_=B[:, 192:256])
                    yield
                    for i, (c, e) in enumerate(pairs):
                        pe.transpose(out=PS[i][0:64, 0:64], in_=GTs[sb + i][0:64, 0:64], identity=ident[0:64, 0:64])
                    for i, (c, e) in enumerate(pairs):
                        r = sb + i; pw = PW[i]
                        act.copy(out=pw[:, 1, :], in_=PS[i][0:64, 0:64])
                        dve.tensor_copy(out=pw[:, 0, :], in_=GTs[r][0:64, 0:64])
                        dve.tensor_tensor(out=pw[:, 2, :], in0=GTs[r][0:64, 0:64], in1=ident[0:64, 0:64], op=ALU.add)
                    yield
                    for lvl in range(1, 6):
                        for i in range(n):
                            pw = PW[i]; B = PS[i]
                            pe.matmul(out=B[0:64, 0:64], lhsT=pw[:, 1, :], rhs=pw[:, 0, :], start=True, stop=True)
                            pe.matmul(out=B[0:64, 64:128], lhsT=pw[:, 0, :], rhs=pw[:, 1, :], start=True, stop=True)
                        for i in range(n):
                            act.copy(out=fl(PW[i][:, 0:2, :]), in_=PS[i][0:64, 0:128])
                        yield
                        for i in range(n):
                            pw = PW[i]
                            pe.matmul(out=PS[i][0:64, 128:192], lhsT=pw[:, 1, :], rhs=pw[:, 2, :], start=True, stop=True)
                        for i in range(n):
                            pw = PW[i]
                            dve.tensor_tensor(out=pw[:, 2, :], in0=pw[:, 2, :], in1=PS[i][0:64, 128:192], op=ALU.add)
                        yield
                    for i, (c, e) in enumerate(pairs):
                        pb = 64 * e; r = sb + i; B = PS[i]
                        pe.matmul(out=B[0:64, 256:320], lhsT=GTs[r][64:128, 0:64], rhs=X[r][64:128, :], start=True, stop=True,
                                  tile_position=(64, 0))
                    for i, (c, e) in enumerate(pairs):
                        pb = 64 * e; r = sb + i; B = PS[i]
                        pe.matmul(out=B[pb:pb + 64, 320:384], lhsT=X[r][0:64, :], rhs=PW[i][:, 2, :], start=True, stop=True,
                                  tile_position=(0, pb))
                    for i, (c, e) in enumerate(pairs):
                        pb = 64 * e; r = sb + i; B = PS[i]
                        act.copy(out=PW[i][:, 0, :], in_=B[0:64, 256:320])
                        act.copy(out=WTs[r][pb:pb + 64, :], in_=B[pb:pb + 64, 320:384])
                    yield
                    for i in range(n):
                        pe.matmul(out=PS[i][0:64, 384:448], lhsT=PW[i][:, 2, :], rhs=PW[i][:, 0, :], start=True, stop=True)
                    for i in range(n):
                        act.copy(out=Ys[sb + i][:], in_=PS[i][0:64, 384:448])
                    yield

                def stage2(chs, sb):
                    for ci, c in enumerate(chs):
                        for e in range(2):
                            pb = 64 * e; r = sb + 2 * ci + e; Hs = Hst[2 * hp + e]
                            pe.matmul(out=PS[6 + e][0:64, 0:64], lhsT=WTs[r][pb:pb + 64, :], rhs=Hs[pb:pb + 64, :], start=True, stop=True,
                                      tile_position=(pb, 0))
                        for e in range(2):
                            r = sb + 2 * ci + e
                            dve.tensor_tensor(out=X[r][0:64, :], in0=PS[6 + e][0:64, 0:64], in1=Ys[r][:], op=ALU.add)
                        yield
                        for e in range(2):
                            pb = 64 * e; r = sb + 2 * ci + e; Hs = Hst[2 * hp + e]; B = PS[6 + e]
                            if not state_only:
                                pe.matmul(out=B[pb:pb + 64, 64:128], lhsT=Hs[pb:pb + 64, :], rhs=AR[pb:pb + 64, c, 1, :], start=True, stop=True,
                                          tile_position=(pb, pb))
                                pe.matmul(out=B[pb:pb + 64, 128:192], lhsT=X[r][:], rhs=GTs[r][:, 64:128], start=True, stop=True,
                                          tile_position=(0, pb))
                            pe.matmul(out=B[pb:pb + 64, 192:256], lhsT=BKHt[r][:], rhs=X[r][:], start=True, stop=True, tile_position=(0, pb))
                        for e in range(2):
                            pb = 64 * e; Hs = Hst[2 * hp + e]; B = PS[6 + e]
                            if not state_only:
                                act.copy(out=oT[pb:pb + 64, c * 64:(c + 1) * 64], in_=B[pb:pb + 64, 64:128])
                                dve.tensor_tensor(out=oT[pb:pb + 64, c * 64:(c + 1) * 64], in0=oT[pb:pb + 64, c * 64:(c + 1) * 64],
                                                  in1=B[pb:pb + 64, 128:192], op=ALU.add)
                            dve.scalar_tensor_tensor(out=Hs[pb:pb + 64, :], in0=Hs[pb:pb + 64, :], scalar=PC[pb:pb + 64, c:c + 1],
                                                     in1=B[pb:pb + 64, 192:256], op0=ALU.mult, op1=ALU.add)
                        yield

                batches = [list(range(c0, min(c0 + CB, NCH))) for c0 in range(0, NCH, CB)]
                prev = None
                for bi in range(len(batches) + 1):
                    g1 = None; g2 = None
                    if bi < len(batches):
                        chs = batches[bi]
                        g1 = stage1([(c, e) for c in chs for e in range(2)], (bi % 2) * NSL)
                    if prev is not None:
                        g2 = stage2(prev[0], prev[1])
                    while g1 is not None or g2 is not None:
                        for _ in range(2):
                            if g1 is not None:
                                try:
                                    next(g1)
                                except StopIteration:
                                    g1 = None
                        if g2 is not None:
                            try:
                                next(g2)
                            except StopIteration:
                                g2 = None
                    prev = (batches[bi], (bi % 2) * NSL) if bi < len(batches) else None
                if is_last:
                    rwkv_sample(hp, rr, ldc, kmod, vv, kkn, bb, oT, tmpA)
                if state_only:
                    continue
                dve.tensor_tensor(out=osq[:, 0:TP], in0=oT[:, 0:TP], in1=oT[:, 0:TP], op=ALU.mult)
                for (t0, tn) in C.TT:
                    psm = PS[6]; psq = PS[7]
                    pe.matmul(out=psm[:, 0:tn], lhsT=onesbd[:], rhs=oT[:, t0:t0 + tn], start=True, stop=True)
                    pe.matmul(out=psq[:, 0:tn], lhsT=onesbd[:], rhs=osq[:, t0:t0 + tn], start=True, stop=True)
                    mean = tmpA[:, t0:t0 + tn]; var = osq[:, t0:t0 + tn]
                    act.mul(out=mean, in_=psm[:, 0:tn], mul=1.0 / 64)
                    dve.tensor_tensor(out=e1[:, 0:tn], in0=mean, in1=mean, op=ALU.mult)
                    dve.scalar_tensor_tensor(out=var, in0=psq[:, 0:tn], scalar=1.0 / 64, in1=e1[:, 0:tn], op0=ALU.mult, op1=ALU.subtract)
                    act.activation(out=var, in_=var, func=AF.Sqrt, bias=eps_gn[:], scale=1.0)
                    dve.reciprocal(out=var, in_=var)
                    dve.tensor_tensor(out=oT[:, t0:t0 + tn], in0=oT[:, t0:t0 + tn], in1=mean, op=ALU.subtract)
                    dve.tensor_tensor(out=oT[:, t0:t0 + tn], in0=oT[:, t0:t0 + tn], in1=var, op=ALU.mult)
                dve.tensor_scalar(out=oT[:, 0:TP], in0=oT[:, 0:TP], scalar1=gnw_s[:, hp:hp + 1], scalar2=gnb_s[:, hp:hp + 1], op0=ALU.mult, op1=ALU.add)
                dve.tensor_tensor(out=oT[:, 0:TP], in0=oT[:, 0:TP], in1=bonus[:, 0:TP], op=ALU.add)
                w = load_w(w_in, C.OFF_RWZ // 128 + hp)
                for (t0, tn) in C.TT:
                    psz = PS[6]
                    mm_acc(psz[:, 0:tn], w, KC, lambda kc: hT[:, kc, t0:t0 + tn].r32())
                    act.activation(out=tmpA[:, t0:t0 + tn], in_=psz[:, 0:tn], func=AF.Silu)
                    dve.tensor_tensor(out=orT[:, hp, t0:t0 + tn], in0=oT[:, t0:t0 + tn], in1=tmpA[:, t0:t0 + tn], op=ALU.mult)
            barrier()

    NPT = (NSC * C.H) // 128 if NSC * C.H >= 128 else 1
    NPP = min(128, NSC * C.H)
    BPT = NPP // C.H
    tokm = fw.sb("tokm", [NSC, 2, 6, 64])
    o_bh_all = fw.sb("o_bh_all", [NPP, NPT, 64])

    def rwkv_sample(hp, rr, ldc, kmod, vv, kkn, bb, oT, tmpA):
        act.activation(out=tmpA[:, TP:W1], in_=ldc[:, TP:W1], func=AF.Exp)
        vecs = [rr, tmpA, kmod, vv, kkn, bb]
        for vi, t in enumerate(vecs):
            pe.transpose(out=PS[6 + vi // 3][0:NSC, (vi % 3) * 128:(vi % 3 + 1) * 128], in_=t[:, TP:W1], identity=ident[:])
        act.copy(out=tokm[:, :, 0:3, :], in_=PS[6][0:NSC, 0:384].re("p (v h n) -> p h v n", v=3, h=2))
        act.copy(out=tokm[:, :, 3:6, :], in_=PS[7][0:NSC, 0:384].re("p (v h n) -> p h v n", v=3, h=2))
        st(scr_a[:, 2 * hp:2 * hp + 2, :, :], tokm[:])

    def rwkv_sample_core():
        with fw.scope():
            lsb = fw.sb
            S = lsb("S_s", [NPP, 64, 64]); T1 = lsb("T1_s", [NPP, 64, 64]); vec = lsb("vec_s", [NPP, 6, 64])
            sa = lsb("sa_s", [NPP, 64]); ob = lsb("ob_s", [NPP, 64]); st1 = lsb("st1_s", [NPP, 4])
            gw = lsb("gw_s", [NPP, 64]); gb = lsb("gb_s", [NPP, 64]); rkb = lsb("rkb_s", [NPP, 64])
            ld(gw[:], gnw_bh[0:NPP, :]); ld(gb[:], gnb_bh[0:NPP, :]); ld(rkb[:], rk_bh[0:NPP, :])
            for pt in range(NPT):
                ld(S[:].re("p a b -> p (a b)"), wkv0[pt * NPP:(pt + 1) * NPP, :])
                ld(vec[:], View(scr_a.buf, scr_a.ap[pt * BPT:(pt + 1) * BPT].rearrange("b h v n -> (b h) v n")))
                R = vec[:, 0, :]; Dc = vec[:, 1, :]; K = vec[:, 2, :]; V = vec[:, 3, :]; KK = vec[:, 4, :]; Bv = vec[:, 5, :]
                bj = lambda v: v[:, None, :].bc([NPP, 64, 64])
                bi_ = lambda v: v[:, :, None].bc([NPP, 64, 64])
                dve.tensor_tensor(out=T1[:], in0=S[:], in1=bj(KK), op=ALU.mult)
                dve.tensor_reduce(out=sa[:], in_=T1[:], axis=AX.X, op=ALU.add)
                dve.tensor_scalar_mul(out=sa[:], in0=sa[:], scalar1=-1.0)
                dve.tensor_tensor(out=S[:], in0=S[:], in1=bj(Dc), op=ALU.mult)
                dve.tensor_tensor(out=T1[:], in0=bi_(sa[:]), in1=bj(Bv), op=ALU.mult)
                dve.tensor_tensor(out=S[:], in0=S[:], in1=T1[:], op=ALU.add)
                dve.tensor_tensor(out=T1[:], in0=bi_(V), in1=bj(K), op=ALU.mult)
                dve.tensor_tensor(out=S[:], in0=S[:], in1=T1[:], op=ALU.add)
                st(swkv_o[pt * NPP:(pt + 1) * NPP, :], S[:].re("p a b -> p (a b)"))
                dve.tensor_tensor(out=T1[:], in0=S[:], in1=bj(R), op=ALU.mult)
                dve.tensor_reduce(out=ob[:], in_=T1[:], axis=AX.X, op=ALU.add)
                dve.tensor_reduce(out=st1[:, 0:1], in_=ob[:], axis=AX.X, op=ALU.add)
                dve.tensor_scalar_mul(out=st1[:, 0:1], in0=st1[:, 0:1], scalar1=1.0 / 64)
                dve.tensor_scalar(out=ob[:], in0=ob[:], scalar1=st1[:, 0:1], scalar2=None, op0=ALU.subtract)
                dve.tensor_tensor(out=sa[:], in0=ob[:], in1=ob[:], op=ALU.mult)
                dve.tensor_reduce(out=st1[:, 1:2], in_=sa[:], axis=AX.X, op=ALU.add)
                act.activation(out=st1[:, 1:2], in_=st1[:, 1:2], func=AF.Sqrt, bias=eps_gn[0:NPP, :], scale=1.0 / 64)
                dve.reciprocal(out=st1[:, 1:2], in_=st1[:, 1:2])
                dve.tensor_scalar(out=ob[:], in0=ob[:], scalar1=st1[:, 1:2], scalar2=None, op0=ALU.mult)
                dve.tensor_tensor(out=ob[:], in0=ob[:], in1=gw[:], op=ALU.mult)
                dve.tensor_tensor(out=ob[:], in0=ob[:], in1=gb[:], op=ALU.add)
                dve.tensor_tensor(out=sa[:], in0=R, in1=K, op=ALU.mult)
                dve.tensor_tensor(out=sa[:], in0=sa[:], in1=rkb[:], op=ALU.mult)
                dve.tensor_reduce(out=st1[:, 2:3], in_=sa[:], axis=AX.X, op=ALU.add)
                dve.scalar_tensor_tensor(out=o_bh_all[:, pt, :], in0=V, scalar=st1[:, 2:3], in1=ob[:], op0=ALU.mult, op1=ALU.add)
                st(View(scr_b.buf, scr_b.ap[pt * BPT:(pt + 1) * BPT].rearrange("b (h n) -> (b h) n", n=64)), o_bh_all[:, pt, :])
            otok = lsb("otok_s", [NSC, C.DR]); ld(otok[:], scr_b[:])
            zt_ = lsb("zt_s", [P, NSC]); of_ = lsb("of_s", [P, NSC])
            for hp in range(NHP):
                pe.transpose(out=PS[6][:, 0:NSC], in_=otok[:, hp * 128:(hp + 1) * 128], identity=ident[0:NSC, 0:NSC])
                act.copy(out=of_[:], in_=PS[6][:, 0:NSC])
                w = load_w(w_in, C.OFF_RWZ // 128 + hp)
                mm_acc(PS[7][:, 0:NSC], w, KC, lambda kc: hT[:, kc, TP:TP + NSC].r32())
                act.activation(out=zt_[:], in_=PS[7][:, 0:NSC], func=AF.Silu)
                dve.tensor_tensor(out=orT[:, hp, TP:TP + NSC], in0=of_[:], in1=zt_[:], op=ALU.mult)
            barrier()

    def post_phase(is_last, p, yp):
        with fw.scope():
            lsb = fw.sb
            NTT = TP // 128
            xn = [lsb("xn%d" % i, [P, D]) for i in range(NTT)]
            xns = lsb("xns", [NSC, D]) if is_last else None
            g1 = lsb("g1", [P, TW]); g2 = lsb("g2", [P, TW]); MT = lsb("MT", [P, TW])
            sq = lsb("sqp", [P, D], mybir.dt.bfloat16); ssp = lsb("ssp", [P, NTT + 1])
            fg = lsb("fg", [P, D]); ld(fg[:], fing[:])
            pt_tiles = list(C.TT) + ([(TP, NSC)] if is_last else [])
            for tt in range(NTT):
                ld(xn[tt][:], xb[p * TP + tt * 128:p * TP + (tt + 1) * 128, :])
            if is_last:
                ld(xns[:], xs[:])
            for db in range(KC):
                wg1 = load_w(w_in, C.OFF_G1 // 128 + db)
                for (t0, tn) in pt_tiles:
                    psg = PS[0 + (t0 // 512) % 2]
                    mm_acc(psg[:, 0:tn], wg1, KC, lambda kc: hT[:, kc, t0:t0 + tn].r32())
                    act.activation(out=g1[:, t0:t0 + tn], in_=psg[:, 0:tn], func=AF.Sigmoid)
                wo = load_w(w_out, db)
                for (t0, tn) in pt_tiles:
                    ps1 = PS[2 + (t0 // 512) % 2]
                    mm_acc(ps1[:, 0:tn], wo, NUB, lambda kc: osT[:, kc, t0:t0 + tn])
                    dve.tensor_tensor(out=MT[:, t0:t0 + tn], in0=g1[:, t0:t0 + tn], in1=ps1[:, 0:tn], op=ALU.mult)
                wg2 = load_w(w_in, C.OFF_G2 // 128 + db)
                for (t0, tn) in pt_tiles:
                    psg2 = PS[0 + (t0 // 512) % 2]
                    mm_acc(psg2[:, 0:tn], wg2, KC, lambda kc: hT[:, kc, t0:t0 + tn].r32())
                    act.activation(out=g2[:, t0:t0 + tn], in_=psg2[:, 0:tn], func=AF.Sigmoid)
                    ps2 = PS[2 + (t0 // 512) % 2]
                    for kc in range(NHP):
                        pe.matmul(out=ps2[:, 0:tn], lhsT=wo[:, NUB + kc, :].r32(), rhs=orT[:, kc, t0:t0 + tn],
                                  start=(kc == 0), stop=(kc == NHP - 1), inc=(kc == NHP - 1))
                    dve.tensor_tensor(out=g2[:, t0:t0 + tn], in0=g2[:, t0:t0 + tn], in1=ps2[:, 0:tn], op=ALU.mult)
                    dve.tensor_tensor(out=MT[:, t0:t0 + tn], in0=MT[:, t0:t0 + tn], in1=g2[:, t0:t0 + tn], op=ALU.add)
                    if t0 < TP:
                        dve.tensor_scalar_mul(out=MT[:, t0:t0 + tn], in0=MT[:, t0:t0 + tn], scalar1=gt_p[:, db:db + 1])
                        for q in range(tn // 128):
                            tt = t0 // 128 + q
                            pst = PS[4 + q % 2]
                            pe.transpose(out=pst[:, 0:128], in_=MT[:, t0 + q * 128:t0 + (q + 1) * 128], identity=ident[:])
                            dve.tensor_tensor(out=xn[tt][:, db * 128:(db + 1) * 128], in0=xn[tt][:, db * 128:(db + 1) * 128],
                                              in1=pst[:, 0:128], op=ALU.add)
                    else:
                        dve.tensor_tensor(out=MT[:, TP:TP + NSC], in0=MT[:, TP:TP + NSC], in1=modT[:, 2 * KC + db, 0:NSC], op=ALU.mult)
                        pst = PS[6]
                        pe.transpose(out=pst[0:NSC, 0:128], in_=MT[:, TP:TP + NSC], identity=ident[:])
                        dve.tensor_tensor(out=xns[:, db * 128:(db + 1) * 128], in0=xns[:, db * 128:(db + 1) * 128],
                                          in1=pst[0:NSC, 0:128], op=ALU.add)
            for tt in range(NTT):
                s = ssp[:, tt:tt + 1]
                act.activation(out=sq[:], in_=xn[tt][:], func=AF.Square, accum_out=s)
                rstd_of(s, 128)
                dve.scalar_tensor_tensor(out=xn[tt][:], in0=xn[tt][:], scalar=s, in1=fg[:], op0=ALU.mult, op1=ALU.mult)
                st(y_o[yp * TP + tt * 128:yp * TP + (tt + 1) * 128, :], xn[tt][:])
            if is_last:
                s = ssp[0:NSC, NTT:NTT + 1]
                act.activation(out=sq[0:NSC, :], in_=xns[:], func=AF.Square, accum_out=s)
                rstd_of(s, NSC)
                dve.scalar_tensor_tensor(out=xns[:], in0=xns[:], scalar=s, in1=fg[0:NSC, :], op0=ALU.mult, op1=ALU.mult)
                st(ys_o[:], xns[:])
            barrier()

    def sample_prep():
        with fw.scope():
            lsb = fw.sb
            xs_t = lsb("xs_t", [NSC, D]); sh_t = lsb("sh_t", [NSC, D]); sqs = lsb("sqs", [NSC, D]); s1 = lsb("s1s", [NSC, 1])
            xsT = lsb("xsT", [P, KC, NSC]); hs_tok = lsb("hs_tok", [NSC, D]); stg = lsb("stg", [NSC, NSB * 128])
            ld(xs_t[:], xs[:]); ld(sh_t[:], sshift[:])
            act.activation(out=sqs[:], in_=xs_t[:], func=AF.Square, accum_out=s1[:])
            rstd_of(s1[:], NSC)
            act.activation(out=xs_t[:], in_=xs_t[:], func=AF.Copy, scale=s1[:])
            for kc in range(KC):
                pe.transpose(out=PS[0][:, kc * NSC:(kc + 1) * NSC], in_=xs_t[:, kc * 128:(kc + 1) * 128], identity=ident[0:NSC, 0:NSC])
                pe.transpose(out=PS[1][:, kc * NSC:(kc + 1) * NSC], in_=sh_t[:, kc * 128:(kc + 1) * 128], identity=ident[0:NSC, 0:NSC])
            dve.tensor_tensor(out=xsT[:], in0=PS[0][:, 0:KC * NSC].re("p (k n) -> p k n", n=NSC), in1=sceff[:, :, 0:NSC], op=ALU.mult)
            dve.tensor_tensor(out=xsT[:], in0=xsT[:], in1=modT[:, 0:KC, 0:NSC], op=ALU.add)
            dve.tensor_copy(out=hT[:, :, TP:TP + NSC], in_=xsT[:])
            act.copy(out=hT[:, :, TP + NSC:TP + NS2], in_=PS[1][:, 0:KC * NSC].re("p (k n) -> p k n", n=NSC))
            for kc in range(KC):
                pe.transpose(out=PS[2 + kc // 4 % 2][0:NSC, (kc % 4) * 128:(kc % 4 + 1) * 128], in_=xsT[:, kc, :], identity=ident[:])
                act.copy(out=hs_tok[:, kc * 128:(kc + 1) * 128], in_=PS[2 + kc // 4 % 2][0:NSC, (kc % 4) * 128:(kc % 4 + 1) * 128])
            st(sshift_o[:], hs_tok[:])
            for (src, dst) in ((s5re0, x0r), (s5im0, x0i)):
                ld(stg[:], src[:])
                for s in range(NSB):
                    pe.transpose(out=PS[4][:, s * NSC:(s + 1) * NSC], in_=stg[:, s * 128:(s + 1) * 128], identity=ident[0:NSC, 0:NSC])
                act.copy(out=dst[:].re("p s n -> p (s n)"), in_=PS[4][:, 0:NSB * NSC])
            barrier()

    def finals():
        with fw.scope():
            lsb = fw.sb
            xf = [lsb("xf%d" % i, [P, NSB]) for i in range(4)]
            dve.tensor_tensor(out=xf[0][:], in0=cs1[:], in1=s5car_r[:], op=ALU.mult)
            dve.tensor_tensor(out=xf[1][:], in0=sn1[:], in1=s5car_i[:], op=ALU.mult)
            dve.tensor_tensor(out=xf[0][:], in0=xf[0][:], in1=xf[1][:], op=ALU.add)
            dve.tensor_tensor(out=xf[2][:], in0=cs1[:], in1=s5car_i[:], op=ALU.mult)
            dve.tensor_tensor(out=xf[3][:], in0=sn1[:], in1=s5car_r[:], op=ALU.mult)
            dve.tensor_tensor(out=xf[2][:], in0=xf[2][:], in1=xf[3][:], op=ALU.subtract)
            xo = lsb("xfo", [NSB, 2, P])
            pe.transpose(out=PS[0][0:NSB, 0:128], in_=xf[0][:], identity=ident[:])
            pe.transpose(out=PS[0][0:NSB, 128:256], in_=xf[2][:], identity=ident[:])
            act.copy(out=xo[:].re("p a b -> p (a b)"), in_=PS[0][0:NSB, 0:256])
            st(ps5re_o[:], xo[:, 0, :]); st(ps5im_o[:], xo[:, 1, :])
            ho = lsb("hlo", [KC, P])
            pe.transpose(out=PS[1][0:KC, 0:128], in_=hlast[:], identity=ident[:])
            act.copy(out=ho[:], in_=PS[1][0:KC, 0:128])
            st(pshift_o[:], ho[:])
            so = lsb("wkvo", [64, C.H, 64])
            for h in range(C.H):
                pb = 64 * (h % 2)
                pe.transpose(out=PS[2][0:64, (h % 8) * 64:(h % 8 + 1) * 64], in_=Hst[h][pb:pb + 64, :], identity=ident[pb:pb + 64, pb:pb + 64],
                             tile_position=(pb, 0))
                act.copy(out=so[:, h, :], in_=PS[2][0:64, (h % 8) * 64:(h % 8 + 1) * 64])
            st(View(pwkv_o.buf, pwkv_o.ap.rearrange("h i j -> i h j")), so[:])
            so2 = lsb("s5o", [NSC, 512])
            for (src, dsto) in ((x1r, ss5re_o), (x1i, ss5im_o)):
                for g4 in range(NSB // 4):
                    for j in range(4):
                        pe.transpose(out=PS[3][0:NSC, j * 128:(j + 1) * 128], in_=src[:, g4 * 4 + j, :], identity=ident[:])
                    act.copy(out=so2[:], in_=PS[3][0:NSC, :])
                    st(dsto[:, g4 * 512:(g4 + 1) * 512], so2[:])

    try:
      for p in range(NPASS):
          if stop(1):
              break
          is_last = (p == NPASS - 1)
          if is_last:
              sample_prep()
          phase_x(View(xb.buf, xb.ap[p * TP:(p + 1) * TP, :]), is_last)
          barrier()
          if stop(2):
              break
          NA = NPASS // 2
          so = p < NA
          s5_phase(is_last, so)
          if stop(3):
              break
          rwkv_phase(is_last, so)
          if stop(4):
              break
          if p == NA - 1:
              dve.tensor_scalar_mul(out=s5car_r[:], in0=s5car_r[:], scalar1=flag_s[:, 0:1])
              dve.tensor_scalar_mul(out=s5car_i[:], in0=s5car_i[:], scalar1=flag_s[:, 0:1])
              dve.tensor_scalar_mul(out=rw_car[:], in0=rw_car[:], scalar1=flag_s[:, 0:1])
              for hh in range(C.H):
                  dve.tensor_scalar_mul(out=Hst[hh][:], in0=Hst[hh][:], scalar1=flag_s[:, 0:1])
          if so:
              continue
          if is_last:
              rwkv_sample_core()
          post_phase(is_last, p, p - NA)
          if stop(5):
              break
      if not stop(6):
          finals()
    except StopBuild:
        pass
    fw.finish()
    return nc, fw, es


def _blk(w, kcn):
    K, N = w.shape
    return np.ascontiguousarray(w.reshape(kcn, 128, N // 128, 128).transpose(2, 1, 0, 3))


def _fm(v):
    v = np.asarray(v).reshape(-1)
    return np.ascontiguousarray(v.reshape(-1, 128).T)


def make_in_maps(cfg, inp, n_cores, n_seq, single_half=None):
    C = cfg
    f = np.float32
    H = C.H; NSB = C.NSB
    sh = {}
    sh["w_ada"] = _blk(inp["w_ada"][0], C.KC); sh["w_in"] = _blk(inp["w_in"][0], C.KC)
    sh["w_glu"] = _blk(inp["w_glu"][0], C.NUB); sh["w_out"] = _blk(inp["w_out"][0], C.KC)
    sh["b_ada"] = _fm(inp["b_ada"][0]); sh["norm_g"] = _fm(inp["norm_g"][0]); sh["mu"] = _fm(inp["mu_rw"][0])
    sh["A_re"] = _fm(inp["A_re"][0]); sh["A_im"] = _fm(inp["A_im"][0])
    sh["lstep"] = np.ascontiguousarray(np.repeat(inp["log_step"][0].reshape(NSB, 2), 64, axis=1).T)
    for k in ("B_re", "B_im"):
        sh[k] = np.ascontiguousarray(inp[k][0].reshape(NSB, 2, 64, 16).transpose(1, 2, 0, 3).reshape(128, NSB, 16))
    for k in ("C_re", "C_im"):
        sh[k] = np.ascontiguousarray(inp[k][0].reshape(NSB, 2, 16, 64).transpose(1, 3, 0, 2).reshape(128, NSB, 16))
    sh["Dsk"] = _fm(inp["D_skip"][0]); sh["b_glu"] = _fm(inp["b_glu"][0])
    for k in ("w0", "a0", "k_k", "k_a", "r_k", "gn_w", "gn_b"):
        sh[k] = _fm(inp[k][0])
    sh["w2"] = np.ascontiguousarray(inp["w2"][0]); sh["a2"] = np.ascontiguousarray(inp["a2"][0])
    sh["fing"] = np.ascontiguousarray(np.broadcast_to(inp["final_g"].reshape(1, -1), (128, C.D)))
    rep = max(1, 128 // H)
    sh["gnw_bh"] = np.ascontiguousarray(np.tile(inp["gn_w"][0].reshape(H, 64), (rep, 1))[:128])
    sh["gnb_bh"] = np.ascontiguousarray(np.tile(inp["gn_b"][0].reshape(H, 64), (rep, 1))[:128])
    sh["rk_bh"] = np.ascontiguousarray(np.tile(inp["r_k"][0].reshape(H, 64), (rep, 1))[:128])
    if sh["gnw_bh"].shape[0] < 128:
        for k in ("gnw_bh", "gnb_bh", "rk_bh"):
            sh[k] = np.ascontiguousarray(np.concatenate([sh[k], np.zeros((128 - sh[k].shape[0], 64), f)], 0))
    sh["ident"] = np.eye(128, dtype=f)
    ms = np.triu(np.ones((64, 64), f), 1); mi_ = np.triu(np.ones((64, 64), f), 0)
    sh["maskgt"] = np.block([[ms, mi_], [ms, mi_]]).astype(f)
    ob = np.zeros((128, 128), f); ob[:64, :64] = 1; ob[64:, 64:] = 1
    sh["onesbd"] = ob
    rm = np.ones((128, C.TP), f); rm[:, ::64] = 0
    sh["resetm"] = rm
    rmk = np.zeros((128, 4), f)
    for j in range(4):
        rmk[32 * j:32 * j + 32, j] = 1
    sh["rowmask"] = rmk
    sh["iota"] = np.ascontiguousarray(np.broadcast_to(np.arange(128, dtype=f).reshape(1, -1), (128, 128)))
    sh = {k: np.ascontiguousarray(v, dtype=f) for k, v in sh.items()}
    maps = []
    NSC = C.NSC
    for c in range(n_cores):
        b = c % n_seq
        half = (c // n_seq) if single_half is None else single_half
        rows = slice(c * NSC, (c + 1) * NSC)
        m = dict(sh)
        HL = C.TP * (C.NPASS // 2)
        xp = inp["x_prompt"][b]
        m["xb"] = np.ascontiguousarray(np.concatenate([xp[0:HL], xp[half * HL:(half + 1) * HL]], 0), dtype=f)
        m["flag"] = np.full((128, 1), float(half), f)
        cp = inp["c_prompt"][b:b + 1]
        m["c17"] = np.ascontiguousarray(np.concatenate([inp["c_sample"][rows], cp, cp], 0), dtype=f)
        m["xs"] = np.ascontiguousarray(inp["x_sample"][rows, 0, :], dtype=f)
        m["sshift"] = np.ascontiguousarray(inp["state_shift"][0, rows], dtype=f)
        m["s5re0"] = np.ascontiguousarray(inp["state_s5_re"][0, rows].reshape(NSC, -1), dtype=f)
        m["s5im0"] = np.ascontiguousarray(inp["state_s5_im"][0, rows].reshape(NSC, -1), dtype=f)
        m["wkv0"] = np.ascontiguousarray(inp["state_wkv"][0, rows].reshape(NSC * H, 4096), dtype=f)
        maps.append(m)
    return maps


def assemble(cfg, res, n_cores, n_seq):
    C = cfg; f = np.float32
    G = C.G; H = C.H; NSC = C.NSC
    R0 = lambda c, k: np.asarray(res[c][k], dtype=f)
    if n_cores >= 2 * n_seq:
        y_p = np.stack([np.concatenate([R0(b, "y"), R0(b + n_seq, "y")], 0) for b in range(n_seq)], 0)
        R = lambda c, k: R0(c + n_seq, k) if k in ("ps5re", "ps5im", "pwkv", "pshift") else R0(c, k)
    else:
        y_p = np.stack([R0(b, "y") for b in range(n_seq)], 0)
        R = R0
    y_s = np.concatenate([R(c, "ys") for c in range(n_cores)], 0)[:, None, :]
    re_p = np.stack([R(b, "ps5re").reshape(G, 64) for b in range(n_seq)], 0)[None]
    im_p = np.stack([R(b, "ps5im").reshape(G, 64) for b in range(n_seq)], 0)[None]
    wkv_p = np.stack([R(b, "pwkv") for b in range(n_seq)], 0)[None]
    sh_p = np.stack([R(b, "pshift").reshape(-1) for b in range(n_seq)], 0)[None]
    re_s = np.concatenate([R(c, "ss5re").reshape(NSC, G, 64) for c in range(n_cores)], 0)[None]
    im_s = np.concatenate([R(c, "ss5im").reshape(NSC, G, 64) for c in range(n_cores)], 0)[None]
    wkv_s = np.concatenate([R(c, "swkv").reshape(NSC, H, 64, 64) for c in range(n_cores)], 0)[None]
    sh_s = np.concatenate([R(c, "sshift_o") for c in range(n_cores)], 0)[None]
    return (y_p, y_s, re_p, im_p, wkv_p, sh_p, re_s, im_s, wkv_s, sh_s)


def kernel(**inputs):
    cfg = Cfg()
    inp = {k: np.asarray(v) for k, v in inputs.items()}
    nc, fw, es = build(cfg)
    maps = make_in_maps(cfg, inp, 8, 4)
    res = run_bass_kernel_spmd(nc, maps, core_ids=list(range(8)))
    return assemble(cfg, res.results, 8, 4)
```

```python
import math
import os
from contextlib import ExitStack
import numpy as np
import concourse.bass as bass
import concourse.mybir as mybir
from concourse.bass_utils import run_bass_kernel_spmd

F32 = mybir.dt.float32
F32R = mybir.dt.float32r
AF = mybir.ActivationFunctionType
ALU = mybir.AluOpType
AX = mybir.AxisListType


class Cfg:
    def __init__(self, D=2048, TP=512, NSC=16, NPASS=4):
        self.NPASS = NPASS
        self.D = D; self.TP = TP; self.NSC = NSC
        self.KC = D // 128
        self.DS = D // 2; self.DR = D // 2
        self.G = self.DS // 16; self.NSB = self.G // 2; self.NUB = self.DS // 128
        self.H = self.DR // 64; self.NHP = self.DR // 128
        self.NCH = TP // 64
        self.OFF_U = 0; self.OFF_Z = self.DS; self.OFF_RW = 2 * self.DS
        self.NSH = 3 * self.DR + 128
        self.OFF_RWZ = self.OFF_RW + self.NSH
        self.OFF_G1 = self.OFF_RWZ + self.DR
        self.OFF_G2 = self.OFF_G1 + D
        self.NIN = self.OFF_G2 + D
        self.NRB = self.NSH // 128
        self.GB = min(2, self.NUB)
        self.TT = [(i, min(512, TP - i)) for i in range(0, TP, 512)]
        self.TW = TP + 2 * NSC


class TL:
    def __init__(self, sem, name):
        self.sem = sem; self.count = 0; self.name = name


class Buf:
    __slots__ = ("w", "r", "excl")

    def __init__(self):
        self.w = None; self.r = {}; self.excl = False


class View:
    __slots__ = ("buf", "ap")

    def __init__(self, buf, ap):
        self.buf = buf; self.ap = ap

    def __getitem__(self, idx):
        return View(self.buf, self.ap[idx])

    def re(self, s, **kw):
        return View(self.buf, self.ap.rearrange(s, **kw))

    def bc(self, shape):
        return View(self.buf, self.ap.to_broadcast(shape))

    def r32(self):
        return View(self.buf, self.ap.bitcast(F32R))


class Tile:
    def __init__(self, ap, buf=None):
        self.ap = ap; self.buf = buf or Buf()

    def __getitem__(self, idx):
        return View(self.buf, self.ap[idx])

    def sub(self, idx):
        return Tile(self.ap[idx])


class Eng:
    def __init__(self, fw, name, h, tl, self_sync):
        self.fw = fw; self.name = name; self.h = h; self.tl = tl
        self.self_sync = self_sync; self.seen = {}

    def wait_for(self, deps):
        for tl, cnt in deps.items():
            if tl is self.tl and not self.self_sync:
                continue
            if self.seen.get(tl, 0) >= cnt:
                continue
            self.h.wait_ge(tl.sem, cnt)
            self.seen[tl] = cnt

    def __getattr__(self, op):
        def f(inc=True, **kw):
            return self.fw._issue(self, op, kw, inc)
        return f


def _deps(reads, writes):
    deps = {}

    def add(tc):
        if tc is None:
            return
        tl, c = tc
        if deps.get(tl, 0) < c:
            deps[tl] = c
    for v in reads:
        add(v.buf.w)
        if v.buf.excl:
            for tl, c in v.buf.r.items():
                add((tl, c))
    for v in writes:
        add(v.buf.w)
        for tl, c in v.buf.r.items():
            add((tl, c))
    return deps


class FW:
    def __init__(self, nc, es):
        self.nc = nc; self.es = es; self.nsem = 0; self.es_stack = [es]; self.uid = 0
        self.pe = Eng(self, "pe", nc.tensor, self.tl("pe"), False)
        self.act = Eng(self, "act", nc.scalar, self.tl("act"), True)
        self.dve = Eng(self, "dve", nc.vector, self.tl("dve"), True)
        self.pool = Eng(self, "pool", nc.gpsimd, self.tl("pool"), True)
        self.sp = Eng(self, "sp", nc.sync, self.tl("sp"), False)
        self.dma_tls = []

    def tl(self, name):
        self.nsem += 1
        return TL(self.es.enter_context(self.nc.semaphore(name)), name)

    def sb(self, name, shape, dtype=F32):
        self.uid += 1
        return Tile(self.es_stack[-1].enter_context(self.nc.sbuf_tensor("%s_%d" % (name, self.uid), list(shape), dtype))[:])

    def barrier(self):
        engs = (self.pe, self.act, self.dve, self.pool)
        tls = [e.tl for e in engs] + self.dma_tls
        for e in engs + (self.sp,):
            e.wait_for({tl: tl.count for tl in tls if tl.count and tl is not e.tl})

    def scope(self):
        fw = self

        class _S:
            def __enter__(self_):
                self_.les = ExitStack(); fw.es_stack.append(self_.les); return self_

            def __exit__(self_, *a):
                if fw.es_stack[-1] is not self_.les:
                    return False
                fw.barrier(); fw.es_stack.pop(); self_.les.close(); return False
        return _S()

    def ps(self, name, shape=(128, 512)):
        t = Tile(self.es.enter_context(self.nc.psum_tensor(name, list(shape), F32))[:])
        t.buf.excl = True
        return t

    def dram(self, name, shape, kind):
        return Tile(self.nc.dram_tensor(name, list(shape), F32, kind=kind).ap())

    def _issue(self, eng, op, kw, inc):
        reads = [v for k, v in kw.items() if isinstance(v, View) and k not in ("out", "accum_out", "ap")]
        writes = [v for k, v in kw.items() if isinstance(v, View) and k in ("out", "accum_out", "ap")]
        eng.wait_for(_deps(reads, writes))
        args = {k: (v.ap if isinstance(v, View) else v) for k, v in kw.items()}
        inst = getattr(eng.h, op)(**args)
        if inc:
            eng.tl.count += 1
            inst.then_inc(eng.tl.sem, 1)
            stamp = eng.tl.count
        else:
            stamp = eng.tl.count + 1
        for v in reads:
            if v.buf.r.get(eng.tl, 0) < stamp:
                v.buf.r[eng.tl] = stamp
        for v in writes:
            v.buf.w = (eng.tl, stamp); v.buf.r = {}
        return inst

    def dma(self, q, tl, out, in_, **kw):
        eng = {"sp": self.sp, "pool": self.pool, "act": self.act}[q]
        deps = _deps([in_], [out])
        if tl.count:
            deps[tl] = max(deps.get(tl, 0), tl.count)
        eng.wait_for(deps)
        inst = eng.h.dma_start(out=out.ap, in_=in_.ap, **kw)
        tl.count += 16
        inst.then_inc(tl.sem, 16)
        in_.buf.r[tl] = tl.count
        out.buf.w = (tl, tl.count); out.buf.r = {}
        if tl not in self.dma_tls:
            self.dma_tls.append(tl)

    def finish(self):
        for tl in self.dma_tls:
            self.sp.h.wait_ge(tl.sem, tl.count)
        for e in (self.pe, self.act, self.dve, self.pool):
            if e.tl.count:
                self.sp.h.wait_ge(e.tl.sem, e.tl.count)


def build(cfg, dbg=False):
    nc = bass.Bass("TRN2", target_bir_lowering=False)
    es = ExitStack()
    fw = FW(nc, es)
    pe, act, dve, pool = fw.pe, fw.act, fw.dve, fw.pool
    C = cfg
    D, KC, TP, NSC, TW = C.D, C.KC, C.TP, C.NSC, C.TW
    NS2 = 2 * NSC
    NSB, NUB, NHP, NCH, NRB, GB = C.NSB, C.NUB, C.NHP, C.NCH, C.NRB, C.GB
    P = 128

    def din(name, shape):
        return fw.dram(name, shape, "ExternalInput")

    def dout(name, shape):
        return fw.dram(name, shape, "ExternalOutput")

    xb = din("xb", [TP * C.NPASS, D]); flag = din("flag", [P, 1])
    c17 = din("c17", [NSC + 2, D]); xs = din("xs", [NSC, D]); sshift = din("sshift", [NSC, D])
    s5re0 = din("s5re0", [NSC, NSB * 128]); s5im0 = din("s5im0", [NSC, NSB * 128])
    wkv0 = din("wkv0", [NSC * C.H, 4096])
    w_ada = din("w_ada", [3 * KC, P, KC, 128])
    w_in = din("w_in", [C.NIN // 128, P, KC, 128])
    w_glu = din("w_glu", [NUB, P, NUB, 128])
    w_out = din("w_out", [KC, P, KC, 128])
    b_ada = din("b_ada", [P, 3 * KC]); norm_g = din("norm_g", [P, KC]); mu = din("mu", [P, NRB])
    A_re = din("A_re", [P, NSB]); A_im = din("A_im", [P, NSB]); lstep = din("lstep", [P, NSB])
    B_re = din("B_re", [P, NSB, 16]); B_im = din("B_im", [P, NSB, 16])
    C_re = din("C_re", [P, NSB, 16]); C_im = din("C_im", [P, NSB, 16])
    Dsk = din("Dsk", [P, NUB]); b_glu = din("b_glu", [P, NUB])
    w0 = din("w0", [P, NHP]); a0 = din("a0", [P, NHP]); k_k = din("k_k", [P, NHP]); k_a = din("k_a", [P, NHP])
    r_k = din("r_k", [P, NHP]); gn_w = din("gn_w", [P, NHP]); gn_b = din("gn_b", [P, NHP])
    w2 = din("w2", [64, C.DR]); a2 = din("a2", [64, C.DR])
    fing = din("fing", [P, D])
    gnw_bh = din("gnw_bh", [P, 64]); gnb_bh = din("gnb_bh", [P, 64]); rk_bh = din("rk_bh", [P, 64])
    ident_d = din("ident", [P, P]); maskgt_d = din("maskgt", [P, P]); onesbd_d = din("onesbd", [P, P])
    reset_d = din("resetm", [P, TP]); iota_d = din("iota", [P, 128]); rowmask_d = din("rowmask", [P, 4])

    y_o = dout("y", [TP * (C.NPASS // 2), D]); ys_o = dout("ys", [NSC, D])
    ps5re_o = dout("ps5re", [NSB, P]); ps5im_o = dout("ps5im", [NSB, P])
    pwkv_o = dout("pwkv", [C.H, 64, 64]); pshift_o = dout("pshift", [KC, P])
    ss5re_o = dout("ss5re", [NSC, NSB * 128]); ss5im_o = dout("ss5im", [NSC, NSB * 128])
    swkv_o = dout("swkv", [NSC * C.H, 4096]); sshift_o = dout("sshift_o", [NSC, D])
    scr_a = fw.dram("scr_a", [NSC, C.H, 6, 64], "Internal")
    scr_b = fw.dram("scr_b", [NSC, C.DR], "Internal")
    dbg_o = {}

    tl_misc = [fw.tl("m%d" % i) for i in range(6)]
    mi = [0]

    def ld(out, in_, q="sp"):
        tl = tl_misc[mi[0] % len(tl_misc)]; mi[0] += 1
        fw.dma(q, tl, out, in_)

    tl_out = [fw.tl("o%d" % i) for i in range(4)]
    oi = [0]

    def st(out, in_, **kw):
        tl = tl_out[oi[0] % len(tl_out)]; oi[0] += 1
        fw.dma("sp", tl, out, in_, **kw)

    def const(name, d, shape):
        t = fw.sb(name, shape); ld(t[:], d[:]); return t
    ident = const("ident_s", ident_d, [P, P]); maskgt = const("maskgt_s", maskgt_d, [P, P])
    onesbd = const("onesbd_s", onesbd_d, [P, P]); resetm = const("reset_s", reset_d, [P, TP])
    flag_s = const("flag_s", flag, [P, 1]); iota = const("iota_s", iota_d, [P, 128]); rowmask = const("rowmask_s", rowmask_d, [P, 4])
    b_ada_s = const("b_ada_s", b_ada, [P, 3 * KC]); norm_g_s = const("norm_g_s", norm_g, [P, KC])
    mu_s = const("mu_s", mu, [P, NRB])
    Dsk_s = const("Dsk_s", Dsk, [P, NUB]); b_glu_s = const("b_glu_s", b_glu, [P, NUB])
    w0_s = const("w0_s", w0, [P, NHP]); a0_s = const("a0_s", a0, [P, NHP]); kk_s = const("kk_s", k_k, [P, NHP])
    ka_s = const("ka_s", k_a, [P, NHP]); rk_s = const("rk_s", r_k, [P, NHP])
    gnw_s = const("gnw_s", gn_w, [P, NHP]); gnb_s = const("gnb_s", gn_b, [P, NHP])
    w2_s = fw.sb("w2a2_s", [P, C.DR]); ld(w2_s[0:64, :], w2[:])
    a2_s = w2_s; ld(a2_s[64:128, :], a2[:])
    omu_s = fw.sb("omu_s", [P, NRB])
    dve.tensor_scalar(out=omu_s[:], in0=mu_s[:], scalar1=-1.0, scalar2=1.0, op0=ALU.mult, op1=ALU.add)

    PS = [fw.ps("psb%d" % i) for i in range(8)]

    NRING = 3
    ring = [fw.sb("wring%d" % i, [P, KC, 128], F32R) for i in range(NRING)]
    ring_tl = [fw.tl("wr%d" % i) for i in range(NRING)]
    ri = [0]

    def load_w(wd, blk, nk=KC):
        i = ri[0] % NRING; ri[0] += 1
        fw.dma("pool", ring_tl[i], ring[i][:, 0:nk, :], View(wd.buf, wd.ap[blk, :, 0:nk, :]))
        return ring[i]

    def mm_acc(psv, wslot, nk, rhs_fn, ncols=128):
        for kc in range(nk):
            pe.matmul(out=psv, lhsT=wslot[:, kc, 0:ncols].r32(), rhs=rhs_fn(kc),
                      start=(kc == 0), stop=(kc == nk - 1), inc=(kc == nk - 1))

    s5 = {}
    tA = const("A_re_s", A_re, [P, NSB]); tAi = const("A_im_s", A_im, [P, NSB]); tls = const("lstep_s", lstep, [P, NSB])
    step = fw.sb("s5step", [P, NSB]); act.activation(out=step[:], in_=tls[:], func=AF.Exp)
    lam = fw.sb("s5lam", [P, NSB]); dve.tensor_scalar_min(out=lam[:], in0=tA[:], scalar1=-1e-4)
    mag = fw.sb("s5mag", [P, NSB]); tmpc = fw.sb("s5tmp", [P, NSB]); theta = fw.sb("s5theta", [P, NSB])
    dve.tensor_tensor(out=tmpc[:], in0=lam[:], in1=step[:], op=ALU.mult)
    act.activation(out=mag[:], in_=tmpc[:], func=AF.Exp)
    dve.tensor_tensor(out=theta[:], in0=tAi[:], in1=step[:], op=ALU.mult)
    TWO_PI = 2.0 * math.pi

    I32 = mybir.dt.int32

    def sin_into(dst, x, shape, phase):
        with fw.scope():
            ki = fw.sb("sr_ki", shape, I32); kf = fw.sb("sr_kf", shape)
            dve.tensor_scalar_add(out=dst, in0=x, scalar1=float(phase))
            dve.tensor_scalar_mul(out=ki[:], in0=dst, scalar1=1.0 / TWO_PI)
            dve.tensor_copy(out=kf[:], in_=ki[:])
            dve.scalar_tensor_tensor(out=dst, in0=kf[:], scalar=-TWO_PI, in1=dst, op0=ALU.mult, op1=ALU.add)
            dve.tensor_scalar(out=kf[:], in0=dst, scalar1=math.pi, scalar2=-TWO_PI, op0=ALU.is_gt, op1=ALU.mult)
            dve.tensor_tensor(out=dst, in0=dst, in1=kf[:], op=ALU.add)
            dve.tensor_scalar(out=kf[:], in0=dst, scalar1=-math.pi, scalar2=TWO_PI, op0=ALU.is_lt, op1=ALU.mult)
            dve.tensor_tensor(out=dst, in0=dst, in1=kf[:], op=ALU.add)
            dve.tensor_scalar(out=dst, in0=dst, scalar1=math.pi, scalar2=-math.pi, op0=ALU.min, op1=ALU.max)
            act.activation(out=dst, in_=dst, func=AF.Sin)

    def sincos(name, ang_view, shape):
        sn = fw.sb(name + "_sin", shape); cs = fw.sb(name + "_cos", shape)
        sin_into(sn[:], ang_view, shape, 0.0)
        sin_into(cs[:], ang_view, shape, 0.5 * math.pi)
        return sn, cs
    thp = theta
    sn1, cs1 = sincos("s5t1", thp[:], [P, NSB])
    abr = fw.sb("s5abr", [P, NSB]); abi = fw.sb("s5abi", [P, NSB])
    dve.tensor_tensor(out=abr[:], in0=mag[:], in1=cs1[:], op=ALU.mult)
    dve.tensor_tensor(out=abi[:], in0=mag[:], in1=sn1[:], op=ALU.mult)
    den = fw.sb("s5den", [P, NSB]); t2 = fw.sb("s5t2", [P, NSB]); fre = fw.sb("s5fre", [P, NSB]); fim = fw.sb("s5fim", [P, NSB])
    abm1 = fw.sb("s5abm1", [P, NSB])
    dve.tensor_tensor(out=den[:], in0=lam[:], in1=lam[:], op=ALU.mult)
    dve.tensor_tensor(out=t2[:], in0=tAi[:], in1=tAi[:], op=ALU.mult)
    dve.tensor_tensor(out=den[:], in0=den[:], in1=t2[:], op=ALU.add)
    dve.reciprocal(out=den[:], in_=den[:])
    dve.tensor_scalar_add(out=abm1[:], in0=abr[:], scalar1=-1.0)
    dve.tensor_tensor(out=fre[:], in0=abm1[:], in1=lam[:], op=ALU.mult)
    dve.tensor_tensor(out=t2[:], in0=abi[:], in1=tAi[:], op=ALU.mult)
    dve.tensor_tensor(out=fre[:], in0=fre[:], in1=t2[:], op=ALU.add)
    dve.tensor_tensor(out=fre[:], in0=fre[:], in1=den[:], op=ALU.mult)
    dve.tensor_tensor(out=fim[:], in0=abi[:], in1=lam[:], op=ALU.mult)
    dve.tensor_tensor(out=t2[:], in0=abm1[:], in1=tAi[:], op=ALU.mult)
    dve.tensor_tensor(out=fim[:], in0=fim[:], in1=t2[:], op=ALU.subtract)
    dve.tensor_tensor(out=fim[:], in0=fim[:], in1=den[:], op=ALU.mult)
    LB = [fw.sb("s5LB" + nm, [P, NUB, 128]) for nm in ("re", "im")]
    LC = [fw.sb("s5ZC" + nm, [P, NSB, 2, 16]) for nm in ("re", "im")]
    with fw.scope():
        Br = const("B_re_s", B_re, [P, NSB, 16]); Bi = const("B_im_s", B_im, [P, NSB, 16])
        bbr = fw.sb("s5bbr", [P, NSB, 16]); bbi = fw.sb("s5bbi", [P, NSB, 16]); bt_ = fw.sb("s5bt", [P, NSB, 16])
        fre_b = fre[:, :, None].bc([P, NSB, 16]); fim_b = fim[:, :, None].bc([P, NSB, 16])
        dve.tensor_tensor(out=bbr[:], in0=Br[:], in1=fre_b, op=ALU.mult)
        dve.tensor_tensor(out=bt_[:], in0=Bi[:], in1=fim_b, op=ALU.mult)
        dve.tensor_tensor(out=bbr[:], in0=bbr[:], in1=bt_[:], op=ALU.subtract)
        dve.tensor_tensor(out=bbi[:], in0=Bi[:], in1=fre_b, op=ALU.mult)
        dve.tensor_tensor(out=bt_[:], in0=Br[:], in1=fim_b, op=ALU.mult)
        dve.tensor_tensor(out=bbi[:], in0=bbi[:], in1=bt_[:], op=ALU.add)
        for li, (nm, src) in enumerate((("re", bbr), ("im", bbi))):
            Z = fw.sb("s5ZB" + nm, [P, NSB, 2, 16])
            dve.memset(ap=Z[:], constant=0.0)
            dve.tensor_copy(out=Z[0:64, :, 0, :], in_=src[0:64, :, :])
            dve.tensor_copy(out=Z[64:128, :, 1, :], in_=src[64:128, :, :])
            L = LB[li]
            for q4 in range(NUB):
                pe.transpose(out=PS[0][:, 0:128], in_=Z[:, 4 * q4:4 * q4 + 4, :, :].re("p a b c -> p (a b c)"), identity=ident[:])
                act.copy(out=L[:, q4, :], in_=PS[0][:, 0:128])
        Cr = const("C_re_s", C_re, [P, NSB, 16]); Ci = const("C_im_s", C_im, [P, NSB, 16])
        for li, (nm, src, sgn) in enumerate((("re", Cr, 1.0), ("im", Ci, -1.0))):
            Z = LC[li]
            dve.memset(ap=Z[:], constant=0.0)
            dve.tensor_scalar_mul(out=Z[0:64, :, 0, :], in0=src[0:64, :, :], scalar1=sgn)
            dve.tensor_scalar_mul(out=Z[64:128, :, 1, :], in0=src[64:128, :, :], scalar1=sgn)
    def make_tables():
      sinT = fw.sb("s5sinT", [P, NSB, 128]); cosT = fw.sb("s5cosT", [P, NSB, 128])
      with fw.scope():
        aA = fw.sb("s5aA", [P, NSB, 8]); aB = fw.sb("s5aB", [P, NSB, 16])
        sA = fw.sb("s5sA", [P, NSB, 8]); cA = fw.sb("s5cA", [P, NSB, 8])
        sB = fw.sb("s5sB", [P, NSB, 16]); cB = fw.sb("s5cB", [P, NSB, 16])
        tq = fw.sb("s5tq", [P, NSB, 128])
        dve.tensor_tensor(out=aB[:], in0=iota[:, None, 0:16].bc([P, NSB, 16]), in1=thp[:, :, None].bc([P, NSB, 16]), op=ALU.mult)
        dve.tensor_tensor(out=aA[:], in0=iota[:, None, 0:8].bc([P, NSB, 8]), in1=thp[:, :, None].bc([P, NSB, 8]), op=ALU.mult)
        dve.tensor_scalar_mul(out=aA[:], in0=aA[:], scalar1=16.0)
        sin_into(sA[:], aA[:], [P, NSB, 8], 0.0); sin_into(cA[:], aA[:], [P, NSB, 8], 0.5 * math.pi)
        sin_into(sB[:], aB[:], [P, NSB, 16], 0.0); sin_into(cB[:], aB[:], [P, NSB, 16], 0.5 * math.pi)
        shp = [P, NSB, 8, 16]
        A4 = lambda t: t[:, :, :, None].bc(shp)
        B4 = lambda t: t[:, :, None, :].bc(shp)
        T4 = lambda t: t[:].re("p s (a b) -> p s a b", a=8)
        dve.tensor_tensor(out=T4(cosT), in0=A4(cA), in1=B4(cB), op=ALU.mult)
        dve.tensor_tensor(out=T4(tq), in0=A4(sA), in1=B4(sB), op=ALU.mult)
        dve.tensor_tensor(out=cosT[:], in0=cosT[:], in1=tq[:], op=ALU.subtract)
        dve.tensor_tensor(out=T4(sinT), in0=A4(sA), in1=B4(cB), op=ALU.mult)
        dve.tensor_tensor(out=T4(tq), in0=A4(cA), in1=B4(sB), op=ALU.mult)
        dve.tensor_tensor(out=sinT[:], in0=sinT[:], in1=tq[:], op=ALU.add)
      return sinT, cosT
    angL = fw.sb("s5angL", [P, NSB]); dve.tensor_scalar_mul(out=angL[:], in0=thp[:], scalar1=128.0)
    snL, csL = sincos("s5tL", angL[:], [P, NSB])
    s5car_r = fw.sb("s5car_r", [P, NSB]); s5car_i = fw.sb("s5car_i", [P, NSB])
    dve.memset(ap=s5car_r[:], constant=0.0); dve.memset(ap=s5car_i[:], constant=0.0)

    NM = NSC + 2
    scT = fw.sb("scT", [P, KC, NM], F32R)
    with fw.scope():
        c_s = fw.sb("c_s", [NM, D]); ld(c_s[:], c17[:])
        csl = fw.sb("csl", [NM, D]); act.activation(out=csl[:], in_=c_s[:], func=AF.Silu)
        for kc in range(KC):
            pe.transpose(out=PS[1][:, kc * NM:(kc + 1) * NM], in_=csl[:, kc * 128:(kc + 1) * 128], identity=ident[0:NM, 0:NM])
        act.copy(out=scT[:].re("p k n -> p (k n)"), in_=PS[1][:, 0:KC * NM])
    modT = fw.sb("modT", [P, 3 * KC, NM])
    for fb in range(3 * KC):
        w = load_w(w_ada, fb)
        psb = PS[2 + fb % 2]
        mm_acc(psb[:, 0:NM], w, KC, lambda kc: scT[:, kc, :])
        act.activation(out=modT[:, fb, :], in_=psb[:, 0:NM], func=AF.Identity, bias=b_ada_s[:, fb:fb + 1], scale=1.0)
    sceff = fw.sb("sceff", [P, KC, NM])
    dve.tensor_scalar_add(out=sceff[:], in0=modT[:, KC:2 * KC, :], scalar1=1.0)
    dve.tensor_tensor(out=sceff[:], in0=sceff[:], in1=norm_g_s[:, :, None].bc([P, KC, NM]), op=ALU.mult)
    sc_p = fw.sb("sc_p", [P, KC]); sh_p = fw.sb("sh_p", [P, KC]); gt_p = fw.sb("gt_p", [P, KC])
    dve.tensor_copy(out=sc_p[:], in_=sceff[:, :, NSC]); dve.tensor_copy(out=sh_p[:], in_=modT[:, 0:KC, NSC])
    dve.tensor_copy(out=gt_p[:], in_=modT[:, 2 * KC:3 * KC, NSC])

    hT = fw.sb("hT", [P, KC, TW], F32R)
    rw_car = fw.sb("rw_car", [P, NRB]); dve.memset(ap=rw_car[:], constant=0.0)
    Hst = [fw.sb("Hst%d" % h, [P, 64]) for h in range(C.H)]
    for h in range(C.H):
        dve.memset(ap=Hst[h][:], constant=0.0)

    eps_rms = fw.sb("eps_rms", [P, 1]); dve.memset(ap=eps_rms[:], constant=1e-6)
    eps_gn = fw.sb("eps_gn", [P, 1]); dve.memset(ap=eps_gn[:], constant=64e-5)

    def rstd_of(ss, n):
        act.activation(out=ss, in_=ss, func=AF.Sqrt, bias=eps_rms[0:n, :], scale=1.0 / D)
        dve.reciprocal(out=ss, in_=ss)

    xt_tl = [fw.tl("xt%d" % i) for i in range(2)]
    ssq = fw.sb("ssq", [P, 2])
    hlast = fw.sb("hlast", [P, KC])

    def phase_x(xd, is_b):
      with fw.scope():
        xt = [fw.sb("xt%d" % i, [P, D]) for i in range(2)]
        xsq = fw.sb("xsq", [P, D], mybir.dt.bfloat16)
        for tt in range(TP // 128):
            i = tt % 2
            fw.dma("sp", xt_tl[i], xt[i][:], xd[tt * 128:(tt + 1) * 128, :])
            s = ssq[:, i:i + 1]
            act.activation(out=xsq[:], in_=xt[i][:], func=AF.Square, accum_out=s)
            rstd_of(s, 128)
            act.activation(out=xt[i][:], in_=xt[i][:], func=AF.Copy, scale=s)
            for kc in range(KC):
                psb = PS[(kc // 4) % 2]
                pe.transpose(out=psb[:, (kc % 4) * 128:(kc % 4 + 1) * 128], in_=xt[i][:, kc * 128:(kc + 1) * 128], identity=ident[:])
                act.activation(out=hT[:, kc, tt * 128:(tt + 1) * 128], in_=psb[:, (kc % 4) * 128:(kc % 4 + 1) * 128],
                               func=AF.Identity, scale=sc_p[:, kc:kc + 1], bias=sh_p[:, kc:kc + 1])
                if is_b and tt == TP // 128 - 1:
                    act.activation(out=hlast[:, kc:kc + 1], in_=psb[:, (kc % 4) * 128 + 127:(kc % 4) * 128 + 128],
                                   func=AF.Identity, scale=sc_p[:, kc:kc + 1], bias=sh_p[:, kc:kc + 1])

    NPASS = C.NPASS
    W1 = TP + NSC
    UID = [0]

    def tiles(is_last):
        t = list(C.TT)
        if is_last:
            t.append((TP, NS2))
        return t

    barrier = fw.barrier
    STOP = float(os.environ.get("K_STOP", "99"))

    class StopBuild(Exception):
        pass

    def stop(n):
        return STOP <= n

    def chk(n):
        if STOP <= n:
            while len(fw.es_stack) > 1:
                fw.es_stack.pop().close()
            raise StopBuild()

    osT = fw.sb("osT", [P, NUB, TW], F32R)
    orT = fw.sb("orT", [P, NHP, TW], F32R)
    x0r = fw.sb("x0r", [P, NSB, NSC]); x0i = fw.sb("x0i", [P, NSB, NSC])
    x1r = fw.sb("x1r", [P, NSB, NSC]); x1i = fw.sb("x1i", [P, NSB, NSC])

    def gelu_inplace(v, tmp_v):
        dve.tensor_tensor(out=tmp_v, in0=v, in1=v, op=ALU.mult)
        dve.tensor_scalar(out=tmp_v, in0=tmp_v, scalar1=0.044715, scalar2=1.0, op0=ALU.mult, op1=ALU.add)
        dve.tensor_tensor(out=tmp_v, in0=tmp_v, in1=v, op=ALU.mult)
        act.activation(out=tmp_v, in_=tmp_v, func=AF.Sigmoid, scale=2.0 * math.sqrt(2.0 / math.pi))
        dve.tensor_tensor(out=v, in0=v, in1=tmp_v, op=ALU.mult)

    def s5_phase(is_last, state_only=False):
        Wd = TW if is_last else TP
        with fw.scope():
            lsb = fw.sb
            sinT, cosT = make_tables()
            chk(2.1)
            uT = lsb("uT", [P, TW]); yT = lsb("yT", [P, NUB, TW], F32R)
            scanscope = fw.scope(); scanscope.__enter__()
            zr = lsb("s5zr", [P, 4, 128]); zi = lsb("s5zi", [P, 4, 128])
            q1 = lsb("s5q1", [P, 4, 128])
            sr = lsb("s5sr", [P, 4, 128]); si = lsb("s5si", [P, 4, 128])
            xr = zr; xi = zi
            cq = [lsb("s5cq%d" % i, [P, 4]) for i in range(4)]
            LBz = [lsb("s5LBz%d" % i, [P, 4, 128]) for i in range(2)]
            sq = [lsb("s5sq%d" % i, [P, 4, NSC]) for i in range(2)]
            for blk in range(NUB):
                chk(2.12)
                w = load_w(w_in, C.OFF_U // 128 + blk)
                chk(2.15)
                for (t0, tn) in tiles(is_last):
                    psb = PS[2 + (t0 // 512) % 2]
                    mm_acc(psb[:, 0:tn], w, KC, lambda kc: hT[:, kc, t0:t0 + tn].r32())
                    chk(2.17)
                    act.copy(out=uT[:, t0:t0 + tn], in_=psb[:, 0:tn])
                chk(2.2)
                for li in range(2):
                    for j in range(4):
                        dve.tensor_scalar_mul(out=LBz[li][:, j, :], in0=LB[li][:, blk, :], scalar1=rowmask[:, j:j + 1])
                s0 = blk * 4; sl = slice(s0, s0 + 4)
                cT = cosT[:, sl, :]; sT = sinT[:, sl, :]
                for ts in range(TP // 128):
                    bur = PS[4 + (ts % 2) * 2]; bui = PS[5 + (ts % 2) * 2]
                    for j in range(4):
                        for (L, psb) in ((LBz[0], bur), (LBz[1], bui)):
                            pe.matmul(out=psb[:, j * 128:(j + 1) * 128], lhsT=L[:, j, :],
                                      rhs=uT[:, ts * 128:(ts + 1) * 128],
                                      start=True, stop=True, inc=(j == 3))
                    chk(2.4)
                    br = bur[:, :].re("p (a b) -> p a b", a=4); bi = bui[:, :].re("p (a b) -> p a b", a=4)
                    dve.tensor_tensor(out=zr[:], in0=br, in1=cT, op=ALU.mult)
                    dve.tensor_tensor(out=q1[:], in0=bi, in1=sT, op=ALU.mult)
                    dve.tensor_tensor(out=zr[:], in0=zr[:], in1=q1[:], op=ALU.add)
                    dve.tensor_tensor(out=zi[:], in0=bi, in1=cT, op=ALU.mult)
                    dve.tensor_tensor(out=q1[:], in0=br, in1=sT, op=ALU.mult)
                    dve.tensor_tensor(out=zi[:], in0=zi[:], in1=q1[:], op=ALU.subtract)
                    chk(2.5)
                    for j in range(4):
                        s = s0 + j
                        dve.tensor_tensor_scan(out=sr[:, j, :], data0=mag[:, s:s + 1].bc([P, 128]), data1=zr[:, j, :],
                                               initial=s5car_r[:, s:s + 1], op0=ALU.mult, op1=ALU.add)
                        dve.tensor_tensor_scan(out=si[:, j, :], data0=mag[:, s:s + 1].bc([P, 128]), data1=zi[:, j, :],
                                               initial=s5car_i[:, s:s + 1], op0=ALU.mult, op1=ALU.add)
                    chk(2.6)
                    la = sr[:, :, 127]; lb = si[:, :, 127]
                    dve.tensor_tensor(out=cq[0][:], in0=csL[:, sl], in1=la, op=ALU.mult)
                    dve.tensor_tensor(out=cq[1][:], in0=snL[:, sl], in1=lb, op=ALU.mult)
                    dve.tensor_tensor(out=cq[2][:], in0=snL[:, sl], in1=la, op=ALU.mult)
                    dve.tensor_tensor(out=cq[3][:], in0=csL[:, sl], in1=lb, op=ALU.mult)
                    dve.tensor_tensor(out=s5car_r[:, sl], in0=cq[0][:], in1=cq[1][:], op=ALU.subtract)
                    dve.tensor_tensor(out=s5car_i[:, sl], in0=cq[2][:], in1=cq[3][:], op=ALU.add)
                    if state_only:
                        continue
                    dve.tensor_tensor(out=xr[:], in0=sr[:], in1=cT, op=ALU.mult)
                    dve.tensor_tensor(out=q1[:], in0=si[:], in1=sT, op=ALU.mult)
                    dve.tensor_tensor(out=xr[:], in0=xr[:], in1=q1[:], op=ALU.subtract)
                    dve.tensor_tensor(out=xi[:], in0=sr[:], in1=sT, op=ALU.mult)
                    dve.tensor_tensor(out=q1[:], in0=si[:], in1=cT, op=ALU.mult)
                    dve.tensor_tensor(out=xi[:], in0=xi[:], in1=q1[:], op=ALU.add)
                    chk(2.7)
                    psy = PS[0]
                    for j in range(4):
                        s = s0 + j
                        pe.matmul(out=psy[32 * j:32 * j + 32, 0:128], lhsT=LC[0][:, s, :, :].re("p a b -> p (a b)"),
                                  rhs=xr[:, j, :], start=True, stop=False, tile_position=(0, 32 * j), inc=False)
                        pe.matmul(out=psy[32 * j:32 * j + 32, 0:128], lhsT=LC[1][:, s, :, :].re("p a b -> p (a b)"),
                                  rhs=xi[:, j, :], start=False, stop=True, tile_position=(0, 32 * j), inc=(j == 3))
                    dve.scalar_tensor_tensor(out=yT[:, blk, ts * 128:(ts + 1) * 128], in0=uT[:, ts * 128:(ts + 1) * 128],
                                             scalar=Dsk_s[:, blk:blk + 1], in1=psy[:, 0:128], op0=ALU.mult, op1=ALU.add)
                if is_last:
                    psr = PS[6]; psi = PS[7]
                    for j in range(4):
                        for (L, psb) in ((LBz[0], psr), (LBz[1], psi)):
                            pe.matmul(out=psb[:, j * NSC:(j + 1) * NSC], lhsT=L[:, j, :],
                                      rhs=uT[:, TP:TP + NSC],
                                      start=True, stop=True, inc=(j == 3))
                    abr_b = abr[:, sl, None].bc([P, 4, NSC]); abi_b = abi[:, sl, None].bc([P, 4, NSC])
                    dve.tensor_tensor(out=sq[0][:], in0=x0r[:, sl, :], in1=abr_b, op=ALU.mult)
                    dve.tensor_tensor(out=sq[1][:], in0=x0i[:, sl, :], in1=abi_b, op=ALU.mult)
                    dve.tensor_tensor(out=sq[0][:], in0=sq[0][:], in1=sq[1][:], op=ALU.subtract)
                    dve.tensor_tensor(out=x1r[:, sl, :], in0=sq[0][:], in1=psr[:, 0:4 * NSC].re("p (a b) -> p a b", a=4), op=ALU.add)
                    dve.tensor_tensor(out=sq[0][:], in0=x0i[:, sl, :], in1=abr_b, op=ALU.mult)
                    dve.tensor_tensor(out=sq[1][:], in0=x0r[:, sl, :], in1=abi_b, op=ALU.mult)
                    dve.tensor_tensor(out=sq[0][:], in0=sq[0][:], in1=sq[1][:], op=ALU.add)
                    dve.tensor_tensor(out=x1i[:, sl, :], in0=sq[0][:], in1=psi[:, 0:4 * NSC].re("p (a b) -> p a b", a=4), op=ALU.add)
                    psy = PS[0]
                    for j in range(4):
                        s = s0 + j
                        pe.matmul(out=psy[32 * j:32 * j + 32, 0:NSC], lhsT=LC[0][:, s, :, :].re("p a b -> p (a b)"),
                                  rhs=x1r[:, s, :], start=True, stop=False, tile_position=(0, 32 * j), inc=False)
                        pe.matmul(out=psy[32 * j:32 * j + 32, 0:NSC], lhsT=LC[1][:, s, :, :].re("p a b -> p (a b)"),
                                  rhs=x1i[:, s, :], start=False, stop=True, tile_position=(0, 32 * j), inc=(j == 3))
                    dve.scalar_tensor_tensor(out=yT[:, blk, TP:TP + NSC], in0=uT[:, TP:TP + NSC],
                                             scalar=Dsk_s[:, blk:blk + 1], in1=psy[:, 0:NSC], op0=ALU.mult, op1=ALU.add)
            chk(2.8)
            scanscope.__exit__(None, None, None)
            if state_only:
                return
            gtmp = lsb("gtmp", [P, TW]); gs = lsb("gs", [P, TW]); zs = lsb("zs", [P, TW])
            Wy = W1 if is_last else TP
            for blk in range(NUB):
                gelu_inplace(yT[:, blk, 0:Wy], gtmp[:, 0:Wy])
            gt_tiles = list(C.TT) + ([(TP, NSC)] if is_last else [])
            for jb in range(NUB):
                w = load_w(w_glu, jb, nk=NUB)
                w2_ = load_w(w_in, C.OFF_Z // 128 + jb)
                for (t0, tn) in gt_tiles:
                    psb = PS[2]; psz = PS[3]
                    mm_acc(psb[:, 0:tn], w, NUB, lambda kc: yT[:, kc, t0:t0 + tn])
                    act.activation(out=gs[:, t0:t0 + tn], in_=psb[:, 0:tn], func=AF.Sigmoid, bias=b_glu_s[:, jb:jb + 1], scale=1.0)
                    mm_acc(psz[:, 0:tn], w2_, KC, lambda kc: hT[:, kc, t0:t0 + tn].r32())
                    act.activation(out=zs[:, t0:t0 + tn], in_=psz[:, 0:tn], func=AF.Silu)
                    dve.tensor_tensor(out=gs[:, t0:t0 + tn], in0=gs[:, t0:t0 + tn], in1=zs[:, t0:t0 + tn], op=ALU.mult)
                    dve.tensor_tensor(out=osT[:, jb, t0:t0 + tn], in0=gs[:, t0:t0 + tn], in1=yT[:, jb, t0:t0 + tn], op=ALU.mult)
            barrier()

    EM05 = math.exp(-0.5)

    def rwkv_phase(is_last, state_only=False):
        Wd = W1 if is_last else TP
        with fw.scope():
            lsb = fw.sb
            Pb = lsb("Pb", [P, 1 + TW]); tmpA = lsb("tmpA", [P, W1])
            lor = lsb("lor", [P, W1])
            rr = lsb("rr", [P, W1]); kr = lsb("kr", [P, W1]); vv = lsb("vv", [P, W1])
            ldc = lsb("ldc", [P, W1]); alr = lsb("alr", [P, W1]); kkn = lsb("kkn", [P, W1]); kmod = lsb("kmod", [P, W1])
            bb = lsb("bb", [P, W1]); cl = lsb("cl", [P, TP]); e1 = lsb("e1", [P, TP]); e2 = Tile(kr.ap[:, 0:TP], kr.buf)
            cend = lsb("cend", [P, NCH]); PC = lsb("PC", [P, NCH])
            AR = lsb("AR", [P, NCH, 2, 64]); BK = lsb("BK", [P, NCH, 2, 64]); AV = lsb("AV", [P, NCH, 2, 64]); BKH = lsb("BKH", [P, NCH, 2, 64])
            bonus = alr; oT = Pb; osq = kkn
            CB = 3; NSL = 2 * CB
            GTs = [lsb("GTs%d" % e, [P, P]) for e in range(2 * NSL)]
            X = [lsb("X%d" % e, [P, 64]) for e in range(2 * NSL)]
            BKHt = [lsb("BKHt%d" % e, [P, 64]) for e in range(2 * NSL)]
            PW = [lsb("PW%d" % e, [64, 3, 64]) for e in range(NSL)]
            Ys = [lsb("Ys%d" % e, [64, 64]) for e in range(2 * NSL)]
            WTs = [lsb("WTs%d" % e, [P, 64]) for e in range(2 * NSL)]

            def proj_shift(rb, dst):
                w = load_w(w_in, C.OFF_RW // 128 + rb)
                for (t0, tn) in tiles(is_last):
                    psb = PS[6 + (t0 // 512) % 2]
                    mm_acc(psb[:, 0:tn], w, KC, lambda kc: hT[:, kc, t0:t0 + tn].r32())
                    act.copy(out=Pb[:, 1 + t0:1 + t0 + tn], in_=psb[:, 0:tn])
                act.copy(out=Pb[:, 0:1], in_=rw_car[:, rb:rb + 1])
                dve.tensor_scalar_mul(out=tmpA[:, 0:TP], in0=Pb[:, 0:TP], scalar1=mu_s[:, rb:rb + 1])
                dve.scalar_tensor_tensor(out=dst[:, 0:TP], in0=Pb[:, 1:TP + 1], scalar=omu_s[:, rb:rb + 1], in1=tmpA[:, 0:TP],
                                         op0=ALU.mult, op1=ALU.add)
                dve.tensor_copy(out=rw_car[:, rb:rb + 1], in_=Pb[:, TP:TP + 1])
                if is_last:
                    dve.tensor_scalar_mul(out=tmpA[:, TP:W1], in0=Pb[:, 1 + TP + NSC:1 + TP + NS2], scalar1=mu_s[:, rb:rb + 1])
                    dve.scalar_tensor_tensor(out=dst[:, TP:W1], in0=Pb[:, 1 + TP:1 + TP + NSC], scalar=omu_s[:, rb:rb + 1],
                                             in1=tmpA[:, TP:W1], op0=ALU.mult, op1=ALU.add)

            ct_tiles = list(C.TT) + ([(TP, NSC)] if is_last else [])
            proj_shift(3 * NHP, lor)
            act.activation(out=lor[0:64, 0:Wd], in_=lor[0:64, 0:Wd], func=AF.Tanh)
            for hp in range(NHP):
                proj_shift(hp, rr); proj_shift(NHP + hp, kr); proj_shift(2 * NHP + hp, vv)
                for (t0, tn) in ct_tiles:
                    psb = PS[6]
                    pe.matmul(out=psb[:, 0:tn], lhsT=w2_s[0:64, hp * 128:(hp + 1) * 128], rhs=lor[0:64, t0:t0 + tn], start=True, stop=True)
                    act.activation(out=ldc[:, t0:t0 + tn], in_=psb[:, 0:tn], func=AF.Sigmoid, bias=w0_s[:, hp:hp + 1], scale=1.0)
                    psb = PS[7]
                    pe.matmul(out=psb[:, 0:tn], lhsT=a2_s[64:128, hp * 128:(hp + 1) * 128], rhs=lor[64:128, t0:t0 + tn], start=True, stop=True)
                    act.activation(out=alr[:, t0:t0 + tn], in_=psb[:, 0:tn], func=AF.Sigmoid, bias=a0_s[:, hp:hp + 1], scale=1.0)
                dve.tensor_scalar_mul(out=ldc[:, 0:Wd], in0=ldc[:, 0:Wd], scalar1=-EM05)
                dve.tensor_scalar_mul(out=kkn[:, 0:Wd], in0=kr[:, 0:Wd], scalar1=kk_s[:, hp:hp + 1])
                dve.tensor_tensor(out=tmpA[:, 0:Wd], in0=kkn[:, 0:Wd], in1=kkn[:, 0:Wd], op=ALU.mult)
                for (t0, tn) in ct_tiles:
                    psb = PS[6]
                    pe.matmul(out=psb[:, 0:tn], lhsT=onesbd[:], rhs=tmpA[:, t0:t0 + tn], start=True, stop=True)
                    act.activation(out=bb[:, t0:t0 + tn], in_=psb[:, 0:tn], func=AF.Sqrt)
                dve.tensor_scalar_max(out=bb[:, 0:Wd], in0=bb[:, 0:Wd], scalar1=1e-12)
                dve.reciprocal(out=bb[:, 0:Wd], in_=bb[:, 0:Wd])
                dve.tensor_tensor(out=kkn[:, 0:Wd], in0=kkn[:, 0:Wd], in1=bb[:, 0:Wd], op=ALU.mult)
                dve.tensor_scalar(out=kmod[:, 0:Wd], in0=alr[:, 0:Wd], scalar1=-1.0, scalar2=ka_s[:, hp:hp + 1], op0=ALU.add, op1=ALU.mult)
                dve.scalar_tensor_tensor(out=kmod[:, 0:Wd], in0=kmod[:, 0:Wd], scalar=1.0, in1=kr[:, 0:Wd], op0=ALU.add, op1=ALU.mult)
                dve.tensor_tensor(out=bb[:, 0:Wd], in0=kkn[:, 0:Wd], in1=alr[:, 0:Wd], op=ALU.mult)
                if not state_only:
                    dve.scalar_tensor_tensor(out=tmpA[:, 0:Wd], in0=rr[:, 0:Wd], scalar=rk_s[:, hp:hp + 1], in1=kmod[:, 0:Wd], op0=ALU.mult, op1=ALU.mult)
                for (t0, tn) in ([] if state_only else ct_tiles):
                    psb = PS[6]
                    pe.matmul(out=psb[:, 0:tn], lhsT=onesbd[:], rhs=tmpA[:, t0:t0 + tn], start=True, stop=True)
                    dve.tensor_tensor(out=bonus[:, t0:t0 + tn], in0=psb[:, 0:tn], in1=vv[:, t0:t0 + tn], op=ALU.mult)
                dve.tensor_tensor_scan(out=cl[:], data0=resetm[:], data1=ldc[:, 0:TP], initial=0.0, op0=ALU.mult, op1=ALU.add)
                dve.tensor_copy(out=cend[:], in_=cl[:].re("p (c t) -> p c t", t=64)[:, :, 63])
                act.activation(out=PC[:], in_=cend[:], func=AF.Exp)
                c4 = lambda t: t[:, 0:TP].re("p (c t) -> p c t", t=64)
                act.activation(out=e1[:], in_=cl[:], func=AF.Exp)
                dve.tensor_tensor(out=c4(AR)[:, :, :] if False else AR[:, :, 1, :], in0=c4(rr), in1=c4(e1), op=ALU.mult)
                dve.tensor_tensor(out=e2[:], in0=cl[:], in1=ldc[:, 0:TP], op=ALU.subtract)
                act.activation(out=e2[:], in_=e2[:], func=AF.Exp)
                dve.scalar_tensor_tensor(out=AR[:, :, 0, :], in0=c4(kkn), scalar=-1.0, in1=c4(e2), op0=ALU.mult, op1=ALU.mult)
                dve.tensor_copy(out=AV[:, :, 0, :], in_=AR[:, :, 0, :])
                dve.tensor_copy(out=AV[:, :, 1, :], in_=c4(vv))
                act.activation(out=e1[:], in_=cl[:], func=AF.Exp, scale=-1.0)
                dve.tensor_tensor(out=BK[:, :, 0, :], in0=c4(bb), in1=c4(e1), op=ALU.mult)
                dve.tensor_tensor(out=BK[:, :, 1, :], in0=c4(kmod), in1=c4(e1), op=ALU.mult)
                dve.tensor_tensor(out=c4(e2), in0=cend[:, :, None].bc([P, NCH, 64]), in1=c4(cl), op=ALU.subtract)
                act.activation(out=e2[:], in_=e2[:], func=AF.Exp)
                dve.tensor_tensor(out=BKH[:, :, 0, :], in0=c4(bb), in1=c4(e2), op=ALU.mult)
                dve.tensor_tensor(out=BKH[:, :, 1, :], in0=c4(kmod), in1=c4(e2), op=ALU.mult)
                def fl(v):
                    return v.re("p a b -> p (a b)")

                def stage1(pairs, sb):
                    n = len(pairs)
                    for i, (c, e) in enumerate(pairs):
                        pb = 64 * e; B = PS[i]
                        pe.matmul(out=B[:, 0:128], lhsT=fl(BK[pb:pb + 64, c, :, :]), rhs=fl(AR[pb:pb + 64, c, :, :]),
                                  start=True, stop=True, tile_position=(pb, 0))
                        pe.transpose(out=B[:, 128:192], in_=fl(AV[pb:pb + 64, c, :, :]), identity=ident[pb:pb + 64, pb:pb + 64],
                                     tile_position=(pb, 0))
                        pe.transpose(out=B[:, 192:256], in_=fl(BKH[pb:pb + 64, c, :, :]), identity=ident[pb:pb + 64, pb:pb + 64],
                                     tile_position=(pb, 0))
                    for i, (c, e) in enumerate(pairs):
                        B = PS[i]; r = sb + i
                        dve.tensor_tensor(out=GTs[r][:], in0=B[:, 0:128], in1=maskgt[:], op=ALU.mult)
                        act.copy(out=X[r][:], in_=B[:, 128:192])
                        act.copy(out=BKHt[r][:], in_=B[:, 192:256])
                    yield
                    for i, (c, e) in enumerate(pairs):
                        pe.transpose(out=PS[i][0:64, 0:64], in_=GTs[sb + i][0:64, 0:64], identity=ident[0:64, 0:64])
                    for i, (c, e) in enumerate(pairs):
                        r = sb + i; pw = PW[i]
                        act.copy(out=pw[:, 1, :], in_=PS[i][0:64, 0:64])
                        dve.tensor_copy(out=pw[:, 0, :], in_=GTs[r][0:64, 0:64])
                        dve.tensor_tensor(out=pw[:, 2, :], in0=GTs[r][0:64, 0:64], in1=ident[0:64, 0:64], op=ALU.add)
                    yield
                    for lvl in range(1, 6):
                        for i in range(n):
                            pw = PW[i]; B = PS[i]
                            pe.matmul(out=B[0:64, 0:64], lhsT=pw[:, 1, :], rhs=pw[:, 0, :], start=True, stop=True)
                            pe.matmul(out=B[0:64, 64:128], lhsT=pw[:, 0, :], rhs=pw[:, 1, :], start=True, stop=True)
                        for i in range(n):
                            act.copy(out=fl(PW[i][:, 0:2, :]), in_=PS[i][0:64, 0:128])
                        yield
                        for i in range(n):
                            pw = PW[i]
                            pe.matmul(out=PS[i][0:64, 128:192], lhsT=pw[:, 1, :], rhs=pw[:, 2, :], start=True, stop=True)
                        for i in range(n):
                            pw = PW[i]
                            dve.tensor_tensor(out=pw[:, 2, :], in0=pw[:, 2, :], in1=PS[i][0:64, 128:192], op=ALU.add)
                        yield
                    for i, (c, e) in enumerate(pairs):
                        pb = 64 * e; r = sb + i; B = PS[i]
                        pe.matmul(out=B[0:64, 256:320], lhsT=GTs[r][64:128, 0:64], rhs=X[r][64:128, :], start=True, stop=True,
                                  tile_position=(64, 0))
                    for i, (c, e) in enumerate(pairs):
                        pb = 64 * e; r = sb + i; B = PS[i]
                        pe.matmul(out=B[pb:pb + 64, 320:384], lhsT=X[r][0:64, :], rhs=PW[i][:, 2, :], start=True, stop=True,
                                  tile_position=(0, pb))
                    for i, (c, e) in enumerate(pairs):
                        pb = 64 * e; r = sb + i; B = PS[i]
                        act.copy(out=PW[i][:, 0, :], in_=B[0:64, 256:320])
                        act.copy(out=WTs[r][pb:pb + 64, :], in_=B[pb:pb + 64, 320:384])
                    yield
                    for i in range(n):
                        pe.matmul(out=PS[i][0:64, 384:448], lhsT=PW[i][:, 2, :], rhs=PW[i][:, 0, :], start=True, stop=True)
                    for i in range(n):
                        act.copy(out=Ys[sb + i][:], in_=PS[i][0:64, 384:448])
                    yield

                def stage2(chs, sb):
                    for ci, c in enumerate(chs):
                        for e in range(2):
                            pb = 64 * e; r = sb + 2 * ci + e; Hs = Hst[2 * hp + e]
                            pe.matmul(out=PS[6 + e][0:64, 0:64], lhsT=WTs[r][pb:pb + 64, :], rhs=Hs[pb:pb + 64, :], start=True, stop=True,
                                      tile_position=(pb, 0))
                        for e in range(2):
                            r = sb + 2 * ci + e
                            dve.tensor_tensor(out=X[r][0:64, :], in0=PS[6 + e][0:64, 0:64], in1=Ys[r][:], op=ALU.add)
                        yield
                        for e in range(2):
                            pb = 64 * e; r = sb + 2 * ci + e; Hs = Hst[2 * hp + e]; B = PS[6 + e]
                            if not state_only:
                                pe.matmul(out=B[pb:pb + 64, 64:128], lhsT=Hs[pb:pb + 64, :], rhs=AR[pb:pb + 64, c, 1, :], start=True, stop=True,
                                          tile_position=(pb, pb))
                                pe.matmul(out=B[pb:pb + 64, 128:192], lhsT=X[r][:], rhs=GTs[r][:, 64:128], start=True, stop=True,
                                          tile_position=(0, pb))
                            pe.matmul(out=B[pb:pb + 64, 192:256], lhsT=BKHt[r][:], rhs=X[r][:], start=True, stop=True, tile_position=(0, pb))
                        for e in range(2):
                            pb = 64 * e; Hs = Hst[2 * hp + e]; B = PS[6 + e]
                            if not state_only:
                                act.copy(out=oT[pb:pb + 64, c * 64:(c + 1) * 64], in_=B[pb:pb + 64, 64:128])
                                dve.tensor_tensor(out=oT[pb:pb + 64, c * 64:(c + 1) * 64], in0=oT[pb:pb + 64, c * 64:(c + 1) * 64],
                                                  in1=B[pb:pb + 64, 128:192], op=ALU.add)
                            dve.scalar_tensor_tensor(out=Hs[pb:pb + 64, :], in0=Hs[pb:pb + 64, :], scalar=PC[pb:pb + 64, c:c + 1],
                                                     in1=B[pb:pb + 64, 192:256], op0=ALU.mult, op1=ALU.add)
                        yield

                batches = [list(range(c0, min(c0 + CB, NCH))) for c0 in range(0, NCH, CB)]
                prev = None
                for bi in range(len(batches) + 1):
                    g1 = None; g2 = None
                    if bi < len(batches):
                        chs = batches[bi]
                        g1 = stage1([(c, e) for c in chs for e in range(2)], (bi % 2) * NSL)
                    if prev is not None:
                        g2 = stage2(prev[0], prev[1])
                    while g1 is not None or g2 is not None:
                        for _ in range(2):
                            if g1 is not None:
                                try:
                                    next(g1)
                                except StopIteration:
                                    g1 = None
                        if g2 is not None:
                            try:
                                next(g2)
                            except StopIteration:
                                g2 = None
                    prev = (batches[bi], (bi % 2) * NSL) if bi < len(batches) else None
                if is_last:
                    rwkv_sample(hp, rr, ldc, kmod, vv, kkn, bb, oT, tmpA)
                if state_only:
                    continue
                dve.tensor_tensor(out=osq[:, 0:TP], in0=oT[:, 0:TP], in1=oT[:, 0:TP], op=ALU.mult)
                for (t0, tn) in C.TT:
                    psm = PS[6]; psq = PS[7]
                    pe.matmul(out=psm[:, 0:tn], lhsT=onesbd[:], rhs=oT[:, t0:t0 + tn], start=True, stop=True)
                    pe.matmul(out=psq[:, 0:tn], lhsT=onesbd[:], rhs=osq[:, t0:t0 + tn], start=True, stop=True)
                    mean = tmpA[:, t0:t0 + tn]; var = osq[:, t0:t0 + tn]
                    act.mul(out=mean, in_=psm[:, 0:tn], mul=1.0 / 64)
                    dve.tensor_tensor(out=e1[:, 0:tn], in0=mean, in1=mean, op=ALU.mult)
                    dve.scalar_tensor_tensor(out=var, in0=psq[:, 0:tn], scalar=1.0 / 64, in1=e1[:, 0:tn], op0=ALU.mult, op1=ALU.subtract)
                    act.activation(out=var, in_=var, func=AF.Sqrt, bias=eps_gn[:], scale=1.0)
                    dve.reciprocal(out=var, in_=var)
                    dve.tensor_tensor(out=oT[:, t0:t0 + tn], in0=oT[:, t0:t0 + tn], in1=mean, op=ALU.subtract)
                    dve.tensor_tensor(out=oT[:, t0:t0 + tn], in0=oT[:, t0:t0 + tn], in1=var, op=ALU.mult)
                dve.tensor_scalar(out=oT[:, 0:TP], in0=oT[:, 0:TP], scalar1=gnw_s[:, hp:hp + 1], scalar2=gnb_s[:, hp:hp + 1], op0=ALU.mult, op1=ALU.add)
                dve.tensor_tensor(out=oT[:, 0:TP], in0=oT[:, 0:TP], in1=bonus[:, 0:TP], op=ALU.add)
                w = load_w(w_in, C.OFF_RWZ // 128 + hp)
                for (t0, tn) in C.TT:
                    psz = PS[6]
                    mm_acc(psz[:, 0:tn], w, KC, lambda kc: hT[:, kc, t0:t0 + tn].r32())
                    act.activation(out=tmpA[:, t0:t0 + tn], in_=psz[:, 0:tn], func=AF.Silu)
                    dve.tensor_tensor(out=orT[:, hp, t0:t0 + tn], in0=oT[:, t0:t0 + tn], in1=tmpA[:, t0:t0 + tn], op=ALU.mult)
            barrier()

    NPT = (NSC * C.H) // 128 if NSC * C.H >= 128 else 1
    NPP = min(128, NSC * C.H)
    BPT = NPP // C.H
    tokm = fw.sb("tokm", [NSC, 2, 6, 64])
    o_bh_all = fw.sb("o_bh_all", [NPP, NPT, 64])

    def rwkv_sample(hp, rr, ldc, kmod, vv, kkn, bb, oT, tmpA):
        act.activation(out=tmpA[:, TP:W1], in_=ldc[:, TP:W1], func=AF.Exp)
        vecs = [rr, tmpA, kmod, vv, kkn, bb]
        for vi, t in enumerate(vecs):
            pe.transpose(out=PS[6 + vi // 3][0:NSC, (vi % 3) * 128:(vi % 3 + 1) * 128], in_=t[:, TP:W1], identity=ident[:])
        act.copy(out=tokm[:, :, 0:3, :], in_=PS[6][0:NSC, 0:384].re("p (v h n) -> p h v n", v=3, h=2))
        act.copy(out=tokm[:, :, 3:6, :], in_=PS[7][0:NSC, 0:384].re("p (v h n) -> p h v n", v=3, h=2))
        st(scr_a[:, 2 * hp:2 * hp + 2, :, :], tokm[:])

    def rwkv_sample_core():
        with fw.scope():
            lsb = fw.sb
            S = lsb("S_s", [NPP, 64, 64]); T1 = lsb("T1_s", [NPP, 64, 64]); vec = lsb("vec_s", [NPP, 6, 64])
            sa = lsb("sa_s", [NPP, 64]); ob = lsb("ob_s", [NPP, 64]); st1 = lsb("st1_s", [NPP, 4])
            gw = lsb("gw_s", [NPP, 64]); gb = lsb("gb_s", [NPP, 64]); rkb = lsb("rkb_s", [NPP, 64])
            ld(gw[:], gnw_bh[0:NPP, :]); ld(gb[:], gnb_bh[0:NPP, :]); ld(rkb[:], rk_bh[0:NPP, :])
            for pt in range(NPT):
                ld(S[:].re("p a b -> p (a b)"), wkv0[pt * NPP:(pt + 1) * NPP, :])
                ld(vec[:], View(scr_a.buf, scr_a.ap[pt * BPT:(pt + 1) * BPT].rearrange("b h v n -> (b h) v n")))
                R = vec[:, 0, :]; Dc = vec[:, 1, :]; K = vec[:, 2, :]; V = vec[:, 3, :]; KK = vec[:, 4, :]; Bv = vec[:, 5, :]
                bj = lambda v: v[:, None, :].bc([NPP, 64, 64])
                bi_ = lambda v: v[:, :, None].bc([NPP, 64, 64])
                dve.tensor_tensor(out=T1[:], in0=S[:], in1=bj(KK), op=ALU.mult)
                dve.tensor_reduce(out=sa[:], in_=T1[:], axis=AX.X, op=ALU.add)
                dve.tensor_scalar_mul(out=sa[:], in0=sa[:], scalar1=-1.0)
                dve.tensor_tensor(out=S[:], in0=S[:], in1=bj(Dc), op=ALU.mult)
                dve.tensor_tensor(out=T1[:], in0=bi_(sa[:]), in1=bj(Bv), op=ALU.mult)
                dve.tensor_tensor(out=S[:], in0=S[:], in1=T1[:], op=ALU.add)
                dve.tensor_tensor(out=T1[:], in0=bi_(V), in1=bj(K), op=ALU.mult)
                dve.tensor_tensor(out=S[:], in0=S[:], in1=T1[:], op=ALU.add)
                st(swkv_o[pt * NPP:(pt + 1) * NPP, :], S[:].re("p a b -> p (a b)"))
                dve.tensor_tensor(out=T1[:], in0=S[:], in1=bj(R), op=ALU.mult)
                dve.tensor_reduce(out=ob[:], in_=T1[:], axis=AX.X, op=ALU.add)
                dve.tensor_reduce(out=st1[:, 0:1], in_=ob[:], axis=AX.X, op=ALU.add)
                dve.tensor_scalar_mul(out=st1[:, 0:1], in0=st1[:, 0:1], scalar1=1.0 / 64)
                dve.tensor_scalar(out=ob[:], in0=ob[:], scalar1=st1[:, 0:1], scalar2=None, op0=ALU.subtract)
                dve.tensor_tensor(out=sa[:], in0=ob[:], in1=ob[:], op=ALU.mult)
                dve.tensor_reduce(out=st1[:, 1:2], in_=sa[:], axis=AX.X, op=ALU.add)
                act.activation(out=st1[:, 1:2], in_=st1[:, 1:2], func=AF.Sqrt, bias=eps_gn[0:NPP, :], scale=1.0 / 64)
                dve.reciprocal(out=st1[:, 1:2], in_=st1[:, 1:2])
                dve.tensor_scalar(out=ob[:], in0=ob[:], scalar1=st1[:, 1:2], scalar2=None, op0=ALU.mult)
                dve.tensor_tensor(out=ob[:], in0=ob[:], in1=gw[:], op=ALU.mult)
                dve.tensor_tensor(out=ob[:], in0=ob[:], in1=gb[:], op=ALU.add)
                dve.tensor_tensor(out=sa[:], in0=R, in1=K, op=ALU.mult)
                dve.tensor_tensor(out=sa[:], in0=sa[:], in1=rkb[:], op=ALU.mult)
                dve.tensor_reduce(out=st1[:, 2:3], in_=sa[:], axis=AX.X, op=ALU.add)
                dve.scalar_tensor_tensor(out=o_bh_all[:, pt, :], in0=V, scalar=st1[:, 2:3], in1=ob[:], op0=ALU.mult, op1=ALU.add)
                st(View(scr_b.buf, scr_b.ap[pt * BPT:(pt + 1) * BPT].rearrange("b (h n) -> (b h) n", n=64)), o_bh_all[:, pt, :])
            otok = lsb("otok_s", [NSC, C.DR]); ld(otok[:], scr_b[:])
            zt_ = lsb("zt_s", [P, NSC]); of_ = lsb("of_s", [P, NSC])
            for hp in range(NHP):
                pe.transpose(out=PS[6][:, 0:NSC], in_=otok[:, hp * 128:(hp + 1) * 128], identity=ident[0:NSC, 0:NSC])
                act.copy(out=of_[:], in_=PS[6][:, 0:NSC])
                w = load_w(w_in, C.OFF_RWZ // 128 + hp)
                mm_acc(PS[7][:, 0:NSC], w, KC, lambda kc: hT[:, kc, TP:TP + NSC].r32())
                act.activation(out=zt_[:], in_=PS[7][:, 0:NSC], func=AF.Silu)
                dve.tensor_tensor(out=orT[:, hp, TP:TP + NSC], in0=of_[:], in1=zt_[:], op=ALU.mult)
            barrier()

    def post_phase(is_last, p, yp):
        with fw.scope():
            lsb = fw.sb
            NTT = TP // 128
            xn = [lsb("xn%d" % i, [P, D]) for i in range(NTT)]
            xns = lsb("xns", [NSC, D]) if is_last else None
            g1 = lsb("g1", [P, TW]); g2 = lsb("g2", [P, TW]); MT = lsb("MT", [P, TW])
            sq = lsb("sqp", [P, D], mybir.dt.bfloat16); ssp = lsb("ssp", [P, NTT + 1])
            fg = lsb("fg", [P, D]); ld(fg[:], fing[:])
            pt_tiles = list(C.TT) + ([(TP, NSC)] if is_last else [])
            for tt in range(NTT):
                ld(xn[tt][:], xb[p * TP + tt * 128:p * TP + (tt + 1) * 128, :])
            if is_last:
                ld(xns[:], xs[:])
            for db in range(KC):
                wg1 = load_w(w_in, C.OFF_G1 // 128 + db)
                for (t0, tn) in pt_tiles:
                    psg = PS[0 + (t0 // 512) % 2]
                    mm_acc(psg[:, 0:tn], wg1, KC, lambda kc: hT[:, kc, t0:t0 + tn].r32())
                    act.activation(out=g1[:, t0:t0 + tn], in_=psg[:, 0:tn], func=AF.Sigmoid)
                wo = load_w(w_out, db)
                for (t0, tn) in pt_tiles:
                    ps1 = PS[2 + (t0 // 512) % 2]
                    mm_acc(ps1[:, 0:tn], wo, NUB, lambda kc: osT[:, kc, t0:t0 + tn])
                    dve.tensor_tensor(out=MT[:, t0:t0 + tn], in0=g1[:, t0:t0 + tn], in1=ps1[:, 0:tn], op=ALU.mult)
                wg2 = load_w(w_in, C.OFF_G2 // 128 + db)
                for (t0, tn) in pt_tiles:
                    psg2 = PS[0 + (t0 // 512) % 2]
                    mm_acc(psg2[:, 0:tn], wg2, KC, lambda kc: hT[:, kc, t0:t0 + tn].r32())
                    act.activation(out=g2[:, t0:t0 + tn], in_=psg2[:, 0:tn], func=AF.Sigmoid)
                    ps2 = PS[2 + (t0 // 512) % 2]
                    for kc in range(NHP):
                        pe.matmul(out=ps2[:, 0:tn], lhsT=wo[:, NUB + kc, :].r32(), rhs=orT[:, kc, t0:t0 + tn],
                                  start=(kc == 0), stop=(kc == NHP - 1), inc=(kc == NHP - 1))
                    dve.tensor_tensor(out=g2[:, t0:t0 + tn], in0=g2[:, t0:t0 + tn], in1=ps2[:, 0:tn], op=ALU.mult)
                    dve.tensor_tensor(out=MT[:, t0:t0 + tn], in0=MT[:, t0:t0 + tn], in1=g2[:, t0:t0 + tn], op=ALU.add)
                    if t0 < TP:
                        dve.tensor_scalar_mul(out=MT[:, t0:t0 + tn], in0=MT[:, t0:t0 + tn], scalar1=gt_p[:, db:db + 1])
                        for q in range(tn // 128):
                            tt = t0 // 128 + q
                            pst = PS[4 + q % 2]
                            pe.transpose(out=pst[:, 0:128], in_=MT[:, t0 + q * 128:t0 + (q + 1) * 128], identity=ident[:])
                            dve.tensor_tensor(out=xn[tt][:, db * 128:(db + 1) * 128], in0=xn[tt][:, db * 128:(db + 1) * 128],
                                              in1=pst[:, 0:128], op=ALU.add)
                    else:
                        dve.tensor_tensor(out=MT[:, TP:TP + NSC], in0=MT[:, TP:TP + NSC], in1=modT[:, 2 * KC + db, 0:NSC], op=ALU.mult)
                        pst = PS[6]
                        pe.transpose(out=pst[0:NSC, 0:128], in_=MT[:, TP:TP + NSC], identity=ident[:])
                        dve.tensor_tensor(out=xns[:, db * 128:(db + 1) * 128], in0=xns[:, db * 128:(db + 1) * 128],
                                          in1=pst[0:NSC, 0:128], op=ALU.add)
            for tt in range(NTT):
                s = ssp[:, tt:tt + 1]
                act.activation(out=sq[:], in_=xn[tt][:], func=AF.Square, accum_out=s)
                rstd_of(s, 128)
                dve.scalar_tensor_tensor(out=xn[tt][:], in0=xn[tt][:], scalar=s, in1=fg[:], op0=ALU.mult, op1=ALU.mult)
                st(y_o[yp * TP + tt * 128:yp * TP + (tt + 1) * 128, :], xn[tt][:])
            if is_last:
                s = ssp[0:NSC, NTT:NTT + 1]
                act.activation(out=sq[0:NSC, :], in_=xns[:], func=AF.Square, accum_out=s)
                rstd_of(s, NSC)
                dve.scalar_tensor_tensor(out=xns[:], in0=xns[:], scalar=s, in1=fg[0:NSC, :], op0=ALU.mult, op1=ALU.mult)
                st(ys_o[:], xns[:])
            barrier()

    def sample_prep():
        with fw.scope():
            lsb = fw.sb
            xs_t = lsb("xs_t", [NSC, D]); sh_t = lsb("sh_t", [NSC, D]); sqs = lsb("sqs", [NSC, D]); s1 = lsb("s1s", [NSC, 1])
            xsT = lsb("xsT", [P, KC, NSC]); hs_tok = lsb("hs_tok", [NSC, D]); stg = lsb("stg", [NSC, NSB * 128])
            ld(xs_t[:], xs[:]); ld(sh_t[:], sshift[:])
            act.activation(out=sqs[:], in_=xs_t[:], func=AF.Square, accum_out=s1[:])
            rstd_of(s1[:], NSC)
            act.activation(out=xs_t[:], in_=xs_t[:], func=AF.Copy, scale=s1[:])
            for kc in range(KC):
                pe.transpose(out=PS[0][:, kc * NSC:(kc + 1) * NSC], in_=xs_t[:, kc * 128:(kc + 1) * 128], identity=ident[0:NSC, 0:NSC])
                pe.transpose(out=PS[1][:, kc * NSC:(kc + 1) * NSC], in_=sh_t[:, kc * 128:(kc + 1) * 128], identity=ident[0:NSC, 0:NSC])
            dve.tensor_tensor(out=xsT[:], in0=PS[0][:, 0:KC * NSC].re("p (k n) -> p k n", n=NSC), in1=sceff[:, :, 0:NSC], op=ALU.mult)
            dve.tensor_tensor(out=xsT[:], in0=xsT[:], in1=modT[:, 0:KC, 0:NSC], op=ALU.add)
            dve.tensor_copy(out=hT[:, :, TP:TP + NSC], in_=xsT[:])
            act.copy(out=hT[:, :, TP + NSC:TP + NS2], in_=PS[1][:, 0:KC * NSC].re("p (k n) -> p k n", n=NSC))
            for kc in range(KC):
                pe.transpose(out=PS[2 + kc // 4 % 2][0:NSC, (kc % 4) * 128:(kc % 4 + 1) * 128], in_=xsT[:, kc, :], identity=ident[:])
                act.copy(out=hs_tok[:, kc * 128:(kc + 1) * 128], in_=PS[2 + kc // 4 % 2][0:NSC, (kc % 4) * 128:(kc % 4 + 1) * 128])
            st(sshift_o[:], hs_tok[:])
            for (src, dst) in ((s5re0, x0r), (s5im0, x0i)):
                ld(stg[:], src[:])
                for s in range(NSB):
                    pe.transpose(out=PS[4][:, s * NSC:(s + 1) * NSC], in_=stg[:, s * 128:(s + 1) * 128], identity=ident[0:NSC, 0:NSC])
                act.copy(out=dst[:].re("p s n -> p (s n)"), in_=PS[4][:, 0:NSB * NSC])
            barrier()

    def finals():
        with fw.scope():
            lsb = fw.sb
            xf = [lsb("xf%d" % i, [P, NSB]) for i in range(4)]
            dve.tensor_tensor(out=xf[0][:], in0=cs1[:], in1=s5car_r[:], op=ALU.mult)
            dve.tensor_tensor(out=xf[1][:], in0=sn1[:], in1=s5car_i[:], op=ALU.mult)
            dve.tensor_tensor(out=xf[0][:], in0=xf[0][:], in1=xf[1][:], op=ALU.add)
            dve.tensor_tensor(out=xf[2][:], in0=cs1[:], in1=s5car_i[:], op=ALU.mult)
            dve.tensor_tensor(out=xf[3][:], in0=sn1[:], in1=s5car_r[:], op=ALU.mult)
            dve.tensor_tensor(out=xf[2][:], in0=xf[2][:], in1=xf[3][:], op=ALU.subtract)
            xo = lsb("xfo", [NSB, 2, P])
            pe.transpose(out=PS[0][0:NSB, 0:128], in_=xf[0][:], identity=ident[:])
            pe.transpose(out=PS[0][0:NSB, 128:256], in_=xf[2][:], identity=ident[:])
            act.copy(out=xo[:].re("p a b -> p (a b)"), in_=PS[0][0:NSB, 0:256])
            st(ps5re_o[:], xo[:, 0, :]); st(ps5im_o[:], xo[:, 1, :])
            ho = lsb("hlo", [KC, P])
            pe.transpose(out=PS[1][0:KC, 0:128], in_=hlast[:], identity=ident[:])
            act.copy(out=ho[:], in_=PS[1][0:KC, 0:128])
            st(pshift_o[:], ho[:])
            so = lsb("wkvo", [64, C.H, 64])
            for h in range(C.H):
                pb = 64 * (h % 2)
                pe.transpose(out=PS[2][0:64, (h % 8) * 64:(h % 8 + 1) * 64], in_=Hst[h][pb:pb + 64, :], identity=ident[pb:pb + 64, pb:pb + 64],
                             tile_position=(pb, 0))
                act.copy(out=so[:, h, :], in_=PS[2][0:64, (h % 8) * 64:(h % 8 + 1) * 64])
            st(View(pwkv_o.buf, pwkv_o.ap.rearrange("h i j -> i h j")), so[:])
            so2 = lsb("s5o", [NSC, 512])
            for (src, dsto) in ((x1r, ss5re_o), (x1i, ss5im_o)):
                for g4 in range(NSB // 4):
                    for j in range(4):
                        pe.transpose(out=PS[3][0:NSC, j * 128:(j + 1) * 128], in_=src[:, g4 * 4 + j, :], identity=ident[:])
                    act.copy(out=so2[:], in_=PS[3][0:NSC, :])
                    st(dsto[:, g4 * 512:(g4 + 1) * 512], so2[:])

    try:
      for p in range(NPASS):
          if stop(1):
              break
          is_last = (p == NPASS - 1)
          if is_last:
              sample_prep()
          phase_x(View(xb.buf, xb.ap[p * TP:(p + 1) * TP, :]), is_last)
          barrier()
          if stop(2):
              break
          NA = NPASS // 2
          so = p < NA
          s5_phase(is_last, so)
          if stop(3):
              break
          rwkv_phase(is_last, so)
          if stop(4):
              break
          if p == NA - 1:
              dve.tensor_scalar_mul(out=s5car_r[:], in0=s5car_r[:], scalar1=flag_s[:, 0:1])
              dve.tensor_scalar_mul(out=s5car_i[:], in0=s5car_i[:], scalar1=flag_s[:, 0:1])
              dve.tensor_scalar_mul(out=rw_car[:], in0=rw_car[:], scalar1=flag_s[:, 0:1])
              for hh in range(C.H):
                  dve.tensor_scalar_mul(out=Hst[hh][:], in0=Hst[hh][:], scalar1=flag_s[:, 0:1])
          if so:
              continue
          if is_last:
              rwkv_sample_core()
          post_phase(is_last, p, p - NA)
          if stop(5):
              break
      if not stop(6):
          finals()
    except StopBuild:
        pass
    fw.finish()
    return nc, fw, es


def _blk(w, kcn):
    K, N = w.shape
    return np.ascontiguousarray(w.reshape(kcn, 128, N // 128, 128).transpose(2, 1, 0, 3))


def _fm(v):
    v = np.asarray(v).reshape(-1)
    return np.ascontiguousarray(v.reshape(-1, 128).T)


def make_in_maps(cfg, inp, n_cores, n_seq, single_half=None):
    C = cfg
    f = np.float32
    H = C.H; NSB = C.NSB
    sh = {}
    sh["w_ada"] = _blk(inp["w_ada"][0], C.KC); sh["w_in"] = _blk(inp["w_in"][0], C.KC)
    sh["w_glu"] = _blk(inp["w_glu"][0], C.NUB); sh["w_out"] = _blk(inp["w_out"][0], C.KC)
    sh["b_ada"] = _fm(inp["b_ada"][0]); sh["norm_g"] = _fm(inp["norm_g"][0]); sh["mu"] = _fm(inp["mu_rw"][0])
    sh["A_re"] = _fm(inp["A_re"][0]); sh["A_im"] = _fm(inp["A_im"][0])
    sh["lstep"] = np.ascontiguousarray(np.repeat(inp["log_step"][0].reshape(NSB, 2), 64, axis=1).T)
    for k in ("B_re", "B_im"):
        sh[k] = np.ascontiguousarray(inp[k][0].reshape(NSB, 2, 64, 16).transpose(1, 2, 0, 3).reshape(128, NSB, 16))
    for k in ("C_re", "C_im"):
        sh[k] = np.ascontiguousarray(inp[k][0].reshape(NSB, 2, 16, 64).transpose(1, 3, 0, 2).reshape(128, NSB, 16))
    sh["Dsk"] = _fm(inp["D_skip"][0]); sh["b_glu"] = _fm(inp["b_glu"][0])
    for k in ("w0", "a0", "k_k", "k_a", "r_k", "gn_w", "gn_b"):
        sh[k] = _fm(inp[k][0])
    sh["w2"] = np.ascontiguousarray(inp["w2"][0]); sh["a2"] = np.ascontiguousarray(inp["a2"][0])
    sh["fing"] = np.ascontiguousarray(np.broadcast_to(inp["final_g"].reshape(1, -1), (128, C.D)))
    rep = max(1, 128 // H)
    sh["gnw_bh"] = np.ascontiguousarray(np.tile(inp["gn_w"][0].reshape(H, 64), (rep, 1))[:128])
    sh["gnb_bh"] = np.ascontiguousarray(np.tile(inp["gn_b"][0].reshape(H, 64), (rep, 1))[:128])
    sh["rk_bh"] = np.ascontiguousarray(np.tile(inp["r_k"][0].reshape(H, 64), (rep, 1))[:128])
    if sh["gnw_bh"].shape[0] < 128:
        for k in ("gnw_bh", "gnb_bh", "rk_bh"):
            sh[k] = np.ascontiguousarray(np.concatenate([sh[k], np.zeros((128 - sh[k].shape[0], 64), f)], 0))
    sh["ident"] = np.eye(128, dtype=f)
    ms = np.triu(np.ones((64, 64), f), 1); mi_ = np.triu(np.ones((64, 64), f), 0)
    sh["maskgt"] = np.block([[ms, mi_], [ms, mi_]]).astype(f)
    ob = np.zeros((128, 128), f); ob[:64, :64] = 1; ob[64:, 64:] = 1
    sh["onesbd"] = ob
    rm = np.ones((128, C.TP), f); rm[:, ::64] = 0
    sh["resetm"] = rm
    rmk = np.zeros((128, 4), f)
    for j in range(4):
        rmk[32 * j:32 * j + 32, j] = 1
    sh["rowmask"] = rmk
    sh["iota"] = np.ascontiguousarray(np.broadcast_to(np.arange(128, dtype=f).reshape(1, -1), (128, 128)))
    sh = {k: np.ascontiguousarray(v, dtype=f) for k, v in sh.items()}
    maps = []
    NSC = C.NSC
    for c in range(n_cores):
        b = c % n_seq
        half = (c // n_seq) if single_half is None else single_half
        rows = slice(c * NSC, (c + 1) * NSC)
        m = dict(sh)
        HL = C.TP * (C.NPASS // 2)
        xp = inp["x_prompt"][b]
        m["xb"] = np.ascontiguousarray(np.concatenate([xp[0:HL], xp[half * HL:(half + 1) * HL]], 0), dtype=f)
        m["flag"] = np.full((128, 1), float(half), f)
        cp = inp["c_prompt"][b:b + 1]
        m["c17"] = np.ascontiguousarray(np.concatenate([inp["c_sample"][rows], cp, cp], 0), dtype=f)
        m["xs"] = np.ascontiguousarray(inp["x_sample"][rows, 0, :], dtype=f)
        m["sshift"] = np.ascontiguousarray(inp["state_shift"][0, rows], dtype=f)
        m["s5re0"] = np.ascontiguousarray(inp["state_s5_re"][0, rows].reshape(NSC, -1), dtype=f)
        m["s5im0"] = np.ascontiguousarray(inp["state_s5_im"][0, rows].reshape(NSC, -1), dtype=f)
        m["wkv0"] = np.ascontiguousarray(inp["state_wkv"][0, rows].reshape(NSC * H, 4096), dtype=f)
        maps.append(m)
    return maps


def assemble(cfg, res, n_cores, n_seq):
    C = cfg; f = np.float32
    G = C.G; H = C.H; NSC = C.NSC
    R0 = lambda c, k: np.asarray(res[c][k], dtype=f)
    if n_cores >= 2 * n_seq:
        y_p = np.stack([np.concatenate([R0(b, "y"), R0(b + n_seq, "y")], 0) for b in range(n_seq)], 0)
        R = lambda c, k: R0(c + n_seq, k) if k in ("ps5re", "ps5im", "pwkv", "pshift") else R0(c, k)
    else:
        y_p = np.stack([R0(b, "y") for b in range(n_seq)], 0)
        R = R0
    y_s = np.concatenate([R(c, "ys") for c in range(n_cores)], 0)[:, None, :]
    re_p = np.stack([R(b, "ps5re").reshape(G, 64) for b in range(n_seq)], 0)[None]
    im_p = np.stack([R(b, "ps5im").reshape(G, 64) for b in range(n_seq)], 0)[None]
    wkv_p = np.stack([R(b, "pwkv") for b in range(n_seq)], 0)[None]
    sh_p = np.stack([R(b, "pshift").reshape(-1) for b in range(n_seq)], 0)[None]
    re_s = np.concatenate([R(c, "ss5re").reshape(NSC, G, 64) for c in range(n_cores)], 0)[None]
    im_s = np.concatenate([R(c, "ss5im").reshape(NSC, G, 64) for c in range(n_cores)], 0)[None]
    wkv_s = np.concatenate([R(c, "swkv").reshape(NSC, H, 64, 64) for c in range(n_cores)], 0)[None]
    sh_s = np.concatenate([R(c, "sshift_o") for c in range(n_cores)], 0)[None]
    return (y_p, y_s, re_p, im_p, wkv_p, sh_p, re_s, im_s, wkv_s, sh_s)


def kernel(**inputs):
    cfg = Cfg()
    inp = {k: np.asarray(v) for k, v in inputs.items()}
    nc, fw, es = build(cfg)
    maps = make_in_maps(cfg, inp, 8, 4)
    res = run_bass_kernel_spmd(nc, maps, core_ids=list(range(8)))
    return assemble(cfg, res.results, 8, 4)
```
